# Optimizing a Trainium2 kernel written in Bass

```python
import jax
import jax.numpy as jnp
from jax import lax
import numpy as np

D_MODEL = 1024
BATCH = 4
SEQ = 4096
DEPTH = 1

CTX_LEN = 256
GRID_W = 64
CHUNK = 64
EPS = 1e-6
GLA_HEADS = 4
GLA_DK = D_MODEL // 2
GLA_DV = D_MODEL
GLA_DK_HEAD = GLA_DK // GLA_HEADS
GLA_DV_HEAD = GLA_DV // GLA_HEADS
GLA_RANK = 16
GLA_TAU = 16.0
MLSTM_HEADS = 4
MLSTM_DK = D_MODEL // 2
MLSTM_DV = D_MODEL
MLSTM_DK_HEAD = MLSTM_DK // MLSTM_HEADS
MLSTM_DV_HEAD = MLSTM_DV // MLSTM_HEADS
MLSTM_CONV = 3
D_FF = -(-8 * D_MODEL // (3 * 256)) * 256
IN_SIZES = (GLA_DK, GLA_DK, GLA_DV, GLA_DV, 2 * GLA_RANK,
            MLSTM_DK, MLSTM_DK, MLSTM_DV, MLSTM_DV, 4 * MLSTM_HEADS,
            2 * D_MODEL)
N_IN = sum(IN_SIZES)

kernel_name = 'hybrid_gla_mlstm_prefix_dit_block'


def _rms_norm(x, g):
    xf = x.astype(jnp.float32)
    y = xf * lax.rsqrt(jnp.mean(xf * xf, axis=-1, keepdims=True) + EPS)
    return (y * g.astype(jnp.float32)).astype(x.dtype)


def _modulate(h, shift, scale):
    return h * (1 + scale) + shift


def _swiglu(u, w_in, w_out):
    a, b = jnp.split(u @ w_in, 2, axis=-1)
    return (jax.nn.silu(a) * b) @ w_out


def _split_cols(a):
    idx = [int(s) for s in np.cumsum(IN_SIZES)[:-1]]
    return jnp.split(a, idx, axis=-1)


def _heads(a, n):
    B, T, C = a.shape
    return a.reshape(B, T, n, C // n).transpose(0, 2, 1, 3)


def _head_rms_norm(h, g, dtype):
    h = h * lax.rsqrt(jnp.mean(h * h, axis=-1, keepdims=True) + EPS)
    B, H, T, d = h.shape
    return (h.transpose(0, 2, 1, 3).reshape(B, T, H * d) * g.astype(jnp.float32)).astype(dtype)


def _to_colmajor(a):
    B, T, C = a.shape
    rows = T // GRID_W
    return a.reshape(B, rows, GRID_W, C).swapaxes(1, 2).reshape(B, T, C)


def _from_colmajor(a):
    B, T, C = a.shape
    rows = T // GRID_W
    return a.reshape(B, GRID_W, rows, C).swapaxes(1, 2).reshape(B, T, C)


def _centred_conv(a, w, b):
    pad = MLSTM_CONV // 2
    T = a.shape[1]
    ap = jnp.pad(a, ((0, 0), (pad, pad), (0, 0)))
    return sum(ap[:, j:j + T] * w[j] for j in range(MLSTM_CONV)) + b


def _chunks(a):
    B, H, T = a.shape[:3]
    a = a.reshape((B, H, T // CHUNK, CHUNK) + a.shape[3:])
    return jnp.moveaxis(a, 2, 0)


def _unchunk(o, T):
    o = jnp.moveaxis(o, 0, 2)
    return o.reshape(o.shape[:2] + (T,) + o.shape[4:])


def _gla_scan(q, k, v, log_a, s0):
    mask = jnp.tril(jnp.ones((CHUNK, CHUNK), dtype=bool))

    def step(s, inp):
        qc, kc, vc, gc = inp
        b = jnp.cumsum(gc, axis=2)
        b_last = b[:, :, -1:, :]
        q_dec = qc * jnp.exp(b)
        scores = jnp.einsum('bhld,bhmd->bhlm', q_dec, kc * jnp.exp(-b))
        scores = jnp.where(mask, scores, 0.0)
        o = (jnp.einsum('bhlm,bhmv->bhlv', scores, vc)
             + jnp.einsum('bhld,bhdv->bhlv', q_dec, s))
        s_new = (jnp.exp(b_last[:, :, 0, :, None]) * s
                 + jnp.einsum('bhld,bhlv->bhdv', kc * jnp.exp(b_last - b), vc))
        return s_new, o

    s_fin, o = lax.scan(step, s0, tuple(_chunks(a) for a in (q, k, v, log_a)))
    return _unchunk(o, q.shape[2]), s_fin


def _mlstm_scan(q, k, v, log_i, log_f, state0):
    mask = jnp.tril(jnp.ones((CHUNK, CHUNK), dtype=bool))

    def step(carry, inp):
        c, nv, m = carry
        qc, kc, vc, ic, fc = inp
        b = jnp.cumsum(fc, axis=-1)
        log_intra = b[..., :, None] - b[..., None, :] + ic[..., None, :]
        log_intra = jnp.where(mask, log_intra, -jnp.inf)
        log_inter = b + m[..., None]
        m_q = jnp.maximum(log_inter, jnp.max(log_intra, axis=-1))
        w_intra = jnp.exp(log_intra - m_q[..., None]) * jnp.einsum('bhld,bhmd->bhlm', qc, kc)
        w_inter = jnp.exp(log_inter - m_q)
        num = (jnp.einsum('bhlm,bhmv->bhlv', w_intra, vc)
               + w_inter[..., None] * jnp.einsum('bhld,bhdv->bhlv', qc, c))
        den = jnp.sum(w_intra, axis=-1) + w_inter * jnp.einsum('bhld,bhd->bhl', qc, nv)
        h = num / jnp.maximum(jnp.abs(den), jnp.exp(-m_q))[..., None]
        b_last = b[..., -1]
        log_key = b_last[..., None] - b + ic
        m_new = jnp.maximum(b_last + m, jnp.max(log_key, axis=-1))
        w_key = jnp.exp(log_key - m_new[..., None])
        decay = jnp.exp(b_last + m - m_new)
        c_new = decay[..., None, None] * c + jnp.einsum('bhl,bhld,bhlv->bhdv', w_key, kc, vc)
        n_new = decay[..., None] * nv + jnp.einsum('bhl,bhld->bhd', w_key, kc)
        return (c_new, n_new, m_new), h

    state, h = lax.scan(step, state0, tuple(_chunks(a) for a in (q, k, v, log_i, log_f)))
    return _unchunk(h, q.shape[2]), state


def _flip_t(a):
    return jnp.flip(a, axis=2)


def _identity(a):
    return a


def _bidirectional(scan_fn, lat_dirs, ctx_dirs, init):
    outs_l, outs_c = [], []
    for d in range(2):
        flip = _flip_t if d == 1 else _identity
        o_c, st = scan_fn(*[flip(a) for a in ctx_dirs[d]], init)
        o_l, _ = scan_fn(*[flip(a) for a in lat_dirs[d]], st)
        outs_l.append(flip(o_l))
        outs_c.append(flip(o_c))
    return outs_l[0] + outs_l[1], outs_c[0] + outs_c[1]


def _gla_inputs(q, k, v, lr, w_up, b_dec):
    B, T, _ = q.shape
    q = _heads(q, GLA_HEADS).astype(jnp.float32) * GLA_DK_HEAD ** -0.5
    k = _heads(k, GLA_HEADS).astype(jnp.float32)
    v = _heads(v, GLA_HEADS).astype(jnp.float32)
    lr = lr.reshape(B, T, 2, GLA_RANK)
    z = jnp.einsum('btdr,drk->dbtk', lr, w_up) + b_dec[:, None, None, :]
    log_a = jax.nn.log_sigmoid(z.astype(jnp.float32)) / GLA_TAU
    return ((q, k, v, _heads(log_a[0], GLA_HEADS)),
            (q, k, v, _heads(log_a[1], GLA_HEADS)))


def _mlstm_inputs(q, k, v, gates, conv_w, conv_b, b_gate):
    qk = jax.nn.silu(_centred_conv(jnp.concatenate([q, k], axis=-1), conv_w, conv_b))
    q, k = jnp.split(qk, 2, axis=-1)
    q = _heads(q, MLSTM_HEADS).astype(jnp.float32)
    k = _heads(k, MLSTM_HEADS).astype(jnp.float32) * MLSTM_DK_HEAD ** -0.5
    v = _heads(v, MLSTM_HEADS).astype(jnp.float32)
    B, T, _ = gates.shape
    g = (gates.reshape(B, T, 4, MLSTM_HEADS) + b_gate).astype(jnp.float32)
    g = g.transpose(2, 0, 3, 1)
    return ((q, k, v, g[0], jax.nn.log_sigmoid(g[1])),
            (q, k, v, g[2], jax.nn.log_sigmoid(g[3])))


def _mixer(u, u_c, w_in, gla_w_up, gla_b_dec, gla_norm_g, conv_w, conv_b, b_gate,
           mlstm_norm_g, w_br_gla, w_br_mlstm, w_out, need_ctx):
    B = u.shape[0]
    p = _split_cols(u @ w_in)
    pc = _split_cols(u_c @ w_in)
    gla_lat = _gla_inputs(p[0], p[1], p[2], p[4], gla_w_up, gla_b_dec)
    gla_ctx = _gla_inputs(pc[0], pc[1], pc[2], pc[4], gla_w_up, gla_b_dec)
    s0 = jnp.zeros((B, GLA_HEADS, GLA_DK_HEAD, GLA_DV_HEAD), jnp.float32)
    o_l, o_c = _bidirectional(_gla_scan, gla_lat, gla_ctx, s0)
    cm = [_to_colmajor(p[j]) for j in (5, 6, 7, 9)]
    m_lat = _mlstm_inputs(*cm, conv_w, conv_b, b_gate)
    m_ctx = _mlstm_inputs(pc[5], pc[6], pc[7], pc[9], conv_w, conv_b, b_gate)
    st0 = (jnp.zeros((B, MLSTM_HEADS, MLSTM_DK_HEAD, MLSTM_DV_HEAD), jnp.float32),
           jnp.zeros((B, MLSTM_HEADS, MLSTM_DK_HEAD), jnp.float32),
           jnp.zeros((B, MLSTM_HEADS), jnp.float32))
    h_l, h_c = _bidirectional(_mlstm_scan, m_lat, m_ctx, st0)

    def merge(pp, o, h):
        y_gla = _head_rms_norm(o, gla_norm_g, u.dtype) * jax.nn.silu(pp[3])
        y_m = h * jax.nn.sigmoid(pp[8])
        gate_gla, gate_m = jnp.split(jax.nn.sigmoid(pp[10]), 2, axis=-1)
        y = gate_gla * (y_gla @ w_br_gla) + gate_m * (y_m @ w_br_mlstm)
        return y @ w_out

    mix = merge(p, o_l, _from_colmajor(_head_rms_norm(h_l, mlstm_norm_g, u.dtype)))
    mix_c = merge(pc, o_c, _head_rms_norm(h_c, mlstm_norm_g, u.dtype)) if need_ctx else None
    return mix, mix_c


def setup_inputs(seed: int = 0) -> dict:
    key = jax.random.key(seed)
    ks = jax.random.split(key, 24)

    def nrm(k, shape, scale):
        return jax.random.normal(k, shape, jnp.float32) * scale

    D, L = D_MODEL, DEPTH
    f_bias = jnp.stack([jnp.zeros((MLSTM_HEADS,), jnp.float32),
                        jnp.linspace(3.0, 6.0, MLSTM_HEADS, dtype=jnp.float32),
                        jnp.zeros((MLSTM_HEADS,), jnp.float32),
                        jnp.linspace(3.0, 6.0, MLSTM_HEADS, dtype=jnp.float32)])
    return {
        'x': nrm(ks[0], (BATCH, SEQ, D), 1.0),
        'c': nrm(ks[1], (BATCH, D), 1.0),
        'ctx': nrm(ks[2], (BATCH, CTX_LEN, D), 1.0),
        'c_ctx': nrm(ks[3], (D,), 1.0),
        'w_ada': nrm(ks[4], (L, D, 6 * D), D ** -0.5),
        'b_ada': nrm(ks[5], (L, 6 * D), 0.01),
        'norm1_g': 1.0 + nrm(ks[6], (L, D), 0.01),
        'w_in': nrm(ks[7], (L, D, N_IN), D ** -0.5),
        'gla_w_up': nrm(ks[8], (L, 2, GLA_RANK, GLA_DK), GLA_RANK ** -0.5),
        'gla_b_dec': nrm(ks[9], (L, 2, GLA_DK), 0.1),
        'gla_norm_g': 1.0 + nrm(ks[10], (L, GLA_DV), 0.01),
        'mlstm_conv_w': nrm(ks[11], (L, MLSTM_CONV, 2 * MLSTM_DK), MLSTM_CONV ** -0.5),
        'mlstm_conv_b': nrm(ks[12], (L, 2 * MLSTM_DK), 0.01),
        'mlstm_b_gate': nrm(ks[13], (L, 4, MLSTM_HEADS), 0.1) + f_bias,
        'mlstm_norm_g': 1.0 + nrm(ks[14], (L, MLSTM_DV), 0.01),
        'w_br_gla': nrm(ks[15], (L, GLA_DV, D), GLA_DV ** -0.5),
        'w_br_mlstm': nrm(ks[16], (L, MLSTM_DV, D), MLSTM_DV ** -0.5),
        'w_out': nrm(ks[17], (L, D, D), D ** -0.5),
        'norm2_g': 1.0 + nrm(ks[18], (L, D), 0.01),
        'w_ffn_in': nrm(ks[19], (L, D, 2 * D_FF), D ** -0.5),
        'w_ffn_out': nrm(ks[20], (L, D_FF, D), D_FF ** -0.5),
        'final_g': 1.0 + nrm(ks[21], (D,), 0.01),
    }


def reference(x, c, ctx, c_ctx, w_ada, b_ada, norm1_g, w_in, gla_w_up, gla_b_dec,
              gla_norm_g, mlstm_conv_w, mlstm_conv_b, mlstm_b_gate, mlstm_norm_g,
              w_br_gla, w_br_mlstm, w_out, norm2_g, w_ffn_in, w_ffn_out, final_g):
    h_ctx = ctx
    silu_c = jax.nn.silu(c)
    silu_cc = jax.nn.silu(c_ctx)
    for i in range(DEPTH):
        need_ctx = i + 1 < DEPTH
        mod = (silu_c @ w_ada[i] + b_ada[i])[:, None, :]
        mod_c = silu_cc @ w_ada[i] + b_ada[i]
        sh1, sc1, g1, sh2, sc2, g2 = jnp.split(mod, 6, axis=-1)
        sh1c, sc1c, g1c, sh2c, sc2c, g2c = jnp.split(mod_c, 6, axis=-1)
        u = _modulate(_rms_norm(x, norm1_g[i]), sh1, sc1)
        u_c = _modulate(_rms_norm(h_ctx, norm1_g[i]), sh1c, sc1c)
        mix, mix_c = _mixer(u, u_c, w_in[i], gla_w_up[i], gla_b_dec[i], gla_norm_g[i],
                            mlstm_conv_w[i], mlstm_conv_b[i], mlstm_b_gate[i], mlstm_norm_g[i],
                            w_br_gla[i], w_br_mlstm[i], w_out[i], need_ctx)
        x = x + g1 * mix
        x = x + g2 * _swiglu(_modulate(_rms_norm(x, norm2_g[i]), sh2, sc2),
                             w_ffn_in[i], w_ffn_out[i])
        if need_ctx:
            h_ctx = h_ctx + g1c * mix_c
            h_ctx = h_ctx + g2c * _swiglu(_modulate(_rms_norm(h_ctx, norm2_g[i]), sh2c, sc2c),
                                          w_ffn_in[i], w_ffn_out[i])
    return _rms_norm(x, final_g)
```

```python
import math
from contextlib import ExitStack
import numpy as np
import concourse.bass as bass
import concourse.mybir as mybir
from concourse.bass_utils import run_bass_kernel_spmd

F32 = mybir.dt.float32
BF16 = mybir.dt.bfloat16
AF = mybir.ActivationFunctionType
ALU = mybir.AluOpType

D = 1024
T = 4096
TC = 256
NIN = 8240
DFF = 2816
NT_OWN = 16
NT_LAT = 32
UW = 4356
CTX0 = 1
LAT0 = 259
ALPHA = 128.0 ** -0.5
NV = 96
NB = 5136
C_ID, C_MIF, C_MIB, C_MAF, C_MAB, C_ONE = 0, 128, 256, 384, 512, 640


class Tok:
    __slots__ = ("w", "r", "ex")

    def __init__(self, ex=False):
        self.w = None
        self.r = {}
        self.ex = ex


class K:
    def __init__(self, nc, stack, ndma=8):
        self.nc = nc
        self.eng = {"pe": nc.tensor, "act": nc.scalar, "dve": nc.vector, "pool": nc.gpsimd, "sp": nc.sync}
        self.semh = {}
        self.cnt = {}
        self.waited = {e: {} for e in self.eng}
        for e in self.eng:
            self.semh[e] = stack.enter_context(nc.semaphore("s_" + e))
            self.cnt[e] = 0
        self.dma_slots = {}
        self.dma_rr = {}
        for q in ("sp", "pool"):
            self.dma_slots[q] = []
            for i in range(ndma):
                nm = "d_%s%d" % (q, i)
                self.semh[nm] = stack.enter_context(nc.semaphore(nm))
                self.cnt[nm] = 0
                self.dma_slots[q].append(nm)
            self.dma_rr[q] = 0
        self.nwaits = 0
        self.nops = 0
        self.log = {e: [] for e in self.eng}

    def _wait(self, e, s, v):
        if self.waited[e].get(s, 0) >= v:
            return
        self.eng[e].wait_ge(self.semh[s], v)
        self.waited[e][s] = v
        self.nwaits += 1
        self.log[e].append(("wait", s, v))

    def _deps(self, e, reads, writes):
        deps = {}
        for t in reads:
            if t.w is not None:
                s, v = t.w
                deps[s] = max(deps.get(s, 0), v)
        for t in writes:
            if t.w is not None:
                s, v = t.w
                deps[s] = max(deps.get(s, 0), v)
            for s, v in t.r.items():
                deps[s] = max(deps.get(s, 0), v)
        for s, v in deps.items():
            if e == "pe" and s == "pe":
                continue
            self._wait(e, s, v)

    def _record(self, ticket, reads, writes):
        s, v = ticket
        for t in reads:
            t.r[s] = max(t.r.get(s, 0), v)
        for t in writes:
            t.w = ticket
            t.r = {}

    def op(self, e, fn, reads=(), writes=(), signal=True):
        if e != "pe":
            exr = [t for t in reads if t.ex]
            if exr:
                writes = list(writes) + exr
        self._deps(e, reads, writes)
        ins = fn(self.eng[e])
        self.nops += 1
        if signal:
            self.cnt[e] += 1
            ins.then_inc(self.semh[e], 1)
            ticket = (e, self.cnt[e])
            self.log[e].append(("inc", e, 1, self.nops))
        else:
            assert e == "pe"
            ticket = (e, self.cnt[e] + 1)
        self._record(ticket, reads, writes)
        return ticket

    def dma(self, q, out, in_, reads=(), writes=(), **kw):
        slots = self.dma_slots[q]
        nm = slots[self.dma_rr[q] % len(slots)]
        self.dma_rr[q] += 1
        if self.cnt[nm] > 0:
            self._wait(q, nm, self.cnt[nm])
        self._deps(q, reads, writes)
        ins = self.eng[q].dma_start(out=out, in_=in_, **kw)
        self.cnt[nm] += 16
        ins.then_inc(self.semh[nm], 16)
        self.log[q].append(("inc", nm, 16, self.nops))
        ticket = (nm, self.cnt[nm])
        self._record(ticket, reads, writes)
        self.nops += 1
        return ticket

    def check_deadlock(self):
        sem = {}
        pos = {e: 0 for e in self.eng}
        progress = True
        while progress:
            progress = False
            for e in self.eng:
                lg = self.log[e]
                while pos[e] < len(lg):
                    it = lg[pos[e]]
                    if it[0] == "wait":
                        if sem.get(it[1], 0) >= it[2]:
                            pos[e] += 1
                            progress = True
                        else:
                            break
                    else:
                        sem[it[1]] = sem.get(it[1], 0) + it[2]
                        pos[e] += 1
                        progress = True
        stuck = {e: (pos[e], len(self.log[e]), self.log[e][pos[e]]) for e in self.eng if pos[e] < len(self.log[e])}
        if stuck:
            print("DEADLOCK:", stuck, {k_: v for k_, v in sem.items()})
        else:
            print("deadlock check: OK")
        return not stuck

    def barrier(self):
        snap = dict(self.cnt)
        for e in self.eng:
            for s_, v in snap.items():
                if v > 0:
                    self._wait(e, s_, v)

    def finish(self, toks, e="sp"):
        for t in toks:
            if t.w is not None:
                self._wait(e, t.w[0], t.w[1])


def build(stage=9):
    nc = bass.Bass("TRN2", target_bir_lowering=False)
    di = lambda n, s, dt=F32: nc.dram_tensor(n, s, dt, kind="ExternalInput").ap()
    x_d = di("x", [T, D])
    ctx_d = di("ctx", [TC, D])
    cct_d = di("cct", [128, 8, 2])
    wada_d = di("w_ada", [D, 6 * D])
    win_d = di("w_in", [D, NIN])
    wup_d = di("w_up", [2, 16, 512])
    bdec_d = di("bdec", [1, 1024])
    cst_d = di("consts", [128, 768])
    vpp_d = di("vpp", [128, NV])
    vbc_d = di("vbc", [1, NB])
    wbg_d = di("w_br_gla", [D, D])
    wbm_d = di("w_br_m", [D, D])
    wo_d = di("w_out", [D, D])
    wfi_d = di("w_ffn_in", [D, 2 * DFF])
    wfo_d = di("w_ffn_out", [DFF, D])
    out_d = nc.dram_tensor("out", [NT_OWN * 128, D], F32, kind="ExternalOutput").ap()
    of_scr = nc.dram_tensor("of_scr", [NT_OWN * 128, D], F32, kind="Internal" if stage >= 9 else "ExternalOutput").ap()
    hf_scr = nc.dram_tensor("hf_scr", [T, D], F32, kind="Internal" if stage >= 9 else "ExternalOutput").ap()
    if stage < 9:
        ygla_scr = nc.dram_tensor("ygla", [NT_OWN * 128, D], BF16, kind="ExternalOutput").ap()
        ym_scr = nc.dram_tensor("ym", [T, D], BF16, kind="ExternalOutput").ap()
    else:
        ygla_scr = nc.dram_tensor("ygla", [NT_OWN * 128, D], BF16, kind="Internal").ap()
        ym_scr = nc.dram_tensor("ym", [T, D], BF16, kind="Internal").ap()
    x1_scr = nc.dram_tensor("x1_scr", [NT_OWN * 128, D], F32, kind="Internal" if stage >= 9 else "ExternalOutput").ap()
    t_of_scr = [Tok() for _ in range(NT_OWN)]
    t_hf_scr = [Tok() for _ in range(NT_LAT)]
    t_ygla_scr = [Tok() for _ in range(NT_OWN)]
    t_ym_scr = Tok()
    t_x1_scr = [Tok() for _ in range(NT_OWN)]
    t_out = Tok()

    with ExitStack() as st0:
        k = K(nc, st0)
        op = k.op

        dbg_list = []
        dbg_pool = [st0.enter_context(nc.sbuf_tensor("dbgs_%d" % i, [128, 128], F32)) for i in range(16 if stage < 3 else 0)]

        def dbg(name, ap, toks, n):
            if stage >= 3:
                return
            dd = nc.dram_tensor("dbg_" + name, [128, n], F32, kind="ExternalOutput").ap()
            stg = dbg_pool.pop()[:, 0:n]
            tk = Tok()
            np_ = ap.shape[0]
            op("dve", lambda e: e.memset(stg, 0.0), writes=[tk])
            op("dve", lambda e: e.tensor_copy(out=stg[0:np_, :], in_=ap), reads=toks, writes=[tk])
            td = Tok()
            k.dma("pool", dd[:, :], stg, reads=[tk], writes=[td])
            dbg_list.append(td)

        uniq = [0]

        def alloc(st, name, shape, dt):
            uniq[0] += 1
            return st.enter_context(nc.sbuf_tensor("sb%d_%s" % (uniq[0], name), shape, dt))

        cst = alloc(st0, "cst", [128, 768], F32); t_cst = Tok()
        vpp = alloc(st0, "vpp", [128, NV], F32); t_vpp = Tok()
        identb = alloc(st0, "identb", [128, 128], BF16); t_identb = Tok()
        mask4 = [alloc(st0, "mask4_%d" % d, [128, 4, 128], BF16) for d in range(2)]; t_mask4 = Tok()
        onesb = alloc(st0, "onesb", [128, 1], BF16); t_onesb = Tok()
        cvals = alloc(st0, "cvals", [128, 4], F32); t_cv = Tok()
        modpp = alloc(st0, "modpp", [128, 4, 8, 2], F32); t_modpp = Tok()
        scU = alloc(st0, "scU", [128, 8, 2], F32); t_scU = Tok()
        sc2 = alloc(st0, "sc2", [128, 8], F32); t_sc2 = Tok()
        G1 = alloc(st0, "G1", [128, D], F32); t_G1 = Tok()
        G2 = alloc(st0, "G2", [128, D], F32); t_G2 = Tok()
        ps = [st0.enter_context(nc.psum_tensor("ps%d" % i, [128, 512], F32)) for i in range(8)]
        t_ps = [Tok(ex=True) for _ in range(8)]
        ident = cst[:, C_ID:C_ID + 128]
        MI = [cst[:, C_MIF:C_MIF + 128], cst[:, C_MIB:C_MIB + 128]]
        MA = [cst[:, C_MAF:C_MAF + 128], cst[:, C_MAB:C_MAB + 128]]
        ones = cst[:, C_ONE:C_ONE + 128]
        one_c = cvals[:, 0:1]
        eps_c = cvals[:, 1:2]
        lna_c = cvals[:, 2:3]
        VP_BADA, VP_N1, VP_N2, VP_CW, VP_CB = 0, 48, 56, 64, 88

        k.dma("sp", cst[:], cst_d[:, :], writes=[t_cst])
        k.dma("sp", vpp[:], vpp_d[:, :], writes=[t_vpp])
        op("dve", lambda e: e.memset(cvals[:, 0:1], 1.0), writes=[t_cv])
        op("dve", lambda e: e.memset(cvals[:, 1:2], 1e-6), writes=[t_cv])
        op("dve", lambda e: e.memset(cvals[:, 2:3], math.log(ALPHA)), writes=[t_cv])
        op("dve", lambda e: e.memset(cvals[:, 3:4], 0.0), writes=[t_cv])
        op("dve", lambda e: e.tensor_copy(out=identb[:], in_=ident), reads=[t_cst], writes=[t_identb])
        op("dve", lambda e: e.tensor_copy(out=onesb[:], in_=cst[:, C_ONE:C_ONE + 1]), reads=[t_cst], writes=[t_onesb])
        for d in range(2):
            for h in range(4):
                op("dve", lambda e: e.tensor_copy(out=mask4[d][:, h, :], in_=MI[d]), reads=[t_cst], writes=[t_mask4])

        prep_banks = [5, 6, 7]
        prr = [0]

        def pbank():
            i = prep_banks[prr[0] % len(prep_banks)]
            prr[0] += 1
            return ps[i], t_ps[i]

        with ExitStack() as st:
            scc = alloc(st, "scc", [128, 8, 2], F32); t_scc = Tok()
            wa = [alloc(st, "wa%d" % i, [128, 8, D], F32) for i in range(2)]; t_wa = [Tok(), Tok()]
            bbc = alloc(st, "bbc", [128, D], F32); t_bbc = Tok()
            k.dma("sp", scc[:], cct_d[:, :, :], writes=[t_scc])
            op("act", lambda e: e.activation(out=scc[:], in_=scc[:], func=AF.Silu), reads=[t_scc], writes=[t_scc])
            psA, t_psA = ps[0], t_ps[0]
            mrow = alloc(st, "mrow", [2, D], F32); t_mrow = Tok()
            gi = 0
            for g in range(6):
                w_, tw_ = wa[g % 2], t_wa[g % 2]
                for kc in range(8):
                    k.dma("sp", w_[:, kc, :], wada_d[kc * 128:(kc + 1) * 128, g * D:(g + 1) * D], writes=[tw_])
                for half in range(2):
                    pb, tpb = pbank()
                    for kc in range(8):
                        op("pe", lambda e: e.matmul(pb[0:2, :], lhsT=scc[:, kc, :], rhs=w_[:, kc, half * 512:(half + 1) * 512],
                                                    start=(kc == 0), stop=(kc == 7)),
                           reads=[tw_, t_scc], writes=[tpb], signal=(kc == 7))
                    op("act", lambda e: e.copy(out=mrow[:, half * 512:(half + 1) * 512], in_=pb[0:2, :]), reads=[tpb], writes=[t_mrow])
                if g in (0, 1, 3, 4):
                    for j in range(8):
                        c0 = (gi * 8 + j) * 2
                        op("pe", lambda e: e.transpose(out=psA[:, c0:c0 + 2], in_=mrow[0:2, j * 128:(j + 1) * 128], identity=cst[0:2, C_ID:C_ID + 2]),
                           reads=[t_mrow, t_cst], writes=[t_psA], signal=(j == 7))
                    gi += 1
                else:
                    Gt, tG = (G1, t_G1) if g == 2 else (G2, t_G2)
                    voff = 0 if g == 2 else 1024
                    k.dma("sp", bbc[:], vbc_d[0:1, voff:voff + 1024].to_broadcast([128, 1024]), writes=[t_bbc])
                    for half in range(2):
                        pb, tpb = pbank()
                        op("pe", lambda e: e.matmul(pb[:, :], lhsT=cst[0:1, C_ONE:C_ONE + 128], rhs=mrow[0:1, half * 512:(half + 1) * 512],
                                                    start=True, stop=True),
                           reads=[t_mrow, t_cst], writes=[tpb])
                        op("dve", lambda e: e.tensor_tensor(out=Gt[:, half * 512:(half + 1) * 512], in0=pb[:, :],
                                                            in1=bbc[:, half * 512:(half + 1) * 512], op=ALU.add),
                           reads=[tpb, t_bbc], writes=[tG])
            psAv = psA[:, 0:64].rearrange("p (g j s) -> p g j s", g=4, j=8, s=2)
            for gi_, g in enumerate((0, 1, 3, 4)):
                for s in range(2):
                    op("dve", lambda e: e.tensor_tensor(out=modpp[:, gi_, :, s], in0=psAv[:, gi_, :, s],
                                                        in1=vpp[:, VP_BADA + g * 8:VP_BADA + (g + 1) * 8], op=ALU.add),
                       reads=[t_psA, t_vpp], writes=[t_modpp])
            for s in range(2):
                op("dve", lambda e: e.scalar_tensor_tensor(out=scU[:, :, s], in0=modpp[:, 1, :, s], scalar=1.0,
                                                           in1=vpp[:, VP_N1:VP_N1 + 8], op0=ALU.add, op1=ALU.mult),
                   reads=[t_modpp, t_vpp], writes=[t_scU])
            op("dve", lambda e: e.scalar_tensor_tensor(out=sc2[:, :], in0=modpp[:, 3, :, 0], scalar=1.0,
                                                       in1=vpp[:, VP_N2:VP_N2 + 8], op0=ALU.add, op1=ALU.mult),
               reads=[t_modpp, t_vpp], writes=[t_sc2])


        def norm_transpose(xs, t_xs, xn, t_xn, small, t_small, junk, t_junk, scale_ap, bias_ap, tsb, dst, t_dst):
            norm_part(xs, t_xs, xn, t_xn, small, t_small, junk, t_junk)
            transpose_part(xn, t_xn, scale_ap, bias_ap, tsb, dst, t_dst)

        def norm_part(xs, t_xs, xn, t_xn, small, t_small, junk, t_junk):
            op("act", lambda e: e.activation(out=junk[:], in_=xs, func=AF.Square, accum_out=small[:, 0:1]),
               reads=[t_xs], writes=[t_junk, t_small])
            op("act", lambda e: e.activation(out=small[:, 1:2], in_=small[:, 0:1], func=AF.Sqrt, bias=eps_c, scale=1.0 / D),
               reads=[t_small, t_cv], writes=[t_small])
            op("dve", lambda e: e.reciprocal(out=small[:, 1:2], in_=small[:, 1:2]), reads=[t_small], writes=[t_small])
            op("act", lambda e: e.activation(out=xn[:], in_=xs, func=AF.Copy, scale=small[:, 1:2]),
               reads=[t_xs, t_small], writes=[t_xn])

        def transpose_part(xn, t_xn, scale_ap, bias_ap, tsb, dst, t_dst):
            for half in range(2):
                pb, tpb = pbank()
                for j in range(4):
                    kc = half * 4 + j
                    op("pe", lambda e: e.transpose(out=pb[:, j * 128:(j + 1) * 128], in_=xn[:, kc * 128:(kc + 1) * 128], identity=ident),
                       reads=[t_xn, t_cst], writes=[tpb], signal=(j == 3))
                for j in range(4):
                    kc = half * 4 + j
                    if j % 2 == 0:
                        op("act", lambda e: e.activation(out=dst(kc), in_=pb[:, j * 128:(j + 1) * 128], func=AF.Identity,
                                                         scale=scale_ap(kc), bias=bias_ap(kc)),
                           reads=[tpb] + tsb, writes=[t_dst])
                    else:
                        op("dve", lambda e: e.tensor_scalar(out=dst(kc), in0=pb[:, j * 128:(j + 1) * 128], scalar1=scale_ap(kc),
                                                            scalar2=bias_ap(kc), op0=ALU.mult, op1=ALU.add),
                           reads=[tpb] + tsb, writes=[t_dst])

        with ExitStack() as stm:
            uT = alloc(stm, "uT", [128, 8, UW], BF16)
            t_uT = [Tok() for _ in range(34)]
            t_guard = Tok()
            wmix = alloc(stm, "wmix", [128, 8, 3104], BF16); t_wmix = Tok()
            for c in (0, 257, 258, UW - 1):
                op("dve", lambda e: e.memset(uT[:, :, c:c + 1], 0.0), writes=[t_guard])

            def tile_col(ti):
                return CTX0 + ti * 128 if ti < 2 else LAT0 + (ti - 2) * 128

            def phase_U(colmajor):
                with ExitStack() as st:
                    xs = [alloc(st, "xs%d" % i, [128, D], F32) for i in range(2)]; t_xs = [Tok(), Tok()]
                    xn = [alloc(st, "xn%d" % i, [128, D], F32) for i in range(2)]; t_xn = [Tok(), Tok()]
                    junk = alloc(st, "junkU", [128, D], F32); t_junk = Tok()
                    small = [alloc(st, "smallU%d" % i, [128, 2], F32) for i in range(2)]; t_small = [Tok(), Tok()]
                    xv = x_d.rearrange("(r c) d -> c r d", c=64)

                    def partA(ti):
                        b = ti % 2
                        if ti < 2:
                            k.dma("sp", xs[b][:], ctx_d[ti * 128:(ti + 1) * 128, :], writes=[t_xs[b]])
                        else:
                            lt = ti - 2
                            if colmajor:
                                for cl in range(2):
                                    k.dma("sp", xs[b][cl * 64:(cl + 1) * 64, :], xv[2 * lt + cl], writes=[t_xs[b]])
                            else:
                                k.dma("sp", xs[b][:], x_d[lt * 128:(lt + 1) * 128, :], writes=[t_xs[b]])
                        norm_part(xs[b][:], t_xs[b], xn[b], t_xn[b], small[b], t_small[b], junk, t_junk)

                    def partB(ti):
                        b = ti % 2
                        s = 1 if ti < 2 else 0
                        c0 = tile_col(ti)
                        transpose_part(xn[b], t_xn[b], lambda kc: scU[:, kc, s:s + 1], lambda kc: modpp[:, 0, kc, s:s + 1], [t_scU, t_modpp],
                                       lambda kc: uT[:, kc, c0:c0 + 128], t_uT[ti])

                    partA(0)
                    for ti in range(34):
                        if ti + 1 < 34:
                            partA(ti + 1)
                        partB(ti)

            def step(g):
                try:
                    next(g)
                except StopIteration:
                    pass

            def exhaust(g):
                for _ in g:
                    pass

            def load_w(dst, t_dst, src, c0, c1, nk=8, q="pool"):
                for kc in range(nk):
                    k.dma(q, dst[:, kc, 0:c1 - c0], src[kc * 128:(kc + 1) * 128, c0:c1], writes=[t_dst])

            def gla_phase():
                print("gla start cnt", dict(k.cnt))
                with ExitStack() as st:
                    wup = alloc(st, "wup", [16, 2, 512], F32); t_wup = Tok()
                    bdec = alloc(st, "bdec", [1, 1024], F32); t_bdec = Tok()
                    gnorm = alloc(st, "gnorm", [128, D], F32); t_gnorm = Tok()
                    lrT = alloc(st, "lrT", [16, 128], F32); t_lrT = Tok()
                    tmp = alloc(st, "gtmp", [128, 512], F32); t_tmp = Tok()
                    sp = alloc(st, "gsp", [128, 512], F32); t_sp = Tok()
                    dkk = alloc(st, "dkk", [128, 512], F32); t_dkk = Tok()
                    Ep = alloc(st, "Ep", [128, 512], F32); t_Ep = Tok()
                    Em = alloc(st, "Em", [128, 512], F32); t_Em = Tok()
                    kk = [alloc(st, "kk%d" % i, [128, 512], BF16) for i in range(2)]; t_kk = [Tok(), Tok()]
                    v = [alloc(st, "v%d" % i, [128, D], BF16) for i in range(2)]; t_v = [Tok(), Tok()]
                    Gg = [alloc(st, "Gg%d" % i, [128, D], F32) for i in range(2)]; t_Gg = [Tok(), Tok()]
                    qd = [alloc(st, "qd%d" % i, [128, 4, 128], BF16) for i in range(2)]; t_qd = [Tok(), Tok()]
                    kd = [alloc(st, "kd%d" % i, [128, 4, 128], BF16) for i in range(2)]; t_kd = [Tok(), Tok()]
                    dec = [alloc(st, "dec%d" % i, [128, 4], F32) for i in range(2)]; t_dec = [Tok(), Tok()]
                    wT = alloc(st, "wT", [128, 4, 128], BF16); t_wT = Tok()
                    S = alloc(st, "S", [128, 4, 256], F32); t_S = Tok()
                    Sb = alloc(st, "Sb", [128, 4, 256], BF16); t_Sb = Tok()
                    ofl = [alloc(st, "ofl%d" % i, [128, D], F32) for i in range(2)]; t_ofl = [Tok(), Tok()]
                    osum = alloc(st, "osum", [128, D], F32); t_osum = Tok()
                    junk = alloc(st, "junkG", [128, 256], F32); t_junk = Tok()
                    ssq = alloc(st, "ssqG", [128, 8], F32); t_ssq = Tok()
                    yb = [alloc(st, "yb%d" % i, [128, D], BF16) for i in range(2)]; t_yb = [Tok(), Tok()]

                    for d_ in range(2):
                        k.dma("sp", wup[:, d_, :], wup_d[d_], writes=[t_wup])
                    k.dma("sp", bdec[:], bdec_d[:, :], writes=[t_bdec])
                    k.dma("sp", gnorm[:], vbc_d[0:1, 3072:4096].to_broadcast([128, 1024]), writes=[t_gnorm])

                    def prep(dr, ti, full, i):
                        b = i % 2
                        c0 = tile_col(ti)
                        tu = t_uT[ti]
                        uts = lambda kc: uT[:, kc, c0:c0 + 128]
                        pb, tpb = pbank()
                        for kc in range(8):
                            op("pe", lambda e: e.matmul(pb[0:16, 0:128], lhsT=wmix[:, kc, 3072 + 16 * dr:3088 + 16 * dr], rhs=uts(kc),
                                                        start=(kc == 0), stop=(kc == 7)),
                               reads=[t_wmix, tu], writes=[tpb], signal=(kc == 7))
                        op("act", lambda e: e.copy(out=lrT[:], in_=pb[0:16, 0:128]), reads=[tpb], writes=[t_lrT])
                        pb, tpb = pbank()
                        op("pe", lambda e: e.matmul(pb[:, :], lhsT=lrT[0:16, :], rhs=wup[0:16, dr, :], start=True, stop=False),
                           reads=[t_lrT, t_wup], writes=[tpb], signal=False)
                        op("pe", lambda e: e.matmul(pb[:, :], lhsT=cst[0:1, C_ONE:C_ONE + 128], rhs=bdec[0:1, dr * 512:(dr + 1) * 512],
                                                    start=False, stop=True),
                           reads=[t_cst, t_bdec], writes=[tpb])
                        op("act", lambda e: e.activation(out=tmp[:], in_=pb[:, :], func=AF.Exp, scale=-1.0), reads=[tpb], writes=[t_tmp])
                        op("act", lambda e: e.activation(out=sp[:], in_=tmp[:], func=AF.Ln, bias=one_c, scale=1.0),
                           reads=[t_tmp, t_cv], writes=[t_sp])
                        yield
                        pb, tpb = pbank()
                        op("pe", lambda e: e.matmul(pb[:, :], lhsT=MA[dr], rhs=sp[:], start=True, stop=True),
                           reads=[t_cst, t_sp], writes=[tpb])
                        op("act", lambda e: e.activation(out=dkk[:], in_=pb[:, :], func=AF.Exp, scale=-1.0 / 16), reads=[tpb], writes=[t_dkk])
                        pb, tpb = pbank()
                        for kc in range(8):
                            op("pe", lambda e: e.matmul(pb[:, :], lhsT=uts(kc), rhs=wmix[:, kc, 512:1024], start=(kc == 0), stop=(kc == 7)),
                               reads=[t_wmix, tu], writes=[tpb], signal=(kc == 7))
                        op("dve", lambda e: e.tensor_tensor(out=kk[b][:], in0=pb[:, :], in1=dkk[:], op=ALU.mult),
                           reads=[tpb, t_dkk], writes=[t_kk[b]])
                        yield
                        for half in range(2):
                            pb, tpb = pbank()
                            for kc in range(8):
                                op("pe", lambda e: e.matmul(pb[:, :], lhsT=uts(kc), rhs=wmix[:, kc, 1024 + half * 512:1536 + half * 512],
                                                            start=(kc == 0), stop=(kc == 7)),
                                   reads=[t_wmix, tu], writes=[tpb], signal=(kc == 7))
                            if half == 0:
                                op("act", lambda e: e.copy(out=v[b][:, 0:512], in_=pb[:, :]), reads=[tpb], writes=[t_v[b]])
                            else:
                                op("dve", lambda e: e.tensor_copy(out=v[b][:, 512:1024], in_=pb[:, :]), reads=[tpb], writes=[t_v[b]])
                        yield
                        if full and dr == 1:
                            for half in range(2):
                                pb, tpb = pbank()
                                for kc in range(8):
                                    op("pe", lambda e: e.matmul(pb[:, :], lhsT=uts(kc), rhs=wmix[:, kc, 2048 + half * 512:2560 + half * 512],
                                                                start=(kc == 0), stop=(kc == 7)),
                                       reads=[t_wmix, tu], writes=[tpb], signal=(kc == 7))
                                op("act", lambda e: e.activation(out=Gg[b][:, half * 512:(half + 1) * 512], in_=pb[:, :], func=AF.Silu),
                                   reads=[tpb], writes=[t_Gg[b]])
                            op("dve", lambda e: e.tensor_tensor(out=Gg[b][:], in0=Gg[b][:], in1=gnorm[:], op=ALU.mult),
                               reads=[t_Gg[b], t_gnorm], writes=[t_Gg[b]])
                        yield
                        pb, tpb = pbank()
                        for h in range(4):
                            op("pe", lambda e: e.matmul(pb[:, h * 128:(h + 1) * 128], lhsT=sp[:, h * 128:(h + 1) * 128], rhs=MI[dr],
                                                        start=True, stop=True),
                               reads=[t_sp, t_cst], writes=[tpb], signal=(h == 3))
                        op("act", lambda e: e.activation(out=Ep[:], in_=pb[:, :], func=AF.Exp, scale=-1.0 / 16), reads=[tpb], writes=[t_Ep])
                        if full:
                            op("act", lambda e: e.activation(out=Em[:], in_=pb[:, :], func=AF.Exp, scale=1.0 / 16), reads=[tpb], writes=[t_Em])
                        lastc = 127 if dr == 0 else 0
                        Epv = Ep[:].rearrange("p (h l) -> p h l", h=4)
                        op("dve", lambda e: e.tensor_copy(out=dec[b][:], in_=Epv[:, :, lastc]), reads=[t_Ep], writes=[t_dec[b]])
                        yield
                        if full:
                            for which in range(2):
                                if which == 1:
                                    yield
                                pb, tpb = pbank()
                                for h in range(4):
                                    for kc in range(8):
                                        cc = which * 512 + h * 128
                                        op("pe", lambda e: e.matmul(pb[:, h * 128:(h + 1) * 128], lhsT=wmix[:, kc, cc:cc + 128], rhs=uts(kc),
                                                                    start=(kc == 0), stop=(kc == 7)),
                                           reads=[t_wmix, tu], writes=[tpb], signal=(kc == 7 and h == 3))
                                if which == 0:
                                    op("dve", lambda e: e.scalar_tensor_tensor(out=qd[b][:].rearrange("p h l -> p (h l)"), in0=pb[:, :], scalar=ALPHA,
                                                                               in1=Ep[:], op0=ALU.mult, op1=ALU.mult),
                                       reads=[tpb, t_Ep], writes=[t_qd[b]])
                                else:
                                    op("dve", lambda e: e.tensor_tensor(out=kd[b][:].rearrange("p h l -> p (h l)"), in0=pb[:, :], in1=Em[:], op=ALU.mult),
                                       reads=[tpb, t_Em], writes=[t_kd[b]])

                    def scan(dr, ti, full, i, own):
                        b = i % 2
                        if full:
                            for h in range(4):
                                op("pe", lambda e: e.matmul(ps[0][:, h * 128:(h + 1) * 128], lhsT=kd[b][:, h, :], rhs=qd[b][:, h, :], start=True, stop=True),
                                   reads=[t_kd[b], t_qd[b]], writes=[t_ps[0]], signal=(h == 3))
                        yield
                        if full:
                            op("dve", lambda e: e.tensor_tensor(out=wT[:].rearrange("p h l -> p (h l)"), in0=ps[0][:, :],
                                                                in1=mask4[dr][:].rearrange("p h l -> p (h l)"), op=ALU.mult),
                               reads=[t_ps[0], t_mask4], writes=[t_wT])
                        yield
                        if full:
                            for h in range(4):
                                pbk = ps[1 + h // 2]; tpbk = t_ps[1 + h // 2]
                                oc = (h % 2) * 256
                                op("pe", lambda e: e.matmul(pbk[:, oc:oc + 256], lhsT=wT[:, h, :], rhs=v[b][:, h * 256:(h + 1) * 256], start=True, stop=False),
                                   reads=[t_wT, t_v[b]], writes=[tpbk], signal=False)
                                op("pe", lambda e: e.matmul(pbk[:, oc:oc + 256], lhsT=qd[b][:, h, :], rhs=Sb[:, h, :], start=False, stop=True),
                                   reads=[t_qd[b], t_Sb], writes=[tpbk], signal=(h % 2 == 1))
                        for h in range(4):
                            pbk = ps[3 + h // 2]; tpbk = t_ps[3 + h // 2]
                            oc = (h % 2) * 256
                            op("pe", lambda e: e.matmul(pbk[:, oc:oc + 256], lhsT=kk[b][:, h * 128:(h + 1) * 128], rhs=v[b][:, h * 256:(h + 1) * 256],
                                                        start=True, stop=True),
                               reads=[t_kk[b], t_v[b]], writes=[tpbk], signal=(h % 2 == 1))
                        yield
                        for h in range(4):
                            pbk = ps[3 + h // 2]; tpbk = t_ps[3 + h // 2]
                            oc = (h % 2) * 256
                            op("dve", lambda e: e.scalar_tensor_tensor(out=S[:, h, :], in0=S[:, h, :], scalar=dec[b][:, h:h + 1], in1=pbk[:, oc:oc + 256],
                                                                       op0=ALU.mult, op1=ALU.add),
                               reads=[tpbk, t_dec[b]], writes=[t_S])
                        op("pool", lambda e: e.tensor_copy(out=Sb[:].rearrange("p h l -> p (h l)"), in_=S[:].rearrange("p h l -> p (h l)")),
                           reads=[t_S], writes=[t_Sb])
                        yield
                        if not full:
                            return
                        if dr == 0:
                            ob = ofl[b]
                            for half in range(2):
                                op("act", lambda e: e.copy(out=ob[:, half * 512:(half + 1) * 512], in_=ps[1 + half][:, :]),
                                   reads=[t_ps[1 + half]], writes=[t_ofl[b]])
                            k.dma("pool", of_scr[own * 128:(own + 1) * 128, :], ob[:], reads=[t_ofl[b]], writes=[t_of_scr[own]])

                        else:
                            for half in range(2):
                                op("dve", lambda e: e.tensor_tensor(out=osum[:, half * 512:(half + 1) * 512], in0=ps[1 + half][:, :],
                                                                    in1=ofl[b][:, half * 512:(half + 1) * 512], op=ALU.add),
                                   reads=[t_ps[1 + half], t_ofl[b]], writes=[t_osum])
                            for h in range(4):
                                op("act", lambda e: e.activation(out=junk[:], in_=osum[:, h * 256:(h + 1) * 256], func=AF.Square, accum_out=ssq[:, h:h + 1]),
                                   reads=[t_osum], writes=[t_junk, t_ssq])
                            op("act", lambda e: e.activation(out=ssq[:, 4:8], in_=ssq[:, 0:4], func=AF.Ln, bias=eps_c, scale=1.0 / 256),
                               reads=[t_ssq, t_cv], writes=[t_ssq])
                            op("act", lambda e: e.activation(out=ssq[:, 4:8], in_=ssq[:, 4:8], func=AF.Exp, scale=-0.5), reads=[t_ssq], writes=[t_ssq])
                            for h in range(4):
                                op("dve", lambda e: e.scalar_tensor_tensor(out=yb[b][:, h * 256:(h + 1) * 256], in0=osum[:, h * 256:(h + 1) * 256],
                                                                           scalar=ssq[:, 4 + h:5 + h], in1=Gg[b][:, h * 256:(h + 1) * 256],
                                                                           op0=ALU.mult, op1=ALU.mult),
                                   reads=[t_osum, t_ssq, t_Gg[b]], writes=[t_yb[b]])
                            k.dma("pool", ygla_scr[own * 128:(own + 1) * 128, :], yb[b][:], reads=[t_yb[b]], writes=[t_ygla_scr[own]])

                    for dr in range(2):
                        op("dve", lambda e: e.memset(S[:].rearrange("p h l -> p (h l)"), 0.0), writes=[t_S])
                        op("dve", lambda e: e.memset(Sb[:].rearrange("p h l -> p (h l)"), 0.0), writes=[t_Sb])
                        if dr == 0:
                            seq = [(0, False, None), (1, False, None)] + [(2 + lt, True, lt) for lt in range(NT_OWN)]
                        else:
                            seq = [(1, False, None), (0, False, None)] + [(2 + lt, lt < NT_OWN, lt if lt < NT_OWN else None)
                                                                          for lt in range(NT_LAT - 1, -1, -1)]
                        exhaust(prep(dr, seq[0][0], seq[0][1], 0))
                        for i, (ti, full, own) in enumerate(seq):
                            if dr == 1 and full:
                                k.dma("sp", ofl[i % 2][:], of_scr[own * 128:(own + 1) * 128, :], reads=[t_of_scr[own]], writes=[t_ofl[i % 2]])
                            P = prep(dr, seq[i + 1][0], seq[i + 1][1], i + 1) if i + 1 < len(seq) else iter(())
                            S_ = scan(dr, ti, full, i, own)
                            step(S_); step(S_)
                            step(P); step(P)
                            step(S_); step(S_)
                            step(P); step(P); step(P)
                            exhaust(P)
                            exhaust(S_)

            def mlstm_phase():
                print("mlstm start cnt", dict(k.cnt))
                with ExitStack() as st:
                    bg = alloc(st, "bg", [128, 16], F32); t_bg = Tok()
                    mnorm = alloc(st, "mnorm", [128, D], F32); t_mnorm = Tok()
                    Gt4 = [alloc(st, "Gt%d" % i, [128, 16], F32) for i in range(4)]; t_Gt4 = [Tok() for _ in range(4)]
                    sm4 = [alloc(st, "sm%d" % i, [128, 16], F32) for i in range(4)]; t_sm4 = [Tok() for _ in range(4)]
                    scall = alloc(st, "scall", [128, 34, 16], F32); t_scall = Tok()
                    accS = [alloc(st, "accS%d" % i, [128, 512], F32) for i in range(2)]; t_accS = [Tok(), Tok()]
                    halo = [alloc(st, "halo%d" % i, [128, 16], F32) for i in range(2)]; t_halo = [Tok(), Tok()]
                    qcS = [alloc(st, "qcS%d" % i, [128, 4, 512], BF16) for i in range(2)]; t_qcS = [Tok(), Tok()]
                    kcS = [alloc(st, "kcS%d" % i, [128, 4, 512], BF16) for i in range(2)]; t_kcS = [Tok(), Tok()]
                    kk = [alloc(st, "mkk%d" % i, [128, 512], BF16) for i in range(2)]; t_kk = [Tok(), Tok()]
                    v = [alloc(st, "mv%d" % i, [128, D], BF16) for i in range(2)]; t_v = [Tok(), Tok()]
                    Gg = [alloc(st, "mGg%d" % i, [128, D], F32) for i in range(2)]; t_Gg = [Tok(), Tok()]
                    wT = alloc(st, "mwT", [128, 4, 128], BF16); t_wT = Tok()
                    C = alloc(st, "C", [128, 4, 256], F32); t_C = Tok()
                    Cb = alloc(st, "Cb", [128, 4, 256], BF16); t_Cb = Tok()
                    nst = alloc(st, "nst", [128, 4], F32); t_n = Tok()
                    nb = alloc(st, "nb", [128, 4], BF16); t_nb = Tok()
                    rr = alloc(st, "rr", [128, 16], F32); t_rr = Tok()
                    hfl = [alloc(st, "hfl%d" % i, [128, D], F32) for i in range(2)]; t_hfl = [Tok(), Tok()]
                    hs = alloc(st, "hs", [128, D], F32); t_hs = Tok()
                    junk = alloc(st, "junkM", [128, 256], F32); t_junk = Tok()
                    ssq = alloc(st, "ssqM", [128, 8], F32); t_ssq = Tok()
                    yb = [alloc(st, "myb%d" % i, [128, D], BF16) for i in range(2)]; t_yb = [Tok(), Tok()]
                    smb = ps[5]
                    t_smg = t_smb = t_ps[5]
                    sms = ps[4]
                    t_smden = t_smkvn = t_ps[4]
                    mprep = [5, 0]
                    mrr = [0]

                    def mbank():
                        i = mprep[mrr[0] % 2]
                        mrr[0] += 1
                        return ps[i], t_ps[i]

                    k.dma("sp", bg[:], vbc_d[0:1, 5120:5136].to_broadcast([128, 16]), writes=[t_bg])
                    k.dma("sp", mnorm[:], vbc_d[0:1, 4096:5120].to_broadcast([128, 1024]), writes=[t_mnorm])
                    op("dve", lambda e: e.tensor_scalar(out=mnorm[:], in0=mnorm[:], scalar1=0.5, scalar2=None, op0=ALU.mult), writes=[t_mnorm])
                    ymv = ym_scr.rearrange("(r c) d -> c r d", c=64)

                    def convq(gb, c0, N, tiles):
                        rd = [t_wmix, t_guard] + [t_uT[t_] for t_ in tiles]
                        lo, hi = min(tiles), max(tiles)
                        if lo not in (0, 2):
                            rd.append(t_uT[lo - 1])
                        if hi not in (1, 33):
                            rd.append(t_uT[hi + 1])
                        for ci in range(8):
                            h0 = 48 + 2 * ci
                            for kc in range(8):
                                op("pe", lambda e: e.matmul(sms[:, h0:h0 + 2], lhsT=wmix[:, kc, ci * 128:(ci + 1) * 128], rhs=uT[:, kc, c0 - 1:c0 + N + 1:N + 1],
                                                            start=(kc == 0), stop=(kc == 7)),
                                   reads=rd, writes=[t_ps[4]], signal=(kc == 7 and ci == 7))
                        op("dve", lambda e: e.tensor_copy(out=halo[gb][:], in_=sms[:, 48:64]), reads=[t_ps[4]], writes=[t_halo[gb]])
                        cwf = lambda ci, tap: vpp[:, VP_CW + tap * 8 + ci:VP_CW + tap * 8 + ci + 1]
                        banks = {}

                        def stA(ci):
                            pb, tpb = ps[6 + ci % 2], t_ps[6 + ci % 2]
                            banks[ci] = (pb, tpb)
                            for kc in range(8):
                                op("pe", lambda e: e.matmul(pb[:, 0:N], lhsT=wmix[:, kc, ci * 128:(ci + 1) * 128], rhs=uT[:, kc, c0:c0 + N],
                                                            start=(kc == 0), stop=(kc == 7)),
                                   reads=rd, writes=[tpb], signal=(kc == 7))
                            acc, ta = accS[ci % 2], t_accS[ci % 2]
                            op("act", lambda e: e.activation(out=acc[:, 0:N], in_=pb[:, 0:N], func=AF.Identity, scale=cwf(ci, 1), bias=vpp[:, VP_CB + ci:VP_CB + ci + 1]),
                               reads=[tpb, t_vpp], writes=[ta])

                        def stB(ci):
                            pb, tpb = banks[ci]
                            acc, ta = accS[ci % 2], t_accS[ci % 2]
                            h0 = 2 * ci
                            op("dve", lambda e: e.scalar_tensor_tensor(out=acc[:, 1:N], in0=pb[:, 0:N - 1], scalar=cwf(ci, 0), in1=acc[:, 1:N], op0=ALU.mult, op1=ALU.add),
                               reads=[tpb, t_vpp], writes=[ta])
                            op("dve", lambda e: e.scalar_tensor_tensor(out=acc[:, 0:N - 1], in0=pb[:, 1:N], scalar=cwf(ci, 2), in1=acc[:, 0:N - 1], op0=ALU.mult, op1=ALU.add),
                               reads=[tpb, t_vpp], writes=[ta])
                            op("dve", lambda e: e.scalar_tensor_tensor(out=acc[:, 0:1], in0=halo[gb][:, h0:h0 + 1], scalar=cwf(ci, 0), in1=acc[:, 0:1], op0=ALU.mult, op1=ALU.add),
                               reads=[t_halo[gb], t_vpp], writes=[ta])
                            op("dve", lambda e: e.scalar_tensor_tensor(out=acc[:, N - 1:N], in0=halo[gb][:, h0 + 1:h0 + 2], scalar=cwf(ci, 2), in1=acc[:, N - 1:N], op0=ALU.mult, op1=ALU.add),
                               reads=[t_halo[gb], t_vpp], writes=[ta])

                        def stC(ci):
                            acc, ta = accS[ci % 2], t_accS[ci % 2]
                            dstt, tdst = (qcS[gb], t_qcS[gb]) if ci < 4 else (kcS[gb], t_kcS[gb])
                            op("act", lambda e: e.activation(out=dstt[:, ci % 4, 0:N], in_=acc[:, 0:N], func=AF.Silu), reads=[ta], writes=[tdst])

                        stA(0)
                        for ci in range(8):
                            if ci + 1 < 8:
                                stA(ci + 1)
                            stB(ci)
                            stC(ci)
                            yield

                    def gates_prologue(dr, seq):
                        for i, (ti, full, lt) in enumerate(seq):
                            c0 = tile_col(ti)
                            tu = t_uT[ti]
                            pbk, tpbk = ps[4 + i % 4], t_ps[4 + i % 4]
                            Gt_, sm_ = Gt4[i % 4], sm4[i % 4]
                            tG, tS = t_Gt4[i % 4], t_sm4[i % 4]
                            for kc in range(8):
                                op("pe", lambda e: e.matmul(pbk[:, 0:16], lhsT=uT[:, kc, c0:c0 + 128], rhs=wmix[:, kc, 3072:3088], start=(kc == 0), stop=(kc == 7)),
                                   reads=[t_wmix, tu], writes=[tpbk], signal=(kc == 7))
                            op("dve", lambda e: e.tensor_tensor(out=Gt_[:], in0=pbk[:, 0:16], in1=bg[:], op=ALU.add), reads=[tpbk, t_bg], writes=[tG])
                            ig = Gt_[:, 8 * dr:8 * dr + 4]
                            fg = Gt_[:, 8 * dr + 4:8 * dr + 8]
                            op("act", lambda e: e.activation(out=sm_[:, 0:4], in_=fg, func=AF.Exp, scale=-1.0), reads=[tG], writes=[tS])
                            op("act", lambda e: e.activation(out=sm_[:, 4:8], in_=sm_[:, 0:4], func=AF.Ln, bias=one_c, scale=1.0),
                               reads=[tS, t_cv], writes=[tS])
                            op("pe", lambda e: e.matmul(pbk[:, 16:20], lhsT=MI[dr], rhs=sm_[:, 4:8], start=True, stop=True),
                               reads=[t_cst, tS], writes=[tpbk], signal=False)
                            op("pe", lambda e: e.matmul(pbk[:, 20:24], lhsT=ones, rhs=sm_[:, 4:8], start=True, stop=True),
                               reads=[t_cst, tS], writes=[tpbk])
                            op("act", lambda e: e.activation(out=scall[:, i, 0:8], in_=pbk[:, 16:24], func=AF.Exp, scale=-1.0), reads=[tpbk], writes=[t_scall])
                            op("dve", lambda e: e.tensor_tensor(out=sm_[:, 8:12], in0=ig, in1=pbk[:, 16:20], op=ALU.add), reads=[tG, tpbk], writes=[tS])
                            op("dve", lambda e: e.tensor_tensor(out=sm_[:, 12:16], in0=sm_[:, 8:12], in1=pbk[:, 20:24], op=ALU.subtract),
                               reads=[tS, tpbk], writes=[tS])
                            op("act", lambda e: e.activation(out=scall[:, i, 8:16], in_=sm_[:, 8:16], func=AF.Exp, bias=lna_c, scale=1.0),
                               reads=[tS, t_cv], writes=[t_scall])

                    def prep(dr, ti, full, i, gb, off):
                        b = i % 2
                        c0 = tile_col(ti)
                        tu = t_uT[ti]
                        uts = lambda kc: uT[:, kc, c0:c0 + 128]
                        qc_ = lambda h: qcS[gb][:, h, off:off + 128]
                        kc__ = lambda h: kcS[gb][:, h, off:off + 128]
                        s_ = scall[:, i, :]
                        pb, tpb = mbank()
                        pbb = pb[:, 0:256].bitcast(BF16)
                        for h in range(4):
                            op("pe", lambda e: e.transpose(out=pbb[:, h * 128:(h + 1) * 128], in_=kc__(h), identity=identb[:]),
                               reads=[t_kcS[gb], t_identb], writes=[tpb], signal=(h == 3))
                        for h in range(4):
                            op("dve", lambda e: e.tensor_scalar(out=kk[b][:, h * 128:(h + 1) * 128], in0=pbb[:, h * 128:(h + 1) * 128],
                                                                scalar1=s_[:, 12 + h:13 + h], scalar2=None, op0=ALU.mult),
                               reads=[tpb, t_scall], writes=[t_kk[b]])
                        yield
                        for half in range(2):
                            pb, tpb = mbank()
                            for kc in range(8):
                                op("pe", lambda e: e.matmul(pb[:, :], lhsT=uts(kc), rhs=wmix[:, kc, 1024 + half * 512:1536 + half * 512],
                                                            start=(kc == 0), stop=(kc == 7)),
                                   reads=[t_wmix, tu], writes=[tpb], signal=(kc == 7))
                            op("act", lambda e: e.copy(out=v[b][:, half * 512:(half + 1) * 512], in_=pb[:, :]), reads=[tpb], writes=[t_v[b]])
                        yield
                        if full and dr == 1:
                            for half in range(2):
                                pb, tpb = mbank()
                                for kc in range(8):
                                    op("pe", lambda e: e.matmul(pb[:, :], lhsT=uts(kc), rhs=wmix[:, kc, 2048 + half * 512:2560 + half * 512],
                                                                start=(kc == 0), stop=(kc == 7)),
                                       reads=[t_wmix, tu], writes=[tpb], signal=(kc == 7))
                                op("act", lambda e: e.activation(out=Gg[b][:, half * 512:(half + 1) * 512], in_=pb[:, :], func=AF.Tanh, scale=0.5),
                                   reads=[tpb], writes=[t_Gg[b]])
                            op("dve", lambda e: e.scalar_tensor_tensor(out=Gg[b][:], in0=Gg[b][:], scalar=1.0, in1=mnorm[:], op0=ALU.add, op1=ALU.mult),
                               reads=[t_Gg[b], t_mnorm], writes=[t_Gg[b]])

                    def scan(dr, ti, full, i, lt, gb, off):
                        b = i % 2
                        s_ = scall[:, i, :]
                        qc_ = lambda h: qcS[gb][:, h, off:off + 128]
                        kc__ = lambda h: kcS[gb][:, h, off:off + 128]
                        if full:
                            for h in range(4):
                                op("pe", lambda e: e.matmul(ps[0][:, h * 128:(h + 1) * 128], lhsT=kc__(h), rhs=qc_(h), start=True, stop=True),
                                   reads=[t_kcS[gb], t_qcS[gb]], writes=[t_ps[0]], signal=(h == 3))
                        yield
                        if full:
                            for h in range(4):
                                op("dve", lambda e: e.scalar_tensor_tensor(out=wT[:, h, :], in0=ps[0][:, h * 128:(h + 1) * 128], scalar=s_[:, 8 + h:9 + h],
                                                                           in1=MI[dr], op0=ALU.mult, op1=ALU.mult),
                                   reads=[t_ps[0], t_scall, t_cst], writes=[t_wT])
                        yield
                        if full:
                            for h in range(4):
                                pbk = ps[1 + h // 2]; tpbk = t_ps[1 + h // 2]
                                oc = (h % 2) * 256
                                op("pe", lambda e: e.matmul(pbk[:, oc:oc + 256], lhsT=wT[:, h, :], rhs=v[b][:, h * 256:(h + 1) * 256], start=True, stop=False),
                                   reads=[t_wT, t_v[b]], writes=[tpbk], signal=False)
                                op("pe", lambda e: e.matmul(pbk[:, oc:oc + 256], lhsT=qc_(h), rhs=Cb[:, h, :], start=False, stop=True),
                                   reads=[t_qcS[gb], t_Cb], writes=[tpbk], signal=(h % 2 == 1))
                            for h in range(4):
                                op("pe", lambda e: e.matmul(sms[:, 32 + h:33 + h], lhsT=wT[:, h, :], rhs=onesb[:, 0:1], start=True, stop=False),
                                   reads=[t_wT, t_onesb], writes=[t_smden], signal=False)
                                op("pe", lambda e: e.matmul(sms[:, 32 + h:33 + h], lhsT=qc_(h), rhs=nb[:, h:h + 1], start=False, stop=True),
                                   reads=[t_qcS[gb], t_nb], writes=[t_smden], signal=(h == 3))
                        def kvmm(hh):
                            for h in hh:
                                oc = (h % 2) * 256
                                op("pe", lambda e: e.matmul(ps[3][:, oc:oc + 256], lhsT=kk[b][:, h * 128:(h + 1) * 128], rhs=v[b][:, h * 256:(h + 1) * 256],
                                                            start=True, stop=True),
                                   reads=[t_kk[b], t_v[b]], writes=[t_ps[3]], signal=(h % 2 == 1))

                        def cupd(hh):
                            for h in hh:
                                oc = (h % 2) * 256
                                op("dve", lambda e: e.scalar_tensor_tensor(out=C[:, h, :], in0=C[:, h, :], scalar=s_[:, 4 + h:5 + h], in1=ps[3][:, oc:oc + 256],
                                                                           op0=ALU.mult, op1=ALU.add),
                                   reads=[t_ps[3], t_scall], writes=[t_C])
                        kvmm((0, 1))
                        for h in range(4):
                            op("pe", lambda e: e.matmul(sms[:, 40 + h:41 + h], lhsT=kk[b][:, h * 128:(h + 1) * 128], rhs=onesb[:, 0:1], start=True, stop=True),
                               reads=[t_kk[b], t_onesb], writes=[t_smkvn], signal=(h == 3))
                        yield
                        cupd((0, 1))
                        yield
                        kvmm((2, 3))
                        yield
                        cupd((2, 3))
                        op("dve", lambda e: e.tensor_tensor(out=nst[:], in0=nst[:], in1=s_[:, 4:8], op=ALU.mult), reads=[t_scall], writes=[t_n])
                        op("dve", lambda e: e.tensor_tensor(out=nst[:], in0=nst[:], in1=sms[:, 40:44], op=ALU.add), reads=[t_smkvn], writes=[t_n])
                        op("pool", lambda e: e.tensor_copy(out=Cb[:].rearrange("p h l -> p (h l)"), in_=C[:].rearrange("p h l -> p (h l)")),
                           reads=[t_C], writes=[t_Cb])
                        op("pool", lambda e: e.tensor_copy(out=nb[:], in_=nst[:]), reads=[t_n], writes=[t_nb])
                        yield
                        if not full:
                            return
                        op("dve", lambda e: e.tensor_tensor(out=rr[:, 0:4], in0=s_[:, 0:4], in1=sms[:, 32:36], op=ALU.mult),
                           reads=[t_scall, t_smden], writes=[t_rr])
                        op("dve", lambda e: e.tensor_scalar(out=rr[:, 4:8], in0=rr[:, 0:4], scalar1=-1.0, scalar2=None, op0=ALU.mult),
                           reads=[t_rr], writes=[t_rr])
                        op("dve", lambda e: e.tensor_tensor(out=rr[:, 4:8], in0=rr[:, 4:8], in1=rr[:, 0:4], op=ALU.max),
                           reads=[t_rr], writes=[t_rr])
                        op("dve", lambda e: e.tensor_scalar(out=rr[:, 4:8], in0=rr[:, 4:8], scalar1=1.0, scalar2=None, op0=ALU.max),
                           reads=[t_rr], writes=[t_rr])
                        op("dve", lambda e: e.reciprocal(out=rr[:, 8:12], in_=rr[:, 4:8]), reads=[t_rr], writes=[t_rr])
                        op("dve", lambda e: e.tensor_tensor(out=rr[:, 12:16], in0=rr[:, 8:12], in1=s_[:, 0:4], op=ALU.mult),
                           reads=[t_rr, t_scall], writes=[t_rr])
                        if dr == 0:
                            ob = hfl[b]
                            for h in range(4):
                                pbk = ps[1 + h // 2]; tpbk = t_ps[1 + h // 2]
                                oc = (h % 2) * 256
                                op("act", lambda e: e.activation(out=ob[:, h * 256:(h + 1) * 256], in_=pbk[:, oc:oc + 256], func=AF.Copy, scale=rr[:, 12 + h:13 + h]),
                                   reads=[tpbk, t_rr], writes=[t_hfl[b]])
                            k.dma("pool", hf_scr[lt * 128:(lt + 1) * 128, :], ob[:], reads=[t_hfl[b]], writes=[t_hf_scr[lt]])
                        else:
                            for h in range(4):
                                pbk = ps[1 + h // 2]; tpbk = t_ps[1 + h // 2]
                                oc = (h % 2) * 256
                                op("dve", lambda e: e.scalar_tensor_tensor(out=hs[:, h * 256:(h + 1) * 256], in0=pbk[:, oc:oc + 256], scalar=rr[:, 12 + h:13 + h],
                                                                           in1=hfl[b][:, h * 256:(h + 1) * 256], op0=ALU.mult, op1=ALU.add),
                                   reads=[tpbk, t_rr, t_hfl[b]], writes=[t_hs])
                            for h in range(4):
                                op("act", lambda e: e.activation(out=junk[:], in_=hs[:, h * 256:(h + 1) * 256], func=AF.Square, accum_out=ssq[:, h:h + 1]),
                                   reads=[t_hs], writes=[t_junk, t_ssq])
                            op("act", lambda e: e.activation(out=ssq[:, 4:8], in_=ssq[:, 0:4], func=AF.Ln, bias=eps_c, scale=1.0 / 256),
                               reads=[t_ssq, t_cv], writes=[t_ssq])
                            op("act", lambda e: e.activation(out=ssq[:, 4:8], in_=ssq[:, 4:8], func=AF.Exp, scale=-0.5), reads=[t_ssq], writes=[t_ssq])
                            for h in range(4):
                                op("dve", lambda e: e.scalar_tensor_tensor(out=yb[b][:, h * 256:(h + 1) * 256], in0=hs[:, h * 256:(h + 1) * 256],
                                                                           scalar=ssq[:, 4 + h:5 + h], in1=Gg[b][:, h * 256:(h + 1) * 256],
                                                                           op0=ALU.mult, op1=ALU.mult),
                                   reads=[t_hs, t_ssq, t_Gg[b]], writes=[t_yb[b]])
                            for cl in range(2):
                                k.dma("pool", ymv[2 * lt + cl], yb[b][cl * 64:(cl + 1) * 64, :], reads=[t_yb[b]], writes=[t_ym_scr])

                    for dr in range(2):
                        op("dve", lambda e: e.memset(C[:].rearrange("p h l -> p (h l)"), 0.0), writes=[t_C])
                        op("dve", lambda e: e.memset(Cb[:].rearrange("p h l -> p (h l)"), 0.0), writes=[t_Cb])
                        op("dve", lambda e: e.memset(nst[:], 0.0), writes=[t_n])
                        op("dve", lambda e: e.memset(nb[:], 0.0), writes=[t_nb])
                        if dr == 0:
                            seq = [(0, False, None), (1, False, None)] + [(2 + lt, True, lt) for lt in range(NT_LAT)]
                        else:
                            seq = [(1, False, None), (0, False, None)] + [(2 + lt, True, lt) for lt in range(NT_LAT - 1, -1, -1)]
                        groups = [seq[0:2]] + [seq[2 + 4 * g_:6 + 4 * g_] for g_ in range(NT_LAT // 4)]
                        ginfo = []
                        tinfo = []
                        for gi_, grp in enumerate(groups):
                            tiles = [t_[0] for t_ in grp]
                            c0g = min(tile_col(t_) for t_ in tiles)
                            ginfo.append((gi_ % 2, c0g, 128 * len(tiles), tiles))
                            for t_ in tiles:
                                tinfo.append((gi_, gi_ % 2, tile_col(t_) - c0g))
                        PQ = [convq(*gi__) for gi__ in ginfo]
                        gates_prologue(dr, seq)
                        exhaust(PQ[0])
                        exhaust(prep(dr, seq[0][0], seq[0][1], 0, tinfo[0][1], tinfo[0][2]))
                        for i, (ti, full, lt) in enumerate(seq):
                            if dr == 1 and full:
                                k.dma("sp", hfl[i % 2][:], hf_scr[lt * 128:(lt + 1) * 128, :], reads=[t_hf_scr[lt]], writes=[t_hfl[i % 2]])
                            gcur = tinfo[i][0]
                            nxt = PQ[gcur + 1] if gcur + 1 < len(PQ) else iter(())
                            nch = 8 // len(groups[gcur])
                            P = prep(dr, seq[i + 1][0], seq[i + 1][1], i + 1, tinfo[i + 1][1], tinfo[i + 1][2]) if i + 1 < len(seq) else iter(())
                            S_ = scan(dr, ti, full, i, lt, tinfo[i][1], tinfo[i][2])
                            step(S_); step(S_)
                            for _ in range(nch // 2):
                                step(nxt)
                            step(S_); step(S_)
                            for _ in range(nch - nch // 2):
                                step(nxt)
                            step(S_); step(S_)
                            exhaust(P)
                            exhaust(S_)

            k.barrier()
            load_w(wmix, t_wmix, win_d, 0, 3104)
            phase_U(False)
            k.barrier()
            if stage >= 1:
                gla_phase()
                k.barrier()
            if stage >= 2:
                load_w(wmix, t_wmix, win_d, 3104, 6192)
                phase_U(True)
                k.barrier()
                mlstm_phase()
                k.barrier()


        tb = [0]

        def tbank():
            i = tb[0] % 8
            tb[0] += 1
            return ps[i], t_ps[i]

        def load_w2(dst, t_dst, src, c0, c1, nk):
            for kc in range(nk):
                k.dma("pool", dst[:, kc, 0:c1 - c0], src[kc * 128:(kc + 1) * 128, c0:c1], writes=[t_dst])

        def tail_a(prefetch):
            with ExitStack() as st:
                wgt = alloc(st, "wgt", [128, 8, 2048], BF16)
                wbg = alloc(st, "wbg", [128, 8, D], BF16)
                wbm = alloc(st, "wbm", [128, 8, D], BF16)
                wo = alloc(st, "wo", [128, 8, D], BF16)
                xs = [alloc(st, "txs%d" % i, [128, D], F32) for i in range(2)]; t_xs = [Tok(), Tok()]
                xn = alloc(st, "txn", [128, D], F32); t_xn = Tok()
                small = alloc(st, "tsmall", [128, 2], F32); t_small = Tok()
                uTt = alloc(st, "uTt", [128, 8, 128], BF16); t_uTt = Tok()
                ygt = [alloc(st, "ygt%d" % i, [128, D], BF16) for i in range(2)]; t_ygt = [Tok(), Tok()]
                ymt = [alloc(st, "ymt%d" % i, [128, D], BF16) for i in range(2)]; t_ymt = [Tok(), Tok()]
                yTg = alloc(st, "yTg", [128, 8, 128], BF16); t_yTg = Tok()
                yTm = alloc(st, "yTm", [128, 8, 128], BF16); t_yTm = Tok()
                gsig = alloc(st, "gsig", [128, 2048], F32); t_gsig = Tok()
                ysg = alloc(st, "ysg", [128, D], F32); t_ysg = Tok()
                tmpm = alloc(st, "tmpm", [128, D], F32); t_tmpm = Tok()
                junk, t_junk = tmpm, t_tmpm
                ysb = alloc(st, "ysb", [128, D], BF16); t_ysb = Tok()
                ysT = alloc(st, "ysT", [128, 8, 128], BF16); t_ysT = Tok()
                x1t = [alloc(st, "x1t%d" % i, [128, D], F32) for i in range(2)]; t_x1t = [Tok(), Tok()]
                stagings = [(gsig[:, 0:1024], Tok()), (gsig[:, 1024:2048], Tok()), (ysg[:, :], Tok()), (tmpm[:, :], Tok())]
                pieces = []
                t_wgt, t_wbg, t_wbm, t_wo = [], [], [], []
                for kc in range(8):
                    for half in range(2):
                        tk = Tok(); t_wgt.append(tk)
                        pieces.append((wgt[:, kc, half * 1024:(half + 1) * 1024], win_d[kc * 128:(kc + 1) * 128, 6192 + half * 1024:6192 + (half + 1) * 1024], tk))
                for wt_, wd_, tl_ in ((wbg, wbg_d, t_wbg), (wbm, wbm_d, t_wbm), (wo, wo_d, t_wo)):
                    for kc in range(8):
                        tk = Tok(); tl_.append(tk)
                        pieces.append((wt_[:, kc, :], wd_[kc * 128:(kc + 1) * 128, :], tk))
                cengs = ["dve", "act", "pool"]
                for j, (dst, src, tk) in enumerate(pieces):
                    stg, tstg = stagings[j % 4]
                    k.dma("sp", stg, src, writes=[tstg])
                    ce = cengs[j % 3]
                    if ce == "act":
                        op("act", lambda e: e.copy(out=dst, in_=stg), reads=[tstg], writes=[tk])
                    else:
                        op(ce, lambda e: e.tensor_copy(out=dst, in_=stg), reads=[tstg], writes=[tk])
                for real, stgs in ((t_gsig, (stagings[0][1], stagings[1][1])), (t_ysg, (stagings[2][1],)), (t_tmpm, (stagings[3][1],))):
                    mr = {}
                    for ts in stgs:
                        if ts.w is not None:
                            mr[ts.w[0]] = max(mr.get(ts.w[0], 0), ts.w[1])
                        for s_k, v_k in ts.r.items():
                            mr[s_k] = max(mr.get(s_k, 0), v_k)
                    real.w = None
                    real.r = mr

                def tr8(src, t_src, dst, t_dst):
                    pb, tpb = tbank()
                    pbb = pb[:, :].bitcast(BF16)
                    for kc in range(8):
                        op("pe", lambda e: e.transpose(out=pbb[:, kc * 128:(kc + 1) * 128], in_=src[:, kc * 128:(kc + 1) * 128], identity=identb[:]),
                           reads=[t_src, t_identb], writes=[tpb], signal=(kc == 7))
                    op("act", lambda e: e.copy(out=dst[:].rearrange("p a b -> p (a b)"), in_=pbb[:, :]), reads=[tpb], writes=[t_dst])

                for i in range(NT_OWN):
                    b = i % 2
                    k.dma("sp", xs[b][:], x_d[i * 128:(i + 1) * 128, :], writes=[t_xs[b]])
                    k.dma("sp", ygt[b][:], ygla_scr[i * 128:(i + 1) * 128, :], reads=[t_ygla_scr[i]], writes=[t_ygt[b]])
                    k.dma("sp", ymt[b][:], ym_scr[i * 128:(i + 1) * 128, :], reads=[t_ym_scr], writes=[t_ymt[b]])
                    norm_transpose(xs[b][:], t_xs[b], xn, t_xn, small, t_small, junk, t_junk,
                                   lambda kc: scU[:, kc, 0:1], lambda kc: modpp[:, 0, kc, 0:1], [t_scU, t_modpp],
                                   lambda kc: uTt[:, kc, :], t_uTt)
                    for q in range(4):
                        pb, tpb = tbank()
                        for kc in range(8):
                            op("pe", lambda e: e.matmul(pb[:, :], lhsT=uTt[:, kc, :], rhs=wgt[:, kc, q * 512:(q + 1) * 512], start=(kc == 0), stop=(kc == 7)),
                               reads=[t_uTt] + t_wgt, writes=[tpb], signal=(kc == 7))
                        op("act", lambda e: e.activation(out=gsig[:, q * 512:(q + 1) * 512], in_=pb[:, :], func=AF.Sigmoid), reads=[tpb], writes=[t_gsig])
                    tr8(ygt[b], t_ygt[b], yTg, t_yTg)
                    tr8(ymt[b], t_ymt[b], yTm, t_yTm)
                    for half in range(2):
                        pb, tpb = tbank()
                        for kc in range(8):
                            op("pe", lambda e: e.matmul(pb[:, :], lhsT=yTg[:, kc, :], rhs=wbg[:, kc, half * 512:(half + 1) * 512], start=(kc == 0), stop=(kc == 7)),
                               reads=[t_yTg] + t_wbg, writes=[tpb], signal=(kc == 7))
                        op("dve", lambda e: e.tensor_tensor(out=ysg[:, half * 512:(half + 1) * 512], in0=pb[:, :], in1=gsig[:, half * 512:(half + 1) * 512], op=ALU.mult),
                           reads=[tpb, t_gsig], writes=[t_ysg])
                    for half in range(2):
                        pb, tpb = tbank()
                        for kc in range(8):
                            op("pe", lambda e: e.matmul(pb[:, :], lhsT=yTm[:, kc, :], rhs=wbm[:, kc, half * 512:(half + 1) * 512], start=(kc == 0), stop=(kc == 7)),
                               reads=[t_yTm] + t_wbm, writes=[tpb], signal=(kc == 7))
                        op("dve", lambda e: e.tensor_tensor(out=tmpm[:, half * 512:(half + 1) * 512], in0=pb[:, :], in1=gsig[:, 1024 + half * 512:1536 + half * 512], op=ALU.mult),
                           reads=[tpb, t_gsig], writes=[t_tmpm])
                    op("dve", lambda e: e.tensor_tensor(out=ysb[:], in0=ysg[:], in1=tmpm[:], op=ALU.add), reads=[t_ysg, t_tmpm], writes=[t_ysb])
                    tr8(ysb, t_ysb, ysT, t_ysT)
                    for half in range(2):
                        pb, tpb = tbank()
                        for kc in range(8):
                            op("pe", lambda e: e.matmul(pb[:, :], lhsT=ysT[:, kc, :], rhs=wo[:, kc, half * 512:(half + 1) * 512], start=(kc == 0), stop=(kc == 7)),
                               reads=[t_ysT] + t_wo, writes=[tpb], signal=(kc == 7))
                        op("dve", lambda e: e.tensor_tensor(out=x1t[b][:, half * 512:(half + 1) * 512], in0=pb[:, :], in1=G1[:, half * 512:(half + 1) * 512], op=ALU.mult),
                           reads=[tpb, t_G1], writes=[t_x1t[b]])
                    op("dve", lambda e: e.tensor_tensor(out=x1t[b][:], in0=x1t[b][:], in1=xs[b][:], op=ALU.add), reads=[t_xs[b]], writes=[t_x1t[b]])
                    k.dma("pool", x1_scr[i * 128:(i + 1) * 128, :], x1t[b][:], reads=[t_x1t[b]], writes=[t_x1_scr[i]])
                    for _ in range(3):
                        if prefetch:
                            prefetch.pop(0)()

        def tail_b(wfo, t_wfo, wfi0, t_wfi0, prefetch):
            with ExitStack() as st:
                while prefetch:
                    prefetch.pop(0)()
                wfiR = alloc(st, "wfiR", [128, 8, 2 * (DFF - 384)], BF16)

                def wcol(part, fc):
                    if fc < 3:
                        return wfi0, part * 384 + fc * 128
                    return wfiR, part * (DFF - 384) + (fc - 3) * 128
                fgbc = alloc(st, "fgbc", [128, D], F32); t_fgbc = Tok()
                x1t = [alloc(st, "fx1t%d" % i, [128, D], F32) for i in range(4)]; t_x1t = [Tok() for _ in range(4)]
                xn = alloc(st, "fxn", [128, D], F32); t_xn = Tok()
                small = alloc(st, "fsmall", [128, 4], F32); t_small = Tok()
                u2T = alloc(st, "u2T", [128, 8, 512], BF16); t_u2T = Tok()
                sa = [alloc(st, "sa%d" % i, [128, 512], F32) for i in range(1)]; t_sa = [Tok()]
                hT = alloc(st, "hT", [128, 22, 512], BF16); t_hT = Tok()
                x2 = alloc(st, "x2", [128, D], F32); t_x2 = Tok()
                k.dma("sp", fgbc[:], vbc_d[0:1, 2048:3072].to_broadcast([128, 1024]), writes=[t_fgbc])
                t_wfib = [[t_wfi0], [], [], []]
                fcb = [0, 3, 9, 15, 22]
                hTf = hT[:].rearrange("p a b -> p (a b)").bitcast(F32)
                stg_t = [Tok() for _ in range(5)]
                cengs = ["dve", "act", "pool"]
                j = 0
                for blk in range(1, 4):
                    f0, f1 = fcb[blk] * 128, fcb[blk + 1] * 128
                    for part in range(2):
                        for kc in range(8):
                            c_ = part * (DFF - 384) + f0 - 384
                            n_ = f1 - f0
                            stg = hTf[:, (j % 5) * 1024:(j % 5) * 1024 + n_]
                            tstg = stg_t[j % 5]
                            tk = Tok()
                            t_wfib[blk].append(tk)
                            k.dma("sp", stg, wfi_d[kc * 128:(kc + 1) * 128, part * DFF + f0:part * DFF + f1], writes=[tstg])
                            dst = wfiR[:, kc, c_:c_ + n_]
                            ce = cengs[j % 3]
                            if ce == "act":
                                op("act", lambda e: e.copy(out=dst, in_=stg), reads=[tstg], writes=[tk])
                            else:
                                op(ce, lambda e: e.tensor_copy(out=dst, in_=stg), reads=[tstg], writes=[tk])
                            j += 1
                mr = {}
                for ts in stg_t:
                    if ts.w is not None:
                        mr[ts.w[0]] = max(mr.get(ts.w[0], 0), ts.w[1])
                    for s_k, v_k in ts.r.items():
                        mr[s_k] = max(mr.get(s_k, 0), v_k)
                t_hT.w = None
                t_hT.r = mr
                for s_i in range(NT_OWN // 4):
                    for j in range(4):
                        ti = s_i * 4 + j
                        k.dma("sp", x1t[j][:], x1_scr[ti * 128:(ti + 1) * 128, :], reads=[t_x1_scr[ti]], writes=[t_x1t[j]])
                        norm_transpose(x1t[j][:], t_x1t[j], xn, t_xn, small, t_small, x2, t_x2,
                                       lambda kc: sc2[:, kc:kc + 1], lambda kc: modpp[:, 2, kc, 0:1], [t_sc2, t_modpp],
                                       lambda kc: u2T[:, kc, j * 128:(j + 1) * 128], t_u2T)
                    for fc in range(22):
                        pa, tpa = tbank()
                        for kc in range(8):
                            wt_, wc_ = wcol(0, fc)
                            op("pe", lambda e: e.matmul(pa[:, :], lhsT=wt_[:, kc, wc_:wc_ + 128], rhs=u2T[:, kc, :], start=(kc == 0), stop=(kc == 7)),
                               reads=t_wfib[0 if fc < 3 else (1 if fc < 9 else (2 if fc < 15 else 3))] + [t_u2T], writes=[tpa], signal=(kc == 7))
                        pbk, tpbk = tbank()
                        for kc in range(8):
                            wt_, wc_ = wcol(1, fc)
                            op("pe", lambda e: e.matmul(pbk[:, :], lhsT=wt_[:, kc, wc_:wc_ + 128], rhs=u2T[:, kc, :], start=(kc == 0), stop=(kc == 7)),
                               reads=t_wfib[0 if fc < 3 else (1 if fc < 9 else (2 if fc < 15 else 3))] + [t_u2T], writes=[tpbk], signal=(kc == 7))
                        sb_ = 0
                        op("act", lambda e: e.activation(out=sa[sb_][:], in_=pa[:, :], func=AF.Silu), reads=[tpa], writes=[t_sa[sb_]])
                        op("dve", lambda e: e.tensor_tensor(out=hT[:, fc, :], in0=sa[sb_][:], in1=pbk[:, :], op=ALU.mult),
                           reads=[t_sa[sb_], tpbk], writes=[t_hT])
                    for j in range(4):
                        ti = s_i * 4 + j
                        for half in range(2):
                            pb, tpb = tbank()
                            for fc in range(22):
                                op("pe", lambda e: e.matmul(pb[:, :], lhsT=hT[:, fc, j * 128:(j + 1) * 128], rhs=wfo[:, fc, half * 512:(half + 1) * 512],
                                                            start=(fc == 0), stop=(fc == 21)),
                                   reads=[t_hT, t_wfo], writes=[tpb], signal=(fc == 21))
                            op("dve", lambda e: e.tensor_tensor(out=x2[:, half * 512:(half + 1) * 512], in0=pb[:, :], in1=G2[:, half * 512:(half + 1) * 512], op=ALU.mult),
                               reads=[tpb, t_G2], writes=[t_x2])
                        op("dve", lambda e: e.tensor_tensor(out=x2[:], in0=x2[:], in1=x1t[j][:], op=ALU.add), reads=[t_x1t[j]], writes=[t_x2])
                        op("act", lambda e: e.activation(out=xn[:], in_=x2[:], func=AF.Square, accum_out=small[:, 2:3]),
                           reads=[t_x2], writes=[t_xn, t_small])
                        op("act", lambda e: e.activation(out=small[:, 3:4], in_=small[:, 2:3], func=AF.Sqrt, bias=eps_c, scale=1.0 / D),
                           reads=[t_small, t_cv], writes=[t_small])
                        op("dve", lambda e: e.reciprocal(out=small[:, 3:4], in_=small[:, 3:4]), reads=[t_small], writes=[t_small])
                        op("dve", lambda e: e.scalar_tensor_tensor(out=x2[:], in0=x2[:], scalar=small[:, 3:4], in1=fgbc[:], op0=ALU.mult, op1=ALU.mult),
                           reads=[t_small, t_fgbc], writes=[t_x2])
                        k.dma("pool", out_d[ti * 128:(ti + 1) * 128, :], x2[:], reads=[t_x2], writes=[t_out])

        if stage >= 3:
            with ExitStack() as stt:
                wfo = alloc(stt, "wfo", [128, 22, D], BF16); t_wfo = Tok()
                wfi0 = alloc(stt, "wfi0", [128, 8, 2 * 384], BF16); t_wfi0 = Tok()
                prefetch = []
                for part in range(2):
                    for kc in range(8):
                        prefetch.append(lambda part=part, kc=kc: k.dma("pool", wfi0[:, kc, part * 384:(part + 1) * 384],
                                                                      wfi_d[kc * 128:(kc + 1) * 128, part * DFF:part * DFF + 384], writes=[t_wfi0]))
                for fc in range(22):
                    prefetch.append(lambda fc=fc: k.dma("pool", wfo[:, fc, :], wfo_d[fc * 128:(fc + 1) * 128, :], writes=[t_wfo]))
                k.barrier()
                tail_a(prefetch)
                if stage >= 4:
                    k.barrier()
                    tail_b(wfo, t_wfo, wfi0, t_wfi0, prefetch)

        fin = list(t_ygla_scr) + [t_ym_scr, t_out] + t_x1_scr + dbg_list
        k.finish(fin, "sp")
        k.finish(fin, "pool")
        k.check_deadlock()
        print("build: ops", k.nops, "waits", k.nwaits, "cnt", {e: k.cnt[e] for e in k.eng})
    return nc


def _consts():
    c = np.zeros((128, 768), np.float32)
    m = np.arange(128)[:, None]
    l = np.arange(128)[None, :]
    c[:, 0:128] = np.eye(128)
    c[:, 128:256] = (m <= l)
    c[:, 256:384] = (m >= l)
    c[:, 384:512] = (m > l)
    c[:, 512:640] = (m < l)
    c[:, 640:768] = 1.0
    return c


def _variant(inp, flip):
    w_in = inp["w_in"][0]
    w_up = inp["gla_w_up"][0]
    b_dec = inp["gla_b_dec"][0]
    conv_w = inp["mlstm_conv_w"][0]
    b_gate = inp["mlstm_b_gate"][0]
    if flip:
        idx = np.arange(NIN)
        idx[3072:3088] = np.arange(3088, 3104)
        idx[3088:3104] = np.arange(3072, 3088)
        g0 = 6176
        idx[g0:g0 + 8] = np.arange(g0 + 8, g0 + 16)
        idx[g0 + 8:g0 + 16] = np.arange(g0, g0 + 8)
        w_in = w_in[:, idx]
        w_up = w_up[::-1]
        b_dec = b_dec[::-1]
        conv_w = conv_w[::-1]
        b_gate = b_gate[[2, 3, 0, 1]]
    b_ada = inp["b_ada"][0]
    pp = lambda v, n: np.asarray(v).reshape(n, 128).T
    vpp = np.concatenate([pp(b_ada, 48), pp(inp["norm1_g"][0], 8), pp(inp["norm2_g"][0], 8),
                          np.asarray(conv_w).reshape(3, 8, 128).transpose(2, 0, 1).reshape(128, 24),
                          pp(inp["mlstm_conv_b"][0], 8)], axis=1)
    vbc = np.concatenate([b_ada[2048:3072], b_ada[5120:6144], inp["final_g"], inp["gla_norm_g"][0],
                          inp["mlstm_norm_g"][0], np.asarray(b_gate).reshape(16)])[None, :]
    f = lambda a: np.ascontiguousarray(a, dtype=np.float32)
    return dict(w_in=f(w_in), w_up=f(w_up), bdec=f(np.asarray(b_dec).reshape(1, 1024)), vpp=f(vpp), vbc=f(vbc),
                w_ada=f(inp["w_ada"][0]), consts=_consts(), w_br_gla=f(inp["w_br_gla"][0]), w_br_m=f(inp["w_br_mlstm"][0]),
                w_out=f(inp["w_out"][0]), w_ffn_in=f(inp["w_ffn_in"][0]), w_ffn_out=f(inp["w_ffn_out"][0]))


def make_in_maps(inp, cores=None):
    inp = {k_: np.asarray(v) for k_, v in inp.items()}
    var = [_variant(inp, False), _variant(inp, True)]
    maps = []
    for b in range(4):
        for s in range(2):
            m = dict(var[s])
            xb = inp["x"][b]
            cb = inp["ctx"][b]
            if s:
                xb = xb[::-1]
                cb = cb[::-1]
            m["x"] = np.ascontiguousarray(xb, dtype=np.float32)
            m["ctx"] = np.ascontiguousarray(cb, dtype=np.float32)
            cc = np.stack([inp["c"][b], inp["c_ctx"]], -1).reshape(8, 128, 2).transpose(1, 0, 2)
            m["cct"] = np.ascontiguousarray(cc, dtype=np.float32)
            maps.append(m)
    return maps


def kernel(**inputs):
    nc = build(9)
    maps = make_in_maps(inputs)
    res = run_bass_kernel_spmd(nc, maps, core_ids=list(range(8)))
    out = np.zeros((4, T, D), np.float32)
    for b in range(4):
        out[b, 0:2048] = np.asarray(res.results[2 * b]["out"])
        out[b, 2048:] = np.asarray(res.results[2 * b + 1]["out"])[::-1]
    return out
```

```python
import math
from contextlib import ExitStack
import numpy as np
import concourse.bass as bass
import concourse.mybir as mybir
from concourse.bass_utils import run_bass_kernel_spmd

F32 = mybir.dt.float32
BF16 = mybir.dt.bfloat16
AF = mybir.ActivationFunctionType
ALU = mybir.AluOpType

D = 1024
T = 4096
TC = 256
NIN = 8240
DFF = 2816
NT_OWN = 16
NT_LAT = 32
UW = 4356
CTX0 = 1
LAT0 = 259
ALPHA = 128.0 ** -0.5
NV = 96
NB = 5136
C_ID, C_MIF, C_MIB, C_MAF, C_MAB, C_ONE = 0, 128, 256, 384, 512, 640


class Tok:
    __slots__ = ("w", "r", "ex")

    def __init__(self, ex=False):
        self.w = None
        self.r = {}
        self.ex = ex


class K:
    def __init__(self, nc, stack, ndma=8):
        self.nc = nc
        self.eng = {"pe": nc.tensor, "act": nc.scalar, "dve": nc.vector, "pool": nc.gpsimd, "sp": nc.sync}
        self.semh = {}
        self.cnt = {}
        self.waited = {e: {} for e in self.eng}
        for e in self.eng:
            self.semh[e] = stack.enter_context(nc.semaphore("s_" + e))
            self.cnt[e] = 0
        self.dma_slots = {}
        self.dma_rr = {}
        for q in ("sp", "pool"):
            self.dma_slots[q] = []
            for i in range(ndma):
                nm = "d_%s%d" % (q, i)
                self.semh[nm] = stack.enter_context(nc.semaphore(nm))
                self.cnt[nm] = 0
                self.dma_slots[q].append(nm)
            self.dma_rr[q] = 0
        self.nwaits = 0
        self.nops = 0
        self.log = {e: [] for e in self.eng}

    def _wait(self, e, s, v):
        if self.waited[e].get(s, 0) >= v:
            return
        self.eng[e].wait_ge(self.semh[s], v)
        self.waited[e][s] = v
        self.nwaits += 1
        self.log[e].append(("wait", s, v))

    def _deps(self, e, reads, writes):
        deps = {}
        for t in reads:
            if t.w is not None:
                s, v = t.w
                deps[s] = max(deps.get(s, 0), v)
        for t in writes:
            if t.w is not None:
                s, v = t.w
                deps[s] = max(deps.get(s, 0), v)
            for s, v in t.r.items():
                deps[s] = max(deps.get(s, 0), v)
        for s, v in deps.items():
            if e == "pe" and s == "pe":
                continue
            self._wait(e, s, v)

    def _record(self, ticket, reads, writes):
        s, v = ticket
        for t in reads:
            t.r[s] = max(t.r.get(s, 0), v)
        for t in writes:
            t.w = ticket
            t.r = {}

    def op(self, e, fn, reads=(), writes=(), signal=True):
        if e != "pe":
            exr = [t for t in reads if t.ex]
            if exr:
                writes = list(writes) + exr
        self._deps(e, reads, writes)
        ins = fn(self.eng[e])
        self.nops += 1
        if signal:
            self.cnt[e] += 1
            ins.then_inc(self.semh[e], 1)
            ticket = (e, self.cnt[e])
            self.log[e].append(("inc", e, 1, self.nops))
        else:
            assert e == "pe"
            ticket = (e, self.cnt[e] + 1)
        self._record(ticket, reads, writes)
        return ticket

    def dma(self, q, out, in_, reads=(), writes=(), **kw):
        slots = self.dma_slots[q]
        nm = slots[self.dma_rr[q] % len(slots)]
        self.dma_rr[q] += 1
        if self.cnt[nm] > 0:
            self._wait(q, nm, self.cnt[nm])
        self._deps(q, reads, writes)
        ins = self.eng[q].dma_start(out=out, in_=in_, **kw)
        self.cnt[nm] += 16
        ins.then_inc(self.semh[nm], 16)
        self.log[q].append(("inc", nm, 16, self.nops))
        ticket = (nm, self.cnt[nm])
        self._record(ticket, reads, writes)
        self.nops += 1
        return ticket

    def check_deadlock(self):
        sem = {}
        pos = {e: 0 for e in self.eng}
        progress = True
        while progress:
            progress = False
            for e in self.eng:
                lg = self.log[e]
                while pos[e] < len(lg):
                    it = lg[pos[e]]
                    if it[0] == "wait":
                        if sem.get(it[1], 0) >= it[2]:
                            pos[e] += 1
                            progress = True
                        else:
                            break
                    else:
                        sem[it[1]] = sem.get(it[1], 0) + it[2]
                        pos[e] += 1
                        progress = True
        stuck = {e: (pos[e], len(self.log[e]), self.log[e][pos[e]]) for e in self.eng if pos[e] < len(self.log[e])}
        if stuck:
            print("DEADLOCK:", stuck, {k_: v for k_, v in sem.items()})
        else:
            print("deadlock check: OK")
        return not stuck

    def barrier(self):
        snap = dict(self.cnt)
        for e in self.eng:
            for s_, v in snap.items():
                if v > 0:
                    self._wait(e, s_, v)

    def finish(self, toks, e="sp"):
        for t in toks:
            if t.w is not None:
                self._wait(e, t.w[0], t.w[1])


def build(stage=9):
    nc = bass.Bass("TRN2", target_bir_lowering=False)
    di = lambda n, s, dt=F32: nc.dram_tensor(n, s, dt, kind="ExternalInput").ap()
    x_d = di("x", [T, D])
    ctx_d = di("ctx", [TC, D])
    cct_d = di("cct", [128, 8, 2])
    wada_d = di("w_ada", [D, 6 * D])
    win_d = di("w_in", [D, NIN])
    wup_d = di("w_up", [2, 16, 512])
    bdec_d = di("bdec", [1, 1024])
    cst_d = di("consts", [128, 768])
    vpp_d = di("vpp", [128, NV])
    vbc_d = di("vbc", [1, NB])
    wbg_d = di("w_br_gla", [D, D])
    wbm_d = di("w_br_m", [D, D])
    wo_d = di("w_out", [D, D])
    wfi_d = di("w_ffn_in", [D, 2 * DFF])
    wfo_d = di("w_ffn_out", [DFF, D])
    out_d = nc.dram_tensor("out", [NT_OWN * 128, D], F32, kind="ExternalOutput").ap()
    of_scr = nc.dram_tensor("of_scr", [NT_OWN * 128, D], F32, kind="Internal" if stage >= 9 else "ExternalOutput").ap()
    hf_scr = nc.dram_tensor("hf_scr", [T, D], F32, kind="Internal" if stage >= 9 else "ExternalOutput").ap()
    if stage < 9:
        ygla_scr = nc.dram_tensor("ygla", [NT_OWN * 128, D], BF16, kind="ExternalOutput").ap()
        ym_scr = nc.dram_tensor("ym", [T, D], BF16, kind="ExternalOutput").ap()
    else:
        ygla_scr = nc.dram_tensor("ygla", [NT_OWN * 128, D], BF16, kind="Internal").ap()
        ym_scr = nc.dram_tensor("ym", [T, D], BF16, kind="Internal").ap()
    x1_scr = nc.dram_tensor("x1_scr", [NT_OWN * 128, D], F32, kind="Internal" if stage >= 9 else "ExternalOutput").ap()
    t_of_scr = [Tok() for _ in range(NT_OWN)]
    t_hf_scr = [Tok() for _ in range(NT_LAT)]
    t_ygla_scr = [Tok() for _ in range(NT_OWN)]
    t_ym_scr = Tok()
    t_x1_scr = [Tok() for _ in range(NT_OWN)]
    t_out = Tok()

    with ExitStack() as st0:
        k = K(nc, st0)
        op = k.op

        dbg_list = []
        dbg_pool = [st0.enter_context(nc.sbuf_tensor("dbgs_%d" % i, [128, 128], F32)) for i in range(16 if stage < 3 else 0)]

        def dbg(name, ap, toks, n):
            if stage >= 3:
                return
            dd = nc.dram_tensor("dbg_" + name, [128, n], F32, kind="ExternalOutput").ap()
            stg = dbg_pool.pop()[:, 0:n]
            tk = Tok()
            np_ = ap.shape[0]
            op("dve", lambda e: e.memset(stg, 0.0), writes=[tk])
            op("dve", lambda e: e.tensor_copy(out=stg[0:np_, :], in_=ap), reads=toks, writes=[tk])
            td = Tok()
            k.dma("pool", dd[:, :], stg, reads=[tk], writes=[td])
            dbg_list.append(td)

        uniq = [0]

        def alloc(st, name, shape, dt):
            uniq[0] += 1
            return st.enter_context(nc.sbuf_tensor("sb%d_%s" % (uniq[0], name), shape, dt))

        cst = alloc(st0, "cst", [128, 768], F32); t_cst = Tok()
        vpp = alloc(st0, "vpp", [128, NV], F32); t_vpp = Tok()
        identb = alloc(st0, "identb", [128, 128], BF16); t_identb = Tok()
        mask4 = [alloc(st0, "mask4_%d" % d, [128, 4, 128], BF16) for d in range(2)]; t_mask4 = Tok()
        onesb = alloc(st0, "onesb", [128, 1], BF16); t_onesb = Tok()
        cvals = alloc(st0, "cvals", [128, 4], F32); t_cv = Tok()
        modpp = alloc(st0, "modpp", [128, 4, 8, 2], F32); t_modpp = Tok()
        scU = alloc(st0, "scU", [128, 8, 2], F32); t_scU = Tok()
        sc2 = alloc(st0, "sc2", [128, 8], F32); t_sc2 = Tok()
        G1 = alloc(st0, "G1", [128, D], F32); t_G1 = Tok()
        G2 = alloc(st0, "G2", [128, D], F32); t_G2 = Tok()
        ps = [st0.enter_context(nc.psum_tensor("ps%d" % i, [128, 512], F32)) for i in range(8)]
        t_ps = [Tok(ex=True) for _ in range(8)]
        ident = cst[:, C_ID:C_ID + 128]
        MI = [cst[:, C_MIF:C_MIF + 128], cst[:, C_MIB:C_MIB + 128]]
        MA = [cst[:, C_MAF:C_MAF + 128], cst[:, C_MAB:C_MAB + 128]]
        ones = cst[:, C_ONE:C_ONE + 128]
        one_c = cvals[:, 0:1]
        eps_c = cvals[:, 1:2]
        lna_c = cvals[:, 2:3]
        VP_BADA, VP_N1, VP_N2, VP_CW, VP_CB = 0, 48, 56, 64, 88

        k.dma("sp", cst[:], cst_d[:, :], writes=[t_cst])
        k.dma("sp", vpp[:], vpp_d[:, :], writes=[t_vpp])
        op("dve", lambda e: e.memset(cvals[:, 0:1], 1.0), writes=[t_cv])
        op("dve", lambda e: e.memset(cvals[:, 1:2], 1e-6), writes=[t_cv])
        op("dve", lambda e: e.memset(cvals[:, 2:3], math.log(ALPHA)), writes=[t_cv])
        op("dve", lambda e: e.memset(cvals[:, 3:4], 0.0), writes=[t_cv])
        op("dve", lambda e: e.tensor_copy(out=identb[:], in_=ident), reads=[t_cst], writes=[t_identb])
        op("dve", lambda e: e.tensor_copy(out=onesb[:], in_=cst[:, C_ONE:C_ONE + 1]), reads=[t_cst], writes=[t_onesb])
        for d in range(2):
            for h in range(4):
                op("dve", lambda e: e.tensor_copy(out=mask4[d][:, h, :], in_=MI[d]), reads=[t_cst], writes=[t_mask4])

        prep_banks = [5, 6, 7]
        prr = [0]

        def pbank():
            i = prep_banks[prr[0] % len(prep_banks)]
            prr[0] += 1
            return ps[i], t_ps[i]

        with ExitStack() as st:
            scc = alloc(st, "scc", [128, 8, 2], F32); t_scc = Tok()
            wa = [alloc(st, "wa%d" % i, [128, 8, D], F32) for i in range(2)]; t_wa = [[Tok() for _ in range(8)] for _ in range(2)]
            bbc = alloc(st, "bbc", [128, D], F32); t_bbc = Tok()
            k.dma("sp", scc[:], cct_d[:, :, :], writes=[t_scc])
            op("act", lambda e: e.activation(out=scc[:], in_=scc[:], func=AF.Silu), reads=[t_scc], writes=[t_scc])
            psA, t_psA = ps[0], t_ps[0]
            mrow = alloc(st, "mrow", [2, D], F32); t_mrow = Tok()
            gi = 0
            for g in range(6):
                w_, tw_ = wa[g % 2], t_wa[g % 2]
                for kc in range(8):
                    k.dma("sp", w_[:, kc, :], wada_d[kc * 128:(kc + 1) * 128, g * D:(g + 1) * D], writes=[tw_[kc]])
                for half in range(2):
                    pb, tpb = pbank()
                    for kc in range(8):
                        op("pe", lambda e: e.matmul(pb[0:2, :], lhsT=scc[:, kc, :], rhs=w_[:, kc, half * 512:(half + 1) * 512],
                                                    start=(kc == 0), stop=(kc == 7)),
                           reads=[tw_[kc], t_scc], writes=[tpb], signal=(kc == 7))
                    op("act", lambda e: e.copy(out=mrow[:, half * 512:(half + 1) * 512], in_=pb[0:2, :]), reads=[tpb], writes=[t_mrow])
                if g in (0, 1, 3, 4):
                    for j in range(8):
                        c0 = (gi * 8 + j) * 2
                        op("pe", lambda e: e.transpose(out=psA[:, c0:c0 + 2], in_=mrow[0:2, j * 128:(j + 1) * 128], identity=cst[0:2, C_ID:C_ID + 2]),
                           reads=[t_mrow, t_cst], writes=[t_psA], signal=(j == 7))
                    gi += 1
                else:
                    Gt, tG = (G1, t_G1) if g == 2 else (G2, t_G2)
                    voff = 0 if g == 2 else 1024
                    k.dma("sp", bbc[:], vbc_d[0:1, voff:voff + 1024].to_broadcast([128, 1024]), writes=[t_bbc])
                    for half in range(2):
                        pb, tpb = pbank()
                        op("pe", lambda e: e.matmul(pb[:, :], lhsT=cst[0:1, C_ONE:C_ONE + 128], rhs=mrow[0:1, half * 512:(half + 1) * 512],
                                                    start=True, stop=True),
                           reads=[t_mrow, t_cst], writes=[tpb])
                        op("dve", lambda e: e.tensor_tensor(out=Gt[:, half * 512:(half + 1) * 512], in0=pb[:, :],
                                                            in1=bbc[:, half * 512:(half + 1) * 512], op=ALU.add),
                           reads=[tpb, t_bbc], writes=[tG])
            psAv = psA[:, 0:64].rearrange("p (g j s) -> p g j s", g=4, j=8, s=2)
            for gi_, g in enumerate((0, 1, 3, 4)):
                for s in range(2):
                    op("dve", lambda e: e.tensor_tensor(out=modpp[:, gi_, :, s], in0=psAv[:, gi_, :, s],
                                                        in1=vpp[:, VP_BADA + g * 8:VP_BADA + (g + 1) * 8], op=ALU.add),
                       reads=[t_psA, t_vpp], writes=[t_modpp])
            for s in range(2):
                op("dve", lambda e: e.scalar_tensor_tensor(out=scU[:, :, s], in0=modpp[:, 1, :, s], scalar=1.0,
                                                           in1=vpp[:, VP_N1:VP_N1 + 8], op0=ALU.add, op1=ALU.mult),
                   reads=[t_modpp, t_vpp], writes=[t_scU])
            op("dve", lambda e: e.scalar_tensor_tensor(out=sc2[:, :], in0=modpp[:, 3, :, 0], scalar=1.0,
                                                       in1=vpp[:, VP_N2:VP_N2 + 8], op0=ALU.add, op1=ALU.mult),
               reads=[t_modpp, t_vpp], writes=[t_sc2])


        def norm_transpose(xs, t_xs, xn, t_xn, small, t_small, junk, t_junk, scale_ap, bias_ap, tsb, dst, t_dst):
            norm_part(xs, t_xs, xn, t_xn, small, t_small, junk, t_junk)
            transpose_part(xn, t_xn, scale_ap, bias_ap, tsb, dst, t_dst)

        def norm_part(xs, t_xs, xn, t_xn, small, t_small, junk, t_junk):
            op("act", lambda e: e.activation(out=junk[:], in_=xs, func=AF.Square, accum_out=small[:, 0:1]),
               reads=[t_xs], writes=[t_junk, t_small])
            op("act", lambda e: e.activation(out=small[:, 1:2], in_=small[:, 0:1], func=AF.Sqrt, bias=eps_c, scale=1.0 / D),
               reads=[t_small, t_cv], writes=[t_small])
            op("dve", lambda e: e.reciprocal(out=small[:, 1:2], in_=small[:, 1:2]), reads=[t_small], writes=[t_small])
            op("act", lambda e: e.activation(out=xn[:], in_=xs, func=AF.Copy, scale=small[:, 1:2]),
               reads=[t_xs, t_small], writes=[t_xn])

        def transpose_part(xn, t_xn, scale_ap, bias_ap, tsb, dst, t_dst):
            for half in range(2):
                pb, tpb = pbank()
                for j in range(4):
                    kc = half * 4 + j
                    op("pe", lambda e: e.transpose(out=pb[:, j * 128:(j + 1) * 128], in_=xn[:, kc * 128:(kc + 1) * 128], identity=ident),
                       reads=[t_xn, t_cst], writes=[tpb], signal=(j == 3))
                for j in range(4):
                    kc = half * 4 + j
                    if j % 2 == 0:
                        op("act", lambda e: e.activation(out=dst(kc), in_=pb[:, j * 128:(j + 1) * 128], func=AF.Identity,
                                                         scale=scale_ap(kc), bias=bias_ap(kc)),
                           reads=[tpb] + tsb, writes=[t_dst])
                    else:
                        op("dve", lambda e: e.tensor_scalar(out=dst(kc), in0=pb[:, j * 128:(j + 1) * 128], scalar1=scale_ap(kc),
                                                            scalar2=bias_ap(kc), op0=ALU.mult, op1=ALU.add),
                           reads=[tpb] + tsb, writes=[t_dst])

        with ExitStack() as stm:
            uT = alloc(stm, "uT", [128, 8, UW], BF16)
            t_uT = [Tok() for _ in range(34)]
            t_guard = Tok()
            wmix = alloc(stm, "wmix", [128, 8, 3104], BF16); t_wmix = [Tok() for _ in range(8)]
            for c in (0, 257, 258, UW - 1):
                op("dve", lambda e: e.memset(uT[:, :, c:c + 1], 0.0), writes=[t_guard])

            def tile_col(ti):
                return CTX0 + ti * 128 if ti < 2 else LAT0 + (ti - 2) * 128

            def phase_U(colmajor):
                with ExitStack() as st:
                    xs = [alloc(st, "xs%d" % i, [128, D], F32) for i in range(2)]; t_xs = [Tok(), Tok()]
                    xn = [alloc(st, "xn%d" % i, [128, D], F32) for i in range(2)]; t_xn = [Tok(), Tok()]
                    junk = alloc(st, "junkU", [128, D], F32); t_junk = Tok()
                    small = [alloc(st, "smallU%d" % i, [128, 2], F32) for i in range(2)]; t_small = [Tok(), Tok()]
                    xv = x_d.rearrange("(r c) d -> c r d", c=64)

                    def partA(ti):
                        b = ti % 2
                        if ti < 2:
                            k.dma("sp", xs[b][:], ctx_d[ti * 128:(ti + 1) * 128, :], writes=[t_xs[b]])
                        else:
                            lt = ti - 2
                            if colmajor:
                                for cl in range(2):
                                    k.dma("sp", xs[b][cl * 64:(cl + 1) * 64, :], xv[2 * lt + cl], writes=[t_xs[b]])
                            else:
                                k.dma("sp", xs[b][:], x_d[lt * 128:(lt + 1) * 128, :], writes=[t_xs[b]])
                        norm_part(xs[b][:], t_xs[b], xn[b], t_xn[b], small[b], t_small[b], junk, t_junk)

                    def partB(ti):
                        b = ti % 2
                        s = 1 if ti < 2 else 0
                        c0 = tile_col(ti)
                        transpose_part(xn[b], t_xn[b], lambda kc: scU[:, kc, s:s + 1], lambda kc: modpp[:, 0, kc, s:s + 1], [t_scU, t_modpp],
                                       lambda kc: uT[:, kc, c0:c0 + 128], t_uT[ti])

                    partA(0)
                    for ti in range(34):
                        if ti + 1 < 34:
                            partA(ti + 1)
                        partB(ti)

            def step(g):
                try:
                    next(g)
                except StopIteration:
                    pass

            def exhaust(g):
                for _ in g:
                    pass

            def load_w(dst, t_dst, src, c0, c1, nk=8, q="pool"):
                for kc in range(nk):
                    k.dma(q, dst[:, kc, 0:c1 - c0], src[kc * 128:(kc + 1) * 128, c0:c1], writes=[t_dst[kc]])

            def gla_phase():
                print("gla start cnt", dict(k.cnt))
                with ExitStack() as st:
                    wup = alloc(st, "wup", [16, 2, 512], F32); t_wup = Tok()
                    bdec = alloc(st, "bdec", [1, 1024], F32); t_bdec = Tok()
                    gnorm = alloc(st, "gnorm", [128, D], F32); t_gnorm = Tok()
                    lrT = alloc(st, "lrT", [16, 128], F32); t_lrT = Tok()
                    tmp = alloc(st, "gtmp", [128, 512], F32); t_tmp = Tok()
                    sp = alloc(st, "gsp", [128, 512], F32); t_sp = Tok()
                    dkk = alloc(st, "dkk", [128, 512], F32); t_dkk = Tok()
                    Ep = alloc(st, "Ep", [128, 512], F32); t_Ep = Tok()
                    Em = alloc(st, "Em", [128, 512], F32); t_Em = Tok()
                    kk = [alloc(st, "kk%d" % i, [128, 512], BF16) for i in range(2)]; t_kk = [Tok(), Tok()]
                    v = [alloc(st, "v%d" % i, [128, D], BF16) for i in range(2)]; t_v = [Tok(), Tok()]
                    Gg = [alloc(st, "Gg%d" % i, [128, D], F32) for i in range(2)]; t_Gg = [Tok(), Tok()]
                    qd = [alloc(st, "qd%d" % i, [128, 4, 128], BF16) for i in range(2)]; t_qd = [Tok(), Tok()]
                    kd = [alloc(st, "kd%d" % i, [128, 4, 128], BF16) for i in range(2)]; t_kd = [Tok(), Tok()]
                    dec = [alloc(st, "dec%d" % i, [128, 4], F32) for i in range(2)]; t_dec = [Tok(), Tok()]
                    wT = alloc(st, "wT", [128, 4, 128], BF16); t_wT = Tok()
                    S = alloc(st, "S", [128, 4, 256], F32); t_S = Tok()
                    Sb = alloc(st, "Sb", [128, 4, 256], BF16); t_Sb = Tok()
                    ofl = [alloc(st, "ofl%d" % i, [128, D], F32) for i in range(2)]; t_ofl = [Tok(), Tok()]
                    osum = alloc(st, "osum", [128, D], F32); t_osum = Tok()
                    junk = alloc(st, "junkG", [128, 256], F32); t_junk = Tok()
                    ssq = alloc(st, "ssqG", [128, 8], F32); t_ssq = Tok()
                    yb = [alloc(st, "yb%d" % i, [128, D], BF16) for i in range(2)]; t_yb = [Tok(), Tok()]

                    for d_ in range(2):
                        k.dma("sp", wup[:, d_, :], wup_d[d_], writes=[t_wup])
                    k.dma("sp", bdec[:], bdec_d[:, :], writes=[t_bdec])
                    k.dma("sp", gnorm[:], vbc_d[0:1, 3072:4096].to_broadcast([128, 1024]), writes=[t_gnorm])

                    def prep(dr, ti, full, i):
                        b = i % 2
                        c0 = tile_col(ti)
                        tu = t_uT[ti]
                        uts = lambda kc: uT[:, kc, c0:c0 + 128]
                        pb, tpb = pbank()
                        for kc in range(8):
                            op("pe", lambda e: e.matmul(pb[0:16, 0:128], lhsT=wmix[:, kc, 3072 + 16 * dr:3088 + 16 * dr], rhs=uts(kc),
                                                        start=(kc == 0), stop=(kc == 7)),
                               reads=t_wmix + [tu], writes=[tpb], signal=(kc == 7))
                        op("act", lambda e: e.copy(out=lrT[:], in_=pb[0:16, 0:128]), reads=[tpb], writes=[t_lrT])
                        pb, tpb = pbank()
                        op("pe", lambda e: e.matmul(pb[:, :], lhsT=lrT[0:16, :], rhs=wup[0:16, dr, :], start=True, stop=False),
                           reads=[t_lrT, t_wup], writes=[tpb], signal=False)
                        op("pe", lambda e: e.matmul(pb[:, :], lhsT=cst[0:1, C_ONE:C_ONE + 128], rhs=bdec[0:1, dr * 512:(dr + 1) * 512],
                                                    start=False, stop=True),
                           reads=[t_cst, t_bdec], writes=[tpb])
                        op("act", lambda e: e.activation(out=tmp[:], in_=pb[:, :], func=AF.Exp, scale=-1.0), reads=[tpb], writes=[t_tmp])
                        op("act", lambda e: e.activation(out=sp[:], in_=tmp[:], func=AF.Ln, bias=one_c, scale=1.0),
                           reads=[t_tmp, t_cv], writes=[t_sp])
                        yield
                        pb, tpb = pbank()
                        op("pe", lambda e: e.matmul(pb[:, :], lhsT=MA[dr], rhs=sp[:], start=True, stop=True),
                           reads=[t_cst, t_sp], writes=[tpb])
                        op("act", lambda e: e.activation(out=dkk[:], in_=pb[:, :], func=AF.Exp, scale=-1.0 / 16), reads=[tpb], writes=[t_dkk])
                        pb, tpb = pbank()
                        for kc in range(8):
                            op("pe", lambda e: e.matmul(pb[:, :], lhsT=uts(kc), rhs=wmix[:, kc, 512:1024], start=(kc == 0), stop=(kc == 7)),
                               reads=t_wmix + [tu], writes=[tpb], signal=(kc == 7))
                        op("dve", lambda e: e.tensor_tensor(out=kk[b][:], in0=pb[:, :], in1=dkk[:], op=ALU.mult),
                           reads=[tpb, t_dkk], writes=[t_kk[b]])
                        yield
                        for half in range(2):
                            pb, tpb = pbank()
                            for kc in range(8):
                                op("pe", lambda e: e.matmul(pb[:, :], lhsT=uts(kc), rhs=wmix[:, kc, 1024 + half * 512:1536 + half * 512],
                                                            start=(kc == 0), stop=(kc == 7)),
                                   reads=t_wmix + [tu], writes=[tpb], signal=(kc == 7))
                            if half == 0:
                                op("act", lambda e: e.copy(out=v[b][:, 0:512], in_=pb[:, :]), reads=[tpb], writes=[t_v[b]])
                            else:
                                op("dve", lambda e: e.tensor_copy(out=v[b][:, 512:1024], in_=pb[:, :]), reads=[tpb], writes=[t_v[b]])
                        yield
                        if full and dr == 1:
                            for half in range(2):
                                pb, tpb = pbank()
                                for kc in range(8):
                                    op("pe", lambda e: e.matmul(pb[:, :], lhsT=uts(kc), rhs=wmix[:, kc, 2048 + half * 512:2560 + half * 512],
                                                                start=(kc == 0), stop=(kc == 7)),
                                       reads=t_wmix + [tu], writes=[tpb], signal=(kc == 7))
                                op("act", lambda e: e.activation(out=Gg[b][:, half * 512:(half + 1) * 512], in_=pb[:, :], func=AF.Silu),
                                   reads=[tpb], writes=[t_Gg[b]])
                            op("dve", lambda e: e.tensor_tensor(out=Gg[b][:], in0=Gg[b][:], in1=gnorm[:], op=ALU.mult),
                               reads=[t_Gg[b], t_gnorm], writes=[t_Gg[b]])
                        yield
                        pb, tpb = pbank()
                        for h in range(4):
                            op("pe", lambda e: e.matmul(pb[:, h * 128:(h + 1) * 128], lhsT=sp[:, h * 128:(h + 1) * 128], rhs=MI[dr],
                                                        start=True, stop=True),
                               reads=[t_sp, t_cst], writes=[tpb], signal=(h == 3))
                        op("act", lambda e: e.activation(out=Ep[:], in_=pb[:, :], func=AF.Exp, scale=-1.0 / 16), reads=[tpb], writes=[t_Ep])
                        if full:
                            op("act", lambda e: e.activation(out=Em[:], in_=pb[:, :], func=AF.Exp, scale=1.0 / 16), reads=[tpb], writes=[t_Em])
                        lastc = 127 if dr == 0 else 0
                        Epv = Ep[:].rearrange("p (h l) -> p h l", h=4)
                        op("dve", lambda e: e.tensor_copy(out=dec[b][:], in_=Epv[:, :, lastc]), reads=[t_Ep], writes=[t_dec[b]])
                        yield
                        if full:
                            for which in range(2):
                                if which == 1:
                                    yield
                                pb, tpb = pbank()
                                for h in range(4):
                                    for kc in range(8):
                                        cc = which * 512 + h * 128
                                        op("pe", lambda e: e.matmul(pb[:, h * 128:(h + 1) * 128], lhsT=wmix[:, kc, cc:cc + 128], rhs=uts(kc),
                                                                    start=(kc == 0), stop=(kc == 7)),
                                           reads=t_wmix + [tu], writes=[tpb], signal=(kc == 7 and h == 3))
                                if which == 0:
                                    op("dve", lambda e: e.scalar_tensor_tensor(out=qd[b][:].rearrange("p h l -> p (h l)"), in0=pb[:, :], scalar=ALPHA,
                                                                               in1=Ep[:], op0=ALU.mult, op1=ALU.mult),
                                       reads=[tpb, t_Ep], writes=[t_qd[b]])
                                else:
                                    op("dve", lambda e: e.tensor_tensor(out=kd[b][:].rearrange("p h l -> p (h l)"), in0=pb[:, :], in1=Em[:], op=ALU.mult),
                                       reads=[tpb, t_Em], writes=[t_kd[b]])

                    def scan(dr, ti, full, i, own):
                        b = i % 2
                        if full:
                            for h in range(4):
                                op("pe", lambda e: e.matmul(ps[0][:, h * 128:(h + 1) * 128], lhsT=kd[b][:, h, :], rhs=qd[b][:, h, :], start=True, stop=True),
                                   reads=[t_kd[b], t_qd[b]], writes=[t_ps[0]], signal=(h == 3))
                        yield
                        if full:
                            op("dve", lambda e: e.tensor_tensor(out=wT[:].rearrange("p h l -> p (h l)"), in0=ps[0][:, :],
                                                                in1=mask4[dr][:].rearrange("p h l -> p (h l)"), op=ALU.mult),
                               reads=[t_ps[0], t_mask4], writes=[t_wT])
                        yield
                        if full:
                            for h in range(4):
                                pbk = ps[1 + h // 2]; tpbk = t_ps[1 + h // 2]
                                oc = (h % 2) * 256
                                op("pe", lambda e: e.matmul(pbk[:, oc:oc + 256], lhsT=wT[:, h, :], rhs=v[b][:, h * 256:(h + 1) * 256], start=True, stop=False),
                                   reads=[t_wT, t_v[b]], writes=[tpbk], signal=False)
                                op("pe", lambda e: e.matmul(pbk[:, oc:oc + 256], lhsT=qd[b][:, h, :], rhs=Sb[:, h, :], start=False, stop=True),
                                   reads=[t_qd[b], t_Sb], writes=[tpbk], signal=(h % 2 == 1))
                        for h in range(4):
                            pbk = ps[3 + h // 2]; tpbk = t_ps[3 + h // 2]
                            oc = (h % 2) * 256
                            op("pe", lambda e: e.matmul(pbk[:, oc:oc + 256], lhsT=kk[b][:, h * 128:(h + 1) * 128], rhs=v[b][:, h * 256:(h + 1) * 256],
                                                        start=True, stop=True),
                               reads=[t_kk[b], t_v[b]], writes=[tpbk], signal=(h % 2 == 1))
                        yield
                        for h in range(4):
                            pbk = ps[3 + h // 2]; tpbk = t_ps[3 + h // 2]
                            oc = (h % 2) * 256
                            op("dve", lambda e: e.scalar_tensor_tensor(out=S[:, h, :], in0=S[:, h, :], scalar=dec[b][:, h:h + 1], in1=pbk[:, oc:oc + 256],
                                                                       op0=ALU.mult, op1=ALU.add),
                               reads=[tpbk, t_dec[b]], writes=[t_S])
                        op("pool", lambda e: e.tensor_copy(out=Sb[:].rearrange("p h l -> p (h l)"), in_=S[:].rearrange("p h l -> p (h l)")),
                           reads=[t_S], writes=[t_Sb])
                        yield
                        if not full:
                            return
                        if dr == 0:
                            ob = ofl[b]
                            for half in range(2):
                                op("act", lambda e: e.copy(out=ob[:, half * 512:(half + 1) * 512], in_=ps[1 + half][:, :]),
                                   reads=[t_ps[1 + half]], writes=[t_ofl[b]])
                            k.dma("pool", of_scr[own * 128:(own + 1) * 128, :], ob[:], reads=[t_ofl[b]], writes=[t_of_scr[own]])

                        else:
                            for half in range(2):
                                op("dve", lambda e: e.tensor_tensor(out=osum[:, half * 512:(half + 1) * 512], in0=ps[1 + half][:, :],
                                                                    in1=ofl[b][:, half * 512:(half + 1) * 512], op=ALU.add),
                                   reads=[t_ps[1 + half], t_ofl[b]], writes=[t_osum])
                            for h in range(4):
                                op("act", lambda e: e.activation(out=junk[:], in_=osum[:, h * 256:(h + 1) * 256], func=AF.Square, accum_out=ssq[:, h:h + 1]),
                                   reads=[t_osum], writes=[t_junk, t_ssq])
                            op("act", lambda e: e.activation(out=ssq[:, 4:8], in_=ssq[:, 0:4], func=AF.Ln, bias=eps_c, scale=1.0 / 256),
                               reads=[t_ssq, t_cv], writes=[t_ssq])
                            op("act", lambda e: e.activation(out=ssq[:, 4:8], in_=ssq[:, 4:8], func=AF.Exp, scale=-0.5), reads=[t_ssq], writes=[t_ssq])
                            for h in range(4):
                                op("dve", lambda e: e.scalar_tensor_tensor(out=yb[b][:, h * 256:(h + 1) * 256], in0=osum[:, h * 256:(h + 1) * 256],
                                                                           scalar=ssq[:, 4 + h:5 + h], in1=Gg[b][:, h * 256:(h + 1) * 256],
                                                                           op0=ALU.mult, op1=ALU.mult),
                                   reads=[t_osum, t_ssq, t_Gg[b]], writes=[t_yb[b]])
                            k.dma("pool", ygla_scr[own * 128:(own + 1) * 128, :], yb[b][:], reads=[t_yb[b]], writes=[t_ygla_scr[own]])

                    for dr in range(2):
                        op("dve", lambda e: e.memset(S[:].rearrange("p h l -> p (h l)"), 0.0), writes=[t_S])
                        op("dve", lambda e: e.memset(Sb[:].rearrange("p h l -> p (h l)"), 0.0), writes=[t_Sb])
                        if dr == 0:
                            seq = [(0, False, None), (1, False, None)] + [(2 + lt, True, lt) for lt in range(NT_OWN)]
                        else:
                            seq = [(1, False, None), (0, False, None)] + [(2 + lt, lt < NT_OWN, lt if lt < NT_OWN else None)
                                                                          for lt in range(NT_LAT - 1, -1, -1)]
                        exhaust(prep(dr, seq[0][0], seq[0][1], 0))
                        for i, (ti, full, own) in enumerate(seq):
                            if dr == 1 and full:
                                k.dma("sp", ofl[i % 2][:], of_scr[own * 128:(own + 1) * 128, :], reads=[t_of_scr[own]], writes=[t_ofl[i % 2]])
                            P = prep(dr, seq[i + 1][0], seq[i + 1][1], i + 1) if i + 1 < len(seq) else iter(())
                            S_ = scan(dr, ti, full, i, own)
                            step(S_); step(S_)
                            step(P); step(P)
                            step(S_); step(S_)
                            step(P); step(P); step(P)
                            exhaust(P)
                            exhaust(S_)

            def mlstm_phase():
                print("mlstm start cnt", dict(k.cnt))
                with ExitStack() as st:
                    bg = alloc(st, "bg", [128, 16], F32); t_bg = Tok()
                    mnorm = alloc(st, "mnorm", [128, D], F32); t_mnorm = Tok()
                    Gt4 = [alloc(st, "Gt%d" % i, [128, 16], F32) for i in range(4)]; t_Gt4 = [Tok() for _ in range(4)]
                    sm4 = [alloc(st, "sm%d" % i, [128, 16], F32) for i in range(4)]; t_sm4 = [Tok() for _ in range(4)]
                    scall = alloc(st, "scall", [128, 34, 16], F32); t_scall = Tok()
                    accS = [alloc(st, "accS%d" % i, [128, 512], F32) for i in range(2)]; t_accS = [Tok(), Tok()]
                    halo = [alloc(st, "halo%d" % i, [128, 16], F32) for i in range(2)]; t_halo = [Tok(), Tok()]
                    qcS = [alloc(st, "qcS%d" % i, [128, 4, 512], BF16) for i in range(2)]; t_qcS = [Tok(), Tok()]
                    kcS = [alloc(st, "kcS%d" % i, [128, 4, 512], BF16) for i in range(2)]; t_kcS = [Tok(), Tok()]
                    kk = [alloc(st, "mkk%d" % i, [128, 512], BF16) for i in range(2)]; t_kk = [Tok(), Tok()]
                    v = [alloc(st, "mv%d" % i, [128, D], BF16) for i in range(2)]; t_v = [Tok(), Tok()]
                    Gg = [alloc(st, "mGg%d" % i, [128, D], F32) for i in range(2)]; t_Gg = [Tok(), Tok()]
                    wT = alloc(st, "mwT", [128, 4, 128], BF16); t_wT = Tok()
                    C = alloc(st, "C", [128, 4, 256], F32); t_C = Tok()
                    Cb = alloc(st, "Cb", [128, 4, 256], BF16); t_Cb = Tok()
                    nst = alloc(st, "nst", [128, 4], F32); t_n = Tok()
                    nb = alloc(st, "nb", [128, 4], BF16); t_nb = Tok()
                    rr = alloc(st, "rr", [128, 16], F32); t_rr = Tok()
                    hfl = [alloc(st, "hfl%d" % i, [128, D], F32) for i in range(2)]; t_hfl = [Tok(), Tok()]
                    hs = alloc(st, "hs", [128, D], F32); t_hs = Tok()
                    junk = alloc(st, "junkM", [128, 256], F32); t_junk = Tok()
                    ssq = alloc(st, "ssqM", [128, 8], F32); t_ssq = Tok()
                    yb = [alloc(st, "myb%d" % i, [128, D], BF16) for i in range(2)]; t_yb = [Tok(), Tok()]
                    smb = ps[5]
                    t_smg = t_smb = t_ps[5]
                    sms = ps[4]
                    t_smden = t_smkvn = t_ps[4]
                    mprep = [5, 0]
                    mrr = [0]

                    def mbank():
                        i = mprep[mrr[0] % 2]
                        mrr[0] += 1
                        return ps[i], t_ps[i]

                    k.dma("sp", bg[:], vbc_d[0:1, 5120:5136].to_broadcast([128, 16]), writes=[t_bg])
                    k.dma("sp", mnorm[:], vbc_d[0:1, 4096:5120].to_broadcast([128, 1024]), writes=[t_mnorm])
                    op("dve", lambda e: e.tensor_scalar(out=mnorm[:], in0=mnorm[:], scalar1=0.5, scalar2=None, op0=ALU.mult), writes=[t_mnorm])
                    ymv = ym_scr.rearrange("(r c) d -> c r d", c=64)

                    def convq(gb, c0, N, tiles):
                        rd = t_wmix + [t_guard] + [t_uT[t_] for t_ in tiles]
                        lo, hi = min(tiles), max(tiles)
                        if lo not in (0, 2):
                            rd.append(t_uT[lo - 1])
                        if hi not in (1, 33):
                            rd.append(t_uT[hi + 1])
                        for ci in range(8):
                            h0 = 48 + 2 * ci
                            for kc in range(8):
                                op("pe", lambda e: e.matmul(sms[:, h0:h0 + 2], lhsT=wmix[:, kc, ci * 128:(ci + 1) * 128], rhs=uT[:, kc, c0 - 1:c0 + N + 1:N + 1],
                                                            start=(kc == 0), stop=(kc == 7)),
                                   reads=rd, writes=[t_ps[4]], signal=(kc == 7 and ci == 7))
                        op("dve", lambda e: e.tensor_copy(out=halo[gb][:], in_=sms[:, 48:64]), reads=[t_ps[4]], writes=[t_halo[gb]])
                        cwf = lambda ci, tap: vpp[:, VP_CW + tap * 8 + ci:VP_CW + tap * 8 + ci + 1]
                        banks = {}

                        def stA(ci):
                            pb, tpb = ps[6 + ci % 2], t_ps[6 + ci % 2]
                            banks[ci] = (pb, tpb)
                            for kc in range(8):
                                op("pe", lambda e: e.matmul(pb[:, 0:N], lhsT=wmix[:, kc, ci * 128:(ci + 1) * 128], rhs=uT[:, kc, c0:c0 + N],
                                                            start=(kc == 0), stop=(kc == 7)),
                                   reads=rd, writes=[tpb], signal=(kc == 7))
                            acc, ta = accS[ci % 2], t_accS[ci % 2]
                            op("act", lambda e: e.activation(out=acc[:, 0:N], in_=pb[:, 0:N], func=AF.Identity, scale=cwf(ci, 1), bias=vpp[:, VP_CB + ci:VP_CB + ci + 1]),
                               reads=[tpb, t_vpp], writes=[ta])

                        def stB(ci):
                            pb, tpb = banks[ci]
                            acc, ta = accS[ci % 2], t_accS[ci % 2]
                            h0 = 2 * ci
                            op("dve", lambda e: e.scalar_tensor_tensor(out=acc[:, 1:N], in0=pb[:, 0:N - 1], scalar=cwf(ci, 0), in1=acc[:, 1:N], op0=ALU.mult, op1=ALU.add),
                               reads=[tpb, t_vpp], writes=[ta])
                            op("dve", lambda e: e.scalar_tensor_tensor(out=acc[:, 0:N - 1], in0=pb[:, 1:N], scalar=cwf(ci, 2), in1=acc[:, 0:N - 1], op0=ALU.mult, op1=ALU.add),
                               reads=[tpb, t_vpp], writes=[ta])
                            op("dve", lambda e: e.scalar_tensor_tensor(out=acc[:, 0:1], in0=halo[gb][:, h0:h0 + 1], scalar=cwf(ci, 0), in1=acc[:, 0:1], op0=ALU.mult, op1=ALU.add),
                               reads=[t_halo[gb], t_vpp], writes=[ta])
                            op("dve", lambda e: e.scalar_tensor_tensor(out=acc[:, N - 1:N], in0=halo[gb][:, h0 + 1:h0 + 2], scalar=cwf(ci, 2), in1=acc[:, N - 1:N], op0=ALU.mult, op1=ALU.add),
                               reads=[t_halo[gb], t_vpp], writes=[ta])

                        def stC(ci):
                            acc, ta = accS[ci % 2], t_accS[ci % 2]
                            dstt, tdst = (qcS[gb], t_qcS[gb]) if ci < 4 else (kcS[gb], t_kcS[gb])
                            op("act", lambda e: e.activation(out=dstt[:, ci % 4, 0:N], in_=acc[:, 0:N], func=AF.Silu), reads=[ta], writes=[tdst])

                        stA(0)
                        for ci in range(8):
                            if ci + 1 < 8:
                                stA(ci + 1)
                            stB(ci)
                            stC(ci)
                            yield

                    def gates_prologue(dr, seq):
                        for i, (ti, full, lt) in enumerate(seq):
                            c0 = tile_col(ti)
                            tu = t_uT[ti]
                            pbk, tpbk = ps[4 + i % 4], t_ps[4 + i % 4]
                            Gt_, sm_ = Gt4[i % 4], sm4[i % 4]
                            tG, tS = t_Gt4[i % 4], t_sm4[i % 4]
                            for kc in range(8):
                                op("pe", lambda e: e.matmul(pbk[:, 0:16], lhsT=uT[:, kc, c0:c0 + 128], rhs=wmix[:, kc, 3072:3088], start=(kc == 0), stop=(kc == 7)),
                                   reads=t_wmix + [tu], writes=[tpbk], signal=(kc == 7))
                            op("dve", lambda e: e.tensor_tensor(out=Gt_[:], in0=pbk[:, 0:16], in1=bg[:], op=ALU.add), reads=[tpbk, t_bg], writes=[tG])
                            ig = Gt_[:, 8 * dr:8 * dr + 4]
                            fg = Gt_[:, 8 * dr + 4:8 * dr + 8]
                            op("act", lambda e: e.activation(out=sm_[:, 0:4], in_=fg, func=AF.Exp, scale=-1.0), reads=[tG], writes=[tS])
                            op("act", lambda e: e.activation(out=sm_[:, 4:8], in_=sm_[:, 0:4], func=AF.Ln, bias=one_c, scale=1.0),
                               reads=[tS, t_cv], writes=[tS])
                            op("pe", lambda e: e.matmul(pbk[:, 16:20], lhsT=MI[dr], rhs=sm_[:, 4:8], start=True, stop=True),
                               reads=[t_cst, tS], writes=[tpbk], signal=False)
                            op("pe", lambda e: e.matmul(pbk[:, 20:24], lhsT=ones, rhs=sm_[:, 4:8], start=True, stop=True),
                               reads=[t_cst, tS], writes=[tpbk])
                            op("act", lambda e: e.activation(out=scall[:, i, 0:8], in_=pbk[:, 16:24], func=AF.Exp, scale=-1.0), reads=[tpbk], writes=[t_scall])
                            op("dve", lambda e: e.tensor_tensor(out=sm_[:, 8:12], in0=ig, in1=pbk[:, 16:20], op=ALU.add), reads=[tG, tpbk], writes=[tS])
                            op("dve", lambda e: e.tensor_tensor(out=sm_[:, 12:16], in0=sm_[:, 8:12], in1=pbk[:, 20:24], op=ALU.subtract),
                               reads=[tS, tpbk], writes=[tS])
                            op("act", lambda e: e.activation(out=scall[:, i, 8:16], in_=sm_[:, 8:16], func=AF.Exp, bias=lna_c, scale=1.0),
                               reads=[tS, t_cv], writes=[t_scall])

                    def prep(dr, ti, full, i, gb, off):
                        b = i % 2
                        c0 = tile_col(ti)
                        tu = t_uT[ti]
                        uts = lambda kc: uT[:, kc, c0:c0 + 128]
                        qc_ = lambda h: qcS[gb][:, h, off:off + 128]
                        kc__ = lambda h: kcS[gb][:, h, off:off + 128]
                        s_ = scall[:, i, :]
                        pb, tpb = mbank()
                        pbb = pb[:, 0:256].bitcast(BF16)
                        for h in range(4):
                            op("pe", lambda e: e.transpose(out=pbb[:, h * 128:(h + 1) * 128], in_=kc__(h), identity=identb[:]),
                               reads=[t_kcS[gb], t_identb], writes=[tpb], signal=(h == 3))
                        for h in range(4):
                            op("dve", lambda e: e.tensor_scalar(out=kk[b][:, h * 128:(h + 1) * 128], in0=pbb[:, h * 128:(h + 1) * 128],
                                                                scalar1=s_[:, 12 + h:13 + h], scalar2=None, op0=ALU.mult),
                               reads=[tpb, t_scall], writes=[t_kk[b]])
                        yield
                        for half in range(2):
                            pb, tpb = mbank()
                            for kc in range(8):
                                op("pe", lambda e: e.matmul(pb[:, :], lhsT=uts(kc), rhs=wmix[:, kc, 1024 + half * 512:1536 + half * 512],
                                                            start=(kc == 0), stop=(kc == 7)),
                                   reads=t_wmix + [tu], writes=[tpb], signal=(kc == 7))
                            op("act", lambda e: e.copy(out=v[b][:, half * 512:(half + 1) * 512], in_=pb[:, :]), reads=[tpb], writes=[t_v[b]])
                        yield
                        if full and dr == 1:
                            for half in range(2):
                                pb, tpb = mbank()
                                for kc in range(8):
                                    op("pe", lambda e: e.matmul(pb[:, :], lhsT=uts(kc), rhs=wmix[:, kc, 2048 + half * 512:2560 + half * 512],
                                                                start=(kc == 0), stop=(kc == 7)),
                                       reads=t_wmix + [tu], writes=[tpb], signal=(kc == 7))
                                op("act", lambda e: e.activation(out=Gg[b][:, half * 512:(half + 1) * 512], in_=pb[:, :], func=AF.Tanh, scale=0.5),
                                   reads=[tpb], writes=[t_Gg[b]])
                            op("dve", lambda e: e.scalar_tensor_tensor(out=Gg[b][:], in0=Gg[b][:], scalar=1.0, in1=mnorm[:], op0=ALU.add, op1=ALU.mult),
                               reads=[t_Gg[b], t_mnorm], writes=[t_Gg[b]])

                    def scan(dr, ti, full, i, lt, gb, off):
                        b = i % 2
                        s_ = scall[:, i, :]
                        qc_ = lambda h: qcS[gb][:, h, off:off + 128]
                        kc__ = lambda h: kcS[gb][:, h, off:off + 128]
                        if full:
                            for h in range(4):
                                op("pe", lambda e: e.matmul(ps[0][:, h * 128:(h + 1) * 128], lhsT=kc__(h), rhs=qc_(h), start=True, stop=True),
                                   reads=[t_kcS[gb], t_qcS[gb]], writes=[t_ps[0]], signal=(h == 3))
                        yield
                        if full:
                            for h in range(4):
                                op("dve", lambda e: e.scalar_tensor_tensor(out=wT[:, h, :], in0=ps[0][:, h * 128:(h + 1) * 128], scalar=s_[:, 8 + h:9 + h],
                                                                           in1=MI[dr], op0=ALU.mult, op1=ALU.mult),
                                   reads=[t_ps[0], t_scall, t_cst], writes=[t_wT])
                        yield
                        if full:
                            for h in range(4):
                                pbk = ps[1 + h // 2]; tpbk = t_ps[1 + h // 2]
                                oc = (h % 2) * 256
                                op("pe", lambda e: e.matmul(pbk[:, oc:oc + 256], lhsT=wT[:, h, :], rhs=v[b][:, h * 256:(h + 1) * 256], start=True, stop=False),
                                   reads=[t_wT, t_v[b]], writes=[tpbk], signal=False)
                                op("pe", lambda e: e.matmul(pbk[:, oc:oc + 256], lhsT=qc_(h), rhs=Cb[:, h, :], start=False, stop=True),
                                   reads=[t_qcS[gb], t_Cb], writes=[tpbk], signal=(h % 2 == 1))
                            for h in range(4):
                                op("pe", lambda e: e.matmul(sms[:, 32 + h:33 + h], lhsT=wT[:, h, :], rhs=onesb[:, 0:1], start=True, stop=False),
                                   reads=[t_wT, t_onesb], writes=[t_smden], signal=False)
                                op("pe", lambda e: e.matmul(sms[:, 32 + h:33 + h], lhsT=qc_(h), rhs=nb[:, h:h + 1], start=False, stop=True),
                                   reads=[t_qcS[gb], t_nb], writes=[t_smden], signal=(h == 3))
                        def kvmm(hh):
                            for h in hh:
                                oc = (h % 2) * 256
                                op("pe", lambda e: e.matmul(ps[3][:, oc:oc + 256], lhsT=kk[b][:, h * 128:(h + 1) * 128], rhs=v[b][:, h * 256:(h + 1) * 256],
                                                            start=True, stop=True),
                                   reads=[t_kk[b], t_v[b]], writes=[t_ps[3]], signal=(h % 2 == 1))

                        def cupd(hh):
                            for h in hh:
                                oc = (h % 2) * 256
                                op("dve", lambda e: e.scalar_tensor_tensor(out=C[:, h, :], in0=C[:, h, :], scalar=s_[:, 4 + h:5 + h], in1=ps[3][:, oc:oc + 256],
                                                                           op0=ALU.mult, op1=ALU.add),
                                   reads=[t_ps[3], t_scall], writes=[t_C])
                        kvmm((0, 1))
                        for h in range(4):
                            op("pe", lambda e: e.matmul(sms[:, 40 + h:41 + h], lhsT=kk[b][:, h * 128:(h + 1) * 128], rhs=onesb[:, 0:1], start=True, stop=True),
                               reads=[t_kk[b], t_onesb], writes=[t_smkvn], signal=(h == 3))
                        yield
                        cupd((0, 1))
                        yield
                        kvmm((2, 3))
                        yield
                        cupd((2, 3))
                        op("dve", lambda e: e.tensor_tensor(out=nst[:], in0=nst[:], in1=s_[:, 4:8], op=ALU.mult), reads=[t_scall], writes=[t_n])
                        op("dve", lambda e: e.tensor_tensor(out=nst[:], in0=nst[:], in1=sms[:, 40:44], op=ALU.add), reads=[t_smkvn], writes=[t_n])
                        op("pool", lambda e: e.tensor_copy(out=Cb[:].rearrange("p h l -> p (h l)"), in_=C[:].rearrange("p h l -> p (h l)")),
                           reads=[t_C], writes=[t_Cb])
                        op("pool", lambda e: e.tensor_copy(out=nb[:], in_=nst[:]), reads=[t_n], writes=[t_nb])
                        yield
                        if not full:
                            return
                        op("dve", lambda e: e.tensor_tensor(out=rr[:, 0:4], in0=s_[:, 0:4], in1=sms[:, 32:36], op=ALU.mult),
                           reads=[t_scall, t_smden], writes=[t_rr])
                        op("dve", lambda e: e.tensor_scalar(out=rr[:, 4:8], in0=rr[:, 0:4], scalar1=-1.0, scalar2=None, op0=ALU.mult),
                           reads=[t_rr], writes=[t_rr])
                        op("dve", lambda e: e.tensor_tensor(out=rr[:, 4:8], in0=rr[:, 4:8], in1=rr[:, 0:4], op=ALU.max),
                           reads=[t_rr], writes=[t_rr])
                        op("dve", lambda e: e.tensor_scalar(out=rr[:, 4:8], in0=rr[:, 4:8], scalar1=1.0, scalar2=None, op0=ALU.max),
                           reads=[t_rr], writes=[t_rr])
                        op("dve", lambda e: e.reciprocal(out=rr[:, 8:12], in_=rr[:, 4:8]), reads=[t_rr], writes=[t_rr])
                        op("dve", lambda e: e.tensor_tensor(out=rr[:, 12:16], in0=rr[:, 8:12], in1=s_[:, 0:4], op=ALU.mult),
                           reads=[t_rr, t_scall], writes=[t_rr])
                        if dr == 0:
                            ob = hfl[b]
                            for h in range(4):
                                pbk = ps[1 + h // 2]; tpbk = t_ps[1 + h // 2]
                                oc = (h % 2) * 256
                                op("act", lambda e: e.activation(out=ob[:, h * 256:(h + 1) * 256], in_=pbk[:, oc:oc + 256], func=AF.Copy, scale=rr[:, 12 + h:13 + h]),
                                   reads=[tpbk, t_rr], writes=[t_hfl[b]])
                            k.dma("pool", hf_scr[lt * 128:(lt + 1) * 128, :], ob[:], reads=[t_hfl[b]], writes=[t_hf_scr[lt]])
                        else:
                            for h in range(4):
                                pbk = ps[1 + h // 2]; tpbk = t_ps[1 + h // 2]
                                oc = (h % 2) * 256
                                op("dve", lambda e: e.scalar_tensor_tensor(out=hs[:, h * 256:(h + 1) * 256], in0=pbk[:, oc:oc + 256], scalar=rr[:, 12 + h:13 + h],
                                                                           in1=hfl[b][:, h * 256:(h + 1) * 256], op0=ALU.mult, op1=ALU.add),
                                   reads=[tpbk, t_rr, t_hfl[b]], writes=[t_hs])
                            for h in range(4):
                                op("act", lambda e: e.activation(out=junk[:], in_=hs[:, h * 256:(h + 1) * 256], func=AF.Square, accum_out=ssq[:, h:h + 1]),
                                   reads=[t_hs], writes=[t_junk, t_ssq])
                            op("act", lambda e: e.activation(out=ssq[:, 4:8], in_=ssq[:, 0:4], func=AF.Ln, bias=eps_c, scale=1.0 / 256),
                               reads=[t_ssq, t_cv], writes=[t_ssq])
                            op("act", lambda e: e.activation(out=ssq[:, 4:8], in_=ssq[:, 4:8], func=AF.Exp, scale=-0.5), reads=[t_ssq], writes=[t_ssq])
                            for h in range(4):
                                op("dve", lambda e: e.scalar_tensor_tensor(out=yb[b][:, h * 256:(h + 1) * 256], in0=hs[:, h * 256:(h + 1) * 256],
                                                                           scalar=ssq[:, 4 + h:5 + h], in1=Gg[b][:, h * 256:(h + 1) * 256],
                                                                           op0=ALU.mult, op1=ALU.mult),
                                   reads=[t_hs, t_ssq, t_Gg[b]], writes=[t_yb[b]])
                            for cl in range(2):
                                k.dma("pool", ymv[2 * lt + cl], yb[b][cl * 64:(cl + 1) * 64, :], reads=[t_yb[b]], writes=[t_ym_scr])

                    for dr in range(2):
                        op("dve", lambda e: e.memset(C[:].rearrange("p h l -> p (h l)"), 0.0), writes=[t_C])
                        op("dve", lambda e: e.memset(Cb[:].rearrange("p h l -> p (h l)"), 0.0), writes=[t_Cb])
                        op("dve", lambda e: e.memset(nst[:], 0.0), writes=[t_n])
                        op("dve", lambda e: e.memset(nb[:], 0.0), writes=[t_nb])
                        if dr == 0:
                            seq = [(0, False, None), (1, False, None)] + [(2 + lt, True, lt) for lt in range(NT_LAT)]
                        else:
                            seq = [(1, False, None), (0, False, None)] + [(2 + lt, True, lt) for lt in range(NT_LAT - 1, -1, -1)]
                        groups = [seq[0:2]] + [seq[2 + 4 * g_:6 + 4 * g_] for g_ in range(NT_LAT // 4)]
                        ginfo = []
                        tinfo = []
                        for gi_, grp in enumerate(groups):
                            tiles = [t_[0] for t_ in grp]
                            c0g = min(tile_col(t_) for t_ in tiles)
                            ginfo.append((gi_ % 2, c0g, 128 * len(tiles), tiles))
                            for t_ in tiles:
                                tinfo.append((gi_, gi_ % 2, tile_col(t_) - c0g))
                        PQ = [convq(*gi__) for gi__ in ginfo]
                        gates_prologue(dr, seq)
                        exhaust(PQ[0])
                        exhaust(prep(dr, seq[0][0], seq[0][1], 0, tinfo[0][1], tinfo[0][2]))
                        for i, (ti, full, lt) in enumerate(seq):
                            if dr == 1 and full:
                                k.dma("sp", hfl[i % 2][:], hf_scr[lt * 128:(lt + 1) * 128, :], reads=[t_hf_scr[lt]], writes=[t_hfl[i % 2]])
                            gcur = tinfo[i][0]
                            nxt = PQ[gcur + 1] if gcur + 1 < len(PQ) else iter(())
                            nch = 8 // len(groups[gcur])
                            P = prep(dr, seq[i + 1][0], seq[i + 1][1], i + 1, tinfo[i + 1][1], tinfo[i + 1][2]) if i + 1 < len(seq) else iter(())
                            S_ = scan(dr, ti, full, i, lt, tinfo[i][1], tinfo[i][2])
                            step(S_); step(S_)
                            for _ in range(nch // 2):
                                step(nxt)
                            step(S_); step(S_)
                            for _ in range(nch - nch // 2):
                                step(nxt)
                            step(S_); step(S_)
                            exhaust(P)
                            exhaust(S_)

            k.barrier()
            load_w(wmix, t_wmix, win_d, 0, 3104)
            phase_U(False)
            k.barrier()
            if stage >= 1:
                gla_phase()
                k.barrier()
            if stage >= 2:
                load_w(wmix, t_wmix, win_d, 3104, 6192)
                phase_U(True)
                k.barrier()
                mlstm_phase()
                k.barrier()


        tb = [0]

        def tbank():
            i = tb[0] % 8
            tb[0] += 1
            return ps[i], t_ps[i]

        def load_w2(dst, t_dst, src, c0, c1, nk):
            for kc in range(nk):
                k.dma("pool", dst[:, kc, 0:c1 - c0], src[kc * 128:(kc + 1) * 128, c0:c1], writes=[t_dst])

        def tail_a(prefetch):
            with ExitStack() as st:
                wgt = alloc(st, "wgt", [128, 8, 2048], BF16)
                wbg = alloc(st, "wbg", [128, 8, D], BF16)
                wbm = alloc(st, "wbm", [128, 8, D], BF16)
                wo = alloc(st, "wo", [128, 8, D], BF16)
                xs = [alloc(st, "txs%d" % i, [128, D], F32) for i in range(2)]; t_xs = [Tok(), Tok()]
                xn = alloc(st, "txn", [128, D], F32); t_xn = Tok()
                small = alloc(st, "tsmall", [128, 2], F32); t_small = Tok()
                uTt = alloc(st, "uTt", [128, 8, 128], BF16); t_uTt = Tok()
                ygt = [alloc(st, "ygt%d" % i, [128, D], BF16) for i in range(2)]; t_ygt = [Tok(), Tok()]
                ymt = [alloc(st, "ymt%d" % i, [128, D], BF16) for i in range(2)]; t_ymt = [Tok(), Tok()]
                yTg = alloc(st, "yTg", [128, 8, 128], BF16); t_yTg = Tok()
                yTm = alloc(st, "yTm", [128, 8, 128], BF16); t_yTm = Tok()
                gsig = alloc(st, "gsig", [128, 2048], F32); t_gsig = Tok()
                ysg = alloc(st, "ysg", [128, D], F32); t_ysg = Tok()
                tmpm = alloc(st, "tmpm", [128, D], F32); t_tmpm = Tok()
                junk, t_junk = tmpm, t_tmpm
                ysb = alloc(st, "ysb", [128, D], BF16); t_ysb = Tok()
                ysT = alloc(st, "ysT", [128, 8, 128], BF16); t_ysT = Tok()
                x1t = [alloc(st, "x1t%d" % i, [128, D], F32) for i in range(2)]; t_x1t = [Tok(), Tok()]
                stagings = [(gsig[:, 0:1024], Tok()), (gsig[:, 1024:2048], Tok()), (ysg[:, :], Tok()), (tmpm[:, :], Tok())]
                pieces = []
                t_wgt, t_wbg, t_wbm, t_wo = [], [], [], []
                for kc in range(8):
                    for half in range(2):
                        tk = Tok(); t_wgt.append(tk)
                        pieces.append((wgt[:, kc, half * 1024:(half + 1) * 1024], win_d[kc * 128:(kc + 1) * 128, 6192 + half * 1024:6192 + (half + 1) * 1024], tk))
                for wt_, wd_, tl_ in ((wbg, wbg_d, t_wbg), (wbm, wbm_d, t_wbm), (wo, wo_d, t_wo)):
                    for kc in range(8):
                        tk = Tok(); tl_.append(tk)
                        pieces.append((wt_[:, kc, :], wd_[kc * 128:(kc + 1) * 128, :], tk))
                cengs = ["dve", "act", "pool"]
                for j, (dst, src, tk) in enumerate(pieces):
                    stg, tstg = stagings[j % 4]
                    k.dma("sp", stg, src, writes=[tstg])
                    ce = cengs[j % 3]
                    if ce == "act":
                        op("act", lambda e: e.copy(out=dst, in_=stg), reads=[tstg], writes=[tk])
                    else:
                        op(ce, lambda e: e.tensor_copy(out=dst, in_=stg), reads=[tstg], writes=[tk])
                for real, stgs in ((t_gsig, (stagings[0][1], stagings[1][1])), (t_ysg, (stagings[2][1],)), (t_tmpm, (stagings[3][1],))):
                    mr = {}
                    for ts in stgs:
                        if ts.w is not None:
                            mr[ts.w[0]] = max(mr.get(ts.w[0], 0), ts.w[1])
                        for s_k, v_k in ts.r.items():
                            mr[s_k] = max(mr.get(s_k, 0), v_k)
                    real.w = None
                    real.r = mr

                def tr8(src, t_src, dst, t_dst):
                    pb, tpb = tbank()
                    pbb = pb[:, :].bitcast(BF16)
                    for kc in range(8):
                        op("pe", lambda e: e.transpose(out=pbb[:, kc * 128:(kc + 1) * 128], in_=src[:, kc * 128:(kc + 1) * 128], identity=identb[:]),
                           reads=[t_src, t_identb], writes=[tpb], signal=(kc == 7))
                    op("act", lambda e: e.copy(out=dst[:].rearrange("p a b -> p (a b)"), in_=pbb[:, :]), reads=[tpb], writes=[t_dst])

                for i in range(NT_OWN):
                    b = i % 2
                    k.dma("sp", xs[b][:], x_d[i * 128:(i + 1) * 128, :], writes=[t_xs[b]])
                    k.dma("sp", ygt[b][:], ygla_scr[i * 128:(i + 1) * 128, :], reads=[t_ygla_scr[i]], writes=[t_ygt[b]])
                    k.dma("sp", ymt[b][:], ym_scr[i * 128:(i + 1) * 128, :], reads=[t_ym_scr], writes=[t_ymt[b]])
                    norm_transpose(xs[b][:], t_xs[b], xn, t_xn, small, t_small, junk, t_junk,
                                   lambda kc: scU[:, kc, 0:1], lambda kc: modpp[:, 0, kc, 0:1], [t_scU, t_modpp],
                                   lambda kc: uTt[:, kc, :], t_uTt)
                    for q in range(4):
                        pb, tpb = tbank()
                        for kc in range(8):
                            op("pe", lambda e: e.matmul(pb[:, :], lhsT=uTt[:, kc, :], rhs=wgt[:, kc, q * 512:(q + 1) * 512], start=(kc == 0), stop=(kc == 7)),
                               reads=[t_uTt] + t_wgt, writes=[tpb], signal=(kc == 7))
                        op("act", lambda e: e.activation(out=gsig[:, q * 512:(q + 1) * 512], in_=pb[:, :], func=AF.Sigmoid), reads=[tpb], writes=[t_gsig])
                    tr8(ygt[b], t_ygt[b], yTg, t_yTg)
                    tr8(ymt[b], t_ymt[b], yTm, t_yTm)
                    for half in range(2):
                        pb, tpb = tbank()
                        for kc in range(8):
                            op("pe", lambda e: e.matmul(pb[:, :], lhsT=yTg[:, kc, :], rhs=wbg[:, kc, half * 512:(half + 1) * 512], start=(kc == 0), stop=(kc == 7)),
                               reads=[t_yTg] + t_wbg, writes=[tpb], signal=(kc == 7))
                        op("dve", lambda e: e.tensor_tensor(out=ysg[:, half * 512:(half + 1) * 512], in0=pb[:, :], in1=gsig[:, half * 512:(half + 1) * 512], op=ALU.mult),
                           reads=[tpb, t_gsig], writes=[t_ysg])
                    for half in range(2):
                        pb, tpb = tbank()
                        for kc in range(8):
                            op("pe", lambda e: e.matmul(pb[:, :], lhsT=yTm[:, kc, :], rhs=wbm[:, kc, half * 512:(half + 1) * 512], start=(kc == 0), stop=(kc == 7)),
                               reads=[t_yTm] + t_wbm, writes=[tpb], signal=(kc == 7))
                        op("dve", lambda e: e.tensor_tensor(out=tmpm[:, half * 512:(half + 1) * 512], in0=pb[:, :], in1=gsig[:, 1024 + half * 512:1536 + half * 512], op=ALU.mult),
                           reads=[tpb, t_gsig], writes=[t_tmpm])
                    op("dve", lambda e: e.tensor_tensor(out=ysb[:], in0=ysg[:], in1=tmpm[:], op=ALU.add), reads=[t_ysg, t_tmpm], writes=[t_ysb])
                    tr8(ysb, t_ysb, ysT, t_ysT)
                    for half in range(2):
                        pb, tpb = tbank()
                        for kc in range(8):
                            op("pe", lambda e: e.matmul(pb[:, :], lhsT=ysT[:, kc, :], rhs=wo[:, kc, half * 512:(half + 1) * 512], start=(kc == 0), stop=(kc == 7)),
                               reads=[t_ysT] + t_wo, writes=[tpb], signal=(kc == 7))
                        op("dve", lambda e: e.tensor_tensor(out=x1t[b][:, half * 512:(half + 1) * 512], in0=pb[:, :], in1=G1[:, half * 512:(half + 1) * 512], op=ALU.mult),
                           reads=[tpb, t_G1], writes=[t_x1t[b]])
                    op("dve", lambda e: e.tensor_tensor(out=x1t[b][:], in0=x1t[b][:], in1=xs[b][:], op=ALU.add), reads=[t_xs[b]], writes=[t_x1t[b]])
                    k.dma("pool", x1_scr[i * 128:(i + 1) * 128, :], x1t[b][:], reads=[t_x1t[b]], writes=[t_x1_scr[i]])
                    for _ in range(3):
                        if prefetch:
                            prefetch.pop(0)()

        def tail_b(wfo, t_wfo, wfi0, t_wfi0, prefetch):
            with ExitStack() as st:
                while prefetch:
                    prefetch.pop(0)()
                wfiR = alloc(st, "wfiR", [128, 8, 2 * (DFF - 384)], BF16)

                def wcol(part, fc):
                    if fc < 3:
                        return wfi0, part * 384 + fc * 128
                    return wfiR, part * (DFF - 384) + (fc - 3) * 128
                fgbc = alloc(st, "fgbc", [128, D], F32); t_fgbc = Tok()
                x1t = [alloc(st, "fx1t%d" % i, [128, D], F32) for i in range(4)]; t_x1t = [Tok() for _ in range(4)]
                xn = alloc(st, "fxn", [128, D], F32); t_xn = Tok()
                small = alloc(st, "fsmall", [128, 4], F32); t_small = Tok()
                u2T = alloc(st, "u2T", [128, 8, 512], BF16); t_u2T = Tok()
                sa = [alloc(st, "sa%d" % i, [128, 512], F32) for i in range(1)]; t_sa = [Tok()]
                hT = alloc(st, "hT", [128, 22, 512], BF16); t_hT = Tok()
                x2 = alloc(st, "x2", [128, D], F32); t_x2 = Tok()
                k.dma("sp", fgbc[:], vbc_d[0:1, 2048:3072].to_broadcast([128, 1024]), writes=[t_fgbc])
                t_wfib = [[t_wfi0], [], [], []]
                fcb = [0, 3, 9, 15, 22]
                hTf = hT[:].rearrange("p a b -> p (a b)").bitcast(F32)
                stg_t = [Tok() for _ in range(5)]
                cengs = ["dve", "act", "pool"]
                j = 0
                for blk in range(1, 4):
                    f0, f1 = fcb[blk] * 128, fcb[blk + 1] * 128
                    for part in range(2):
                        for kc in range(8):
                            c_ = part * (DFF - 384) + f0 - 384
                            n_ = f1 - f0
                            stg = hTf[:, (j % 5) * 1024:(j % 5) * 1024 + n_]
                            tstg = stg_t[j % 5]
                            tk = Tok()
                            t_wfib[blk].append(tk)
                            k.dma("sp", stg, wfi_d[kc * 128:(kc + 1) * 128, part * DFF + f0:part * DFF + f1], writes=[tstg])
                            dst = wfiR[:, kc, c_:c_ + n_]
                            ce = cengs[j % 3]
                            if ce == "act":
                                op("act", lambda e: e.copy(out=dst, in_=stg), reads=[tstg], writes=[tk])
                            else:
                                op(ce, lambda e: e.tensor_copy(out=dst, in_=stg), reads=[tstg], writes=[tk])
                            j += 1
                mr = {}
                for ts in stg_t:
                    if ts.w is not None:
                        mr[ts.w[0]] = max(mr.get(ts.w[0], 0), ts.w[1])
                    for s_k, v_k in ts.r.items():
                        mr[s_k] = max(mr.get(s_k, 0), v_k)
                t_hT.w = None
                t_hT.r = mr
                for s_i in range(NT_OWN // 4):
                    for j in range(4):
                        ti = s_i * 4 + j
                        k.dma("sp", x1t[j][:], x1_scr[ti * 128:(ti + 1) * 128, :], reads=[t_x1_scr[ti]], writes=[t_x1t[j]])
                        norm_transpose(x1t[j][:], t_x1t[j], xn, t_xn, small, t_small, x2, t_x2,
                                       lambda kc: sc2[:, kc:kc + 1], lambda kc: modpp[:, 2, kc, 0:1], [t_sc2, t_modpp],
                                       lambda kc: u2T[:, kc, j * 128:(j + 1) * 128], t_u2T)
                    for fc in range(22):
                        pa, tpa = tbank()
                        for kc in range(8):
                            wt_, wc_ = wcol(0, fc)
                            op("pe", lambda e: e.matmul(pa[:, :], lhsT=wt_[:, kc, wc_:wc_ + 128], rhs=u2T[:, kc, :], start=(kc == 0), stop=(kc == 7)),
                               reads=t_wfib[0 if fc < 3 else (1 if fc < 9 else (2 if fc < 15 else 3))] + [t_u2T], writes=[tpa], signal=(kc == 7))
                        pbk, tpbk = tbank()
                        for kc in range(8):
                            wt_, wc_ = wcol(1, fc)
                            op("pe", lambda e: e.matmul(pbk[:, :], lhsT=wt_[:, kc, wc_:wc_ + 128], rhs=u2T[:, kc, :], start=(kc == 0), stop=(kc == 7)),
                               reads=t_wfib[0 if fc < 3 else (1 if fc < 9 else (2 if fc < 15 else 3))] + [t_u2T], writes=[tpbk], signal=(kc == 7))
                        sb_ = 0
                        op("act", lambda e: e.activation(out=sa[sb_][:], in_=pa[:, :], func=AF.Silu), reads=[tpa], writes=[t_sa[sb_]])
                        op("dve", lambda e: e.tensor_tensor(out=hT[:, fc, :], in0=sa[sb_][:], in1=pbk[:, :], op=ALU.mult),
                           reads=[t_sa[sb_], tpbk], writes=[t_hT])
                    for j in range(4):
                        ti = s_i * 4 + j
                        for half in range(2):
                            pb, tpb = tbank()
                            for fc in range(22):
                                op("pe", lambda e: e.matmul(pb[:, :], lhsT=hT[:, fc, j * 128:(j + 1) * 128], rhs=wfo[:, fc, half * 512:(half + 1) * 512],
                                                            start=(fc == 0), stop=(fc == 21)),
                                   reads=[t_hT, t_wfo], writes=[tpb], signal=(fc == 21))
                            op("dve", lambda e: e.tensor_tensor(out=x2[:, half * 512:(half + 1) * 512], in0=pb[:, :], in1=G2[:, half * 512:(half + 1) * 512], op=ALU.mult),
                               reads=[tpb, t_G2], writes=[t_x2])
                        op("dve", lambda e: e.tensor_tensor(out=x2[:], in0=x2[:], in1=x1t[j][:], op=ALU.add), reads=[t_x1t[j]], writes=[t_x2])
                        op("act", lambda e: e.activation(out=xn[:], in_=x2[:], func=AF.Square, accum_out=small[:, 2:3]),
                           reads=[t_x2], writes=[t_xn, t_small])
                        op("act", lambda e: e.activation(out=small[:, 3:4], in_=small[:, 2:3], func=AF.Sqrt, bias=eps_c, scale=1.0 / D),
                           reads=[t_small, t_cv], writes=[t_small])
                        op("dve", lambda e: e.reciprocal(out=small[:, 3:4], in_=small[:, 3:4]), reads=[t_small], writes=[t_small])
                        op("dve", lambda e: e.scalar_tensor_tensor(out=x2[:], in0=x2[:], scalar=small[:, 3:4], in1=fgbc[:], op0=ALU.mult, op1=ALU.mult),
                           reads=[t_small, t_fgbc], writes=[t_x2])
                        k.dma("pool", out_d[ti * 128:(ti + 1) * 128, :], x2[:], reads=[t_x2], writes=[t_out])

        if stage >= 3:
            with ExitStack() as stt:
                wfo = alloc(stt, "wfo", [128, 22, D], BF16); t_wfo = Tok()
                wfi0 = alloc(stt, "wfi0", [128, 8, 2 * 384], BF16); t_wfi0 = Tok()
                prefetch = []
                for part in range(2):
                    for kc in range(8):
                        prefetch.append(lambda part=part, kc=kc: k.dma("pool", wfi0[:, kc, part * 384:(part + 1) * 384],
                                                                      wfi_d[kc * 128:(kc + 1) * 128, part * DFF:part * DFF + 384], writes=[t_wfi0]))
                for fc in range(22):
                    prefetch.append(lambda fc=fc: k.dma("pool", wfo[:, fc, :], wfo_d[fc * 128:(fc + 1) * 128, :], writes=[t_wfo]))
                k.barrier()
                tail_a(prefetch)
                if stage >= 4:
                    k.barrier()
                    tail_b(wfo, t_wfo, wfi0, t_wfi0, prefetch)

        fin = list(t_ygla_scr) + [t_ym_scr, t_out] + t_x1_scr + dbg_list
        k.finish(fin, "sp")
        k.finish(fin, "pool")
        k.check_deadlock()
        print("build: ops", k.nops, "waits", k.nwaits, "cnt", {e: k.cnt[e] for e in k.eng})
    return nc


def _consts():
    c = np.zeros((128, 768), np.float32)
    m = np.arange(128)[:, None]
    l = np.arange(128)[None, :]
    c[:, 0:128] = np.eye(128)
    c[:, 128:256] = (m <= l)
    c[:, 256:384] = (m >= l)
    c[:, 384:512] = (m > l)
    c[:, 512:640] = (m < l)
    c[:, 640:768] = 1.0
    return c


def _variant(inp, flip):
    w_in = inp["w_in"][0]
    w_up = inp["gla_w_up"][0]
    b_dec = inp["gla_b_dec"][0]
    conv_w = inp["mlstm_conv_w"][0]
    b_gate = inp["mlstm_b_gate"][0]
    if flip:
        idx = np.arange(NIN)
        idx[3072:3088] = np.arange(3088, 3104)
        idx[3088:3104] = np.arange(3072, 3088)
        g0 = 6176
        idx[g0:g0 + 8] = np.arange(g0 + 8, g0 + 16)
        idx[g0 + 8:g0 + 16] = np.arange(g0, g0 + 8)
        w_in = w_in[:, idx]
        w_up = w_up[::-1]
        b_dec = b_dec[::-1]
        conv_w = conv_w[::-1]
        b_gate = b_gate[[2, 3, 0, 1]]
    b_ada = inp["b_ada"][0]
    pp = lambda v, n: np.asarray(v).reshape(n, 128).T
    vpp = np.concatenate([pp(b_ada, 48), pp(inp["norm1_g"][0], 8), pp(inp["norm2_g"][0], 8),
                          np.asarray(conv_w).reshape(3, 8, 128).transpose(2, 0, 1).reshape(128, 24),
                          pp(inp["mlstm_conv_b"][0], 8)], axis=1)
    vbc = np.concatenate([b_ada[2048:3072], b_ada[5120:6144], inp["final_g"], inp["gla_norm_g"][0],
                          inp["mlstm_norm_g"][0], np.asarray(b_gate).reshape(16)])[None, :]
    f = lambda a: np.ascontiguousarray(a, dtype=np.float32)
    return dict(w_in=f(w_in), w_up=f(w_up), bdec=f(np.asarray(b_dec).reshape(1, 1024)), vpp=f(vpp), vbc=f(vbc),
                w_ada=f(inp["w_ada"][0]), consts=_consts(), w_br_gla=f(inp["w_br_gla"][0]), w_br_m=f(inp["w_br_mlstm"][0]),
                w_out=f(inp["w_out"][0]), w_ffn_in=f(inp["w_ffn_in"][0]), w_ffn_out=f(inp["w_ffn_out"][0]))


def make_in_maps(inp, cores=None):
    inp = {k_: np.asarray(v) for k_, v in inp.items()}
    var = [_variant(inp, False), _variant(inp, True)]
    maps = []
    for b in range(4):
        for s in range(2):
            m = dict(var[s])
            xb = inp["x"][b]
            cb = inp["ctx"][b]
            if s:
                xb = xb[::-1]
                cb = cb[::-1]
            m["x"] = np.ascontiguousarray(xb, dtype=np.float32)
            m["ctx"] = np.ascontiguousarray(cb, dtype=np.float32)
            cc = np.stack([inp["c"][b], inp["c_ctx"]], -1).reshape(8, 128, 2).transpose(1, 0, 2)
            m["cct"] = np.ascontiguousarray(cc, dtype=np.float32)
            maps.append(m)
    return maps


def kernel(**inputs):
    nc = build(9)
    maps = make_in_maps(inputs)
    res = run_bass_kernel_spmd(nc, maps, core_ids=list(range(8)))
    out = np.zeros((4, T, D), np.float32)
    for b in range(4):
        out[b, 0:2048] = np.asarray(res.results[2 * b]["out"])
        out[b, 2048:] = np.asarray(res.results[2 * b + 1]["out"])[::-1]
    return out
```

```python
import math
from contextlib import ExitStack
import numpy as np
import concourse.bass as bass
import concourse.mybir as mybir
from concourse.bass_utils import run_bass_kernel_spmd

F32 = mybir.dt.float32
BF16 = mybir.dt.bfloat16
AF = mybir.ActivationFunctionType
ALU = mybir.AluOpType

D = 1024
T = 4096
TC = 256
NIN = 8240
DFF = 2816
NT_OWN = 16
NT_LAT = 32
UW = 4356
CTX0 = 1
LAT0 = 259
ALPHA = 128.0 ** -0.5
NV = 96
NB = 5136
C_ID, C_MIF, C_MIB, C_MAF, C_MAB, C_ONE = 0, 128, 256, 384, 512, 640


class Tok:
    __slots__ = ("w", "r", "ex")

    def __init__(self, ex=False):
        self.w = None
        self.r = {}
        self.ex = ex


class K:
    def __init__(self, nc, stack, ndma=8):
        self.nc = nc
        self.eng = {"pe": nc.tensor, "act": nc.scalar, "dve": nc.vector, "pool": nc.gpsimd, "sp": nc.sync}
        self.semh = {}
        self.cnt = {}
        self.waited = {e: {} for e in self.eng}
        for e in self.eng:
            self.semh[e] = stack.enter_context(nc.semaphore("s_" + e))
            self.cnt[e] = 0
        self.dma_slots = {}
        self.dma_rr = {}
        for q in ("sp", "pool"):
            self.dma_slots[q] = []
            for i in range(ndma):
                nm = "d_%s%d" % (q, i)
                self.semh[nm] = stack.enter_context(nc.semaphore(nm))
                self.cnt[nm] = 0
                self.dma_slots[q].append(nm)
            self.dma_rr[q] = 0
        self.nwaits = 0
        self.nops = 0
        self.log = {e: [] for e in self.eng}

    def _wait(self, e, s, v):
        if self.waited[e].get(s, 0) >= v:
            return
        self.eng[e].wait_ge(self.semh[s], v)
        self.waited[e][s] = v
        self.nwaits += 1
        self.log[e].append(("wait", s, v))

    def _deps(self, e, reads, writes):
        deps = {}
        for t in reads:
            if t.w is not None:
                s, v = t.w
                deps[s] = max(deps.get(s, 0), v)
        for t in writes:
            if t.w is not None:
                s, v = t.w
                deps[s] = max(deps.get(s, 0), v)
            for s, v in t.r.items():
                deps[s] = max(deps.get(s, 0), v)
        for s, v in deps.items():
            if e == "pe" and s == "pe":
                continue
            self._wait(e, s, v)

    def _record(self, ticket, reads, writes):
        s, v = ticket
        for t in reads:
            t.r[s] = max(t.r.get(s, 0), v)
        for t in writes:
            t.w = ticket
            t.r = {}

    def op(self, e, fn, reads=(), writes=(), signal=True):
        if e != "pe":
            exr = [t for t in reads if t.ex]
            if exr:
                writes = list(writes) + exr
        self._deps(e, reads, writes)
        ins = fn(self.eng[e])
        self.nops += 1
        if signal:
            self.cnt[e] += 1
            ins.then_inc(self.semh[e], 1)
            ticket = (e, self.cnt[e])
            self.log[e].append(("inc", e, 1, self.nops))
        else:
            assert e == "pe"
            ticket = (e, self.cnt[e] + 1)
        self._record(ticket, reads, writes)
        return ticket

    def dma(self, q, out, in_, reads=(), writes=(), **kw):
        slots = self.dma_slots[q]
        nm = slots[self.dma_rr[q] % len(slots)]
        self.dma_rr[q] += 1
        if self.cnt[nm] > 0:
            self._wait(q, nm, self.cnt[nm])
        self._deps(q, reads, writes)
        ins = self.eng[q].dma_start(out=out, in_=in_, **kw)
        self.cnt[nm] += 16
        ins.then_inc(self.semh[nm], 16)
        self.log[q].append(("inc", nm, 16, self.nops))
        ticket = (nm, self.cnt[nm])
        self._record(ticket, reads, writes)
        self.nops += 1
        return ticket

    def check_deadlock(self):
        sem = {}
        pos = {e: 0 for e in self.eng}
        progress = True
        while progress:
            progress = False
            for e in self.eng:
                lg = self.log[e]
                while pos[e] < len(lg):
                    it = lg[pos[e]]
                    if it[0] == "wait":
                        if sem.get(it[1], 0) >= it[2]:
                            pos[e] += 1
                            progress = True
                        else:
                            break
                    else:
                        sem[it[1]] = sem.get(it[1], 0) + it[2]
                        pos[e] += 1
                        progress = True
        stuck = {e: (pos[e], len(self.log[e]), self.log[e][pos[e]]) for e in self.eng if pos[e] < len(self.log[e])}
        if stuck:
            print("DEADLOCK:", stuck, {k_: v for k_, v in sem.items()})
        else:
            print("deadlock check: OK")
        return not stuck

    def barrier(self):
        snap = dict(self.cnt)
        for e in self.eng:
            for s_, v in snap.items():
                if v > 0:
                    self._wait(e, s_, v)

    def finish(self, toks, e="sp"):
        for t in toks:
            if t.w is not None:
                self._wait(e, t.w[0], t.w[1])


def build(stage=9):
    nc = bass.Bass("TRN2", target_bir_lowering=False)
    di = lambda n, s, dt=F32: nc.dram_tensor(n, s, dt, kind="ExternalInput").ap()
    x_d = di("x", [T, D])
    ctx_d = di("ctx", [TC, D])
    cct_d = di("cct", [128, 8, 2])
    wada_d = di("w_ada", [D, 6 * D])
    win_d = di("w_in", [D, NIN])
    wup_d = di("w_up", [2, 16, 512])
    bdec_d = di("bdec", [1, 1024])
    cst_d = di("consts", [128, 768])
    vpp_d = di("vpp", [128, NV])
    vbc_d = di("vbc", [1, NB])
    wbg_d = di("w_br_gla", [D, D])
    wbm_d = di("w_br_m", [D, D])
    wo_d = di("w_out", [D, D])
    wfi_d = di("w_ffn_in", [D, 2 * DFF])
    wfo_d = di("w_ffn_out", [DFF, D])
    out_d = nc.dram_tensor("out", [NT_OWN * 128, D], F32, kind="ExternalOutput").ap()
    of_scr = nc.dram_tensor("of_scr", [NT_OWN * 128, D], F32, kind="Internal" if stage >= 9 else "ExternalOutput").ap()
    hf_scr = nc.dram_tensor("hf_scr", [T, D], F32, kind="Internal" if stage >= 9 else "ExternalOutput").ap()
    if stage < 9:
        ygla_scr = nc.dram_tensor("ygla", [NT_OWN * 128, D], BF16, kind="ExternalOutput").ap()
        ym_scr = nc.dram_tensor("ym", [T, D], BF16, kind="ExternalOutput").ap()
    else:
        ygla_scr = nc.dram_tensor("ygla", [NT_OWN * 128, D], BF16, kind="Internal").ap()
        ym_scr = nc.dram_tensor("ym", [T, D], BF16, kind="Internal").ap()
    x1_scr = nc.dram_tensor("x1_scr", [NT_OWN * 128, D], F32, kind="Internal" if stage >= 9 else "ExternalOutput").ap()
    t_of_scr = [Tok() for _ in range(NT_OWN)]
    t_hf_scr = [Tok() for _ in range(NT_LAT)]
    t_ygla_scr = [Tok() for _ in range(NT_OWN)]
    t_ym_scr = Tok()
    t_x1_scr = [Tok() for _ in range(NT_OWN)]
    t_out = Tok()

    with ExitStack() as st0:
        k = K(nc, st0)
        op = k.op

        dbg_list = []
        dbg_pool = [st0.enter_context(nc.sbuf_tensor("dbgs_%d" % i, [128, 128], F32)) for i in range(16 if stage < 3 else 0)]

        def dbg(name, ap, toks, n):
            if stage >= 3:
                return
            dd = nc.dram_tensor("dbg_" + name, [128, n], F32, kind="ExternalOutput").ap()
            stg = dbg_pool.pop()[:, 0:n]
            tk = Tok()
            np_ = ap.shape[0]
            op("dve", lambda e: e.memset(stg, 0.0), writes=[tk])
            op("dve", lambda e: e.tensor_copy(out=stg[0:np_, :], in_=ap), reads=toks, writes=[tk])
            td = Tok()
            k.dma("pool", dd[:, :], stg, reads=[tk], writes=[td])
            dbg_list.append(td)

        uniq = [0]

        def alloc(st, name, shape, dt):
            uniq[0] += 1
            return st.enter_context(nc.sbuf_tensor("sb%d_%s" % (uniq[0], name), shape, dt))

        cst = alloc(st0, "cst", [128, 768], F32); t_cst = Tok()
        vpp = alloc(st0, "vpp", [128, NV], F32); t_vpp = Tok()
        identb = alloc(st0, "identb", [128, 128], BF16); t_identb = Tok()
        mask4 = [alloc(st0, "mask4_%d" % d, [128, 4, 128], BF16) for d in range(2)]; t_mask4 = Tok()
        onesb = alloc(st0, "onesb", [128, 1], BF16); t_onesb = Tok()
        cvals = alloc(st0, "cvals", [128, 4], F32); t_cv = Tok()
        modpp = alloc(st0, "modpp", [128, 4, 8, 2], F32); t_modpp = Tok()
        scU = alloc(st0, "scU", [128, 8, 2], F32); t_scU = Tok()
        sc2 = alloc(st0, "sc2", [128, 8], F32); t_sc2 = Tok()
        G1 = alloc(st0, "G1", [128, D], F32); t_G1 = Tok()
        G2 = alloc(st0, "G2", [128, D], F32); t_G2 = Tok()
        ps = [st0.enter_context(nc.psum_tensor("ps%d" % i, [128, 512], F32)) for i in range(8)]
        t_ps = [Tok(ex=True) for _ in range(8)]
        ident = cst[:, C_ID:C_ID + 128]
        MI = [cst[:, C_MIF:C_MIF + 128], cst[:, C_MIB:C_MIB + 128]]
        MA = [cst[:, C_MAF:C_MAF + 128], cst[:, C_MAB:C_MAB + 128]]
        ones = cst[:, C_ONE:C_ONE + 128]
        one_c = cvals[:, 0:1]
        eps_c = cvals[:, 1:2]
        lna_c = cvals[:, 2:3]
        VP_BADA, VP_N1, VP_N2, VP_CW, VP_CB = 0, 48, 56, 64, 88

        k.dma("sp", cst[:], cst_d[:, :], writes=[t_cst])
        k.dma("sp", vpp[:], vpp_d[:, :], writes=[t_vpp])
        op("dve", lambda e: e.memset(cvals[:, 0:1], 1.0), writes=[t_cv])
        op("dve", lambda e: e.memset(cvals[:, 1:2], 1e-6), writes=[t_cv])
        op("dve", lambda e: e.memset(cvals[:, 2:3], math.log(ALPHA)), writes=[t_cv])
        op("dve", lambda e: e.memset(cvals[:, 3:4], 0.0), writes=[t_cv])
        op("dve", lambda e: e.tensor_copy(out=identb[:], in_=ident), reads=[t_cst], writes=[t_identb])
        op("dve", lambda e: e.tensor_copy(out=onesb[:], in_=cst[:, C_ONE:C_ONE + 1]), reads=[t_cst], writes=[t_onesb])
        for d in range(2):
            for h in range(4):
                op("dve", lambda e: e.tensor_copy(out=mask4[d][:, h, :], in_=MI[d]), reads=[t_cst], writes=[t_mask4])

        prep_banks = [5, 6, 7]
        prr = [0]

        def pbank():
            i = prep_banks[prr[0] % len(prep_banks)]
            prr[0] += 1
            return ps[i], t_ps[i]

        with ExitStack() as st:
            scc = alloc(st, "scc", [128, 8, 2], F32); t_scc = Tok()
            wa = [alloc(st, "wa%d" % i, [128, 8, D], F32) for i in range(2)]; t_wa = [[Tok() for _ in range(8)] for _ in range(2)]
            bbc = alloc(st, "bbc", [128, D], F32); t_bbc = Tok()
            k.dma("sp", scc[:], cct_d[:, :, :], writes=[t_scc])
            op("act", lambda e: e.activation(out=scc[:], in_=scc[:], func=AF.Silu), reads=[t_scc], writes=[t_scc])
            psA, t_psA = ps[0], t_ps[0]
            mrow = alloc(st, "mrow", [2, D], F32); t_mrow = Tok()
            gi = 0
            for g in range(6):
                w_, tw_ = wa[g % 2], t_wa[g % 2]
                for kc in range(8):
                    k.dma("sp", w_[:, kc, :], wada_d[kc * 128:(kc + 1) * 128, g * D:(g + 1) * D], writes=[tw_[kc]])
                for half in range(2):
                    pb, tpb = pbank()
                    for kc in range(8):
                        op("pe", lambda e: e.matmul(pb[0:2, :], lhsT=scc[:, kc, :], rhs=w_[:, kc, half * 512:(half + 1) * 512],
                                                    start=(kc == 0), stop=(kc == 7)),
                           reads=[tw_[kc], t_scc], writes=[tpb], signal=(kc == 7))
                    op("act", lambda e: e.copy(out=mrow[:, half * 512:(half + 1) * 512], in_=pb[0:2, :]), reads=[tpb], writes=[t_mrow])
                if g in (0, 1, 3, 4):
                    for j in range(8):
                        c0 = (gi * 8 + j) * 2
                        op("pe", lambda e: e.transpose(out=psA[:, c0:c0 + 2], in_=mrow[0:2, j * 128:(j + 1) * 128], identity=cst[0:2, C_ID:C_ID + 2]),
                           reads=[t_mrow, t_cst], writes=[t_psA], signal=(j == 7))
                    gi += 1
                else:
                    Gt, tG = (G1, t_G1) if g == 2 else (G2, t_G2)
                    voff = 0 if g == 2 else 1024
                    k.dma("sp", bbc[:], vbc_d[0:1, voff:voff + 1024].to_broadcast([128, 1024]), writes=[t_bbc])
                    for half in range(2):
                        pb, tpb = pbank()
                        op("pe", lambda e: e.matmul(pb[:, :], lhsT=cst[0:1, C_ONE:C_ONE + 128], rhs=mrow[0:1, half * 512:(half + 1) * 512],
                                                    start=True, stop=True),
                           reads=[t_mrow, t_cst], writes=[tpb])
                        op("dve", lambda e: e.tensor_tensor(out=Gt[:, half * 512:(half + 1) * 512], in0=pb[:, :],
                                                            in1=bbc[:, half * 512:(half + 1) * 512], op=ALU.add),
                           reads=[tpb, t_bbc], writes=[tG])
            psAv = psA[:, 0:64].rearrange("p (g j s) -> p g j s", g=4, j=8, s=2)
            for gi_, g in enumerate((0, 1, 3, 4)):
                for s in range(2):
                    op("dve", lambda e: e.tensor_tensor(out=modpp[:, gi_, :, s], in0=psAv[:, gi_, :, s],
                                                        in1=vpp[:, VP_BADA + g * 8:VP_BADA + (g + 1) * 8], op=ALU.add),
                       reads=[t_psA, t_vpp], writes=[t_modpp])
            for s in range(2):
                op("dve", lambda e: e.scalar_tensor_tensor(out=scU[:, :, s], in0=modpp[:, 1, :, s], scalar=1.0,
                                                           in1=vpp[:, VP_N1:VP_N1 + 8], op0=ALU.add, op1=ALU.mult),
                   reads=[t_modpp, t_vpp], writes=[t_scU])
            op("dve", lambda e: e.scalar_tensor_tensor(out=sc2[:, :], in0=modpp[:, 3, :, 0], scalar=1.0,
                                                       in1=vpp[:, VP_N2:VP_N2 + 8], op0=ALU.add, op1=ALU.mult),
               reads=[t_modpp, t_vpp], writes=[t_sc2])


        def norm_transpose(xs, t_xs, xn, t_xn, small, t_small, junk, t_junk, scale_ap, bias_ap, tsb, dst, t_dst):
            norm_part(xs, t_xs, xn, t_xn, small, t_small, junk, t_junk)
            transpose_part(xn, t_xn, scale_ap, bias_ap, tsb, dst, t_dst)

        def norm_part(xs, t_xs, xn, t_xn, small, t_small, junk, t_junk):
            op("act", lambda e: e.activation(out=junk[:], in_=xs, func=AF.Square, accum_out=small[:, 0:1]),
               reads=[t_xs], writes=[t_junk, t_small])
            op("act", lambda e: e.activation(out=small[:, 1:2], in_=small[:, 0:1], func=AF.Sqrt, bias=eps_c, scale=1.0 / D),
               reads=[t_small, t_cv], writes=[t_small])
            op("dve", lambda e: e.reciprocal(out=small[:, 1:2], in_=small[:, 1:2]), reads=[t_small], writes=[t_small])
            op("dve", lambda e: e.tensor_scalar(out=xn[:], in0=xs, scalar1=small[:, 1:2], scalar2=None, op0=ALU.mult),
               reads=[t_xs, t_small], writes=[t_xn])

        def transpose_part(xn, t_xn, scale_ap, bias_ap, tsb, dst, t_dst):
            for half in range(2):
                pb, tpb = pbank()
                for j in range(4):
                    kc = half * 4 + j
                    op("pe", lambda e: e.transpose(out=pb[:, j * 128:(j + 1) * 128], in_=xn[:, kc * 128:(kc + 1) * 128], identity=ident),
                       reads=[t_xn, t_cst], writes=[tpb], signal=(j == 3))
                for j in range(4):
                    kc = half * 4 + j
                    if j % 2 == 0:
                        op("act", lambda e: e.activation(out=dst(kc), in_=pb[:, j * 128:(j + 1) * 128], func=AF.Identity,
                                                         scale=scale_ap(kc), bias=bias_ap(kc)),
                           reads=[tpb] + tsb, writes=[t_dst])
                    else:
                        op("dve", lambda e: e.tensor_scalar(out=dst(kc), in0=pb[:, j * 128:(j + 1) * 128], scalar1=scale_ap(kc),
                                                            scalar2=bias_ap(kc), op0=ALU.mult, op1=ALU.add),
                           reads=[tpb] + tsb, writes=[t_dst])

        with ExitStack() as stm:
            uT = alloc(stm, "uT", [128, 8, UW], BF16)
            t_uT = [Tok() for _ in range(34)]
            t_guard = Tok()
            wmix = alloc(stm, "wmix", [128, 8, 3104], BF16); t_wmix = [Tok() for _ in range(8)]
            for c in (0, 257, 258, UW - 1):
                op("dve", lambda e: e.memset(uT[:, :, c:c + 1], 0.0), writes=[t_guard])

            def tile_col(ti):
                return CTX0 + ti * 128 if ti < 2 else LAT0 + (ti - 2) * 128

            def phase_U(colmajor):
                with ExitStack() as st:
                    xs = [alloc(st, "xs%d" % i, [128, D], F32) for i in range(2)]; t_xs = [Tok(), Tok()]
                    xn = [alloc(st, "xn%d" % i, [128, D], F32) for i in range(2)]; t_xn = [Tok(), Tok()]
                    junk = alloc(st, "junkU", [128, D], F32); t_junk = Tok()
                    small = [alloc(st, "smallU%d" % i, [128, 2], F32) for i in range(2)]; t_small = [Tok(), Tok()]
                    xv = x_d.rearrange("(r c) d -> c r d", c=64)

                    def partA(ti):
                        b = ti % 2
                        if ti < 2:
                            k.dma("sp", xs[b][:], ctx_d[ti * 128:(ti + 1) * 128, :], writes=[t_xs[b]])
                        else:
                            lt = ti - 2
                            if colmajor:
                                for cl in range(2):
                                    k.dma("sp", xs[b][cl * 64:(cl + 1) * 64, :], xv[2 * lt + cl], writes=[t_xs[b]])
                            else:
                                k.dma("sp", xs[b][:], x_d[lt * 128:(lt + 1) * 128, :], writes=[t_xs[b]])
                        norm_part(xs[b][:], t_xs[b], xn[b], t_xn[b], small[b], t_small[b], junk, t_junk)

                    def partB(ti):
                        b = ti % 2
                        s = 1 if ti < 2 else 0
                        c0 = tile_col(ti)
                        transpose_part(xn[b], t_xn[b], lambda kc: scU[:, kc, s:s + 1], lambda kc: modpp[:, 0, kc, s:s + 1], [t_scU, t_modpp],
                                       lambda kc: uT[:, kc, c0:c0 + 128], t_uT[ti])

                    partA(0)
                    for ti in range(34):
                        if ti + 1 < 34:
                            partA(ti + 1)
                        partB(ti)

            def step(g):
                try:
                    next(g)
                except StopIteration:
                    pass

            def exhaust(g):
                for _ in g:
                    pass

            def load_w(dst, t_dst, src, c0, c1, nk=8, q="pool"):
                for kc in range(nk):
                    k.dma(q, dst[:, kc, 0:c1 - c0], src[kc * 128:(kc + 1) * 128, c0:c1], writes=[t_dst[kc]])

            def gla_phase():
                print("gla start cnt", dict(k.cnt))
                with ExitStack() as st:
                    wup = alloc(st, "wup", [16, 2, 512], F32); t_wup = Tok()
                    bdec = alloc(st, "bdec", [1, 1024], F32); t_bdec = Tok()
                    gnorm = alloc(st, "gnorm", [128, D], F32); t_gnorm = Tok()
                    lrT = alloc(st, "lrT", [16, 128], F32); t_lrT = Tok()
                    tmp = alloc(st, "gtmp", [128, 512], F32); t_tmp = Tok()
                    sp = alloc(st, "gsp", [128, 512], F32); t_sp = Tok()
                    dkk = alloc(st, "dkk", [128, 512], F32); t_dkk = Tok()
                    Ep = alloc(st, "Ep", [128, 512], F32); t_Ep = Tok()
                    Em = alloc(st, "Em", [128, 512], F32); t_Em = Tok()
                    kk = [alloc(st, "kk%d" % i, [128, 512], BF16) for i in range(2)]; t_kk = [Tok(), Tok()]
                    v = [alloc(st, "v%d" % i, [128, D], BF16) for i in range(2)]; t_v = [Tok(), Tok()]
                    Gg = [alloc(st, "Gg%d" % i, [128, D], F32) for i in range(2)]; t_Gg = [Tok(), Tok()]
                    qd = [alloc(st, "qd%d" % i, [128, 4, 128], BF16) for i in range(2)]; t_qd = [Tok(), Tok()]
                    kd = [alloc(st, "kd%d" % i, [128, 4, 128], BF16) for i in range(2)]; t_kd = [Tok(), Tok()]
                    dec = [alloc(st, "dec%d" % i, [128, 4], F32) for i in range(2)]; t_dec = [Tok(), Tok()]
                    wT = alloc(st, "wT", [128, 4, 128], BF16); t_wT = Tok()
                    S = alloc(st, "S", [128, 4, 256], F32); t_S = Tok()
                    Sb = alloc(st, "Sb", [128, 4, 256], BF16); t_Sb = Tok()
                    ofl = [alloc(st, "ofl%d" % i, [128, D], F32) for i in range(2)]; t_ofl = [Tok(), Tok()]
                    osum = alloc(st, "osum", [128, D], F32); t_osum = Tok()
                    junk = alloc(st, "junkG", [128, 256], F32); t_junk = Tok()
                    ssq = alloc(st, "ssqG", [128, 8], F32); t_ssq = Tok()
                    yb = [alloc(st, "yb%d" % i, [128, D], BF16) for i in range(2)]; t_yb = [Tok(), Tok()]

                    for d_ in range(2):
                        k.dma("sp", wup[:, d_, :], wup_d[d_], writes=[t_wup])
                    k.dma("sp", bdec[:], bdec_d[:, :], writes=[t_bdec])
                    k.dma("sp", gnorm[:], vbc_d[0:1, 3072:4096].to_broadcast([128, 1024]), writes=[t_gnorm])

                    def prep(dr, ti, full, i):
                        b = i % 2
                        c0 = tile_col(ti)
                        tu = t_uT[ti]
                        uts = lambda kc: uT[:, kc, c0:c0 + 128]
                        pb, tpb = pbank()
                        for kc in range(8):
                            op("pe", lambda e: e.matmul(pb[0:16, 0:128], lhsT=wmix[:, kc, 3072 + 16 * dr:3088 + 16 * dr], rhs=uts(kc),
                                                        start=(kc == 0), stop=(kc == 7)),
                               reads=t_wmix + [tu], writes=[tpb], signal=(kc == 7))
                        op("act", lambda e: e.copy(out=lrT[:], in_=pb[0:16, 0:128]), reads=[tpb], writes=[t_lrT])
                        pb, tpb = pbank()
                        op("pe", lambda e: e.matmul(pb[:, :], lhsT=lrT[0:16, :], rhs=wup[0:16, dr, :], start=True, stop=False),
                           reads=[t_lrT, t_wup], writes=[tpb], signal=False)
                        op("pe", lambda e: e.matmul(pb[:, :], lhsT=cst[0:1, C_ONE:C_ONE + 128], rhs=bdec[0:1, dr * 512:(dr + 1) * 512],
                                                    start=False, stop=True),
                           reads=[t_cst, t_bdec], writes=[tpb])
                        op("act", lambda e: e.activation(out=tmp[:], in_=pb[:, :], func=AF.Exp, scale=-1.0), reads=[tpb], writes=[t_tmp])
                        op("act", lambda e: e.activation(out=sp[:], in_=tmp[:], func=AF.Ln, bias=one_c, scale=1.0),
                           reads=[t_tmp, t_cv], writes=[t_sp])
                        yield
                        pb, tpb = pbank()
                        op("pe", lambda e: e.matmul(pb[:, :], lhsT=MA[dr], rhs=sp[:], start=True, stop=True),
                           reads=[t_cst, t_sp], writes=[tpb])
                        op("act", lambda e: e.activation(out=dkk[:], in_=pb[:, :], func=AF.Exp, scale=-1.0 / 16), reads=[tpb], writes=[t_dkk])
                        pb, tpb = pbank()
                        for kc in range(8):
                            op("pe", lambda e: e.matmul(pb[:, :], lhsT=uts(kc), rhs=wmix[:, kc, 512:1024], start=(kc == 0), stop=(kc == 7)),
                               reads=t_wmix + [tu], writes=[tpb], signal=(kc == 7))
                        op("dve", lambda e: e.tensor_tensor(out=kk[b][:], in0=pb[:, :], in1=dkk[:], op=ALU.mult),
                           reads=[tpb, t_dkk], writes=[t_kk[b]])
                        yield
                        for half in range(2):
                            pb, tpb = pbank()
                            for kc in range(8):
                                op("pe", lambda e: e.matmul(pb[:, :], lhsT=uts(kc), rhs=wmix[:, kc, 1024 + half * 512:1536 + half * 512],
                                                            start=(kc == 0), stop=(kc == 7)),
                                   reads=t_wmix + [tu], writes=[tpb], signal=(kc == 7))
                            if half == 0:
                                op("act", lambda e: e.copy(out=v[b][:, 0:512], in_=pb[:, :]), reads=[tpb], writes=[t_v[b]])
                            else:
                                op("dve", lambda e: e.tensor_copy(out=v[b][:, 512:1024], in_=pb[:, :]), reads=[tpb], writes=[t_v[b]])
                        yield
                        if full and dr == 1:
                            for half in range(2):
                                pb, tpb = pbank()
                                for kc in range(8):
                                    op("pe", lambda e: e.matmul(pb[:, :], lhsT=uts(kc), rhs=wmix[:, kc, 2048 + half * 512:2560 + half * 512],
                                                                start=(kc == 0), stop=(kc == 7)),
                                       reads=t_wmix + [tu], writes=[tpb], signal=(kc == 7))
                                op("act", lambda e: e.activation(out=Gg[b][:, half * 512:(half + 1) * 512], in_=pb[:, :], func=AF.Silu),
                                   reads=[tpb], writes=[t_Gg[b]])
                            op("dve", lambda e: e.tensor_tensor(out=Gg[b][:], in0=Gg[b][:], in1=gnorm[:], op=ALU.mult),
                               reads=[t_Gg[b], t_gnorm], writes=[t_Gg[b]])
                        yield
                        pb, tpb = pbank()
                        for h in range(4):
                            op("pe", lambda e: e.matmul(pb[:, h * 128:(h + 1) * 128], lhsT=sp[:, h * 128:(h + 1) * 128], rhs=MI[dr],
                                                        start=True, stop=True),
                               reads=[t_sp, t_cst], writes=[tpb], signal=(h == 3))
                        op("act", lambda e: e.activation(out=Ep[:], in_=pb[:, :], func=AF.Exp, scale=-1.0 / 16), reads=[tpb], writes=[t_Ep])
                        if full:
                            op("act", lambda e: e.activation(out=Em[:], in_=pb[:, :], func=AF.Exp, scale=1.0 / 16), reads=[tpb], writes=[t_Em])
                        lastc = 127 if dr == 0 else 0
                        Epv = Ep[:].rearrange("p (h l) -> p h l", h=4)
                        op("dve", lambda e: e.tensor_copy(out=dec[b][:], in_=Epv[:, :, lastc]), reads=[t_Ep], writes=[t_dec[b]])
                        yield
                        if full:
                            for which in range(2):
                                if which == 1:
                                    yield
                                pb, tpb = pbank()
                                for h in range(4):
                                    for kc in range(8):
                                        cc = which * 512 + h * 128
                                        op("pe", lambda e: e.matmul(pb[:, h * 128:(h + 1) * 128], lhsT=wmix[:, kc, cc:cc + 128], rhs=uts(kc),
                                                                    start=(kc == 0), stop=(kc == 7)),
                                           reads=t_wmix + [tu], writes=[tpb], signal=(kc == 7 and h == 3))
                                if which == 0:
                                    op("dve", lambda e: e.scalar_tensor_tensor(out=qd[b][:].rearrange("p h l -> p (h l)"), in0=pb[:, :], scalar=ALPHA,
                                                                               in1=Ep[:], op0=ALU.mult, op1=ALU.mult),
                                       reads=[tpb, t_Ep], writes=[t_qd[b]])
                                else:
                                    op("dve", lambda e: e.tensor_tensor(out=kd[b][:].rearrange("p h l -> p (h l)"), in0=pb[:, :], in1=Em[:], op=ALU.mult),
                                       reads=[tpb, t_Em], writes=[t_kd[b]])

                    def scan(dr, ti, full, i, own):
                        b = i % 2
                        if full:
                            for h in range(4):
                                op("pe", lambda e: e.matmul(ps[0][:, h * 128:(h + 1) * 128], lhsT=kd[b][:, h, :], rhs=qd[b][:, h, :], start=True, stop=True),
                                   reads=[t_kd[b], t_qd[b]], writes=[t_ps[0]], signal=(h == 3))
                        yield
                        if full:
                            op("dve", lambda e: e.tensor_tensor(out=wT[:].rearrange("p h l -> p (h l)"), in0=ps[0][:, :],
                                                                in1=mask4[dr][:].rearrange("p h l -> p (h l)"), op=ALU.mult),
                               reads=[t_ps[0], t_mask4], writes=[t_wT])
                        yield
                        if full:
                            for h in range(4):
                                pbk = ps[1 + h // 2]; tpbk = t_ps[1 + h // 2]
                                oc = (h % 2) * 256
                                op("pe", lambda e: e.matmul(pbk[:, oc:oc + 256], lhsT=wT[:, h, :], rhs=v[b][:, h * 256:(h + 1) * 256], start=True, stop=False),
                                   reads=[t_wT, t_v[b]], writes=[tpbk], signal=False)
                                op("pe", lambda e: e.matmul(pbk[:, oc:oc + 256], lhsT=qd[b][:, h, :], rhs=Sb[:, h, :], start=False, stop=True),
                                   reads=[t_qd[b], t_Sb], writes=[tpbk], signal=(h % 2 == 1))
                        for h in range(4):
                            pbk = ps[3 + h // 2]; tpbk = t_ps[3 + h // 2]
                            oc = (h % 2) * 256
                            op("pe", lambda e: e.matmul(pbk[:, oc:oc + 256], lhsT=kk[b][:, h * 128:(h + 1) * 128], rhs=v[b][:, h * 256:(h + 1) * 256],
                                                        start=True, stop=True),
                               reads=[t_kk[b], t_v[b]], writes=[tpbk], signal=(h % 2 == 1))
                        yield
                        for h in range(4):
                            pbk = ps[3 + h // 2]; tpbk = t_ps[3 + h // 2]
                            oc = (h % 2) * 256
                            op("dve", lambda e: e.scalar_tensor_tensor(out=S[:, h, :], in0=S[:, h, :], scalar=dec[b][:, h:h + 1], in1=pbk[:, oc:oc + 256],
                                                                       op0=ALU.mult, op1=ALU.add),
                               reads=[tpbk, t_dec[b]], writes=[t_S])
                        op("pool", lambda e: e.tensor_copy(out=Sb[:].rearrange("p h l -> p (h l)"), in_=S[:].rearrange("p h l -> p (h l)")),
                           reads=[t_S], writes=[t_Sb])
                        yield
                        if not full:
                            return
                        if dr == 0:
                            ob = ofl[b]
                            for half in range(2):
                                op("act", lambda e: e.copy(out=ob[:, half * 512:(half + 1) * 512], in_=ps[1 + half][:, :]),
                                   reads=[t_ps[1 + half]], writes=[t_ofl[b]])
                            k.dma("pool", of_scr[own * 128:(own + 1) * 128, :], ob[:], reads=[t_ofl[b]], writes=[t_of_scr[own]])

                        else:
                            for half in range(2):
                                op("dve", lambda e: e.tensor_tensor(out=osum[:, half * 512:(half + 1) * 512], in0=ps[1 + half][:, :],
                                                                    in1=ofl[b][:, half * 512:(half + 1) * 512], op=ALU.add),
                                   reads=[t_ps[1 + half], t_ofl[b]], writes=[t_osum])
                            for h in range(4):
                                op("act", lambda e: e.activation(out=junk[:], in_=osum[:, h * 256:(h + 1) * 256], func=AF.Square, accum_out=ssq[:, h:h + 1]),
                                   reads=[t_osum], writes=[t_junk, t_ssq])
                            op("act", lambda e: e.activation(out=ssq[:, 4:8], in_=ssq[:, 0:4], func=AF.Ln, bias=eps_c, scale=1.0 / 256),
                               reads=[t_ssq, t_cv], writes=[t_ssq])
                            op("act", lambda e: e.activation(out=ssq[:, 4:8], in_=ssq[:, 4:8], func=AF.Exp, scale=-0.5), reads=[t_ssq], writes=[t_ssq])
                            for h in range(4):
                                op("dve", lambda e: e.scalar_tensor_tensor(out=yb[b][:, h * 256:(h + 1) * 256], in0=osum[:, h * 256:(h + 1) * 256],
                                                                           scalar=ssq[:, 4 + h:5 + h], in1=Gg[b][:, h * 256:(h + 1) * 256],
                                                                           op0=ALU.mult, op1=ALU.mult),
                                   reads=[t_osum, t_ssq, t_Gg[b]], writes=[t_yb[b]])
                            k.dma("pool", ygla_scr[own * 128:(own + 1) * 128, :], yb[b][:], reads=[t_yb[b]], writes=[t_ygla_scr[own]])

                    for dr in range(2):
                        op("dve", lambda e: e.memset(S[:].rearrange("p h l -> p (h l)"), 0.0), writes=[t_S])
                        op("dve", lambda e: e.memset(Sb[:].rearrange("p h l -> p (h l)"), 0.0), writes=[t_Sb])
                        if dr == 0:
                            seq = [(0, False, None), (1, False, None)] + [(2 + lt, True, lt) for lt in range(NT_OWN)]
                        else:
                            seq = [(1, False, None), (0, False, None)] + [(2 + lt, lt < NT_OWN, lt if lt < NT_OWN else None)
                                                                          for lt in range(NT_LAT - 1, -1, -1)]
                        exhaust(prep(dr, seq[0][0], seq[0][1], 0))
                        for i, (ti, full, own) in enumerate(seq):
                            if dr == 1 and full:
                                k.dma("sp", ofl[i % 2][:], of_scr[own * 128:(own + 1) * 128, :], reads=[t_of_scr[own]], writes=[t_ofl[i % 2]])
                            P = prep(dr, seq[i + 1][0], seq[i + 1][1], i + 1) if i + 1 < len(seq) else iter(())
                            S_ = scan(dr, ti, full, i, own)
                            step(S_); step(S_)
                            step(P); step(P)
                            step(S_); step(S_)
                            step(P); step(P); step(P)
                            exhaust(P)
                            exhaust(S_)

            def mlstm_phase():
                print("mlstm start cnt", dict(k.cnt))
                with ExitStack() as st:
                    bg = alloc(st, "bg", [128, 16], F32); t_bg = Tok()
                    mnorm = alloc(st, "mnorm", [128, D], F32); t_mnorm = Tok()
                    Gt4 = [alloc(st, "Gt%d" % i, [128, 16], F32) for i in range(4)]; t_Gt4 = [Tok() for _ in range(4)]
                    sm4 = [alloc(st, "sm%d" % i, [128, 16], F32) for i in range(4)]; t_sm4 = [Tok() for _ in range(4)]
                    scall = alloc(st, "scall", [128, 34, 16], F32); t_scall = Tok()
                    accS = [alloc(st, "accS%d" % i, [128, 512], F32) for i in range(2)]; t_accS = [Tok(), Tok()]
                    halo = [alloc(st, "halo%d" % i, [128, 16], F32) for i in range(2)]; t_halo = [Tok(), Tok()]
                    qcS = [alloc(st, "qcS%d" % i, [128, 4, 512], BF16) for i in range(2)]; t_qcS = [Tok(), Tok()]
                    kcS = [alloc(st, "kcS%d" % i, [128, 4, 512], BF16) for i in range(2)]; t_kcS = [Tok(), Tok()]
                    kk = [alloc(st, "mkk%d" % i, [128, 512], BF16) for i in range(2)]; t_kk = [Tok(), Tok()]
                    v = [alloc(st, "mv%d" % i, [128, D], BF16) for i in range(2)]; t_v = [Tok(), Tok()]
                    Gg = [alloc(st, "mGg%d" % i, [128, D], F32) for i in range(2)]; t_Gg = [Tok(), Tok()]
                    wT = alloc(st, "mwT", [128, 4, 128], BF16); t_wT = Tok()
                    C = alloc(st, "C", [128, 4, 256], F32); t_C = Tok()
                    Cb = alloc(st, "Cb", [128, 4, 256], BF16); t_Cb = Tok()
                    nst = alloc(st, "nst", [128, 4], F32); t_n = Tok()
                    nb = alloc(st, "nb", [128, 4], BF16); t_nb = Tok()
                    rr = alloc(st, "rr", [128, 16], F32); t_rr = Tok()
                    hfl = [alloc(st, "hfl%d" % i, [128, D], F32) for i in range(2)]; t_hfl = [Tok(), Tok()]
                    hs = alloc(st, "hs", [128, D], F32); t_hs = Tok()
                    junk = alloc(st, "junkM", [128, 256], F32); t_junk = Tok()
                    ssq = alloc(st, "ssqM", [128, 8], F32); t_ssq = Tok()
                    yb = [alloc(st, "myb%d" % i, [128, D], BF16) for i in range(2)]; t_yb = [Tok(), Tok()]
                    smb = ps[5]
                    t_smg = t_smb = t_ps[5]
                    sms = ps[4]
                    t_smden = t_smkvn = t_ps[4]
                    mprep = [5, 0]
                    mrr = [0]

                    def mbank():
                        i = mprep[mrr[0] % 2]
                        mrr[0] += 1
                        return ps[i], t_ps[i]

                    k.dma("sp", bg[:], vbc_d[0:1, 5120:5136].to_broadcast([128, 16]), writes=[t_bg])
                    k.dma("sp", mnorm[:], vbc_d[0:1, 4096:5120].to_broadcast([128, 1024]), writes=[t_mnorm])
                    op("dve", lambda e: e.tensor_scalar(out=mnorm[:], in0=mnorm[:], scalar1=0.5, scalar2=None, op0=ALU.mult), writes=[t_mnorm])
                    ymv = ym_scr.rearrange("(r c) d -> c r d", c=64)

                    def convq(gb, c0, N, tiles):
                        rd = t_wmix + [t_guard] + [t_uT[t_] for t_ in tiles]
                        lo, hi = min(tiles), max(tiles)
                        if lo not in (0, 2):
                            rd.append(t_uT[lo - 1])
                        if hi not in (1, 33):
                            rd.append(t_uT[hi + 1])
                        for ci in range(8):
                            h0 = 48 + 2 * ci
                            for kc in range(8):
                                op("pe", lambda e: e.matmul(sms[:, h0:h0 + 2], lhsT=wmix[:, kc, ci * 128:(ci + 1) * 128], rhs=uT[:, kc, c0 - 1:c0 + N + 1:N + 1],
                                                            start=(kc == 0), stop=(kc == 7)),
                                   reads=rd, writes=[t_ps[4]], signal=(kc == 7 and ci == 7))
                        op("dve", lambda e: e.tensor_copy(out=halo[gb][:], in_=sms[:, 48:64]), reads=[t_ps[4]], writes=[t_halo[gb]])
                        cwf = lambda ci, tap: vpp[:, VP_CW + tap * 8 + ci:VP_CW + tap * 8 + ci + 1]
                        banks = {}

                        def stA(ci):
                            pb, tpb = ps[6 + ci % 2], t_ps[6 + ci % 2]
                            banks[ci] = (pb, tpb)
                            for kc in range(8):
                                op("pe", lambda e: e.matmul(pb[:, 0:N], lhsT=wmix[:, kc, ci * 128:(ci + 1) * 128], rhs=uT[:, kc, c0:c0 + N],
                                                            start=(kc == 0), stop=(kc == 7)),
                                   reads=rd, writes=[tpb], signal=(kc == 7))
                            acc, ta = accS[ci % 2], t_accS[ci % 2]
                            op("act", lambda e: e.activation(out=acc[:, 0:N], in_=pb[:, 0:N], func=AF.Identity, scale=cwf(ci, 1), bias=vpp[:, VP_CB + ci:VP_CB + ci + 1]),
                               reads=[tpb, t_vpp], writes=[ta])

                        def stB(ci):
                            pb, tpb = banks[ci]
                            acc, ta = accS[ci % 2], t_accS[ci % 2]
                            h0 = 2 * ci
                            op("dve", lambda e: e.scalar_tensor_tensor(out=acc[:, 1:N], in0=pb[:, 0:N - 1], scalar=cwf(ci, 0), in1=acc[:, 1:N], op0=ALU.mult, op1=ALU.add),
                               reads=[tpb, t_vpp], writes=[ta])
                            op("dve", lambda e: e.scalar_tensor_tensor(out=acc[:, 0:N - 1], in0=pb[:, 1:N], scalar=cwf(ci, 2), in1=acc[:, 0:N - 1], op0=ALU.mult, op1=ALU.add),
                               reads=[tpb, t_vpp], writes=[ta])
                            op("dve", lambda e: e.scalar_tensor_tensor(out=acc[:, 0:1], in0=halo[gb][:, h0:h0 + 1], scalar=cwf(ci, 0), in1=acc[:, 0:1], op0=ALU.mult, op1=ALU.add),
                               reads=[t_halo[gb], t_vpp], writes=[ta])
                            op("dve", lambda e: e.scalar_tensor_tensor(out=acc[:, N - 1:N], in0=halo[gb][:, h0 + 1:h0 + 2], scalar=cwf(ci, 2), in1=acc[:, N - 1:N], op0=ALU.mult, op1=ALU.add),
                               reads=[t_halo[gb], t_vpp], writes=[ta])

                        def stC(ci):
                            acc, ta = accS[ci % 2], t_accS[ci % 2]
                            dstt, tdst = (qcS[gb], t_qcS[gb]) if ci < 4 else (kcS[gb], t_kcS[gb])
                            op("act", lambda e: e.activation(out=dstt[:, ci % 4, 0:N], in_=acc[:, 0:N], func=AF.Silu), reads=[ta], writes=[tdst])

                        stA(0)
                        for ci in range(8):
                            if ci + 1 < 8:
                                stA(ci + 1)
                            stB(ci)
                            stC(ci)
                            yield

                    def gates_prologue(dr, seq):
                        for i, (ti, full, lt) in enumerate(seq):
                            c0 = tile_col(ti)
                            tu = t_uT[ti]
                            pbk, tpbk = ps[4 + i % 4], t_ps[4 + i % 4]
                            Gt_, sm_ = Gt4[i % 4], sm4[i % 4]
                            tG, tS = t_Gt4[i % 4], t_sm4[i % 4]
                            for kc in range(8):
                                op("pe", lambda e: e.matmul(pbk[:, 0:16], lhsT=uT[:, kc, c0:c0 + 128], rhs=wmix[:, kc, 3072:3088], start=(kc == 0), stop=(kc == 7)),
                                   reads=t_wmix + [tu], writes=[tpbk], signal=(kc == 7))
                            op("dve", lambda e: e.tensor_tensor(out=Gt_[:], in0=pbk[:, 0:16], in1=bg[:], op=ALU.add), reads=[tpbk, t_bg], writes=[tG])
                            ig = Gt_[:, 8 * dr:8 * dr + 4]
                            fg = Gt_[:, 8 * dr + 4:8 * dr + 8]
                            op("act", lambda e: e.activation(out=sm_[:, 0:4], in_=fg, func=AF.Exp, scale=-1.0), reads=[tG], writes=[tS])
                            op("act", lambda e: e.activation(out=sm_[:, 4:8], in_=sm_[:, 0:4], func=AF.Ln, bias=one_c, scale=1.0),
                               reads=[tS, t_cv], writes=[tS])
                            op("pe", lambda e: e.matmul(pbk[:, 16:20], lhsT=MI[dr], rhs=sm_[:, 4:8], start=True, stop=True),
                               reads=[t_cst, tS], writes=[tpbk], signal=False)
                            op("pe", lambda e: e.matmul(pbk[:, 20:24], lhsT=ones, rhs=sm_[:, 4:8], start=True, stop=True),
                               reads=[t_cst, tS], writes=[tpbk])
                            op("act", lambda e: e.activation(out=scall[:, i, 0:8], in_=pbk[:, 16:24], func=AF.Exp, scale=-1.0), reads=[tpbk], writes=[t_scall])
                            op("dve", lambda e: e.tensor_tensor(out=sm_[:, 8:12], in0=ig, in1=pbk[:, 16:20], op=ALU.add), reads=[tG, tpbk], writes=[tS])
                            op("dve", lambda e: e.tensor_tensor(out=sm_[:, 12:16], in0=sm_[:, 8:12], in1=pbk[:, 20:24], op=ALU.subtract),
                               reads=[tS, tpbk], writes=[tS])
                            op("act", lambda e: e.activation(out=scall[:, i, 8:16], in_=sm_[:, 8:16], func=AF.Exp, bias=lna_c, scale=1.0),
                               reads=[tS, t_cv], writes=[t_scall])

                    def prep(dr, ti, full, i, gb, off):
                        b = i % 2
                        c0 = tile_col(ti)
                        tu = t_uT[ti]
                        uts = lambda kc: uT[:, kc, c0:c0 + 128]
                        qc_ = lambda h: qcS[gb][:, h, off:off + 128]
                        kc__ = lambda h: kcS[gb][:, h, off:off + 128]
                        s_ = scall[:, i, :]
                        pb, tpb = mbank()
                        pbb = pb[:, 0:256].bitcast(BF16)
                        for h in range(4):
                            op("pe", lambda e: e.transpose(out=pbb[:, h * 128:(h + 1) * 128], in_=kc__(h), identity=identb[:]),
                               reads=[t_kcS[gb], t_identb], writes=[tpb], signal=(h == 3))
                        for h in range(4):
                            op("dve", lambda e: e.tensor_scalar(out=kk[b][:, h * 128:(h + 1) * 128], in0=pbb[:, h * 128:(h + 1) * 128],
                                                                scalar1=s_[:, 12 + h:13 + h], scalar2=None, op0=ALU.mult),
                               reads=[tpb, t_scall], writes=[t_kk[b]])
                        yield
                        for half in range(2):
                            pb, tpb = mbank()
                            for kc in range(8):
                                op("pe", lambda e: e.matmul(pb[:, :], lhsT=uts(kc), rhs=wmix[:, kc, 1024 + half * 512:1536 + half * 512],
                                                            start=(kc == 0), stop=(kc == 7)),
                                   reads=t_wmix + [tu], writes=[tpb], signal=(kc == 7))
                            op("act", lambda e: e.copy(out=v[b][:, half * 512:(half + 1) * 512], in_=pb[:, :]), reads=[tpb], writes=[t_v[b]])
                        yield
                        if full and dr == 1:
                            for half in range(2):
                                pb, tpb = mbank()
                                for kc in range(8):
                                    op("pe", lambda e: e.matmul(pb[:, :], lhsT=uts(kc), rhs=wmix[:, kc, 2048 + half * 512:2560 + half * 512],
                                                                start=(kc == 0), stop=(kc == 7)),
                                       reads=t_wmix + [tu], writes=[tpb], signal=(kc == 7))
                                op("act", lambda e: e.activation(out=Gg[b][:, half * 512:(half + 1) * 512], in_=pb[:, :], func=AF.Tanh, scale=0.5),
                                   reads=[tpb], writes=[t_Gg[b]])
                            op("dve", lambda e: e.scalar_tensor_tensor(out=Gg[b][:], in0=Gg[b][:], scalar=1.0, in1=mnorm[:], op0=ALU.add, op1=ALU.mult),
                               reads=[t_Gg[b], t_mnorm], writes=[t_Gg[b]])

                    def scan(dr, ti, full, i, lt, gb, off):
                        b = i % 2
                        s_ = scall[:, i, :]
                        qc_ = lambda h: qcS[gb][:, h, off:off + 128]
                        kc__ = lambda h: kcS[gb][:, h, off:off + 128]
                        if full:
                            for h in range(4):
                                op("pe", lambda e: e.matmul(ps[0][:, h * 128:(h + 1) * 128], lhsT=kc__(h), rhs=qc_(h), start=True, stop=True),
                                   reads=[t_kcS[gb], t_qcS[gb]], writes=[t_ps[0]], signal=(h == 3))
                        yield
                        if full:
                            for h in range(4):
                                op("dve", lambda e: e.scalar_tensor_tensor(out=wT[:, h, :], in0=ps[0][:, h * 128:(h + 1) * 128], scalar=s_[:, 8 + h:9 + h],
                                                                           in1=MI[dr], op0=ALU.mult, op1=ALU.mult),
                                   reads=[t_ps[0], t_scall, t_cst], writes=[t_wT])
                        yield
                        if full:
                            for h in range(4):
                                pbk = ps[1 + h // 2]; tpbk = t_ps[1 + h // 2]
                                oc = (h % 2) * 256
                                op("pe", lambda e: e.matmul(pbk[:, oc:oc + 256], lhsT=wT[:, h, :], rhs=v[b][:, h * 256:(h + 1) * 256], start=True, stop=False),
                                   reads=[t_wT, t_v[b]], writes=[tpbk], signal=False)
                                op("pe", lambda e: e.matmul(pbk[:, oc:oc + 256], lhsT=qc_(h), rhs=Cb[:, h, :], start=False, stop=True),
                                   reads=[t_qcS[gb], t_Cb], writes=[tpbk], signal=(h % 2 == 1))
                            for h in range(4):
                                op("pe", lambda e: e.matmul(sms[:, 32 + h:33 + h], lhsT=wT[:, h, :], rhs=onesb[:, 0:1], start=True, stop=False),
                                   reads=[t_wT, t_onesb], writes=[t_smden], signal=False)
                                op("pe", lambda e: e.matmul(sms[:, 32 + h:33 + h], lhsT=qc_(h), rhs=nb[:, h:h + 1], start=False, stop=True),
                                   reads=[t_qcS[gb], t_nb], writes=[t_smden], signal=(h == 3))
                        def kvmm(hh):
                            for h in hh:
                                oc = (h % 2) * 256
                                op("pe", lambda e: e.matmul(ps[3][:, oc:oc + 256], lhsT=kk[b][:, h * 128:(h + 1) * 128], rhs=v[b][:, h * 256:(h + 1) * 256],
                                                            start=True, stop=True),
                                   reads=[t_kk[b], t_v[b]], writes=[t_ps[3]], signal=(h % 2 == 1))

                        def cupd(hh):
                            for h in hh:
                                oc = (h % 2) * 256
                                op("dve", lambda e: e.scalar_tensor_tensor(out=C[:, h, :], in0=C[:, h, :], scalar=s_[:, 4 + h:5 + h], in1=ps[3][:, oc:oc + 256],
                                                                           op0=ALU.mult, op1=ALU.add),
                                   reads=[t_ps[3], t_scall], writes=[t_C])
                        kvmm((0, 1))
                        for h in range(4):
                            op("pe", lambda e: e.matmul(sms[:, 40 + h:41 + h], lhsT=kk[b][:, h * 128:(h + 1) * 128], rhs=onesb[:, 0:1], start=True, stop=True),
                               reads=[t_kk[b], t_onesb], writes=[t_smkvn], signal=(h == 3))
                        yield
                        cupd((0, 1))
                        yield
                        kvmm((2, 3))
                        yield
                        cupd((2, 3))
                        op("dve", lambda e: e.tensor_tensor(out=nst[:], in0=nst[:], in1=s_[:, 4:8], op=ALU.mult), reads=[t_scall], writes=[t_n])
                        op("dve", lambda e: e.tensor_tensor(out=nst[:], in0=nst[:], in1=sms[:, 40:44], op=ALU.add), reads=[t_smkvn], writes=[t_n])
                        op("pool", lambda e: e.tensor_copy(out=Cb[:].rearrange("p h l -> p (h l)"), in_=C[:].rearrange("p h l -> p (h l)")),
                           reads=[t_C], writes=[t_Cb])
                        op("pool", lambda e: e.tensor_copy(out=nb[:], in_=nst[:]), reads=[t_n], writes=[t_nb])
                        yield
                        if not full:
                            return
                        op("dve", lambda e: e.tensor_tensor(out=rr[:, 0:4], in0=s_[:, 0:4], in1=sms[:, 32:36], op=ALU.mult),
                           reads=[t_scall, t_smden], writes=[t_rr])
                        op("dve", lambda e: e.tensor_scalar(out=rr[:, 4:8], in0=rr[:, 0:4], scalar1=-1.0, scalar2=None, op0=ALU.mult),
                           reads=[t_rr], writes=[t_rr])
                        op("dve", lambda e: e.tensor_tensor(out=rr[:, 4:8], in0=rr[:, 4:8], in1=rr[:, 0:4], op=ALU.max),
                           reads=[t_rr], writes=[t_rr])
                        op("dve", lambda e: e.tensor_scalar(out=rr[:, 4:8], in0=rr[:, 4:8], scalar1=1.0, scalar2=None, op0=ALU.max),
                           reads=[t_rr], writes=[t_rr])
                        op("dve", lambda e: e.reciprocal(out=rr[:, 8:12], in_=rr[:, 4:8]), reads=[t_rr], writes=[t_rr])
                        op("dve", lambda e: e.tensor_tensor(out=rr[:, 12:16], in0=rr[:, 8:12], in1=s_[:, 0:4], op=ALU.mult),
                           reads=[t_rr, t_scall], writes=[t_rr])
                        if dr == 0:
                            ob = hfl[b]
                            for h in range(4):
                                pbk = ps[1 + h // 2]; tpbk = t_ps[1 + h // 2]
                                oc = (h % 2) * 256
                                op("act", lambda e: e.activation(out=ob[:, h * 256:(h + 1) * 256], in_=pbk[:, oc:oc + 256], func=AF.Copy, scale=rr[:, 12 + h:13 + h]),
                                   reads=[tpbk, t_rr], writes=[t_hfl[b]])
                            k.dma("pool", hf_scr[lt * 128:(lt + 1) * 128, :], ob[:], reads=[t_hfl[b]], writes=[t_hf_scr[lt]])
                        else:
                            for h in range(4):
                                pbk = ps[1 + h // 2]; tpbk = t_ps[1 + h // 2]
                                oc = (h % 2) * 256
                                op("dve", lambda e: e.scalar_tensor_tensor(out=hs[:, h * 256:(h + 1) * 256], in0=pbk[:, oc:oc + 256], scalar=rr[:, 12 + h:13 + h],
                                                                           in1=hfl[b][:, h * 256:(h + 1) * 256], op0=ALU.mult, op1=ALU.add),
                                   reads=[tpbk, t_rr, t_hfl[b]], writes=[t_hs])
                            for h in range(4):
                                op("act", lambda e: e.activation(out=junk[:], in_=hs[:, h * 256:(h + 1) * 256], func=AF.Square, accum_out=ssq[:, h:h + 1]),
                                   reads=[t_hs], writes=[t_junk, t_ssq])
                            op("act", lambda e: e.activation(out=ssq[:, 4:8], in_=ssq[:, 0:4], func=AF.Ln, bias=eps_c, scale=1.0 / 256),
                               reads=[t_ssq, t_cv], writes=[t_ssq])
                            op("act", lambda e: e.activation(out=ssq[:, 4:8], in_=ssq[:, 4:8], func=AF.Exp, scale=-0.5), reads=[t_ssq], writes=[t_ssq])
                            for h in range(4):
                                op("dve", lambda e: e.scalar_tensor_tensor(out=yb[b][:, h * 256:(h + 1) * 256], in0=hs[:, h * 256:(h + 1) * 256],
                                                                           scalar=ssq[:, 4 + h:5 + h], in1=Gg[b][:, h * 256:(h + 1) * 256],
                                                                           op0=ALU.mult, op1=ALU.mult),
                                   reads=[t_hs, t_ssq, t_Gg[b]], writes=[t_yb[b]])
                            for cl in range(2):
                                k.dma("pool", ymv[2 * lt + cl], yb[b][cl * 64:(cl + 1) * 64, :], reads=[t_yb[b]], writes=[t_ym_scr])

                    for dr in range(2):
                        op("dve", lambda e: e.memset(C[:].rearrange("p h l -> p (h l)"), 0.0), writes=[t_C])
                        op("dve", lambda e: e.memset(Cb[:].rearrange("p h l -> p (h l)"), 0.0), writes=[t_Cb])
                        op("dve", lambda e: e.memset(nst[:], 0.0), writes=[t_n])
                        op("dve", lambda e: e.memset(nb[:], 0.0), writes=[t_nb])
                        if dr == 0:
                            seq = [(0, False, None), (1, False, None)] + [(2 + lt, True, lt) for lt in range(NT_LAT)]
                        else:
                            seq = [(1, False, None), (0, False, None)] + [(2 + lt, True, lt) for lt in range(NT_LAT - 1, -1, -1)]
                        groups = [seq[0:2]] + [seq[2 + 4 * g_:6 + 4 * g_] for g_ in range(NT_LAT // 4)]
                        ginfo = []
                        tinfo = []
                        for gi_, grp in enumerate(groups):
                            tiles = [t_[0] for t_ in grp]
                            c0g = min(tile_col(t_) for t_ in tiles)
                            ginfo.append((gi_ % 2, c0g, 128 * len(tiles), tiles))
                            for t_ in tiles:
                                tinfo.append((gi_, gi_ % 2, tile_col(t_) - c0g))
                        PQ = [convq(*gi__) for gi__ in ginfo]
                        gates_prologue(dr, seq)
                        exhaust(PQ[0])
                        exhaust(prep(dr, seq[0][0], seq[0][1], 0, tinfo[0][1], tinfo[0][2]))
                        for i, (ti, full, lt) in enumerate(seq):
                            if dr == 1 and full:
                                k.dma("sp", hfl[i % 2][:], hf_scr[lt * 128:(lt + 1) * 128, :], reads=[t_hf_scr[lt]], writes=[t_hfl[i % 2]])
                            gcur = tinfo[i][0]
                            nxt = PQ[gcur + 1] if gcur + 1 < len(PQ) else iter(())
                            nch = 8 // len(groups[gcur])
                            P = prep(dr, seq[i + 1][0], seq[i + 1][1], i + 1, tinfo[i + 1][1], tinfo[i + 1][2]) if i + 1 < len(seq) else iter(())
                            S_ = scan(dr, ti, full, i, lt, tinfo[i][1], tinfo[i][2])
                            step(S_); step(S_)
                            for _ in range(nch // 2):
                                step(nxt)
                            step(S_); step(S_)
                            for _ in range(nch - nch // 2):
                                step(nxt)
                            step(S_); step(S_)
                            exhaust(P)
                            exhaust(S_)

            k.barrier()
            load_w(wmix, t_wmix, win_d, 0, 3104)
            phase_U(False)
            k.barrier()
            if stage >= 1:
                gla_phase()
                k.barrier()
            if stage >= 2:
                load_w(wmix, t_wmix, win_d, 3104, 6192)
                phase_U(True)
                k.barrier()
                mlstm_phase()
                k.barrier()


        tb = [0]

        def tbank():
            i = tb[0] % 8
            tb[0] += 1
            return ps[i], t_ps[i]

        def load_w2(dst, t_dst, src, c0, c1, nk):
            for kc in range(nk):
                k.dma("pool", dst[:, kc, 0:c1 - c0], src[kc * 128:(kc + 1) * 128, c0:c1], writes=[t_dst])

        def tail_a(prefetch):
            with ExitStack() as st:
                wgt = alloc(st, "wgt", [128, 8, 2048], BF16)
                wbg = alloc(st, "wbg", [128, 8, D], BF16)
                wbm = alloc(st, "wbm", [128, 8, D], BF16)
                wo = alloc(st, "wo", [128, 8, D], BF16)
                xs = [alloc(st, "txs%d" % i, [128, D], F32) for i in range(2)]; t_xs = [Tok(), Tok()]
                xn = alloc(st, "txn", [128, D], F32); t_xn = Tok()
                small = alloc(st, "tsmall", [128, 2], F32); t_small = Tok()
                uTt = alloc(st, "uTt", [128, 8, 128], BF16); t_uTt = Tok()
                ygt = [alloc(st, "ygt%d" % i, [128, D], BF16) for i in range(2)]; t_ygt = [Tok(), Tok()]
                ymt = [alloc(st, "ymt%d" % i, [128, D], BF16) for i in range(2)]; t_ymt = [Tok(), Tok()]
                yTg = alloc(st, "yTg", [128, 8, 128], BF16); t_yTg = Tok()
                yTm = alloc(st, "yTm", [128, 8, 128], BF16); t_yTm = Tok()
                gsig = alloc(st, "gsig", [128, 2048], F32); t_gsig = Tok()
                ysg = alloc(st, "ysg", [128, D], F32); t_ysg = Tok()
                tmpm = alloc(st, "tmpm", [128, D], F32); t_tmpm = Tok()
                junk, t_junk = tmpm, t_tmpm
                ysb = alloc(st, "ysb", [128, D], BF16); t_ysb = Tok()
                ysT = alloc(st, "ysT", [128, 8, 128], BF16); t_ysT = Tok()
                x1t = [alloc(st, "x1t%d" % i, [128, D], F32) for i in range(2)]; t_x1t = [Tok(), Tok()]
                stagings = [(gsig[:, 0:1024], Tok()), (gsig[:, 1024:2048], Tok()), (ysg[:, :], Tok()), (tmpm[:, :], Tok())]
                pieces = []
                t_wgt, t_wbg, t_wbm, t_wo = [], [], [], []
                for kc in range(8):
                    for half in range(2):
                        tk = Tok(); t_wgt.append(tk)
                        pieces.append((wgt[:, kc, half * 1024:(half + 1) * 1024], win_d[kc * 128:(kc + 1) * 128, 6192 + half * 1024:6192 + (half + 1) * 1024], tk))
                for wt_, wd_, tl_ in ((wbg, wbg_d, t_wbg), (wbm, wbm_d, t_wbm), (wo, wo_d, t_wo)):
                    for kc in range(8):
                        tk = Tok(); tl_.append(tk)
                        pieces.append((wt_[:, kc, :], wd_[kc * 128:(kc + 1) * 128, :], tk))
                cengs = ["dve", "act", "pool"]
                for j, (dst, src, tk) in enumerate(pieces):
                    stg, tstg = stagings[j % 4]
                    k.dma("sp", stg, src, writes=[tstg])
                    ce = cengs[j % 3]
                    if ce == "act":
                        op("act", lambda e: e.copy(out=dst, in_=stg), reads=[tstg], writes=[tk])
                    else:
                        op(ce, lambda e: e.tensor_copy(out=dst, in_=stg), reads=[tstg], writes=[tk])
                for real, stgs in ((t_gsig, (stagings[0][1], stagings[1][1])), (t_ysg, (stagings[2][1],)), (t_tmpm, (stagings[3][1],))):
                    mr = {}
                    for ts in stgs:
                        if ts.w is not None:
                            mr[ts.w[0]] = max(mr.get(ts.w[0], 0), ts.w[1])
                        for s_k, v_k in ts.r.items():
                            mr[s_k] = max(mr.get(s_k, 0), v_k)
                    real.w = None
                    real.r = mr

                def tr8(src, t_src, dst, t_dst):
                    pb, tpb = tbank()
                    pbb = pb[:, :].bitcast(BF16)
                    for kc in range(8):
                        op("pe", lambda e: e.transpose(out=pbb[:, kc * 128:(kc + 1) * 128], in_=src[:, kc * 128:(kc + 1) * 128], identity=identb[:]),
                           reads=[t_src, t_identb], writes=[tpb], signal=(kc == 7))
                    op("act", lambda e: e.copy(out=dst[:].rearrange("p a b -> p (a b)"), in_=pbb[:, :]), reads=[tpb], writes=[t_dst])

                for i in range(NT_OWN):
                    b = i % 2
                    k.dma("sp", xs[b][:], x_d[i * 128:(i + 1) * 128, :], writes=[t_xs[b]])
                    k.dma("sp", ygt[b][:], ygla_scr[i * 128:(i + 1) * 128, :], reads=[t_ygla_scr[i]], writes=[t_ygt[b]])
                    k.dma("sp", ymt[b][:], ym_scr[i * 128:(i + 1) * 128, :], reads=[t_ym_scr], writes=[t_ymt[b]])
                    norm_transpose(xs[b][:], t_xs[b], xn, t_xn, small, t_small, junk, t_junk,
                                   lambda kc: scU[:, kc, 0:1], lambda kc: modpp[:, 0, kc, 0:1], [t_scU, t_modpp],
                                   lambda kc: uTt[:, kc, :], t_uTt)
                    for q in range(4):
                        pb, tpb = tbank()
                        for kc in range(8):
                            op("pe", lambda e: e.matmul(pb[:, :], lhsT=uTt[:, kc, :], rhs=wgt[:, kc, q * 512:(q + 1) * 512], start=(kc == 0), stop=(kc == 7)),
                               reads=[t_uTt] + t_wgt, writes=[tpb], signal=(kc == 7))
                        op("act", lambda e: e.activation(out=gsig[:, q * 512:(q + 1) * 512], in_=pb[:, :], func=AF.Sigmoid), reads=[tpb], writes=[t_gsig])
                    tr8(ygt[b], t_ygt[b], yTg, t_yTg)
                    tr8(ymt[b], t_ymt[b], yTm, t_yTm)
                    for half in range(2):
                        pb, tpb = tbank()
                        for kc in range(8):
                            op("pe", lambda e: e.matmul(pb[:, :], lhsT=yTg[:, kc, :], rhs=wbg[:, kc, half * 512:(half + 1) * 512], start=(kc == 0), stop=(kc == 7)),
                               reads=[t_yTg] + t_wbg, writes=[tpb], signal=(kc == 7))
                        op("dve", lambda e: e.tensor_tensor(out=ysg[:, half * 512:(half + 1) * 512], in0=pb[:, :], in1=gsig[:, half * 512:(half + 1) * 512], op=ALU.mult),
                           reads=[tpb, t_gsig], writes=[t_ysg])
                    for half in range(2):
                        pb, tpb = tbank()
                        for kc in range(8):
                            op("pe", lambda e: e.matmul(pb[:, :], lhsT=yTm[:, kc, :], rhs=wbm[:, kc, half * 512:(half + 1) * 512], start=(kc == 0), stop=(kc == 7)),
                               reads=[t_yTm] + t_wbm, writes=[tpb], signal=(kc == 7))
                        op("dve", lambda e: e.tensor_tensor(out=tmpm[:, half * 512:(half + 1) * 512], in0=pb[:, :], in1=gsig[:, 1024 + half * 512:1536 + half * 512], op=ALU.mult),
                           reads=[tpb, t_gsig], writes=[t_tmpm])
                    op("dve", lambda e: e.tensor_tensor(out=ysb[:], in0=ysg[:], in1=tmpm[:], op=ALU.add), reads=[t_ysg, t_tmpm], writes=[t_ysb])
                    tr8(ysb, t_ysb, ysT, t_ysT)
                    for half in range(2):
                        pb, tpb = tbank()
                        for kc in range(8):
                            op("pe", lambda e: e.matmul(pb[:, :], lhsT=ysT[:, kc, :], rhs=wo[:, kc, half * 512:(half + 1) * 512], start=(kc == 0), stop=(kc == 7)),
                               reads=[t_ysT] + t_wo, writes=[tpb], signal=(kc == 7))
                        op("dve", lambda e: e.tensor_tensor(out=x1t[b][:, half * 512:(half + 1) * 512], in0=pb[:, :], in1=G1[:, half * 512:(half + 1) * 512], op=ALU.mult),
                           reads=[tpb, t_G1], writes=[t_x1t[b]])
                    op("dve", lambda e: e.tensor_tensor(out=x1t[b][:], in0=x1t[b][:], in1=xs[b][:], op=ALU.add), reads=[t_xs[b]], writes=[t_x1t[b]])
                    k.dma("pool", x1_scr[i * 128:(i + 1) * 128, :], x1t[b][:], reads=[t_x1t[b]], writes=[t_x1_scr[i]])
                    for _ in range(3):
                        if prefetch:
                            prefetch.pop(0)()

        def tail_b(wfo, t_wfo, wfi0, t_wfi0, prefetch):
            with ExitStack() as st:
                while prefetch:
                    prefetch.pop(0)()
                wfiR = alloc(st, "wfiR", [128, 8, 2 * (DFF - 384)], BF16)

                def wcol(part, fc):
                    if fc < 3:
                        return wfi0, part * 384 + fc * 128
                    return wfiR, part * (DFF - 384) + (fc - 3) * 128
                fgbc = alloc(st, "fgbc", [128, D], F32); t_fgbc = Tok()
                x1t = [alloc(st, "fx1t%d" % i, [128, D], F32) for i in range(4)]; t_x1t = [Tok() for _ in range(4)]
                xn = alloc(st, "fxn", [128, D], F32); t_xn = Tok()
                small = alloc(st, "fsmall", [128, 4], F32); t_small = Tok()
                u2T = alloc(st, "u2T", [128, 8, 512], BF16); t_u2T = Tok()
                sa = [alloc(st, "sa%d" % i, [128, 512], F32) for i in range(1)]; t_sa = [Tok()]
                hT = alloc(st, "hT", [128, 22, 512], BF16); t_hT = Tok()
                x2 = alloc(st, "x2", [128, D], F32); t_x2 = Tok()
                k.dma("sp", fgbc[:], vbc_d[0:1, 2048:3072].to_broadcast([128, 1024]), writes=[t_fgbc])
                t_wfib = [[t_wfi0], [], [], []]
                fcb = [0, 3, 9, 15, 22]
                hTf = hT[:].rearrange("p a b -> p (a b)").bitcast(F32)
                stg_t = [Tok() for _ in range(5)]
                cengs = ["dve", "act", "pool"]
                j = 0
                for blk in range(1, 4):
                    f0, f1 = fcb[blk] * 128, fcb[blk + 1] * 128
                    for part in range(2):
                        for kc in range(8):
                            c_ = part * (DFF - 384) + f0 - 384
                            n_ = f1 - f0
                            stg = hTf[:, (j % 5) * 1024:(j % 5) * 1024 + n_]
                            tstg = stg_t[j % 5]
                            tk = Tok()
                            t_wfib[blk].append(tk)
                            k.dma("sp", stg, wfi_d[kc * 128:(kc + 1) * 128, part * DFF + f0:part * DFF + f1], writes=[tstg])
                            dst = wfiR[:, kc, c_:c_ + n_]
                            ce = cengs[j % 3]
                            if ce == "act":
                                op("act", lambda e: e.copy(out=dst, in_=stg), reads=[tstg], writes=[tk])
                            else:
                                op(ce, lambda e: e.tensor_copy(out=dst, in_=stg), reads=[tstg], writes=[tk])
                            j += 1
                mr = {}
                for ts in stg_t:
                    if ts.w is not None:
                        mr[ts.w[0]] = max(mr.get(ts.w[0], 0), ts.w[1])
                    for s_k, v_k in ts.r.items():
                        mr[s_k] = max(mr.get(s_k, 0), v_k)
                t_hT.w = None
                t_hT.r = mr
                for s_i in range(NT_OWN // 4):
                    for j in range(4):
                        ti = s_i * 4 + j
                        k.dma("sp", x1t[j][:], x1_scr[ti * 128:(ti + 1) * 128, :], reads=[t_x1_scr[ti]], writes=[t_x1t[j]])
                        norm_transpose(x1t[j][:], t_x1t[j], xn, t_xn, small, t_small, x2, t_x2,
                                       lambda kc: sc2[:, kc:kc + 1], lambda kc: modpp[:, 2, kc, 0:1], [t_sc2, t_modpp],
                                       lambda kc: u2T[:, kc, j * 128:(j + 1) * 128], t_u2T)
                    for fc in range(22):
                        pa, tpa = tbank()
                        for kc in range(8):
                            wt_, wc_ = wcol(0, fc)
                            op("pe", lambda e: e.matmul(pa[:, :], lhsT=wt_[:, kc, wc_:wc_ + 128], rhs=u2T[:, kc, :], start=(kc == 0), stop=(kc == 7)),
                               reads=t_wfib[0 if fc < 3 else (1 if fc < 9 else (2 if fc < 15 else 3))] + [t_u2T], writes=[tpa], signal=(kc == 7))
                        pbk, tpbk = tbank()
                        for kc in range(8):
                            wt_, wc_ = wcol(1, fc)
                            op("pe", lambda e: e.matmul(pbk[:, :], lhsT=wt_[:, kc, wc_:wc_ + 128], rhs=u2T[:, kc, :], start=(kc == 0), stop=(kc == 7)),
                               reads=t_wfib[0 if fc < 3 else (1 if fc < 9 else (2 if fc < 15 else 3))] + [t_u2T], writes=[tpbk], signal=(kc == 7))
                        sb_ = 0
                        op("act", lambda e: e.activation(out=sa[sb_][:], in_=pa[:, :], func=AF.Silu), reads=[tpa], writes=[t_sa[sb_]])
                        op("dve", lambda e: e.tensor_tensor(out=hT[:, fc, :], in0=sa[sb_][:], in1=pbk[:, :], op=ALU.mult),
                           reads=[t_sa[sb_], tpbk], writes=[t_hT])
                    for j in range(4):
                        ti = s_i * 4 + j
                        for half in range(2):
                            pb, tpb = tbank()
                            for fc in range(22):
                                op("pe", lambda e: e.matmul(pb[:, :], lhsT=hT[:, fc, j * 128:(j + 1) * 128], rhs=wfo[:, fc, half * 512:(half + 1) * 512],
                                                            start=(fc == 0), stop=(fc == 21)),
                                   reads=[t_hT, t_wfo], writes=[tpb], signal=(fc == 21))
                            op("dve", lambda e: e.tensor_tensor(out=x2[:, half * 512:(half + 1) * 512], in0=pb[:, :], in1=G2[:, half * 512:(half + 1) * 512], op=ALU.mult),
                               reads=[tpb, t_G2], writes=[t_x2])
                        op("dve", lambda e: e.tensor_tensor(out=x2[:], in0=x2[:], in1=x1t[j][:], op=ALU.add), reads=[t_x1t[j]], writes=[t_x2])
                        op("act", lambda e: e.activation(out=xn[:], in_=x2[:], func=AF.Square, accum_out=small[:, 2:3]),
                           reads=[t_x2], writes=[t_xn, t_small])
                        op("act", lambda e: e.activation(out=small[:, 3:4], in_=small[:, 2:3], func=AF.Sqrt, bias=eps_c, scale=1.0 / D),
                           reads=[t_small, t_cv], writes=[t_small])
                        op("dve", lambda e: e.reciprocal(out=small[:, 3:4], in_=small[:, 3:4]), reads=[t_small], writes=[t_small])
                        op("dve", lambda e: e.scalar_tensor_tensor(out=x2[:], in0=x2[:], scalar=small[:, 3:4], in1=fgbc[:], op0=ALU.mult, op1=ALU.mult),
                           reads=[t_small, t_fgbc], writes=[t_x2])
                        k.dma("pool", out_d[ti * 128:(ti + 1) * 128, :], x2[:], reads=[t_x2], writes=[t_out])

        if stage >= 3:
            with ExitStack() as stt:
                wfo = alloc(stt, "wfo", [128, 22, D], BF16); t_wfo = Tok()
                wfi0 = alloc(stt, "wfi0", [128, 8, 2 * 384], BF16); t_wfi0 = Tok()
                prefetch = []
                for part in range(2):
                    for kc in range(8):
                        prefetch.append(lambda part=part, kc=kc: k.dma("pool", wfi0[:, kc, part * 384:(part + 1) * 384],
                                                                      wfi_d[kc * 128:(kc + 1) * 128, part * DFF:part * DFF + 384], writes=[t_wfi0]))
                for fc in range(22):
                    prefetch.append(lambda fc=fc: k.dma("pool", wfo[:, fc, :], wfo_d[fc * 128:(fc + 1) * 128, :], writes=[t_wfo]))
                k.barrier()
                tail_a(prefetch)
                if stage >= 4:
                    k.barrier()
                    tail_b(wfo, t_wfo, wfi0, t_wfi0, prefetch)

        fin = list(t_ygla_scr) + [t_ym_scr, t_out] + t_x1_scr + dbg_list
        k.finish(fin, "sp")
        k.finish(fin, "pool")
        k.check_deadlock()
        print("build: ops", k.nops, "waits", k.nwaits, "cnt", {e: k.cnt[e] for e in k.eng})
    return nc


def _consts():
    c = np.zeros((128, 768), np.float32)
    m = np.arange(128)[:, None]
    l = np.arange(128)[None, :]
    c[:, 0:128] = np.eye(128)
    c[:, 128:256] = (m <= l)
    c[:, 256:384] = (m >= l)
    c[:, 384:512] = (m > l)
    c[:, 512:640] = (m < l)
    c[:, 640:768] = 1.0
    return c


def _variant(inp, flip):
    w_in = inp["w_in"][0]
    w_up = inp["gla_w_up"][0]
    b_dec = inp["gla_b_dec"][0]
    conv_w = inp["mlstm_conv_w"][0]
    b_gate = inp["mlstm_b_gate"][0]
    if flip:
        idx = np.arange(NIN)
        idx[3072:3088] = np.arange(3088, 3104)
        idx[3088:3104] = np.arange(3072, 3088)
        g0 = 6176
        idx[g0:g0 + 8] = np.arange(g0 + 8, g0 + 16)
        idx[g0 + 8:g0 + 16] = np.arange(g0, g0 + 8)
        w_in = w_in[:, idx]
        w_up = w_up[::-1]
        b_dec = b_dec[::-1]
        conv_w = conv_w[::-1]
        b_gate = b_gate[[2, 3, 0, 1]]
    b_ada = inp["b_ada"][0]
    pp = lambda v, n: np.asarray(v).reshape(n, 128).T
    vpp = np.concatenate([pp(b_ada, 48), pp(inp["norm1_g"][0], 8), pp(inp["norm2_g"][0], 8),
                          np.asarray(conv_w).reshape(3, 8, 128).transpose(2, 0, 1).reshape(128, 24),
                          pp(inp["mlstm_conv_b"][0], 8)], axis=1)
    vbc = np.concatenate([b_ada[2048:3072], b_ada[5120:6144], inp["final_g"], inp["gla_norm_g"][0],
                          inp["mlstm_norm_g"][0], np.asarray(b_gate).reshape(16)])[None, :]
    f = lambda a: np.ascontiguousarray(a, dtype=np.float32)
    return dict(w_in=f(w_in), w_up=f(w_up), bdec=f(np.asarray(b_dec).reshape(1, 1024)), vpp=f(vpp), vbc=f(vbc),
                w_ada=f(inp["w_ada"][0]), consts=_consts(), w_br_gla=f(inp["w_br_gla"][0]), w_br_m=f(inp["w_br_mlstm"][0]),
                w_out=f(inp["w_out"][0]), w_ffn_in=f(inp["w_ffn_in"][0]), w_ffn_out=f(inp["w_ffn_out"][0]))


def make_in_maps(inp, cores=None):
    inp = {k_: np.asarray(v) for k_, v in inp.items()}
    var = [_variant(inp, False), _variant(inp, True)]
    maps = []
    for b in range(4):
        for s in range(2):
            m = dict(var[s])
            xb = inp["x"][b]
            cb = inp["ctx"][b]
            if s:
                xb = xb[::-1]
                cb = cb[::-1]
            m["x"] = np.ascontiguousarray(xb, dtype=np.float32)
            m["ctx"] = np.ascontiguousarray(cb, dtype=np.float32)
            cc = np.stack([inp["c"][b], inp["c_ctx"]], -1).reshape(8, 128, 2).transpose(1, 0, 2)
            m["cct"] = np.ascontiguousarray(cc, dtype=np.float32)
            maps.append(m)
    return maps


def kernel(**inputs):
    nc = build(9)
    maps = make_in_maps(inputs)
    res = run_bass_kernel_spmd(nc, maps, core_ids=list(range(8)))
    out = np.zeros((4, T, D), np.float32)
    for b in range(4):
        out[b, 0:2048] = np.asarray(res.results[2 * b]["out"])
        out[b, 2048:] = np.asarray(res.results[2 * b + 1]["out"])[::-1]
    return out
```

```python
import math
from contextlib import ExitStack
import numpy as np
import concourse.bass as bass
import concourse.mybir as mybir
from concourse.bass_utils import run_bass_kernel_spmd

F32 = mybir.dt.float32
BF16 = mybir.dt.bfloat16
AF = mybir.ActivationFunctionType
ALU = mybir.AluOpType

D = 1024
T = 4096
TC = 256
NIN = 8240
DFF = 2816
NT_OWN = 16
NT_LAT = 32
UW = 4356
CTX0 = 1
LAT0 = 259
ALPHA = 128.0 ** -0.5
NV = 96
NB = 5136
C_ID, C_MIF, C_MIB, C_MAF, C_MAB, C_ONE = 0, 128, 256, 384, 512, 640


class Tok:
    __slots__ = ("w", "r", "ex")

    def __init__(self, ex=False):
        self.w = None
        self.r = {}
        self.ex = ex


class K:
    def __init__(self, nc, stack, ndma=8):
        self.nc = nc
        self.eng = {"pe": nc.tensor, "act": nc.scalar, "dve": nc.vector, "pool": nc.gpsimd, "sp": nc.sync}
        self.semh = {}
        self.cnt = {}
        self.waited = {e: {} for e in self.eng}
        for e in self.eng:
            self.semh[e] = stack.enter_context(nc.semaphore("s_" + e))
            self.cnt[e] = 0
        self.dma_slots = {}
        self.dma_rr = {}
        for q in ("sp", "pool"):
            self.dma_slots[q] = []
            for i in range(ndma):
                nm = "d_%s%d" % (q, i)
                self.semh[nm] = stack.enter_context(nc.semaphore(nm))
                self.cnt[nm] = 0
                self.dma_slots[q].append(nm)
            self.dma_rr[q] = 0
        self.nwaits = 0
        self.nops = 0
        self.log = {e: [] for e in self.eng}

    def _wait(self, e, s, v):
        if self.waited[e].get(s, 0) >= v:
            return
        self.eng[e].wait_ge(self.semh[s], v)
        self.waited[e][s] = v
        self.nwaits += 1
        self.log[e].append(("wait", s, v))

    def _deps(self, e, reads, writes):
        deps = {}
        for t in reads:
            if t.w is not None:
                s, v = t.w
                deps[s] = max(deps.get(s, 0), v)
        for t in writes:
            if t.w is not None:
                s, v = t.w
                deps[s] = max(deps.get(s, 0), v)
            for s, v in t.r.items():
                deps[s] = max(deps.get(s, 0), v)
        for s, v in deps.items():
            if e == "pe" and s == "pe":
                continue
            self._wait(e, s, v)

    def _record(self, ticket, reads, writes):
        s, v = ticket
        for t in reads:
            t.r[s] = max(t.r.get(s, 0), v)
        for t in writes:
            t.w = ticket
            t.r = {}

    def op(self, e, fn, reads=(), writes=(), signal=True):
        if e != "pe":
            exr = [t for t in reads if t.ex]
            if exr:
                writes = list(writes) + exr
        self._deps(e, reads, writes)
        ins = fn(self.eng[e])
        self.nops += 1
        if signal:
            self.cnt[e] += 1
            ins.then_inc(self.semh[e], 1)
            ticket = (e, self.cnt[e])
            self.log[e].append(("inc", e, 1, self.nops))
        else:
            assert e == "pe"
            ticket = (e, self.cnt[e] + 1)
        self._record(ticket, reads, writes)
        return ticket

    def dma(self, q, out, in_, reads=(), writes=(), **kw):
        slots = self.dma_slots[q]
        nm = slots[self.dma_rr[q] % len(slots)]
        self.dma_rr[q] += 1
        if self.cnt[nm] > 0:
            self._wait(q, nm, self.cnt[nm])
        self._deps(q, reads, writes)
        ins = self.eng[q].dma_start(out=out, in_=in_, **kw)
        self.cnt[nm] += 16
        ins.then_inc(self.semh[nm], 16)
        self.log[q].append(("inc", nm, 16, self.nops))
        ticket = (nm, self.cnt[nm])
        self._record(ticket, reads, writes)
        self.nops += 1
        return ticket

    def check_deadlock(self):
        sem = {}
        pos = {e: 0 for e in self.eng}
        progress = True
        while progress:
            progress = False
            for e in self.eng:
                lg = self.log[e]
                while pos[e] < len(lg):
                    it = lg[pos[e]]
                    if it[0] == "wait":
                        if sem.get(it[1], 0) >= it[2]:
                            pos[e] += 1
                            progress = True
                        else:
                            break
                    else:
                        sem[it[1]] = sem.get(it[1], 0) + it[2]
                        pos[e] += 1
                        progress = True
        stuck = {e: (pos[e], len(self.log[e]), self.log[e][pos[e]]) for e in self.eng if pos[e] < len(self.log[e])}
        if stuck:
            print("DEADLOCK:", stuck, {k_: v for k_, v in sem.items()})
        else:
            print("deadlock check: OK")
        return not stuck

    def barrier(self):
        snap = dict(self.cnt)
        for e in self.eng:
            for s_, v in snap.items():
                if v > 0:
                    self._wait(e, s_, v)

    def finish(self, toks, e="sp"):
        for t in toks:
            if t.w is not None:
                self._wait(e, t.w[0], t.w[1])


def build(stage=9):
    nc = bass.Bass("TRN2", target_bir_lowering=False)
    di = lambda n, s, dt=F32: nc.dram_tensor(n, s, dt, kind="ExternalInput").ap()
    x_d = di("x", [T, D])
    ctx_d = di("ctx", [TC, D])
    cct_d = di("cct", [128, 8, 2])
    wada_d = di("w_ada", [D, 6 * D])
    win_d = di("w_in", [D, NIN])
    wup_d = di("w_up", [2, 16, 512])
    bdec_d = di("bdec", [1, 1024])
    cst_d = di("consts", [128, 768])
    vpp_d = di("vpp", [128, NV])
    vbc_d = di("vbc", [1, NB])
    wbg_d = di("w_br_gla", [D, D])
    wbm_d = di("w_br_m", [D, D])
    wo_d = di("w_out", [D, D])
    wfi_d = di("w_ffn_in", [D, 2 * DFF])
    wfo_d = di("w_ffn_out", [DFF, D])
    out_d = nc.dram_tensor("out", [NT_OWN * 128, D], F32, kind="ExternalOutput").ap()
    of_scr = nc.dram_tensor("of_scr", [NT_OWN * 128, D], F32, kind="Internal" if stage >= 9 else "ExternalOutput").ap()
    hf_scr = nc.dram_tensor("hf_scr", [T, D], F32, kind="Internal" if stage >= 9 else "ExternalOutput").ap()
    if stage < 9:
        ygla_scr = nc.dram_tensor("ygla", [NT_OWN * 128, D], BF16, kind="ExternalOutput").ap()
        ym_scr = nc.dram_tensor("ym", [T, D], BF16, kind="ExternalOutput").ap()
    else:
        ygla_scr = nc.dram_tensor("ygla", [NT_OWN * 128, D], BF16, kind="Internal").ap()
        ym_scr = nc.dram_tensor("ym", [T, D], BF16, kind="Internal").ap()
    x1_scr = nc.dram_tensor("x1_scr", [NT_OWN * 128, D], F32, kind="Internal" if stage >= 9 else "ExternalOutput").ap()
    t_of_scr = [Tok() for _ in range(NT_OWN)]
    t_hf_scr = [Tok() for _ in range(NT_LAT)]
    t_ygla_scr = [Tok() for _ in range(NT_OWN)]
    t_ym_scr = Tok()
    t_x1_scr = [Tok() for _ in range(NT_OWN)]
    t_out = Tok()

    with ExitStack() as st0:
        k = K(nc, st0)
        op = k.op

        dbg_list = []
        dbg_pool = [st0.enter_context(nc.sbuf_tensor("dbgs_%d" % i, [128, 128], F32)) for i in range(16 if stage < 3 else 0)]

        def dbg(name, ap, toks, n):
            if stage >= 3:
                return
            dd = nc.dram_tensor("dbg_" + name, [128, n], F32, kind="ExternalOutput").ap()
            stg = dbg_pool.pop()[:, 0:n]
            tk = Tok()
            np_ = ap.shape[0]
            op("dve", lambda e: e.memset(stg, 0.0), writes=[tk])
            op("dve", lambda e: e.tensor_copy(out=stg[0:np_, :], in_=ap), reads=toks, writes=[tk])
            td = Tok()
            k.dma("pool", dd[:, :], stg, reads=[tk], writes=[td])
            dbg_list.append(td)

        uniq = [0]

        def alloc(st, name, shape, dt):
            uniq[0] += 1
            return st.enter_context(nc.sbuf_tensor("sb%d_%s" % (uniq[0], name), shape, dt))

        cst = alloc(st0, "cst", [128, 768], F32); t_cst = Tok()
        vpp = alloc(st0, "vpp", [128, NV], F32); t_vpp = Tok()
        identb = alloc(st0, "identb", [128, 128], BF16); t_identb = Tok()
        mask4 = [alloc(st0, "mask4_%d" % d, [128, 4, 128], BF16) for d in range(2)]; t_mask4 = Tok()
        onesb = alloc(st0, "onesb", [128, 1], BF16); t_onesb = Tok()
        cvals = alloc(st0, "cvals", [128, 4], F32); t_cv = Tok()
        modpp = alloc(st0, "modpp", [128, 4, 8, 2], F32); t_modpp = Tok()
        scU = alloc(st0, "scU", [128, 8, 2], F32); t_scU = Tok()
        sc2 = alloc(st0, "sc2", [128, 8], F32); t_sc2 = Tok()
        G1 = alloc(st0, "G1", [128, D], F32); t_G1 = Tok()
        G2 = alloc(st0, "G2", [128, D], F32); t_G2 = Tok()
        ps = [st0.enter_context(nc.psum_tensor("ps%d" % i, [128, 512], F32)) for i in range(8)]
        t_ps = [Tok(ex=True) for _ in range(8)]
        ident = cst[:, C_ID:C_ID + 128]
        MI = [cst[:, C_MIF:C_MIF + 128], cst[:, C_MIB:C_MIB + 128]]
        MA = [cst[:, C_MAF:C_MAF + 128], cst[:, C_MAB:C_MAB + 128]]
        ones = cst[:, C_ONE:C_ONE + 128]
        one_c = cvals[:, 0:1]
        eps_c = cvals[:, 1:2]
        lna_c = cvals[:, 2:3]
        VP_BADA, VP_N1, VP_N2, VP_CW, VP_CB = 0, 48, 56, 64, 88

        k.dma("sp", cst[:], cst_d[:, :], writes=[t_cst])
        k.dma("sp", vpp[:], vpp_d[:, :], writes=[t_vpp])
        op("dve", lambda e: e.memset(cvals[:, 0:1], 1.0), writes=[t_cv])
        op("dve", lambda e: e.memset(cvals[:, 1:2], 1e-6), writes=[t_cv])
        op("dve", lambda e: e.memset(cvals[:, 2:3], math.log(ALPHA)), writes=[t_cv])
        op("dve", lambda e: e.memset(cvals[:, 3:4], 0.0), writes=[t_cv])
        op("dve", lambda e: e.tensor_copy(out=identb[:], in_=ident), reads=[t_cst], writes=[t_identb])
        op("dve", lambda e: e.tensor_copy(out=onesb[:], in_=cst[:, C_ONE:C_ONE + 1]), reads=[t_cst], writes=[t_onesb])
        for d in range(2):
            for h in range(4):
                op("dve", lambda e: e.tensor_copy(out=mask4[d][:, h, :], in_=MI[d]), reads=[t_cst], writes=[t_mask4])

        prep_banks = [5, 6, 7]
        prr = [0]

        def pbank():
            i = prep_banks[prr[0] % len(prep_banks)]
            prr[0] += 1
            return ps[i], t_ps[i]

        with ExitStack() as st:
            scc = alloc(st, "scc", [128, 8, 2], F32); t_scc = Tok()
            wa = [alloc(st, "wa%d" % i, [128, 8, D], F32) for i in range(2)]; t_wa = [[Tok() for _ in range(8)] for _ in range(2)]
            bbc = alloc(st, "bbc", [128, D], F32); t_bbc = Tok()
            k.dma("sp", scc[:], cct_d[:, :, :], writes=[t_scc])
            op("act", lambda e: e.activation(out=scc[:], in_=scc[:], func=AF.Silu), reads=[t_scc], writes=[t_scc])
            psA, t_psA = ps[0], t_ps[0]
            mrow = alloc(st, "mrow", [2, D], F32); t_mrow = Tok()
            gi = 0
            for g in range(6):
                w_, tw_ = wa[g % 2], t_wa[g % 2]
                for kc in range(8):
                    k.dma("sp", w_[:, kc, :], wada_d[kc * 128:(kc + 1) * 128, g * D:(g + 1) * D], writes=[tw_[kc]])
                for half in range(2):
                    pb, tpb = pbank()
                    for kc in range(8):
                        op("pe", lambda e: e.matmul(pb[0:2, :], lhsT=scc[:, kc, :], rhs=w_[:, kc, half * 512:(half + 1) * 512],
                                                    start=(kc == 0), stop=(kc == 7)),
                           reads=[tw_[kc], t_scc], writes=[tpb], signal=(kc == 7))
                    op("act", lambda e: e.copy(out=mrow[:, half * 512:(half + 1) * 512], in_=pb[0:2, :]), reads=[tpb], writes=[t_mrow])
                if g in (0, 1, 3, 4):
                    for j in range(8):
                        c0 = (gi * 8 + j) * 2
                        op("pe", lambda e: e.transpose(out=psA[:, c0:c0 + 2], in_=mrow[0:2, j * 128:(j + 1) * 128], identity=cst[0:2, C_ID:C_ID + 2]),
                           reads=[t_mrow, t_cst], writes=[t_psA], signal=(j == 7))
                    gi += 1
                else:
                    Gt, tG = (G1, t_G1) if g == 2 else (G2, t_G2)
                    voff = 0 if g == 2 else 1024
                    k.dma("sp", bbc[:], vbc_d[0:1, voff:voff + 1024].to_broadcast([128, 1024]), writes=[t_bbc])
                    for half in range(2):
                        pb, tpb = pbank()
                        op("pe", lambda e: e.matmul(pb[:, :], lhsT=cst[0:1, C_ONE:C_ONE + 128], rhs=mrow[0:1, half * 512:(half + 1) * 512],
                                                    start=True, stop=True),
                           reads=[t_mrow, t_cst], writes=[tpb])
                        op("dve", lambda e: e.tensor_tensor(out=Gt[:, half * 512:(half + 1) * 512], in0=pb[:, :],
                                                            in1=bbc[:, half * 512:(half + 1) * 512], op=ALU.add),
                           reads=[tpb, t_bbc], writes=[tG])
            psAv = psA[:, 0:64].rearrange("p (g j s) -> p g j s", g=4, j=8, s=2)
            for gi_, g in enumerate((0, 1, 3, 4)):
                for s in range(2):
                    op("dve", lambda e: e.tensor_tensor(out=modpp[:, gi_, :, s], in0=psAv[:, gi_, :, s],
                                                        in1=vpp[:, VP_BADA + g * 8:VP_BADA + (g + 1) * 8], op=ALU.add),
                       reads=[t_psA, t_vpp], writes=[t_modpp])
            for s in range(2):
                op("dve", lambda e: e.scalar_tensor_tensor(out=scU[:, :, s], in0=modpp[:, 1, :, s], scalar=1.0,
                                                           in1=vpp[:, VP_N1:VP_N1 + 8], op0=ALU.add, op1=ALU.mult),
                   reads=[t_modpp, t_vpp], writes=[t_scU])
            op("dve", lambda e: e.scalar_tensor_tensor(out=sc2[:, :], in0=modpp[:, 3, :, 0], scalar=1.0,
                                                       in1=vpp[:, VP_N2:VP_N2 + 8], op0=ALU.add, op1=ALU.mult),
               reads=[t_modpp, t_vpp], writes=[t_sc2])


        def norm_transpose(xs, t_xs, xn, t_xn, small, t_small, junk, t_junk, scale_ap, bias_ap, tsb, dst, t_dst):
            norm_part(xs, t_xs, xn, t_xn, small, t_small, junk, t_junk)
            transpose_part(xn, t_xn, scale_ap, bias_ap, tsb, dst, t_dst)

        def norm_part(xs, t_xs, xn, t_xn, small, t_small, junk, t_junk):
            txl = t_xs if isinstance(t_xs, list) else [t_xs]
            op("act", lambda e: e.activation(out=junk[:], in_=xs, func=AF.Square, accum_out=small[:, 0:1]),
               reads=txl, writes=[t_junk, t_small])
            op("act", lambda e: e.activation(out=small[:, 1:2], in_=small[:, 0:1], func=AF.Sqrt, bias=eps_c, scale=1.0 / D),
               reads=[t_small, t_cv], writes=[t_small])
            op("dve", lambda e: e.reciprocal(out=small[:, 1:2], in_=small[:, 1:2]), reads=[t_small], writes=[t_small])
            op("dve", lambda e: e.tensor_scalar(out=xn[:], in0=xs, scalar1=small[:, 1:2], scalar2=None, op0=ALU.mult),
               reads=txl + [t_small], writes=[t_xn])

        def transpose_part(xn, t_xn, scale_ap, bias_ap, tsb, dst, t_dst):
            for half in range(2):
                pb, tpb = pbank()
                for j in range(4):
                    kc = half * 4 + j
                    op("pe", lambda e: e.transpose(out=pb[:, j * 128:(j + 1) * 128], in_=xn[:, kc * 128:(kc + 1) * 128], identity=ident),
                       reads=[t_xn, t_cst], writes=[tpb], signal=(j == 3))
                for j in range(4):
                    kc = half * 4 + j
                    if j % 2 == 0:
                        op("act", lambda e: e.activation(out=dst(kc), in_=pb[:, j * 128:(j + 1) * 128], func=AF.Identity,
                                                         scale=scale_ap(kc), bias=bias_ap(kc)),
                           reads=[tpb] + tsb, writes=[t_dst])
                    else:
                        op("dve", lambda e: e.tensor_scalar(out=dst(kc), in0=pb[:, j * 128:(j + 1) * 128], scalar1=scale_ap(kc),
                                                            scalar2=bias_ap(kc), op0=ALU.mult, op1=ALU.add),
                           reads=[tpb] + tsb, writes=[t_dst])

        with ExitStack() as stm:
            uT = alloc(stm, "uT", [128, 8, UW], BF16)
            t_uT = [Tok() for _ in range(34)]
            t_guard = Tok()
            wmix = alloc(stm, "wmix", [128, 8, 3104], BF16); t_wmix = [Tok() for _ in range(8)]
            for c in (0, 257, 258, UW - 1):
                op("dve", lambda e: e.memset(uT[:, :, c:c + 1], 0.0), writes=[t_guard])

            def tile_col(ti):
                return CTX0 + ti * 128 if ti < 2 else LAT0 + (ti - 2) * 128

            def phase_U(colmajor):
                with ExitStack() as st:
                    xs = [alloc(st, "xs%d" % i, [128, D], F32) for i in range(2)]; t_xs = [Tok(), Tok()]
                    xn = [alloc(st, "xn%d" % i, [128, D], F32) for i in range(2)]; t_xn = [Tok(), Tok()]
                    junk = alloc(st, "junkU", [128, D], F32); t_junk = Tok()
                    small = [alloc(st, "smallU%d" % i, [128, 2], F32) for i in range(2)]; t_small = [Tok(), Tok()]
                    xv = x_d.rearrange("(r c) d -> c r d", c=64)
                    t_xs2 = [[Tok(), Tok()], [Tok(), Tok()]]

                    def partA(ti):
                        b = ti % 2
                        if ti < 2:
                            k.dma("sp", xs[b][:], ctx_d[ti * 128:(ti + 1) * 128, :], writes=[t_xs[b]])
                        else:
                            lt = ti - 2
                            if colmajor:
                                for cl in range(2):
                                    k.dma("sp", xs[b][cl * 64:(cl + 1) * 64, :], xv[2 * lt + cl], writes=[t_xs2[cl][b]])
                            else:
                                k.dma("sp", xs[b][:], x_d[lt * 128:(lt + 1) * 128, :], writes=[t_xs[b]])
                        norm_part(xs[b][:], [t_xs[b], t_xs2[0][b], t_xs2[1][b]], xn[b], t_xn[b], small[b], t_small[b], junk, t_junk)

                    def partB(ti):
                        b = ti % 2
                        s = 1 if ti < 2 else 0
                        c0 = tile_col(ti)
                        transpose_part(xn[b], t_xn[b], lambda kc: scU[:, kc, s:s + 1], lambda kc: modpp[:, 0, kc, s:s + 1], [t_scU, t_modpp],
                                       lambda kc: uT[:, kc, c0:c0 + 128], t_uT[ti])

                    partA(0)
                    for ti in range(34):
                        if ti + 1 < 34:
                            partA(ti + 1)
                        partB(ti)

            def step(g):
                try:
                    next(g)
                except StopIteration:
                    pass

            def exhaust(g):
                for _ in g:
                    pass

            def load_w(dst, t_dst, src, c0, c1, nk=8, q="pool"):
                for kc in range(nk):
                    k.dma(q, dst[:, kc, 0:c1 - c0], src[kc * 128:(kc + 1) * 128, c0:c1], writes=[t_dst[kc]])

            def gla_phase():
                print("gla start cnt", dict(k.cnt))
                with ExitStack() as st:
                    wup = alloc(st, "wup", [16, 2, 512], F32); t_wup = Tok()
                    bdec = alloc(st, "bdec", [1, 1024], F32); t_bdec = Tok()
                    gnorm = alloc(st, "gnorm", [128, D], F32); t_gnorm = Tok()
                    lrT = alloc(st, "lrT", [16, 128], F32); t_lrT = Tok()
                    tmp = alloc(st, "gtmp", [128, 512], F32); t_tmp = Tok()
                    sp = alloc(st, "gsp", [128, 512], F32); t_sp = Tok()
                    dkk = alloc(st, "dkk", [128, 512], F32); t_dkk = Tok()
                    Ep = alloc(st, "Ep", [128, 512], F32); t_Ep = Tok()
                    Em = alloc(st, "Em", [128, 512], F32); t_Em = Tok()
                    kk = [alloc(st, "kk%d" % i, [128, 512], BF16) for i in range(2)]; t_kk = [Tok(), Tok()]
                    v = [alloc(st, "v%d" % i, [128, D], BF16) for i in range(2)]; t_v = [Tok(), Tok()]
                    Gg = [alloc(st, "Gg%d" % i, [128, D], F32) for i in range(2)]; t_Gg = [Tok(), Tok()]
                    qd = [alloc(st, "qd%d" % i, [128, 4, 128], BF16) for i in range(2)]; t_qd = [Tok(), Tok()]
                    kd = [alloc(st, "kd%d" % i, [128, 4, 128], BF16) for i in range(2)]; t_kd = [Tok(), Tok()]
                    dec = [alloc(st, "dec%d" % i, [128, 4], F32) for i in range(2)]; t_dec = [Tok(), Tok()]
                    wT = alloc(st, "wT", [128, 4, 128], BF16); t_wT = Tok()
                    S = alloc(st, "S", [128, 4, 256], F32); t_S = Tok()
                    Sb = alloc(st, "Sb", [128, 4, 256], BF16); t_Sb = Tok()
                    ofl = [alloc(st, "ofl%d" % i, [128, D], F32) for i in range(2)]; t_ofl = [Tok(), Tok()]
                    osum = alloc(st, "osum", [128, D], F32); t_osum = Tok()
                    junk = alloc(st, "junkG", [128, 256], F32); t_junk = Tok()
                    ssq = alloc(st, "ssqG", [128, 8], F32); t_ssq = Tok()
                    yb = [alloc(st, "yb%d" % i, [128, D], BF16) for i in range(2)]; t_yb = [Tok(), Tok()]

                    for d_ in range(2):
                        k.dma("sp", wup[:, d_, :], wup_d[d_], writes=[t_wup])
                    k.dma("sp", bdec[:], bdec_d[:, :], writes=[t_bdec])
                    k.dma("sp", gnorm[:], vbc_d[0:1, 3072:4096].to_broadcast([128, 1024]), writes=[t_gnorm])

                    def prep(dr, ti, full, i):
                        b = i % 2
                        c0 = tile_col(ti)
                        tu = t_uT[ti]
                        uts = lambda kc: uT[:, kc, c0:c0 + 128]
                        pb, tpb = pbank()
                        for kc in range(8):
                            op("pe", lambda e: e.matmul(pb[0:16, 0:128], lhsT=wmix[:, kc, 3072 + 16 * dr:3088 + 16 * dr], rhs=uts(kc),
                                                        start=(kc == 0), stop=(kc == 7)),
                               reads=t_wmix + [tu], writes=[tpb], signal=(kc == 7))
                        op("act", lambda e: e.copy(out=lrT[:], in_=pb[0:16, 0:128]), reads=[tpb], writes=[t_lrT])
                        pb, tpb = pbank()
                        op("pe", lambda e: e.matmul(pb[:, :], lhsT=lrT[0:16, :], rhs=wup[0:16, dr, :], start=True, stop=False),
                           reads=[t_lrT, t_wup], writes=[tpb], signal=False)
                        op("pe", lambda e: e.matmul(pb[:, :], lhsT=cst[0:1, C_ONE:C_ONE + 128], rhs=bdec[0:1, dr * 512:(dr + 1) * 512],
                                                    start=False, stop=True),
                           reads=[t_cst, t_bdec], writes=[tpb])
                        op("act", lambda e: e.activation(out=tmp[:], in_=pb[:, :], func=AF.Exp, scale=-1.0), reads=[tpb], writes=[t_tmp])
                        op("act", lambda e: e.activation(out=sp[:], in_=tmp[:], func=AF.Ln, bias=one_c, scale=1.0),
                           reads=[t_tmp, t_cv], writes=[t_sp])
                        yield
                        pb, tpb = pbank()
                        op("pe", lambda e: e.matmul(pb[:, :], lhsT=MA[dr], rhs=sp[:], start=True, stop=True),
                           reads=[t_cst, t_sp], writes=[tpb])
                        op("act", lambda e: e.activation(out=dkk[:], in_=pb[:, :], func=AF.Exp, scale=-1.0 / 16), reads=[tpb], writes=[t_dkk])
                        pb, tpb = pbank()
                        for kc in range(8):
                            op("pe", lambda e: e.matmul(pb[:, :], lhsT=uts(kc), rhs=wmix[:, kc, 512:1024], start=(kc == 0), stop=(kc == 7)),
                               reads=t_wmix + [tu], writes=[tpb], signal=(kc == 7))
                        op("dve", lambda e: e.tensor_tensor(out=kk[b][:], in0=pb[:, :], in1=dkk[:], op=ALU.mult),
                           reads=[tpb, t_dkk], writes=[t_kk[b]])
                        yield
                        for half in range(2):
                            pb, tpb = pbank()
                            for kc in range(8):
                                op("pe", lambda e: e.matmul(pb[:, :], lhsT=uts(kc), rhs=wmix[:, kc, 1024 + half * 512:1536 + half * 512],
                                                            start=(kc == 0), stop=(kc == 7)),
                                   reads=t_wmix + [tu], writes=[tpb], signal=(kc == 7))
                            if half == 0:
                                op("act", lambda e: e.copy(out=v[b][:, 0:512], in_=pb[:, :]), reads=[tpb], writes=[t_v[b]])
                            else:
                                op("dve", lambda e: e.tensor_copy(out=v[b][:, 512:1024], in_=pb[:, :]), reads=[tpb], writes=[t_v[b]])
                        yield
                        if full and dr == 1:
                            for half in range(2):
                                pb, tpb = pbank()
                                for kc in range(8):
                                    op("pe", lambda e: e.matmul(pb[:, :], lhsT=uts(kc), rhs=wmix[:, kc, 2048 + half * 512:2560 + half * 512],
                                                                start=(kc == 0), stop=(kc == 7)),
                                       reads=t_wmix + [tu], writes=[tpb], signal=(kc == 7))
                                op("act", lambda e: e.activation(out=Gg[b][:, half * 512:(half + 1) * 512], in_=pb[:, :], func=AF.Silu),
                                   reads=[tpb], writes=[t_Gg[b]])
                            op("dve", lambda e: e.tensor_tensor(out=Gg[b][:], in0=Gg[b][:], in1=gnorm[:], op=ALU.mult),
                               reads=[t_Gg[b], t_gnorm], writes=[t_Gg[b]])
                        yield
                        pb, tpb = pbank()
                        for h in range(4):
                            op("pe", lambda e: e.matmul(pb[:, h * 128:(h + 1) * 128], lhsT=sp[:, h * 128:(h + 1) * 128], rhs=MI[dr],
                                                        start=True, stop=True),
                               reads=[t_sp, t_cst], writes=[tpb], signal=(h == 3))
                        op("act", lambda e: e.activation(out=Ep[:], in_=pb[:, :], func=AF.Exp, scale=-1.0 / 16), reads=[tpb], writes=[t_Ep])
                        if full:
                            op("act", lambda e: e.activation(out=Em[:], in_=pb[:, :], func=AF.Exp, scale=1.0 / 16), reads=[tpb], writes=[t_Em])
                        lastc = 127 if dr == 0 else 0
                        Epv = Ep[:].rearrange("p (h l) -> p h l", h=4)
                        op("dve", lambda e: e.tensor_copy(out=dec[b][:], in_=Epv[:, :, lastc]), reads=[t_Ep], writes=[t_dec[b]])
                        yield
                        if full:
                            for which in range(2):
                                if which == 1:
                                    yield
                                pb, tpb = pbank()
                                for h in range(4):
                                    for kc in range(8):
                                        cc = which * 512 + h * 128
                                        op("pe", lambda e: e.matmul(pb[:, h * 128:(h + 1) * 128], lhsT=wmix[:, kc, cc:cc + 128], rhs=uts(kc),
                                                                    start=(kc == 0), stop=(kc == 7)),
                                           reads=t_wmix + [tu], writes=[tpb], signal=(kc == 7 and h == 3))
                                if which == 0:
                                    op("dve", lambda e: e.scalar_tensor_tensor(out=qd[b][:].rearrange("p h l -> p (h l)"), in0=pb[:, :], scalar=ALPHA,
                                                                               in1=Ep[:], op0=ALU.mult, op1=ALU.mult),
                                       reads=[tpb, t_Ep], writes=[t_qd[b]])
                                else:
                                    op("dve", lambda e: e.tensor_tensor(out=kd[b][:].rearrange("p h l -> p (h l)"), in0=pb[:, :], in1=Em[:], op=ALU.mult),
                                       reads=[tpb, t_Em], writes=[t_kd[b]])

                    def scan(dr, ti, full, i, own):
                        b = i % 2
                        if full:
                            for h in range(4):
                                op("pe", lambda e: e.matmul(ps[0][:, h * 128:(h + 1) * 128], lhsT=kd[b][:, h, :], rhs=qd[b][:, h, :], start=True, stop=True),
                                   reads=[t_kd[b], t_qd[b]], writes=[t_ps[0]], signal=(h == 3))
                        yield
                        if full:
                            op("dve", lambda e: e.tensor_tensor(out=wT[:].rearrange("p h l -> p (h l)"), in0=ps[0][:, :],
                                                                in1=mask4[dr][:].rearrange("p h l -> p (h l)"), op=ALU.mult),
                               reads=[t_ps[0], t_mask4], writes=[t_wT])
                        yield
                        if full:
                            for h in range(4):
                                pbk = ps[1 + h // 2]; tpbk = t_ps[1 + h // 2]
                                oc = (h % 2) * 256
                                op("pe", lambda e: e.matmul(pbk[:, oc:oc + 256], lhsT=wT[:, h, :], rhs=v[b][:, h * 256:(h + 1) * 256], start=True, stop=False),
                                   reads=[t_wT, t_v[b]], writes=[tpbk], signal=False)
                                op("pe", lambda e: e.matmul(pbk[:, oc:oc + 256], lhsT=qd[b][:, h, :], rhs=Sb[:, h, :], start=False, stop=True),
                                   reads=[t_qd[b], t_Sb], writes=[tpbk], signal=(h % 2 == 1))
                        for h in range(4):
                            pbk = ps[3 + h // 2]; tpbk = t_ps[3 + h // 2]
                            oc = (h % 2) * 256
                            op("pe", lambda e: e.matmul(pbk[:, oc:oc + 256], lhsT=kk[b][:, h * 128:(h + 1) * 128], rhs=v[b][:, h * 256:(h + 1) * 256],
                                                        start=True, stop=True),
                               reads=[t_kk[b], t_v[b]], writes=[tpbk], signal=(h % 2 == 1))
                        yield
                        for h in range(4):
                            pbk = ps[3 + h // 2]; tpbk = t_ps[3 + h // 2]
                            oc = (h % 2) * 256
                            op("dve", lambda e: e.scalar_tensor_tensor(out=S[:, h, :], in0=S[:, h, :], scalar=dec[b][:, h:h + 1], in1=pbk[:, oc:oc + 256],
                                                                       op0=ALU.mult, op1=ALU.add),
                               reads=[tpbk, t_dec[b]], writes=[t_S])
                        op("pool", lambda e: e.tensor_copy(out=Sb[:].rearrange("p h l -> p (h l)"), in_=S[:].rearrange("p h l -> p (h l)")),
                           reads=[t_S], writes=[t_Sb])
                        yield
                        if not full:
                            return
                        if dr == 0:
                            ob = ofl[b]
                            for half in range(2):
                                op("act", lambda e: e.copy(out=ob[:, half * 512:(half + 1) * 512], in_=ps[1 + half][:, :]),
                                   reads=[t_ps[1 + half]], writes=[t_ofl[b]])
                            k.dma("pool", of_scr[own * 128:(own + 1) * 128, :], ob[:], reads=[t_ofl[b]], writes=[t_of_scr[own]])

                        else:
                            for half in range(2):
                                op("dve", lambda e: e.tensor_tensor(out=osum[:, half * 512:(half + 1) * 512], in0=ps[1 + half][:, :],
                                                                    in1=ofl[b][:, half * 512:(half + 1) * 512], op=ALU.add),
                                   reads=[t_ps[1 + half], t_ofl[b]], writes=[t_osum])
                            for h in range(4):
                                op("act", lambda e: e.activation(out=junk[:], in_=osum[:, h * 256:(h + 1) * 256], func=AF.Square, accum_out=ssq[:, h:h + 1]),
                                   reads=[t_osum], writes=[t_junk, t_ssq])
                            op("act", lambda e: e.activation(out=ssq[:, 4:8], in_=ssq[:, 0:4], func=AF.Ln, bias=eps_c, scale=1.0 / 256),
                               reads=[t_ssq, t_cv], writes=[t_ssq])
                            op("act", lambda e: e.activation(out=ssq[:, 4:8], in_=ssq[:, 4:8], func=AF.Exp, scale=-0.5), reads=[t_ssq], writes=[t_ssq])
                            for h in range(4):
                                op("dve", lambda e: e.scalar_tensor_tensor(out=yb[b][:, h * 256:(h + 1) * 256], in0=osum[:, h * 256:(h + 1) * 256],
                                                                           scalar=ssq[:, 4 + h:5 + h], in1=Gg[b][:, h * 256:(h + 1) * 256],
                                                                           op0=ALU.mult, op1=ALU.mult),
                                   reads=[t_osum, t_ssq, t_Gg[b]], writes=[t_yb[b]])
                            k.dma("pool", ygla_scr[own * 128:(own + 1) * 128, :], yb[b][:], reads=[t_yb[b]], writes=[t_ygla_scr[own]])

                    for dr in range(2):
                        op("dve", lambda e: e.memset(S[:].rearrange("p h l -> p (h l)"), 0.0), writes=[t_S])
                        op("dve", lambda e: e.memset(Sb[:].rearrange("p h l -> p (h l)"), 0.0), writes=[t_Sb])
                        if dr == 0:
                            seq = [(0, False, None), (1, False, None)] + [(2 + lt, True, lt) for lt in range(NT_OWN)]
                        else:
                            seq = [(1, False, None), (0, False, None)] + [(2 + lt, lt < NT_OWN, lt if lt < NT_OWN else None)
                                                                          for lt in range(NT_LAT - 1, -1, -1)]
                        exhaust(prep(dr, seq[0][0], seq[0][1], 0))
                        for i, (ti, full, own) in enumerate(seq):
                            if dr == 1 and full:
                                k.dma("sp", ofl[i % 2][:], of_scr[own * 128:(own + 1) * 128, :], reads=[t_of_scr[own]], writes=[t_ofl[i % 2]])
                            P = prep(dr, seq[i + 1][0], seq[i + 1][1], i + 1) if i + 1 < len(seq) else iter(())
                            S_ = scan(dr, ti, full, i, own)
                            step(S_); step(S_)
                            step(P); step(P)
                            step(S_); step(S_)
                            step(P); step(P); step(P)
                            exhaust(P)
                            exhaust(S_)

            def mlstm_phase():
                print("mlstm start cnt", dict(k.cnt))
                with ExitStack() as st:
                    bg = alloc(st, "bg", [128, 16], F32); t_bg = Tok()
                    mnorm = alloc(st, "mnorm", [128, D], F32); t_mnorm = Tok()
                    Gt4 = [alloc(st, "Gt%d" % i, [128, 16], F32) for i in range(4)]; t_Gt4 = [Tok() for _ in range(4)]
                    sm4 = [alloc(st, "sm%d" % i, [128, 16], F32) for i in range(4)]; t_sm4 = [Tok() for _ in range(4)]
                    scall = alloc(st, "scall", [128, 34, 16], F32); t_scall = Tok()
                    accS = [alloc(st, "accS%d" % i, [128, 512], F32) for i in range(2)]; t_accS = [Tok(), Tok()]
                    halo = [alloc(st, "halo%d" % i, [128, 16], F32) for i in range(2)]; t_halo = [Tok(), Tok()]
                    qcS = [alloc(st, "qcS%d" % i, [128, 4, 512], BF16) for i in range(2)]; t_qcS = [Tok(), Tok()]
                    kcS = [alloc(st, "kcS%d" % i, [128, 4, 512], BF16) for i in range(2)]; t_kcS = [Tok(), Tok()]
                    kk = [alloc(st, "mkk%d" % i, [128, 512], BF16) for i in range(2)]; t_kk = [Tok(), Tok()]
                    v = [alloc(st, "mv%d" % i, [128, D], BF16) for i in range(2)]; t_v = [Tok(), Tok()]
                    Gg = [alloc(st, "mGg%d" % i, [128, D], F32) for i in range(2)]; t_Gg = [Tok(), Tok()]
                    wT = alloc(st, "mwT", [128, 4, 128], BF16); t_wT = Tok()
                    C = alloc(st, "C", [128, 4, 256], F32); t_C = Tok()
                    Cb = alloc(st, "Cb", [128, 4, 256], BF16); t_Cb = Tok()
                    nst = alloc(st, "nst", [128, 4], F32); t_n = Tok()
                    nb = alloc(st, "nb", [128, 4], BF16); t_nb = Tok()
                    rr = alloc(st, "rr", [128, 16], F32); t_rr = Tok()
                    hfl = [alloc(st, "hfl%d" % i, [128, D], F32) for i in range(2)]; t_hfl = [Tok(), Tok()]
                    hs = alloc(st, "hs", [128, D], F32); t_hs = Tok()
                    junk = alloc(st, "junkM", [128, 256], F32); t_junk = Tok()
                    ssq = alloc(st, "ssqM", [128, 8], F32); t_ssq = Tok()
                    yb = [alloc(st, "myb%d" % i, [128, D], BF16) for i in range(2)]; t_yb = [Tok(), Tok()]
                    smb = ps[5]
                    t_smg = t_smb = t_ps[5]
                    sms = ps[4]
                    t_smden = t_smkvn = t_ps[4]
                    mprep = [5, 0]
                    mrr = [0]

                    def mbank():
                        i = mprep[mrr[0] % 2]
                        mrr[0] += 1
                        return ps[i], t_ps[i]

                    k.dma("sp", bg[:], vbc_d[0:1, 5120:5136].to_broadcast([128, 16]), writes=[t_bg])
                    k.dma("sp", mnorm[:], vbc_d[0:1, 4096:5120].to_broadcast([128, 1024]), writes=[t_mnorm])
                    op("dve", lambda e: e.tensor_scalar(out=mnorm[:], in0=mnorm[:], scalar1=0.5, scalar2=None, op0=ALU.mult), writes=[t_mnorm])
                    ymv = ym_scr.rearrange("(r c) d -> c r d", c=64)

                    def convq(gb, c0, N, tiles):
                        rd = t_wmix + [t_guard] + [t_uT[t_] for t_ in tiles]
                        lo, hi = min(tiles), max(tiles)
                        if lo not in (0, 2):
                            rd.append(t_uT[lo - 1])
                        if hi not in (1, 33):
                            rd.append(t_uT[hi + 1])
                        for ci in range(8):
                            h0 = 48 + 2 * ci
                            for kc in range(8):
                                op("pe", lambda e: e.matmul(sms[:, h0:h0 + 2], lhsT=wmix[:, kc, ci * 128:(ci + 1) * 128], rhs=uT[:, kc, c0 - 1:c0 + N + 1:N + 1],
                                                            start=(kc == 0), stop=(kc == 7)),
                                   reads=rd, writes=[t_ps[4]], signal=(kc == 7 and ci == 7))
                        op("dve", lambda e: e.tensor_copy(out=halo[gb][:], in_=sms[:, 48:64]), reads=[t_ps[4]], writes=[t_halo[gb]])
                        cwf = lambda ci, tap: vpp[:, VP_CW + tap * 8 + ci:VP_CW + tap * 8 + ci + 1]
                        banks = {}

                        def stA(ci):
                            pb, tpb = ps[6 + ci % 2], t_ps[6 + ci % 2]
                            banks[ci] = (pb, tpb)
                            for kc in range(8):
                                op("pe", lambda e: e.matmul(pb[:, 0:N], lhsT=wmix[:, kc, ci * 128:(ci + 1) * 128], rhs=uT[:, kc, c0:c0 + N],
                                                            start=(kc == 0), stop=(kc == 7)),
                                   reads=rd, writes=[tpb], signal=(kc == 7))
                            acc, ta = accS[ci % 2], t_accS[ci % 2]
                            op("act", lambda e: e.activation(out=acc[:, 0:N], in_=pb[:, 0:N], func=AF.Identity, scale=cwf(ci, 1), bias=vpp[:, VP_CB + ci:VP_CB + ci + 1]),
                               reads=[tpb, t_vpp], writes=[ta])

                        def stB(ci):
                            pb, tpb = banks[ci]
                            acc, ta = accS[ci % 2], t_accS[ci % 2]
                            h0 = 2 * ci
                            op("dve", lambda e: e.scalar_tensor_tensor(out=acc[:, 1:N], in0=pb[:, 0:N - 1], scalar=cwf(ci, 0), in1=acc[:, 1:N], op0=ALU.mult, op1=ALU.add),
                               reads=[tpb, t_vpp], writes=[ta])
                            op("dve", lambda e: e.scalar_tensor_tensor(out=acc[:, 0:N - 1], in0=pb[:, 1:N], scalar=cwf(ci, 2), in1=acc[:, 0:N - 1], op0=ALU.mult, op1=ALU.add),
                               reads=[tpb, t_vpp], writes=[ta])
                            op("dve", lambda e: e.scalar_tensor_tensor(out=acc[:, 0:1], in0=halo[gb][:, h0:h0 + 1], scalar=cwf(ci, 0), in1=acc[:, 0:1], op0=ALU.mult, op1=ALU.add),
                               reads=[t_halo[gb], t_vpp], writes=[ta])
                            op("dve", lambda e: e.scalar_tensor_tensor(out=acc[:, N - 1:N], in0=halo[gb][:, h0 + 1:h0 + 2], scalar=cwf(ci, 2), in1=acc[:, N - 1:N], op0=ALU.mult, op1=ALU.add),
                               reads=[t_halo[gb], t_vpp], writes=[ta])

                        def stC(ci):
                            acc, ta = accS[ci % 2], t_accS[ci % 2]
                            dstt, tdst = (qcS[gb], t_qcS[gb]) if ci < 4 else (kcS[gb], t_kcS[gb])
                            op("act", lambda e: e.activation(out=dstt[:, ci % 4, 0:N], in_=acc[:, 0:N], func=AF.Silu), reads=[ta], writes=[tdst])

                        stA(0)
                        for ci in range(8):
                            if ci + 1 < 8:
                                stA(ci + 1)
                            stB(ci)
                            stC(ci)
                            yield

                    def gates_prologue(dr, seq):
                        for i, (ti, full, lt) in enumerate(seq):
                            c0 = tile_col(ti)
                            tu = t_uT[ti]
                            pbk, tpbk = ps[4 + i % 4], t_ps[4 + i % 4]
                            Gt_, sm_ = Gt4[i % 4], sm4[i % 4]
                            tG, tS = t_Gt4[i % 4], t_sm4[i % 4]
                            for kc in range(8):
                                op("pe", lambda e: e.matmul(pbk[:, 0:16], lhsT=uT[:, kc, c0:c0 + 128], rhs=wmix[:, kc, 3072:3088], start=(kc == 0), stop=(kc == 7)),
                                   reads=t_wmix + [tu], writes=[tpbk], signal=(kc == 7))
                            op("dve", lambda e: e.tensor_tensor(out=Gt_[:], in0=pbk[:, 0:16], in1=bg[:], op=ALU.add), reads=[tpbk, t_bg], writes=[tG])
                            ig = Gt_[:, 8 * dr:8 * dr + 4]
                            fg = Gt_[:, 8 * dr + 4:8 * dr + 8]
                            op("act", lambda e: e.activation(out=sm_[:, 0:4], in_=fg, func=AF.Exp, scale=-1.0), reads=[tG], writes=[tS])
                            op("act", lambda e: e.activation(out=sm_[:, 4:8], in_=sm_[:, 0:4], func=AF.Ln, bias=one_c, scale=1.0),
                               reads=[tS, t_cv], writes=[tS])
                            op("pe", lambda e: e.matmul(pbk[:, 16:20], lhsT=MI[dr], rhs=sm_[:, 4:8], start=True, stop=True),
                               reads=[t_cst, tS], writes=[tpbk], signal=False)
                            op("pe", lambda e: e.matmul(pbk[:, 20:24], lhsT=ones, rhs=sm_[:, 4:8], start=True, stop=True),
                               reads=[t_cst, tS], writes=[tpbk])
                            op("act", lambda e: e.activation(out=scall[:, i, 0:8], in_=pbk[:, 16:24], func=AF.Exp, scale=-1.0), reads=[tpbk], writes=[t_scall])
                            op("dve", lambda e: e.tensor_tensor(out=sm_[:, 8:12], in0=ig, in1=pbk[:, 16:20], op=ALU.add), reads=[tG, tpbk], writes=[tS])
                            op("dve", lambda e: e.tensor_tensor(out=sm_[:, 12:16], in0=sm_[:, 8:12], in1=pbk[:, 20:24], op=ALU.subtract),
                               reads=[tS, tpbk], writes=[tS])
                            op("act", lambda e: e.activation(out=scall[:, i, 8:16], in_=sm_[:, 8:16], func=AF.Exp, bias=lna_c, scale=1.0),
                               reads=[tS, t_cv], writes=[t_scall])

                    def prep(dr, ti, full, i, gb, off):
                        b = i % 2
                        c0 = tile_col(ti)
                        tu = t_uT[ti]
                        uts = lambda kc: uT[:, kc, c0:c0 + 128]
                        qc_ = lambda h: qcS[gb][:, h, off:off + 128]
                        kc__ = lambda h: kcS[gb][:, h, off:off + 128]
                        s_ = scall[:, i, :]
                        pb, tpb = mbank()
                        pbb = pb[:, 0:256].bitcast(BF16)
                        for h in range(4):
                            op("pe", lambda e: e.transpose(out=pbb[:, h * 128:(h + 1) * 128], in_=kc__(h), identity=identb[:]),
                               reads=[t_kcS[gb], t_identb], writes=[tpb], signal=(h == 3))
                        for h in range(4):
                            op("dve", lambda e: e.tensor_scalar(out=kk[b][:, h * 128:(h + 1) * 128], in0=pbb[:, h * 128:(h + 1) * 128],
                                                                scalar1=s_[:, 12 + h:13 + h], scalar2=None, op0=ALU.mult),
                               reads=[tpb, t_scall], writes=[t_kk[b]])
                        yield
                        for half in range(2):
                            pb, tpb = mbank()
                            for kc in range(8):
                                op("pe", lambda e: e.matmul(pb[:, :], lhsT=uts(kc), rhs=wmix[:, kc, 1024 + half * 512:1536 + half * 512],
                                                            start=(kc == 0), stop=(kc == 7)),
                                   reads=t_wmix + [tu], writes=[tpb], signal=(kc == 7))
                            op("act", lambda e: e.copy(out=v[b][:, half * 512:(half + 1) * 512], in_=pb[:, :]), reads=[tpb], writes=[t_v[b]])
                        yield
                        if full and dr == 1:
                            for half in range(2):
                                pb, tpb = mbank()
                                for kc in range(8):
                                    op("pe", lambda e: e.matmul(pb[:, :], lhsT=uts(kc), rhs=wmix[:, kc, 2048 + half * 512:2560 + half * 512],
                                                                start=(kc == 0), stop=(kc == 7)),
                                       reads=t_wmix + [tu], writes=[tpb], signal=(kc == 7))
                                op("act", lambda e: e.activation(out=Gg[b][:, half * 512:(half + 1) * 512], in_=pb[:, :], func=AF.Tanh, scale=0.5),
                                   reads=[tpb], writes=[t_Gg[b]])
                            op("dve", lambda e: e.scalar_tensor_tensor(out=Gg[b][:], in0=Gg[b][:], scalar=1.0, in1=mnorm[:], op0=ALU.add, op1=ALU.mult),
                               reads=[t_Gg[b], t_mnorm], writes=[t_Gg[b]])

                    def scan(dr, ti, full, i, lt, gb, off):
                        b = i % 2
                        s_ = scall[:, i, :]
                        qc_ = lambda h: qcS[gb][:, h, off:off + 128]
                        kc__ = lambda h: kcS[gb][:, h, off:off + 128]
                        if full:
                            for h in range(4):
                                op("pe", lambda e: e.matmul(ps[0][:, h * 128:(h + 1) * 128], lhsT=kc__(h), rhs=qc_(h), start=True, stop=True),
                                   reads=[t_kcS[gb], t_qcS[gb]], writes=[t_ps[0]], signal=(h == 3))
                        yield
                        if full:
                            for h in range(4):
                                op("dve", lambda e: e.scalar_tensor_tensor(out=wT[:, h, :], in0=ps[0][:, h * 128:(h + 1) * 128], scalar=s_[:, 8 + h:9 + h],
                                                                           in1=MI[dr], op0=ALU.mult, op1=ALU.mult),
                                   reads=[t_ps[0], t_scall, t_cst], writes=[t_wT])
                        yield
                        if full:
                            for h in range(4):
                                pbk = ps[1 + h // 2]; tpbk = t_ps[1 + h // 2]
                                oc = (h % 2) * 256
                                op("pe", lambda e: e.matmul(pbk[:, oc:oc + 256], lhsT=wT[:, h, :], rhs=v[b][:, h * 256:(h + 1) * 256], start=True, stop=False),
                                   reads=[t_wT, t_v[b]], writes=[tpbk], signal=False)
                                op("pe", lambda e: e.matmul(pbk[:, oc:oc + 256], lhsT=qc_(h), rhs=Cb[:, h, :], start=False, stop=True),
                                   reads=[t_qcS[gb], t_Cb], writes=[tpbk], signal=(h % 2 == 1))
                            for h in range(4):
                                op("pe", lambda e: e.matmul(sms[:, 32 + h:33 + h], lhsT=wT[:, h, :], rhs=onesb[:, 0:1], start=True, stop=False),
                                   reads=[t_wT, t_onesb], writes=[t_smden], signal=False)
                                op("pe", lambda e: e.matmul(sms[:, 32 + h:33 + h], lhsT=qc_(h), rhs=nb[:, h:h + 1], start=False, stop=True),
                                   reads=[t_qcS[gb], t_nb], writes=[t_smden], signal=(h == 3))
                        def kvmm(hh):
                            for h in hh:
                                oc = (h % 2) * 256
                                op("pe", lambda e: e.matmul(ps[3][:, oc:oc + 256], lhsT=kk[b][:, h * 128:(h + 1) * 128], rhs=v[b][:, h * 256:(h + 1) * 256],
                                                            start=True, stop=True),
                                   reads=[t_kk[b], t_v[b]], writes=[t_ps[3]], signal=(h % 2 == 1))

                        def cupd(hh):
                            for h in hh:
                                oc = (h % 2) * 256
                                op("dve", lambda e: e.scalar_tensor_tensor(out=C[:, h, :], in0=C[:, h, :], scalar=s_[:, 4 + h:5 + h], in1=ps[3][:, oc:oc + 256],
                                                                           op0=ALU.mult, op1=ALU.add),
                                   reads=[t_ps[3], t_scall], writes=[t_C])
                        kvmm((0, 1))
                        for h in range(4):
                            op("pe", lambda e: e.matmul(sms[:, 40 + h:41 + h], lhsT=kk[b][:, h * 128:(h + 1) * 128], rhs=onesb[:, 0:1], start=True, stop=True),
                               reads=[t_kk[b], t_onesb], writes=[t_smkvn], signal=(h == 3))
                        yield
                        cupd((0, 1))
                        yield
                        kvmm((2, 3))
                        yield
                        cupd((2, 3))
                        op("dve", lambda e: e.tensor_tensor(out=nst[:], in0=nst[:], in1=s_[:, 4:8], op=ALU.mult), reads=[t_scall], writes=[t_n])
                        op("dve", lambda e: e.tensor_tensor(out=nst[:], in0=nst[:], in1=sms[:, 40:44], op=ALU.add), reads=[t_smkvn], writes=[t_n])
                        op("pool", lambda e: e.tensor_copy(out=Cb[:].rearrange("p h l -> p (h l)"), in_=C[:].rearrange("p h l -> p (h l)")),
                           reads=[t_C], writes=[t_Cb])
                        op("pool", lambda e: e.tensor_copy(out=nb[:], in_=nst[:]), reads=[t_n], writes=[t_nb])
                        yield
                        if not full:
                            return
                        op("dve", lambda e: e.tensor_tensor(out=rr[:, 0:4], in0=s_[:, 0:4], in1=sms[:, 32:36], op=ALU.mult),
                           reads=[t_scall, t_smden], writes=[t_rr])
                        op("dve", lambda e: e.tensor_scalar(out=rr[:, 4:8], in0=rr[:, 0:4], scalar1=-1.0, scalar2=None, op0=ALU.mult),
                           reads=[t_rr], writes=[t_rr])
                        op("dve", lambda e: e.tensor_tensor(out=rr[:, 4:8], in0=rr[:, 4:8], in1=rr[:, 0:4], op=ALU.max),
                           reads=[t_rr], writes=[t_rr])
                        op("dve", lambda e: e.tensor_scalar(out=rr[:, 4:8], in0=rr[:, 4:8], scalar1=1.0, scalar2=None, op0=ALU.max),
                           reads=[t_rr], writes=[t_rr])
                        op("dve", lambda e: e.reciprocal(out=rr[:, 8:12], in_=rr[:, 4:8]), reads=[t_rr], writes=[t_rr])
                        op("dve", lambda e: e.tensor_tensor(out=rr[:, 12:16], in0=rr[:, 8:12], in1=s_[:, 0:4], op=ALU.mult),
                           reads=[t_rr, t_scall], writes=[t_rr])
                        if dr == 0:
                            ob = hfl[b]
                            for h in range(4):
                                pbk = ps[1 + h // 2]; tpbk = t_ps[1 + h // 2]
                                oc = (h % 2) * 256
                                op("act", lambda e: e.activation(out=ob[:, h * 256:(h + 1) * 256], in_=pbk[:, oc:oc + 256], func=AF.Copy, scale=rr[:, 12 + h:13 + h]),
                                   reads=[tpbk, t_rr], writes=[t_hfl[b]])
                            k.dma("pool", hf_scr[lt * 128:(lt + 1) * 128, :], ob[:], reads=[t_hfl[b]], writes=[t_hf_scr[lt]])
                        else:
                            for h in range(4):
                                pbk = ps[1 + h // 2]; tpbk = t_ps[1 + h // 2]
                                oc = (h % 2) * 256
                                op("dve", lambda e: e.scalar_tensor_tensor(out=hs[:, h * 256:(h + 1) * 256], in0=pbk[:, oc:oc + 256], scalar=rr[:, 12 + h:13 + h],
                                                                           in1=hfl[b][:, h * 256:(h + 1) * 256], op0=ALU.mult, op1=ALU.add),
                                   reads=[tpbk, t_rr, t_hfl[b]], writes=[t_hs])
                            for h in range(4):
                                op("act", lambda e: e.activation(out=junk[:], in_=hs[:, h * 256:(h + 1) * 256], func=AF.Square, accum_out=ssq[:, h:h + 1]),
                                   reads=[t_hs], writes=[t_junk, t_ssq])
                            op("act", lambda e: e.activation(out=ssq[:, 4:8], in_=ssq[:, 0:4], func=AF.Ln, bias=eps_c, scale=1.0 / 256),
                               reads=[t_ssq, t_cv], writes=[t_ssq])
                            op("act", lambda e: e.activation(out=ssq[:, 4:8], in_=ssq[:, 4:8], func=AF.Exp, scale=-0.5), reads=[t_ssq], writes=[t_ssq])
                            for h in range(4):
                                op("dve", lambda e: e.scalar_tensor_tensor(out=yb[b][:, h * 256:(h + 1) * 256], in0=hs[:, h * 256:(h + 1) * 256],
                                                                           scalar=ssq[:, 4 + h:5 + h], in1=Gg[b][:, h * 256:(h + 1) * 256],
                                                                           op0=ALU.mult, op1=ALU.mult),
                                   reads=[t_hs, t_ssq, t_Gg[b]], writes=[t_yb[b]])
                            for cl in range(2):
                                k.dma("pool", ymv[2 * lt + cl], yb[b][cl * 64:(cl + 1) * 64, :], reads=[t_yb[b]], writes=[t_ym_scr])

                    for dr in range(2):
                        op("dve", lambda e: e.memset(C[:].rearrange("p h l -> p (h l)"), 0.0), writes=[t_C])
                        op("dve", lambda e: e.memset(Cb[:].rearrange("p h l -> p (h l)"), 0.0), writes=[t_Cb])
                        op("dve", lambda e: e.memset(nst[:], 0.0), writes=[t_n])
                        op("dve", lambda e: e.memset(nb[:], 0.0), writes=[t_nb])
                        if dr == 0:
                            seq = [(0, False, None), (1, False, None)] + [(2 + lt, True, lt) for lt in range(NT_LAT)]
                        else:
                            seq = [(1, False, None), (0, False, None)] + [(2 + lt, True, lt) for lt in range(NT_LAT - 1, -1, -1)]
                        groups = [seq[0:2]] + [seq[2 + 4 * g_:6 + 4 * g_] for g_ in range(NT_LAT // 4)]
                        ginfo = []
                        tinfo = []
                        for gi_, grp in enumerate(groups):
                            tiles = [t_[0] for t_ in grp]
                            c0g = min(tile_col(t_) for t_ in tiles)
                            ginfo.append((gi_ % 2, c0g, 128 * len(tiles), tiles))
                            for t_ in tiles:
                                tinfo.append((gi_, gi_ % 2, tile_col(t_) - c0g))
                        PQ = [convq(*gi__) for gi__ in ginfo]
                        gates_prologue(dr, seq)
                        exhaust(PQ[0])
                        exhaust(prep(dr, seq[0][0], seq[0][1], 0, tinfo[0][1], tinfo[0][2]))
                        for i, (ti, full, lt) in enumerate(seq):
                            if dr == 1 and full:
                                k.dma("sp", hfl[i % 2][:], hf_scr[lt * 128:(lt + 1) * 128, :], reads=[t_hf_scr[lt]], writes=[t_hfl[i % 2]])
                            gcur = tinfo[i][0]
                            nxt = PQ[gcur + 1] if gcur + 1 < len(PQ) else iter(())
                            nch = 8 // len(groups[gcur])
                            P = prep(dr, seq[i + 1][0], seq[i + 1][1], i + 1, tinfo[i + 1][1], tinfo[i + 1][2]) if i + 1 < len(seq) else iter(())
                            S_ = scan(dr, ti, full, i, lt, tinfo[i][1], tinfo[i][2])
                            step(S_); step(S_)
                            for _ in range(nch // 2):
                                step(nxt)
                            step(S_); step(S_)
                            for _ in range(nch - nch // 2):
                                step(nxt)
                            step(S_); step(S_)
                            exhaust(P)
                            exhaust(S_)

            k.barrier()
            load_w(wmix, t_wmix, win_d, 0, 3104)
            phase_U(False)
            k.barrier()
            if stage >= 1:
                gla_phase()
                k.barrier()
            if stage >= 2:
                load_w(wmix, t_wmix, win_d, 3104, 6192)
                phase_U(True)
                k.barrier()
                mlstm_phase()
                k.barrier()


        tb = [0]

        def tbank():
            i = tb[0] % 8
            tb[0] += 1
            return ps[i], t_ps[i]

        def load_w2(dst, t_dst, src, c0, c1, nk):
            for kc in range(nk):
                k.dma("pool", dst[:, kc, 0:c1 - c0], src[kc * 128:(kc + 1) * 128, c0:c1], writes=[t_dst])

        def tail_a(prefetch):
            with ExitStack() as st:
                wgt = alloc(st, "wgt", [128, 8, 2048], BF16)
                wbg = alloc(st, "wbg", [128, 8, D], BF16)
                wbm = alloc(st, "wbm", [128, 8, D], BF16)
                wo = alloc(st, "wo", [128, 8, D], BF16)
                xs = [alloc(st, "txs%d" % i, [128, D], F32) for i in range(2)]; t_xs = [Tok(), Tok()]
                xn = alloc(st, "txn", [128, D], F32); t_xn = Tok()
                small = alloc(st, "tsmall", [128, 2], F32); t_small = Tok()
                uTt = alloc(st, "uTt", [128, 8, 128], BF16); t_uTt = Tok()
                ygt = [alloc(st, "ygt%d" % i, [128, D], BF16) for i in range(2)]; t_ygt = [Tok(), Tok()]
                ymt = [alloc(st, "ymt%d" % i, [128, D], BF16) for i in range(2)]; t_ymt = [Tok(), Tok()]
                yTg = alloc(st, "yTg", [128, 8, 128], BF16); t_yTg = Tok()
                yTm = alloc(st, "yTm", [128, 8, 128], BF16); t_yTm = Tok()
                gsig = alloc(st, "gsig", [128, 2048], F32); t_gsig = Tok()
                ysg = alloc(st, "ysg", [128, D], F32); t_ysg = Tok()
                tmpm = alloc(st, "tmpm", [128, D], F32); t_tmpm = Tok()
                junk, t_junk = tmpm, t_tmpm
                ysb = alloc(st, "ysb", [128, D], BF16); t_ysb = Tok()
                ysT = alloc(st, "ysT", [128, 8, 128], BF16); t_ysT = Tok()
                x1t = [alloc(st, "x1t%d" % i, [128, D], F32) for i in range(2)]; t_x1t = [Tok(), Tok()]
                stagings = [(gsig[:, 0:1024], Tok()), (gsig[:, 1024:2048], Tok()), (ysg[:, :], Tok()), (tmpm[:, :], Tok())]
                pieces = []
                t_wgt, t_wbg, t_wbm, t_wo = [], [], [], []
                for kc in range(8):
                    for half in range(2):
                        tk = Tok(); t_wgt.append(tk)
                        pieces.append((wgt[:, kc, half * 1024:(half + 1) * 1024], win_d[kc * 128:(kc + 1) * 128, 6192 + half * 1024:6192 + (half + 1) * 1024], tk))
                for wt_, wd_, tl_ in ((wbg, wbg_d, t_wbg), (wbm, wbm_d, t_wbm), (wo, wo_d, t_wo)):
                    for kc in range(8):
                        tk = Tok(); tl_.append(tk)
                        pieces.append((wt_[:, kc, :], wd_[kc * 128:(kc + 1) * 128, :], tk))
                cengs = ["dve", "act", "pool"]
                for j, (dst, src, tk) in enumerate(pieces):
                    stg, tstg = stagings[j % 4]
                    k.dma("sp", stg, src, writes=[tstg])
                    ce = cengs[j % 3]
                    if ce == "act":
                        op("act", lambda e: e.copy(out=dst, in_=stg), reads=[tstg], writes=[tk])
                    else:
                        op(ce, lambda e: e.tensor_copy(out=dst, in_=stg), reads=[tstg], writes=[tk])
                for real, stgs in ((t_gsig, (stagings[0][1], stagings[1][1])), (t_ysg, (stagings[2][1],)), (t_tmpm, (stagings[3][1],))):
                    mr = {}
                    for ts in stgs:
                        if ts.w is not None:
                            mr[ts.w[0]] = max(mr.get(ts.w[0], 0), ts.w[1])
                        for s_k, v_k in ts.r.items():
                            mr[s_k] = max(mr.get(s_k, 0), v_k)
                    real.w = None
                    real.r = mr

                def tr8(src, t_src, dst, t_dst):
                    pb, tpb = tbank()
                    pbb = pb[:, :].bitcast(BF16)
                    for kc in range(8):
                        op("pe", lambda e: e.transpose(out=pbb[:, kc * 128:(kc + 1) * 128], in_=src[:, kc * 128:(kc + 1) * 128], identity=identb[:]),
                           reads=[t_src, t_identb], writes=[tpb], signal=(kc == 7))
                    op("act", lambda e: e.copy(out=dst[:].rearrange("p a b -> p (a b)"), in_=pbb[:, :]), reads=[tpb], writes=[t_dst])

                for i in range(NT_OWN):
                    b = i % 2
                    k.dma("sp", xs[b][:], x_d[i * 128:(i + 1) * 128, :], writes=[t_xs[b]])
                    k.dma("sp", ygt[b][:], ygla_scr[i * 128:(i + 1) * 128, :], reads=[t_ygla_scr[i]], writes=[t_ygt[b]])
                    k.dma("sp", ymt[b][:], ym_scr[i * 128:(i + 1) * 128, :], reads=[t_ym_scr], writes=[t_ymt[b]])
                    norm_transpose(xs[b][:], t_xs[b], xn, t_xn, small, t_small, junk, t_junk,
                                   lambda kc: scU[:, kc, 0:1], lambda kc: modpp[:, 0, kc, 0:1], [t_scU, t_modpp],
                                   lambda kc: uTt[:, kc, :], t_uTt)
                    for q in range(4):
                        pb, tpb = tbank()
                        for kc in range(8):
                            op("pe", lambda e: e.matmul(pb[:, :], lhsT=uTt[:, kc, :], rhs=wgt[:, kc, q * 512:(q + 1) * 512], start=(kc == 0), stop=(kc == 7)),
                               reads=[t_uTt] + t_wgt, writes=[tpb], signal=(kc == 7))
                        op("act", lambda e: e.activation(out=gsig[:, q * 512:(q + 1) * 512], in_=pb[:, :], func=AF.Sigmoid), reads=[tpb], writes=[t_gsig])
                    tr8(ygt[b], t_ygt[b], yTg, t_yTg)
                    tr8(ymt[b], t_ymt[b], yTm, t_yTm)
                    for half in range(2):
                        pb, tpb = tbank()
                        for kc in range(8):
                            op("pe", lambda e: e.matmul(pb[:, :], lhsT=yTg[:, kc, :], rhs=wbg[:, kc, half * 512:(half + 1) * 512], start=(kc == 0), stop=(kc == 7)),
                               reads=[t_yTg] + t_wbg, writes=[tpb], signal=(kc == 7))
                        op("dve", lambda e: e.tensor_tensor(out=ysg[:, half * 512:(half + 1) * 512], in0=pb[:, :], in1=gsig[:, half * 512:(half + 1) * 512], op=ALU.mult),
                           reads=[tpb, t_gsig], writes=[t_ysg])
                    for half in range(2):
                        pb, tpb = tbank()
                        for kc in range(8):
                            op("pe", lambda e: e.matmul(pb[:, :], lhsT=yTm[:, kc, :], rhs=wbm[:, kc, half * 512:(half + 1) * 512], start=(kc == 0), stop=(kc == 7)),
                               reads=[t_yTm] + t_wbm, writes=[tpb], signal=(kc == 7))
                        op("dve", lambda e: e.tensor_tensor(out=tmpm[:, half * 512:(half + 1) * 512], in0=pb[:, :], in1=gsig[:, 1024 + half * 512:1536 + half * 512], op=ALU.mult),
                           reads=[tpb, t_gsig], writes=[t_tmpm])
                    op("dve", lambda e: e.tensor_tensor(out=ysb[:], in0=ysg[:], in1=tmpm[:], op=ALU.add), reads=[t_ysg, t_tmpm], writes=[t_ysb])
                    tr8(ysb, t_ysb, ysT, t_ysT)
                    for half in range(2):
                        pb, tpb = tbank()
                        for kc in range(8):
                            op("pe", lambda e: e.matmul(pb[:, :], lhsT=ysT[:, kc, :], rhs=wo[:, kc, half * 512:(half + 1) * 512], start=(kc == 0), stop=(kc == 7)),
                               reads=[t_ysT] + t_wo, writes=[tpb], signal=(kc == 7))
                        op("dve", lambda e: e.tensor_tensor(out=x1t[b][:, half * 512:(half + 1) * 512], in0=pb[:, :], in1=G1[:, half * 512:(half + 1) * 512], op=ALU.mult),
                           reads=[tpb, t_G1], writes=[t_x1t[b]])
                    op("dve", lambda e: e.tensor_tensor(out=x1t[b][:], in0=x1t[b][:], in1=xs[b][:], op=ALU.add), reads=[t_xs[b]], writes=[t_x1t[b]])
                    k.dma("pool", x1_scr[i * 128:(i + 1) * 128, :], x1t[b][:], reads=[t_x1t[b]], writes=[t_x1_scr[i]])
                    for _ in range(3):
                        if prefetch:
                            prefetch.pop(0)()

        def tail_b(wfo, t_wfo, wfi0, t_wfi0, prefetch):
            with ExitStack() as st:
                while prefetch:
                    prefetch.pop(0)()
                wfiR = alloc(st, "wfiR", [128, 8, 2 * (DFF - 384)], BF16)

                def wcol(part, fc):
                    if fc < 3:
                        return wfi0, part * 384 + fc * 128
                    return wfiR, part * (DFF - 384) + (fc - 3) * 128
                fgbc = alloc(st, "fgbc", [128, D], F32); t_fgbc = Tok()
                x1t = [alloc(st, "fx1t%d" % i, [128, D], F32) for i in range(4)]; t_x1t = [Tok() for _ in range(4)]
                xn = alloc(st, "fxn", [128, D], F32); t_xn = Tok()
                small = alloc(st, "fsmall", [128, 4], F32); t_small = Tok()
                u2T = alloc(st, "u2T", [128, 8, 512], BF16); t_u2T = Tok()
                sa = [alloc(st, "sa%d" % i, [128, 512], F32) for i in range(1)]; t_sa = [Tok()]
                hT = alloc(st, "hT", [128, 22, 512], BF16); t_hT = Tok()
                x2 = alloc(st, "x2", [128, D], F32); t_x2 = Tok()
                k.dma("sp", fgbc[:], vbc_d[0:1, 2048:3072].to_broadcast([128, 1024]), writes=[t_fgbc])
                t_wfib = [[t_wfi0], [], [], []]
                fcb = [0, 3, 9, 15, 22]
                hTf = hT[:].rearrange("p a b -> p (a b)").bitcast(F32)
                stg_t = [Tok() for _ in range(5)]
                cengs = ["dve", "act", "pool"]
                j = 0
                for blk in range(1, 4):
                    f0, f1 = fcb[blk] * 128, fcb[blk + 1] * 128
                    for part in range(2):
                        for kc in range(8):
                            c_ = part * (DFF - 384) + f0 - 384
                            n_ = f1 - f0
                            stg = hTf[:, (j % 5) * 1024:(j % 5) * 1024 + n_]
                            tstg = stg_t[j % 5]
                            tk = Tok()
                            t_wfib[blk].append(tk)
                            k.dma("sp", stg, wfi_d[kc * 128:(kc + 1) * 128, part * DFF + f0:part * DFF + f1], writes=[tstg])
                            dst = wfiR[:, kc, c_:c_ + n_]
                            ce = cengs[j % 3]
                            if ce == "act":
                                op("act", lambda e: e.copy(out=dst, in_=stg), reads=[tstg], writes=[tk])
                            else:
                                op(ce, lambda e: e.tensor_copy(out=dst, in_=stg), reads=[tstg], writes=[tk])
                            j += 1
                mr = {}
                for ts in stg_t:
                    if ts.w is not None:
                        mr[ts.w[0]] = max(mr.get(ts.w[0], 0), ts.w[1])
                    for s_k, v_k in ts.r.items():
                        mr[s_k] = max(mr.get(s_k, 0), v_k)
                t_hT.w = None
                t_hT.r = mr
                for s_i in range(NT_OWN // 4):
                    for j in range(4):
                        ti = s_i * 4 + j
                        k.dma("sp", x1t[j][:], x1_scr[ti * 128:(ti + 1) * 128, :], reads=[t_x1_scr[ti]], writes=[t_x1t[j]])
                        norm_transpose(x1t[j][:], t_x1t[j], xn, t_xn, small, t_small, x2, t_x2,
                                       lambda kc: sc2[:, kc:kc + 1], lambda kc: modpp[:, 2, kc, 0:1], [t_sc2, t_modpp],
                                       lambda kc: u2T[:, kc, j * 128:(j + 1) * 128], t_u2T)
                    for fc in range(22):
                        pa, tpa = tbank()
                        for kc in range(8):
                            wt_, wc_ = wcol(0, fc)
                            op("pe", lambda e: e.matmul(pa[:, :], lhsT=wt_[:, kc, wc_:wc_ + 128], rhs=u2T[:, kc, :], start=(kc == 0), stop=(kc == 7)),
                               reads=t_wfib[0 if fc < 3 else (1 if fc < 9 else (2 if fc < 15 else 3))] + [t_u2T], writes=[tpa], signal=(kc == 7))
                        pbk, tpbk = tbank()
                        for kc in range(8):
                            wt_, wc_ = wcol(1, fc)
                            op("pe", lambda e: e.matmul(pbk[:, :], lhsT=wt_[:, kc, wc_:wc_ + 128], rhs=u2T[:, kc, :], start=(kc == 0), stop=(kc == 7)),
                               reads=t_wfib[0 if fc < 3 else (1 if fc < 9 else (2 if fc < 15 else 3))] + [t_u2T], writes=[tpbk], signal=(kc == 7))
                        sb_ = 0
                        op("act", lambda e: e.activation(out=sa[sb_][:], in_=pa[:, :], func=AF.Silu), reads=[tpa], writes=[t_sa[sb_]])
                        op("dve", lambda e: e.tensor_tensor(out=hT[:, fc, :], in0=sa[sb_][:], in1=pbk[:, :], op=ALU.mult),
                           reads=[t_sa[sb_], tpbk], writes=[t_hT])
                    for j in range(4):
                        ti = s_i * 4 + j
                        for half in range(2):
                            pb, tpb = tbank()
                            for fc in range(22):
                                op("pe", lambda e: e.matmul(pb[:, :], lhsT=hT[:, fc, j * 128:(j + 1) * 128], rhs=wfo[:, fc, half * 512:(half + 1) * 512],
                                                            start=(fc == 0), stop=(fc == 21)),
                                   reads=[t_hT, t_wfo], writes=[tpb], signal=(fc == 21))
                            op("dve", lambda e: e.tensor_tensor(out=x2[:, half * 512:(half + 1) * 512], in0=pb[:, :], in1=G2[:, half * 512:(half + 1) * 512], op=ALU.mult),
                               reads=[tpb, t_G2], writes=[t_x2])
                        op("dve", lambda e: e.tensor_tensor(out=x2[:], in0=x2[:], in1=x1t[j][:], op=ALU.add), reads=[t_x1t[j]], writes=[t_x2])
                        op("act", lambda e: e.activation(out=xn[:], in_=x2[:], func=AF.Square, accum_out=small[:, 2:3]),
                           reads=[t_x2], writes=[t_xn, t_small])
                        op("act", lambda e: e.activation(out=small[:, 3:4], in_=small[:, 2:3], func=AF.Sqrt, bias=eps_c, scale=1.0 / D),
                           reads=[t_small, t_cv], writes=[t_small])
                        op("dve", lambda e: e.reciprocal(out=small[:, 3:4], in_=small[:, 3:4]), reads=[t_small], writes=[t_small])
                        op("dve", lambda e: e.scalar_tensor_tensor(out=x2[:], in0=x2[:], scalar=small[:, 3:4], in1=fgbc[:], op0=ALU.mult, op1=ALU.mult),
                           reads=[t_small, t_fgbc], writes=[t_x2])
                        k.dma("pool", out_d[ti * 128:(ti + 1) * 128, :], x2[:], reads=[t_x2], writes=[t_out])

        if stage >= 3:
            with ExitStack() as stt:
                wfo = alloc(stt, "wfo", [128, 22, D], BF16); t_wfo = Tok()
                wfi0 = alloc(stt, "wfi0", [128, 8, 2 * 384], BF16); t_wfi0 = Tok()
                prefetch = []
                for part in range(2):
                    for kc in range(8):
                        prefetch.append(lambda part=part, kc=kc: k.dma("pool", wfi0[:, kc, part * 384:(part + 1) * 384],
                                                                      wfi_d[kc * 128:(kc + 1) * 128, part * DFF:part * DFF + 384], writes=[t_wfi0]))
                for fc in range(22):
                    prefetch.append(lambda fc=fc: k.dma("pool", wfo[:, fc, :], wfo_d[fc * 128:(fc + 1) * 128, :], writes=[t_wfo]))
                k.barrier()
                tail_a(prefetch)
                if stage >= 4:
                    k.barrier()
                    tail_b(wfo, t_wfo, wfi0, t_wfi0, prefetch)

        fin = list(t_ygla_scr) + [t_ym_scr, t_out] + t_x1_scr + dbg_list
        k.finish(fin, "sp")
        k.finish(fin, "pool")
        k.check_deadlock()
        print("build: ops", k.nops, "waits", k.nwaits, "cnt", {e: k.cnt[e] for e in k.eng})
    return nc


def _consts():
    c = np.zeros((128, 768), np.float32)
    m = np.arange(128)[:, None]
    l = np.arange(128)[None, :]
    c[:, 0:128] = np.eye(128)
    c[:, 128:256] = (m <= l)
    c[:, 256:384] = (m >= l)
    c[:, 384:512] = (m > l)
    c[:, 512:640] = (m < l)
    c[:, 640:768] = 1.0
    return c


def _variant(inp, flip):
    w_in = inp["w_in"][0]
    w_up = inp["gla_w_up"][0]
    b_dec = inp["gla_b_dec"][0]
    conv_w = inp["mlstm_conv_w"][0]
    b_gate = inp["mlstm_b_gate"][0]
    if flip:
        idx = np.arange(NIN)
        idx[3072:3088] = np.arange(3088, 3104)
        idx[3088:3104] = np.arange(3072, 3088)
        g0 = 6176
        idx[g0:g0 + 8] = np.arange(g0 + 8, g0 + 16)
        idx[g0 + 8:g0 + 16] = np.arange(g0, g0 + 8)
        w_in = w_in[:, idx]
        w_up = w_up[::-1]
        b_dec = b_dec[::-1]
        conv_w = conv_w[::-1]
        b_gate = b_gate[[2, 3, 0, 1]]
    b_ada = inp["b_ada"][0]
    pp = lambda v, n: np.asarray(v).reshape(n, 128).T
    vpp = np.concatenate([pp(b_ada, 48), pp(inp["norm1_g"][0], 8), pp(inp["norm2_g"][0], 8),
                          np.asarray(conv_w).reshape(3, 8, 128).transpose(2, 0, 1).reshape(128, 24),
                          pp(inp["mlstm_conv_b"][0], 8)], axis=1)
    vbc = np.concatenate([b_ada[2048:3072], b_ada[5120:6144], inp["final_g"], inp["gla_norm_g"][0],
                          inp["mlstm_norm_g"][0], np.asarray(b_gate).reshape(16)])[None, :]
    f = lambda a: np.ascontiguousarray(a, dtype=np.float32)
    return dict(w_in=f(w_in), w_up=f(w_up), bdec=f(np.asarray(b_dec).reshape(1, 1024)), vpp=f(vpp), vbc=f(vbc),
                w_ada=f(inp["w_ada"][0]), consts=_consts(), w_br_gla=f(inp["w_br_gla"][0]), w_br_m=f(inp["w_br_mlstm"][0]),
                w_out=f(inp["w_out"][0]), w_ffn_in=f(inp["w_ffn_in"][0]), w_ffn_out=f(inp["w_ffn_out"][0]))


def make_in_maps(inp, cores=None):
    inp = {k_: np.asarray(v) for k_, v in inp.items()}
    var = [_variant(inp, False), _variant(inp, True)]
    maps = []
    for b in range(4):
        for s in range(2):
            m = dict(var[s])
            xb = inp["x"][b]
            cb = inp["ctx"][b]
            if s:
                xb = xb[::-1]
                cb = cb[::-1]
            m["x"] = np.ascontiguousarray(xb, dtype=np.float32)
            m["ctx"] = np.ascontiguousarray(cb, dtype=np.float32)
            cc = np.stack([inp["c"][b], inp["c_ctx"]], -1).reshape(8, 128, 2).transpose(1, 0, 2)
            m["cct"] = np.ascontiguousarray(cc, dtype=np.float32)
            maps.append(m)
    return maps


def kernel(**inputs):
    nc = build(9)
    maps = make_in_maps(inputs)
    res = run_bass_kernel_spmd(nc, maps, core_ids=list(range(8)))
    out = np.zeros((4, T, D), np.float32)
    for b in range(4):
        out[b, 0:2048] = np.asarray(res.results[2 * b]["out"])
        out[b, 2048:] = np.asarray(res.results[2 * b + 1]["out"])[::-1]
    return out
```

```python
import math
from contextlib import ExitStack
import numpy as np
import concourse.bass as bass
import concourse.mybir as mybir
from concourse.bass_utils import run_bass_kernel_spmd

F32 = mybir.dt.float32
BF16 = mybir.dt.bfloat16
AF = mybir.ActivationFunctionType
ALU = mybir.AluOpType

D = 1024
T = 4096
TC = 256
NIN = 8240
DFF = 2816
NT_OWN = 16
NT_LAT = 32
UW = 4356
CTX0 = 1
LAT0 = 259
ALPHA = 128.0 ** -0.5
NV = 96
NB = 5136
C_ID, C_MIF, C_MIB, C_MAF, C_MAB, C_ONE = 0, 128, 256, 384, 512, 640


class Tok:
    __slots__ = ("w", "r", "ex")

    def __init__(self, ex=False):
        self.w = None
        self.r = {}
        self.ex = ex


class K:
    def __init__(self, nc, stack, ndma=8):
        self.nc = nc
        self.eng = {"pe": nc.tensor, "act": nc.scalar, "dve": nc.vector, "pool": nc.gpsimd, "sp": nc.sync}
        self.semh = {}
        self.cnt = {}
        self.waited = {e: {} for e in self.eng}
        for e in self.eng:
            self.semh[e] = stack.enter_context(nc.semaphore("s_" + e))
            self.cnt[e] = 0
        self.dma_slots = {}
        self.dma_rr = {}
        for q in ("sp", "pool"):
            self.dma_slots[q] = []
            for i in range(ndma):
                nm = "d_%s%d" % (q, i)
                self.semh[nm] = stack.enter_context(nc.semaphore(nm))
                self.cnt[nm] = 0
                self.dma_slots[q].append(nm)
            self.dma_rr[q] = 0
        self.nwaits = 0
        self.nops = 0
        self.log = {e: [] for e in self.eng}

    def _wait(self, e, s, v):
        if self.waited[e].get(s, 0) >= v:
            return
        self.eng[e].wait_ge(self.semh[s], v)
        self.waited[e][s] = v
        self.nwaits += 1
        self.log[e].append(("wait", s, v))

    def _deps(self, e, reads, writes):
        deps = {}
        for t in reads:
            if t.w is not None:
                s, v = t.w
                deps[s] = max(deps.get(s, 0), v)
        for t in writes:
            if t.w is not None:
                s, v = t.w
                deps[s] = max(deps.get(s, 0), v)
            for s, v in t.r.items():
                deps[s] = max(deps.get(s, 0), v)
        for s, v in deps.items():
            if e == "pe" and s == "pe":
                continue
            self._wait(e, s, v)

    def _record(self, ticket, reads, writes):
        s, v = ticket
        for t in reads:
            t.r[s] = max(t.r.get(s, 0), v)
        for t in writes:
            t.w = ticket
            t.r = {}

    def op(self, e, fn, reads=(), writes=(), signal=True):
        if e != "pe":
            exr = [t for t in reads if t.ex]
            if exr:
                writes = list(writes) + exr
        self._deps(e, reads, writes)
        ins = fn(self.eng[e])
        self.nops += 1
        if signal:
            self.cnt[e] += 1
            ins.then_inc(self.semh[e], 1)
            ticket = (e, self.cnt[e])
            self.log[e].append(("inc", e, 1, self.nops))
        else:
            assert e == "pe"
            ticket = (e, self.cnt[e] + 1)
        self._record(ticket, reads, writes)
        return ticket

    def dma(self, q, out, in_, reads=(), writes=(), **kw):
        slots = self.dma_slots[q]
        nm = slots[self.dma_rr[q] % len(slots)]
        self.dma_rr[q] += 1
        if self.cnt[nm] > 0:
            self._wait(q, nm, self.cnt[nm])
        self._deps(q, reads, writes)
        ins = self.eng[q].dma_start(out=out, in_=in_, **kw)
        self.cnt[nm] += 16
        ins.then_inc(self.semh[nm], 16)
        self.log[q].append(("inc", nm, 16, self.nops))
        ticket = (nm, self.cnt[nm])
        self._record(ticket, reads, writes)
        self.nops += 1
        return ticket

    def check_deadlock(self):
        sem = {}
        pos = {e: 0 for e in self.eng}
        progress = True
        while progress:
            progress = False
            for e in self.eng:
                lg = self.log[e]
                while pos[e] < len(lg):
                    it = lg[pos[e]]
                    if it[0] == "wait":
                        if sem.get(it[1], 0) >= it[2]:
                            pos[e] += 1
                            progress = True
                        else:
                            break
                    else:
                        sem[it[1]] = sem.get(it[1], 0) + it[2]
                        pos[e] += 1
                        progress = True
        stuck = {e: (pos[e], len(self.log[e]), self.log[e][pos[e]]) for e in self.eng if pos[e] < len(self.log[e])}
        if stuck:
            print("DEADLOCK:", stuck, {k_: v for k_, v in sem.items()})
        else:
            print("deadlock check: OK")
        return not stuck

    def barrier(self):
        snap = dict(self.cnt)
        for e in self.eng:
            for s_, v in snap.items():
                if v > 0:
                    self._wait(e, s_, v)

    def finish(self, toks, e="sp"):
        for t in toks:
            if t.w is not None:
                self._wait(e, t.w[0], t.w[1])


def build(stage=9):
    nc = bass.Bass("TRN2", target_bir_lowering=False)
    di = lambda n, s, dt=F32: nc.dram_tensor(n, s, dt, kind="ExternalInput").ap()
    x_d = di("x", [T, D])
    ctx_d = di("ctx", [TC, D])
    cct_d = di("cct", [128, 8, 2])
    wada_d = di("w_ada", [D, 6 * D])
    win_d = di("w_in", [D, NIN])
    wup_d = di("w_up", [2, 16, 512])
    bdec_d = di("bdec", [1, 1024])
    cst_d = di("consts", [128, 768])
    vpp_d = di("vpp", [128, NV])
    vbc_d = di("vbc", [1, NB])
    wbg_d = di("w_br_gla", [D, D])
    wbm_d = di("w_br_m", [D, D])
    wo_d = di("w_out", [D, D])
    wfi_d = di("w_ffn_in", [D, 2 * DFF])
    wfo_d = di("w_ffn_out", [DFF, D])
    out_d = nc.dram_tensor("out", [NT_OWN * 128, D], F32, kind="ExternalOutput").ap()
    of_scr = nc.dram_tensor("of_scr", [NT_OWN * 128, D], F32, kind="Internal" if stage >= 9 else "ExternalOutput").ap()
    hf_scr = nc.dram_tensor("hf_scr", [T, D], F32, kind="Internal" if stage >= 9 else "ExternalOutput").ap()
    if stage < 9:
        ygla_scr = nc.dram_tensor("ygla", [NT_OWN * 128, D], BF16, kind="ExternalOutput").ap()
        ym_scr = nc.dram_tensor("ym", [T, D], BF16, kind="ExternalOutput").ap()
    else:
        ygla_scr = nc.dram_tensor("ygla", [NT_OWN * 128, D], BF16, kind="Internal").ap()
        ym_scr = nc.dram_tensor("ym", [T, D], BF16, kind="Internal").ap()
    x1_scr = nc.dram_tensor("x1_scr", [NT_OWN * 128, D], F32, kind="Internal" if stage >= 9 else "ExternalOutput").ap()
    t_of_scr = [Tok() for _ in range(NT_OWN)]
    t_hf_scr = [Tok() for _ in range(NT_LAT)]
    t_ygla_scr = [Tok() for _ in range(NT_OWN)]
    t_ym_scr = Tok()
    t_x1_scr = [Tok() for _ in range(NT_OWN)]
    t_out = Tok()

    with ExitStack() as st0:
        k = K(nc, st0)
        op = k.op

        dbg_list = []
        dbg_pool = [st0.enter_context(nc.sbuf_tensor("dbgs_%d" % i, [128, 128], F32)) for i in range(16 if stage < 3 else 0)]

        def dbg(name, ap, toks, n):
            if stage >= 3:
                return
            dd = nc.dram_tensor("dbg_" + name, [128, n], F32, kind="ExternalOutput").ap()
            stg = dbg_pool.pop()[:, 0:n]
            tk = Tok()
            np_ = ap.shape[0]
            op("dve", lambda e: e.memset(stg, 0.0), writes=[tk])
            op("dve", lambda e: e.tensor_copy(out=stg[0:np_, :], in_=ap), reads=toks, writes=[tk])
            td = Tok()
            k.dma("pool", dd[:, :], stg, reads=[tk], writes=[td])
            dbg_list.append(td)

        uniq = [0]

        def alloc(st, name, shape, dt):
            uniq[0] += 1
            return st.enter_context(nc.sbuf_tensor("sb%d_%s" % (uniq[0], name), shape, dt))

        cst = alloc(st0, "cst", [128, 768], F32); t_cst = Tok()
        vpp = alloc(st0, "vpp", [128, NV], F32); t_vpp = Tok()
        identb = alloc(st0, "identb", [128, 128], BF16); t_identb = Tok()
        mask4 = [alloc(st0, "mask4_%d" % d, [128, 4, 128], BF16) for d in range(2)]; t_mask4 = Tok()
        onesb = alloc(st0, "onesb", [128, 1], BF16); t_onesb = Tok()
        cvals = alloc(st0, "cvals", [128, 4], F32); t_cv = Tok()
        modpp = alloc(st0, "modpp", [128, 4, 8, 2], F32); t_modpp = Tok()
        scU = alloc(st0, "scU", [128, 8, 2], F32); t_scU = Tok()
        sc2 = alloc(st0, "sc2", [128, 8], F32); t_sc2 = Tok()
        G1 = alloc(st0, "G1", [128, D], F32); t_G1 = Tok()
        G2 = alloc(st0, "G2", [128, D], F32); t_G2 = Tok()
        ps = [st0.enter_context(nc.psum_tensor("ps%d" % i, [128, 512], F32)) for i in range(8)]
        t_ps = [Tok(ex=True) for _ in range(8)]
        ident = cst[:, C_ID:C_ID + 128]
        MI = [cst[:, C_MIF:C_MIF + 128], cst[:, C_MIB:C_MIB + 128]]
        MA = [cst[:, C_MAF:C_MAF + 128], cst[:, C_MAB:C_MAB + 128]]
        ones = cst[:, C_ONE:C_ONE + 128]
        one_c = cvals[:, 0:1]
        eps_c = cvals[:, 1:2]
        lna_c = cvals[:, 2:3]
        VP_BADA, VP_N1, VP_N2, VP_CW, VP_CB = 0, 48, 56, 64, 88

        k.dma("sp", cst[:], cst_d[:, :], writes=[t_cst])
        k.dma("sp", vpp[:], vpp_d[:, :], writes=[t_vpp])
        op("dve", lambda e: e.memset(cvals[:, 0:1], 1.0), writes=[t_cv])
        op("dve", lambda e: e.memset(cvals[:, 1:2], 1e-6), writes=[t_cv])
        op("dve", lambda e: e.memset(cvals[:, 2:3], math.log(ALPHA)), writes=[t_cv])
        op("dve", lambda e: e.memset(cvals[:, 3:4], 0.0), writes=[t_cv])
        op("dve", lambda e: e.tensor_copy(out=identb[:], in_=ident), reads=[t_cst], writes=[t_identb])
        op("dve", lambda e: e.tensor_copy(out=onesb[:], in_=cst[:, C_ONE:C_ONE + 1]), reads=[t_cst], writes=[t_onesb])
        for d in range(2):
            for h in range(4):
                op("dve", lambda e: e.tensor_copy(out=mask4[d][:, h, :], in_=MI[d]), reads=[t_cst], writes=[t_mask4])

        prep_banks = [5, 6, 7]
        prr = [0]

        def pbank():
            i = prep_banks[prr[0] % len(prep_banks)]
            prr[0] += 1
            return ps[i], t_ps[i]

        with ExitStack() as st:
            scc = alloc(st, "scc", [128, 8, 2], F32); t_scc = Tok()
            wa = [alloc(st, "wa%d" % i, [128, 8, D], F32) for i in range(2)]; t_wa = [[Tok() for _ in range(8)] for _ in range(2)]
            bbc = alloc(st, "bbc", [128, D], F32); t_bbc = Tok()
            k.dma("sp", scc[:], cct_d[:, :, :], writes=[t_scc])
            op("act", lambda e: e.activation(out=scc[:], in_=scc[:], func=AF.Silu), reads=[t_scc], writes=[t_scc])
            psA, t_psA = ps[0], t_ps[0]
            mrow = alloc(st, "mrow", [2, D], F32); t_mrow = Tok()
            gi = 0
            for g in range(6):
                w_, tw_ = wa[g % 2], t_wa[g % 2]
                for kc in range(8):
                    k.dma("sp", w_[:, kc, :], wada_d[kc * 128:(kc + 1) * 128, g * D:(g + 1) * D], writes=[tw_[kc]])
                for half in range(2):
                    pb, tpb = pbank()
                    for kc in range(8):
                        op("pe", lambda e: e.matmul(pb[0:2, :], lhsT=scc[:, kc, :], rhs=w_[:, kc, half * 512:(half + 1) * 512],
                                                    start=(kc == 0), stop=(kc == 7)),
                           reads=[tw_[kc], t_scc], writes=[tpb], signal=(kc == 7))
                    op("act", lambda e: e.copy(out=mrow[:, half * 512:(half + 1) * 512], in_=pb[0:2, :]), reads=[tpb], writes=[t_mrow])
                if g in (0, 1, 3, 4):
                    for j in range(8):
                        c0 = (gi * 8 + j) * 2
                        op("pe", lambda e: e.transpose(out=psA[:, c0:c0 + 2], in_=mrow[0:2, j * 128:(j + 1) * 128], identity=cst[0:2, C_ID:C_ID + 2]),
                           reads=[t_mrow, t_cst], writes=[t_psA], signal=(j == 7))
                    gi += 1
                else:
                    Gt, tG = (G1, t_G1) if g == 2 else (G2, t_G2)
                    voff = 0 if g == 2 else 1024
                    k.dma("sp", bbc[:], vbc_d[0:1, voff:voff + 1024].to_broadcast([128, 1024]), writes=[t_bbc])
                    for half in range(2):
                        pb, tpb = pbank()
                        op("pe", lambda e: e.matmul(pb[:, :], lhsT=cst[0:1, C_ONE:C_ONE + 128], rhs=mrow[0:1, half * 512:(half + 1) * 512],
                                                    start=True, stop=True),
                           reads=[t_mrow, t_cst], writes=[tpb])
                        op("dve", lambda e: e.tensor_tensor(out=Gt[:, half * 512:(half + 1) * 512], in0=pb[:, :],
                                                            in1=bbc[:, half * 512:(half + 1) * 512], op=ALU.add),
                           reads=[tpb, t_bbc], writes=[tG])
            psAv = psA[:, 0:64].rearrange("p (g j s) -> p g j s", g=4, j=8, s=2)
            for gi_, g in enumerate((0, 1, 3, 4)):
                for s in range(2):
                    op("dve", lambda e: e.tensor_tensor(out=modpp[:, gi_, :, s], in0=psAv[:, gi_, :, s],
                                                        in1=vpp[:, VP_BADA + g * 8:VP_BADA + (g + 1) * 8], op=ALU.add),
                       reads=[t_psA, t_vpp], writes=[t_modpp])
            for s in range(2):
                op("dve", lambda e: e.scalar_tensor_tensor(out=scU[:, :, s], in0=modpp[:, 1, :, s], scalar=1.0,
                                                           in1=vpp[:, VP_N1:VP_N1 + 8], op0=ALU.add, op1=ALU.mult),
                   reads=[t_modpp, t_vpp], writes=[t_scU])
            op("dve", lambda e: e.scalar_tensor_tensor(out=sc2[:, :], in0=modpp[:, 3, :, 0], scalar=1.0,
                                                       in1=vpp[:, VP_N2:VP_N2 + 8], op0=ALU.add, op1=ALU.mult),
               reads=[t_modpp, t_vpp], writes=[t_sc2])


        def norm_transpose(xs, t_xs, xn, t_xn, small, t_small, junk, t_junk, scale_ap, bias_ap, tsb, dst, t_dst):
            norm_part(xs, t_xs, xn, t_xn, small, t_small, junk, t_junk)
            transpose_part(xn, t_xn, scale_ap, bias_ap, tsb, dst, t_dst)

        def norm_part(xs, t_xs, xn, t_xn, small, t_small, junk, t_junk):
            op("act", lambda e: e.activation(out=junk[:], in_=xs, func=AF.Square, accum_out=small[:, 0:1]),
               reads=[t_xs], writes=[t_junk, t_small])
            op("act", lambda e: e.activation(out=small[:, 1:2], in_=small[:, 0:1], func=AF.Sqrt, bias=eps_c, scale=1.0 / D),
               reads=[t_small, t_cv], writes=[t_small])
            op("dve", lambda e: e.reciprocal(out=small[:, 1:2], in_=small[:, 1:2]), reads=[t_small], writes=[t_small])
            op("dve", lambda e: e.tensor_scalar(out=xn[:], in0=xs, scalar1=small[:, 1:2], scalar2=None, op0=ALU.mult),
               reads=[t_xs, t_small], writes=[t_xn])

        def transpose_part(xn, t_xn, scale_ap, bias_ap, tsb, dst, t_dst):
            for half in range(2):
                pb, tpb = pbank()
                for j in range(4):
                    kc = half * 4 + j
                    op("pe", lambda e: e.transpose(out=pb[:, j * 128:(j + 1) * 128], in_=xn[:, kc * 128:(kc + 1) * 128], identity=ident),
                       reads=[t_xn, t_cst], writes=[tpb], signal=(j == 3))
                for j in range(4):
                    kc = half * 4 + j
                    if j % 2 == 0:
                        op("act", lambda e: e.activation(out=dst(kc), in_=pb[:, j * 128:(j + 1) * 128], func=AF.Identity,
                                                         scale=scale_ap(kc), bias=bias_ap(kc)),
                           reads=[tpb] + tsb, writes=[t_dst])
                    else:
                        op("dve", lambda e: e.tensor_scalar(out=dst(kc), in0=pb[:, j * 128:(j + 1) * 128], scalar1=scale_ap(kc),
                                                            scalar2=bias_ap(kc), op0=ALU.mult, op1=ALU.add),
                           reads=[tpb] + tsb, writes=[t_dst])

        with ExitStack() as stm:
            uT = alloc(stm, "uT", [128, 8, UW], BF16)
            t_uT = [Tok() for _ in range(34)]
            t_guard = Tok()
            wmix = alloc(stm, "wmix", [128, 8, 3104], BF16); t_wmix = [Tok() for _ in range(8)]
            for c in (0, 257, 258, UW - 1):
                op("dve", lambda e: e.memset(uT[:, :, c:c + 1], 0.0), writes=[t_guard])

            def tile_col(ti):
                return CTX0 + ti * 128 if ti < 2 else LAT0 + (ti - 2) * 128

            def phase_U(colmajor):
                with ExitStack() as st:
                    xs = [alloc(st, "xs%d" % i, [128, D], F32) for i in range(2)]; t_xs = [Tok(), Tok()]
                    xn = [alloc(st, "xn%d" % i, [128, D], F32) for i in range(2)]; t_xn = [Tok(), Tok()]
                    junk = alloc(st, "junkU", [128, D], F32); t_junk = Tok()
                    small = [alloc(st, "smallU%d" % i, [128, 2], F32) for i in range(2)]; t_small = [Tok(), Tok()]
                    xv = x_d.rearrange("(r c) d -> c r d", c=64)

                    def partA(ti):
                        b = ti % 2
                        if ti < 2:
                            k.dma("sp", xs[b][:], ctx_d[ti * 128:(ti + 1) * 128, :], writes=[t_xs[b]])
                        else:
                            lt = ti - 2
                            if colmajor:
                                for cl in range(2):
                                    k.dma("sp", xs[b][cl * 64:(cl + 1) * 64, :], xv[2 * lt + cl], writes=[t_xs[b]])
                            else:
                                k.dma("sp", xs[b][:], x_d[lt * 128:(lt + 1) * 128, :], writes=[t_xs[b]])
                        norm_part(xs[b][:], t_xs[b], xn[b], t_xn[b], small[b], t_small[b], junk, t_junk)

                    def partB(ti):
                        b = ti % 2
                        s = 1 if ti < 2 else 0
                        c0 = tile_col(ti)
                        transpose_part(xn[b], t_xn[b], lambda kc: scU[:, kc, s:s + 1], lambda kc: modpp[:, 0, kc, s:s + 1], [t_scU, t_modpp],
                                       lambda kc: uT[:, kc, c0:c0 + 128], t_uT[ti])

                    partA(0)
                    for ti in range(34):
                        if ti + 1 < 34:
                            partA(ti + 1)
                        partB(ti)

            def step(g):
                try:
                    next(g)
                except StopIteration:
                    pass

            def exhaust(g):
                for _ in g:
                    pass

            def load_w(dst, t_dst, src, c0, c1, nk=8, q="pool"):
                for kc in range(nk):
                    k.dma(q, dst[:, kc, 0:c1 - c0], src[kc * 128:(kc + 1) * 128, c0:c1], writes=[t_dst[kc]])

            def gla_phase():
                print("gla start cnt", dict(k.cnt))
                with ExitStack() as st:
                    wup = alloc(st, "wup", [16, 2, 512], F32); t_wup = Tok()
                    bdec = alloc(st, "bdec", [1, 1024], F32); t_bdec = Tok()
                    gnorm = alloc(st, "gnorm", [128, D], F32); t_gnorm = Tok()
                    lrT = alloc(st, "lrT", [16, 128], F32); t_lrT = Tok()
                    tmp = alloc(st, "gtmp", [128, 512], F32); t_tmp = Tok()
                    sp = alloc(st, "gsp", [128, 512], F32); t_sp = Tok()
                    dkk = alloc(st, "dkk", [128, 512], F32); t_dkk = Tok()
                    Ep = alloc(st, "Ep", [128, 512], F32); t_Ep = Tok()
                    Em = alloc(st, "Em", [128, 512], F32); t_Em = Tok()
                    kk = [alloc(st, "kk%d" % i, [128, 512], BF16) for i in range(2)]; t_kk = [Tok(), Tok()]
                    v = [alloc(st, "v%d" % i, [128, D], BF16) for i in range(2)]; t_v = [Tok(), Tok()]
                    Gg = [alloc(st, "Gg%d" % i, [128, D], F32) for i in range(2)]; t_Gg = [Tok(), Tok()]
                    qd = [alloc(st, "qd%d" % i, [128, 4, 128], BF16) for i in range(2)]; t_qd = [Tok(), Tok()]
                    kd = [alloc(st, "kd%d" % i, [128, 4, 128], BF16) for i in range(2)]; t_kd = [Tok(), Tok()]
                    dec = [alloc(st, "dec%d" % i, [128, 4], F32) for i in range(2)]; t_dec = [Tok(), Tok()]
                    wT = alloc(st, "wT", [128, 4, 128], BF16); t_wT = Tok()
                    S = alloc(st, "S", [128, 4, 256], F32); t_S = Tok()
                    Sb = alloc(st, "Sb", [128, 4, 256], BF16); t_Sb = Tok()
                    ofl = [alloc(st, "ofl%d" % i, [128, D], F32) for i in range(2)]; t_ofl = [Tok(), Tok()]
                    osum = alloc(st, "osum", [128, D], F32); t_osum = Tok()
                    junk = alloc(st, "junkG", [128, 256], F32); t_junk = Tok()
                    ssq = alloc(st, "ssqG", [128, 8], F32); t_ssq = Tok()
                    yb = [alloc(st, "yb%d" % i, [128, D], BF16) for i in range(2)]; t_yb = [Tok(), Tok()]

                    for d_ in range(2):
                        k.dma("sp", wup[:, d_, :], wup_d[d_], writes=[t_wup])
                    k.dma("sp", bdec[:], bdec_d[:, :], writes=[t_bdec])
                    k.dma("sp", gnorm[:], vbc_d[0:1, 3072:4096].to_broadcast([128, 1024]), writes=[t_gnorm])

                    def prep(dr, ti, full, i):
                        b = i % 2
                        c0 = tile_col(ti)
                        tu = t_uT[ti]
                        uts = lambda kc: uT[:, kc, c0:c0 + 128]
                        pb, tpb = pbank()
                        for kc in range(8):
                            op("pe", lambda e: e.matmul(pb[0:16, 0:128], lhsT=wmix[:, kc, 3072 + 16 * dr:3088 + 16 * dr], rhs=uts(kc),
                                                        start=(kc == 0), stop=(kc == 7)),
                               reads=t_wmix + [tu], writes=[tpb], signal=(kc == 7))
                        op("act", lambda e: e.copy(out=lrT[:], in_=pb[0:16, 0:128]), reads=[tpb], writes=[t_lrT])
                        pb, tpb = pbank()
                        op("pe", lambda e: e.matmul(pb[:, :], lhsT=lrT[0:16, :], rhs=wup[0:16, dr, :], start=True, stop=False),
                           reads=[t_lrT, t_wup], writes=[tpb], signal=False)
                        op("pe", lambda e: e.matmul(pb[:, :], lhsT=cst[0:1, C_ONE:C_ONE + 128], rhs=bdec[0:1, dr * 512:(dr + 1) * 512],
                                                    start=False, stop=True),
                           reads=[t_cst, t_bdec], writes=[tpb])
                        op("act", lambda e: e.activation(out=tmp[:], in_=pb[:, :], func=AF.Exp, scale=-1.0), reads=[tpb], writes=[t_tmp])
                        op("act", lambda e: e.activation(out=sp[:], in_=tmp[:], func=AF.Ln, bias=one_c, scale=1.0),
                           reads=[t_tmp, t_cv], writes=[t_sp])
                        yield
                        pb, tpb = pbank()
                        op("pe", lambda e: e.matmul(pb[:, :], lhsT=MA[dr], rhs=sp[:], start=True, stop=True),
                           reads=[t_cst, t_sp], writes=[tpb])
                        op("act", lambda e: e.activation(out=dkk[:], in_=pb[:, :], func=AF.Exp, scale=-1.0 / 16), reads=[tpb], writes=[t_dkk])
                        pb, tpb = pbank()
                        for kc in range(8):
                            op("pe", lambda e: e.matmul(pb[:, :], lhsT=uts(kc), rhs=wmix[:, kc, 512:1024], start=(kc == 0), stop=(kc == 7)),
                               reads=t_wmix + [tu], writes=[tpb], signal=(kc == 7))
                        op("dve", lambda e: e.tensor_tensor(out=kk[b][:], in0=pb[:, :], in1=dkk[:], op=ALU.mult),
                           reads=[tpb, t_dkk], writes=[t_kk[b]])
                        yield
                        for half in range(2):
                            pb, tpb = pbank()
                            for kc in range(8):
                                op("pe", lambda e: e.matmul(pb[:, :], lhsT=uts(kc), rhs=wmix[:, kc, 1024 + half * 512:1536 + half * 512],
                                                            start=(kc == 0), stop=(kc == 7)),
                                   reads=t_wmix + [tu], writes=[tpb], signal=(kc == 7))
                            if half == 0:
                                op("act", lambda e: e.copy(out=v[b][:, 0:512], in_=pb[:, :]), reads=[tpb], writes=[t_v[b]])
                            else:
                                op("dve", lambda e: e.tensor_copy(out=v[b][:, 512:1024], in_=pb[:, :]), reads=[tpb], writes=[t_v[b]])
                        yield
                        if full and dr == 1:
                            for half in range(2):
                                pb, tpb = pbank()
                                for kc in range(8):
                                    op("pe", lambda e: e.matmul(pb[:, :], lhsT=uts(kc), rhs=wmix[:, kc, 2048 + half * 512:2560 + half * 512],
                                                                start=(kc == 0), stop=(kc == 7)),
                                       reads=t_wmix + [tu], writes=[tpb], signal=(kc == 7))
                                op("act", lambda e: e.activation(out=Gg[b][:, half * 512:(half + 1) * 512], in_=pb[:, :], func=AF.Silu),
                                   reads=[tpb], writes=[t_Gg[b]])
                            op("dve", lambda e: e.tensor_tensor(out=Gg[b][:], in0=Gg[b][:], in1=gnorm[:], op=ALU.mult),
                               reads=[t_Gg[b], t_gnorm], writes=[t_Gg[b]])
                        yield
                        pb, tpb = pbank()
                        for h in range(4):
                            op("pe", lambda e: e.matmul(pb[:, h * 128:(h + 1) * 128], lhsT=sp[:, h * 128:(h + 1) * 128], rhs=MI[dr],
                                                        start=True, stop=True),
                               reads=[t_sp, t_cst], writes=[tpb], signal=(h == 3))
                        op("act", lambda e: e.activation(out=Ep[:], in_=pb[:, :], func=AF.Exp, scale=-1.0 / 16), reads=[tpb], writes=[t_Ep])
                        if full:
                            op("act", lambda e: e.activation(out=Em[:], in_=pb[:, :], func=AF.Exp, scale=1.0 / 16), reads=[tpb], writes=[t_Em])
                        lastc = 127 if dr == 0 else 0
                        Epv = Ep[:].rearrange("p (h l) -> p h l", h=4)
                        op("dve", lambda e: e.tensor_copy(out=dec[b][:], in_=Epv[:, :, lastc]), reads=[t_Ep], writes=[t_dec[b]])
                        yield
                        if full:
                            for which in range(2):
                                if which == 1:
                                    yield
                                pb, tpb = pbank()
                                for h in range(4):
                                    for kc in range(8):
                                        cc = which * 512 + h * 128
                                        op("pe", lambda e: e.matmul(pb[:, h * 128:(h + 1) * 128], lhsT=wmix[:, kc, cc:cc + 128], rhs=uts(kc),
                                                                    start=(kc == 0), stop=(kc == 7)),
                                           reads=t_wmix + [tu], writes=[tpb], signal=(kc == 7 and h == 3))
                                if which == 0:
                                    op("dve", lambda e: e.scalar_tensor_tensor(out=qd[b][:].rearrange("p h l -> p (h l)"), in0=pb[:, :], scalar=ALPHA,
                                                                               in1=Ep[:], op0=ALU.mult, op1=ALU.mult),
                                       reads=[tpb, t_Ep], writes=[t_qd[b]])
                                else:
                                    op("dve", lambda e: e.tensor_tensor(out=kd[b][:].rearrange("p h l -> p (h l)"), in0=pb[:, :], in1=Em[:], op=ALU.mult),
                                       reads=[tpb, t_Em], writes=[t_kd[b]])

                    def scan(dr, ti, full, i, own):
                        b = i % 2
                        if full:
                            for h in range(4):
                                op("pe", lambda e: e.matmul(ps[0][:, h * 128:(h + 1) * 128], lhsT=kd[b][:, h, :], rhs=qd[b][:, h, :], start=True, stop=True),
                                   reads=[t_kd[b], t_qd[b]], writes=[t_ps[0]], signal=(h == 3))
                        yield
                        if full:
                            op("dve", lambda e: e.tensor_tensor(out=wT[:].rearrange("p h l -> p (h l)"), in0=ps[0][:, :],
                                                                in1=mask4[dr][:].rearrange("p h l -> p (h l)"), op=ALU.mult),
                               reads=[t_ps[0], t_mask4], writes=[t_wT])
                        yield
                        if full:
                            for h in range(4):
                                pbk = ps[1 + h // 2]; tpbk = t_ps[1 + h // 2]
                                oc = (h % 2) * 256
                                op("pe", lambda e: e.matmul(pbk[:, oc:oc + 256], lhsT=wT[:, h, :], rhs=v[b][:, h * 256:(h + 1) * 256], start=True, stop=False),
                                   reads=[t_wT, t_v[b]], writes=[tpbk], signal=False)
                                op("pe", lambda e: e.matmul(pbk[:, oc:oc + 256], lhsT=qd[b][:, h, :], rhs=Sb[:, h, :], start=False, stop=True),
                                   reads=[t_qd[b], t_Sb], writes=[tpbk], signal=(h % 2 == 1))
                        for h in range(4):
                            pbk = ps[3 + h // 2]; tpbk = t_ps[3 + h // 2]
                            oc = (h % 2) * 256
                            op("pe", lambda e: e.matmul(pbk[:, oc:oc + 256], lhsT=kk[b][:, h * 128:(h + 1) * 128], rhs=v[b][:, h * 256:(h + 1) * 256],
                                                        start=True, stop=True),
                               reads=[t_kk[b], t_v[b]], writes=[tpbk], signal=(h % 2 == 1))
                        yield
                        for h in range(4):
                            pbk = ps[3 + h // 2]; tpbk = t_ps[3 + h // 2]
                            oc = (h % 2) * 256
                            op("dve", lambda e: e.scalar_tensor_tensor(out=S[:, h, :], in0=S[:, h, :], scalar=dec[b][:, h:h + 1], in1=pbk[:, oc:oc + 256],
                                                                       op0=ALU.mult, op1=ALU.add),
                               reads=[tpbk, t_dec[b]], writes=[t_S])
                        op("pool", lambda e: e.tensor_copy(out=Sb[:].rearrange("p h l -> p (h l)"), in_=S[:].rearrange("p h l -> p (h l)")),
                           reads=[t_S], writes=[t_Sb])
                        yield
                        if not full:
                            return
                        if dr == 0:
                            ob = ofl[b]
                            for half in range(2):
                                op("act", lambda e: e.copy(out=ob[:, half * 512:(half + 1) * 512], in_=ps[1 + half][:, :]),
                                   reads=[t_ps[1 + half]], writes=[t_ofl[b]])
                            k.dma("pool", of_scr[own * 128:(own + 1) * 128, :], ob[:], reads=[t_ofl[b]], writes=[t_of_scr[own]])

                        else:
                            for half in range(2):
                                op("dve", lambda e: e.tensor_tensor(out=osum[:, half * 512:(half + 1) * 512], in0=ps[1 + half][:, :],
                                                                    in1=ofl[b][:, half * 512:(half + 1) * 512], op=ALU.add),
                                   reads=[t_ps[1 + half], t_ofl[b]], writes=[t_osum])
                            for h in range(4):
                                op("act", lambda e: e.activation(out=junk[:], in_=osum[:, h * 256:(h + 1) * 256], func=AF.Square, accum_out=ssq[:, h:h + 1]),
                                   reads=[t_osum], writes=[t_junk, t_ssq])
                            op("act", lambda e: e.activation(out=ssq[:, 4:8], in_=ssq[:, 0:4], func=AF.Ln, bias=eps_c, scale=1.0 / 256),
                               reads=[t_ssq, t_cv], writes=[t_ssq])
                            op("act", lambda e: e.activation(out=ssq[:, 4:8], in_=ssq[:, 4:8], func=AF.Exp, scale=-0.5), reads=[t_ssq], writes=[t_ssq])
                            for h in range(4):
                                op("dve", lambda e: e.scalar_tensor_tensor(out=yb[b][:, h * 256:(h + 1) * 256], in0=osum[:, h * 256:(h + 1) * 256],
                                                                           scalar=ssq[:, 4 + h:5 + h], in1=Gg[b][:, h * 256:(h + 1) * 256],
                                                                           op0=ALU.mult, op1=ALU.mult),
                                   reads=[t_osum, t_ssq, t_Gg[b]], writes=[t_yb[b]])
                            k.dma("pool", ygla_scr[own * 128:(own + 1) * 128, :], yb[b][:], reads=[t_yb[b]], writes=[t_ygla_scr[own]])

                    for dr in range(2):
                        op("dve", lambda e: e.memset(S[:].rearrange("p h l -> p (h l)"), 0.0), writes=[t_S])
                        op("dve", lambda e: e.memset(Sb[:].rearrange("p h l -> p (h l)"), 0.0), writes=[t_Sb])
                        if dr == 0:
                            seq = [(0, False, None), (1, False, None)] + [(2 + lt, True, lt) for lt in range(NT_OWN)]
                        else:
                            seq = [(1, False, None), (0, False, None)] + [(2 + lt, lt < NT_OWN, lt if lt < NT_OWN else None)
                                                                          for lt in range(NT_LAT - 1, -1, -1)]
                        exhaust(prep(dr, seq[0][0], seq[0][1], 0))
                        for i, (ti, full, own) in enumerate(seq):
                            if dr == 1 and full:
                                k.dma("sp", ofl[i % 2][:], of_scr[own * 128:(own + 1) * 128, :], reads=[t_of_scr[own]], writes=[t_ofl[i % 2]])
                            P = prep(dr, seq[i + 1][0], seq[i + 1][1], i + 1) if i + 1 < len(seq) else iter(())
                            S_ = scan(dr, ti, full, i, own)
                            step(S_); step(S_)
                            step(P); step(P)
                            step(S_); step(S_)
                            step(P); step(P); step(P)
                            exhaust(P)
                            exhaust(S_)

            def mlstm_phase():
                print("mlstm start cnt", dict(k.cnt))
                with ExitStack() as st:
                    bg = alloc(st, "bg", [128, 16], F32); t_bg = Tok()
                    mnorm = alloc(st, "mnorm", [128, D], F32); t_mnorm = Tok()
                    Gt4 = [alloc(st, "Gt%d" % i, [128, 16], F32) for i in range(4)]; t_Gt4 = [Tok() for _ in range(4)]
                    sm4 = [alloc(st, "sm%d" % i, [128, 16], F32) for i in range(4)]; t_sm4 = [Tok() for _ in range(4)]
                    scall = alloc(st, "scall", [128, 34, 16], F32); t_scall = Tok()
                    accS = [alloc(st, "accS%d" % i, [128, 512], F32) for i in range(2)]; t_accS = [Tok(), Tok()]
                    halo = [alloc(st, "halo%d" % i, [128, 16], F32) for i in range(2)]; t_halo = [Tok(), Tok()]
                    qcS = [alloc(st, "qcS%d" % i, [128, 4, 512], BF16) for i in range(2)]; t_qcS = [Tok(), Tok()]
                    kcS = [alloc(st, "kcS%d" % i, [128, 4, 512], BF16) for i in range(2)]; t_kcS = [Tok(), Tok()]
                    kk = [alloc(st, "mkk%d" % i, [128, 512], BF16) for i in range(2)]; t_kk = [Tok(), Tok()]
                    v = [alloc(st, "mv%d" % i, [128, D], BF16) for i in range(2)]; t_v = [Tok(), Tok()]
                    Gg = [alloc(st, "mGg%d" % i, [128, D], F32) for i in range(2)]; t_Gg = [Tok(), Tok()]
                    wT = alloc(st, "mwT", [128, 4, 128], BF16); t_wT = Tok()
                    C = alloc(st, "C", [128, 4, 256], F32); t_C = Tok()
                    Cb = alloc(st, "Cb", [128, 4, 256], BF16); t_Cb = Tok()
                    nst = alloc(st, "nst", [128, 4], F32); t_n = Tok()
                    nb = alloc(st, "nb", [128, 4], BF16); t_nb = Tok()
                    rr = alloc(st, "rr", [128, 16], F32); t_rr = Tok()
                    hfl = [alloc(st, "hfl%d" % i, [128, D], F32) for i in range(2)]; t_hfl = [Tok(), Tok()]
                    hs = alloc(st, "hs", [128, D], F32); t_hs = Tok()
                    junk = alloc(st, "junkM", [128, 256], F32); t_junk = Tok()
                    ssq = alloc(st, "ssqM", [128, 8], F32); t_ssq = Tok()
                    yb = [alloc(st, "myb%d" % i, [128, D], BF16) for i in range(2)]; t_yb = [Tok(), Tok()]
                    smb = ps[5]
                    t_smg = t_smb = t_ps[5]
                    sms = ps[4]
                    t_smden = t_smkvn = t_ps[4]
                    mprep = [5, 0]
                    mrr = [0]

                    def mbank():
                        i = mprep[mrr[0] % 2]
                        mrr[0] += 1
                        return ps[i], t_ps[i]

                    k.dma("sp", bg[:], vbc_d[0:1, 5120:5136].to_broadcast([128, 16]), writes=[t_bg])
                    k.dma("sp", mnorm[:], vbc_d[0:1, 4096:5120].to_broadcast([128, 1024]), writes=[t_mnorm])
                    op("dve", lambda e: e.tensor_scalar(out=mnorm[:], in0=mnorm[:], scalar1=0.5, scalar2=None, op0=ALU.mult), writes=[t_mnorm])
                    ymv = ym_scr.rearrange("(r c) d -> c r d", c=64)

                    def convq(gb, c0, N, tiles):
                        rd = t_wmix + [t_guard] + [t_uT[t_] for t_ in tiles]
                        lo, hi = min(tiles), max(tiles)
                        if lo not in (0, 2):
                            rd.append(t_uT[lo - 1])
                        if hi not in (1, 33):
                            rd.append(t_uT[hi + 1])
                        for ci in range(8):
                            h0 = 48 + 2 * ci
                            for kc in range(8):
                                op("pe", lambda e: e.matmul(sms[:, h0:h0 + 2], lhsT=wmix[:, kc, ci * 128:(ci + 1) * 128], rhs=uT[:, kc, c0 - 1:c0 + N + 1:N + 1],
                                                            start=(kc == 0), stop=(kc == 7)),
                                   reads=rd, writes=[t_ps[4]], signal=(kc == 7 and ci == 7))
                        op("dve", lambda e: e.tensor_copy(out=halo[gb][:], in_=sms[:, 48:64]), reads=[t_ps[4]], writes=[t_halo[gb]])
                        cwf = lambda ci, tap: vpp[:, VP_CW + tap * 8 + ci:VP_CW + tap * 8 + ci + 1]
                        banks = {}

                        def stA(ci):
                            pb, tpb = ps[6 + ci % 2], t_ps[6 + ci % 2]
                            banks[ci] = (pb, tpb)
                            for kc in range(8):
                                op("pe", lambda e: e.matmul(pb[:, 0:N], lhsT=wmix[:, kc, ci * 128:(ci + 1) * 128], rhs=uT[:, kc, c0:c0 + N],
                                                            start=(kc == 0), stop=(kc == 7)),
                                   reads=rd, writes=[tpb], signal=(kc == 7))
                            acc, ta = accS[ci % 2], t_accS[ci % 2]
                            op("act", lambda e: e.activation(out=acc[:, 0:N], in_=pb[:, 0:N], func=AF.Identity, scale=cwf(ci, 1), bias=vpp[:, VP_CB + ci:VP_CB + ci + 1]),
                               reads=[tpb, t_vpp], writes=[ta])

                        def stB(ci):
                            pb, tpb = banks[ci]
                            acc, ta = accS[ci % 2], t_accS[ci % 2]
                            h0 = 2 * ci
                            op("dve", lambda e: e.scalar_tensor_tensor(out=acc[:, 1:N], in0=pb[:, 0:N - 1], scalar=cwf(ci, 0), in1=acc[:, 1:N], op0=ALU.mult, op1=ALU.add),
                               reads=[tpb, t_vpp], writes=[ta])
                            op("dve", lambda e: e.scalar_tensor_tensor(out=acc[:, 0:N - 1], in0=pb[:, 1:N], scalar=cwf(ci, 2), in1=acc[:, 0:N - 1], op0=ALU.mult, op1=ALU.add),
                               reads=[tpb, t_vpp], writes=[ta])
                            op("dve", lambda e: e.scalar_tensor_tensor(out=acc[:, 0:1], in0=halo[gb][:, h0:h0 + 1], scalar=cwf(ci, 0), in1=acc[:, 0:1], op0=ALU.mult, op1=ALU.add),
                               reads=[t_halo[gb], t_vpp], writes=[ta])
                            op("dve", lambda e: e.scalar_tensor_tensor(out=acc[:, N - 1:N], in0=halo[gb][:, h0 + 1:h0 + 2], scalar=cwf(ci, 2), in1=acc[:, N - 1:N], op0=ALU.mult, op1=ALU.add),
                               reads=[t_halo[gb], t_vpp], writes=[ta])

                        def stC(ci):
                            acc, ta = accS[ci % 2], t_accS[ci % 2]
                            dstt, tdst = (qcS[gb], t_qcS[gb]) if ci < 4 else (kcS[gb], t_kcS[gb])
                            op("act", lambda e: e.activation(out=dstt[:, ci % 4, 0:N], in_=acc[:, 0:N], func=AF.Silu), reads=[ta], writes=[tdst])

                        stA(0)
                        for ci in range(8):
                            if ci + 1 < 8:
                                stA(ci + 1)
                            stB(ci)
                            stC(ci)
                            yield

                    def gates_prologue(dr, seq):
                        for i, (ti, full, lt) in enumerate(seq):
                            c0 = tile_col(ti)
                            tu = t_uT[ti]
                            pbk, tpbk = ps[4 + i % 4], t_ps[4 + i % 4]
                            Gt_, sm_ = Gt4[i % 4], sm4[i % 4]
                            tG, tS = t_Gt4[i % 4], t_sm4[i % 4]
                            for kc in range(8):
                                op("pe", lambda e: e.matmul(pbk[:, 0:16], lhsT=uT[:, kc, c0:c0 + 128], rhs=wmix[:, kc, 3072:3088], start=(kc == 0), stop=(kc == 7)),
                                   reads=t_wmix + [tu], writes=[tpbk], signal=(kc == 7))
                            op("dve", lambda e: e.tensor_tensor(out=Gt_[:], in0=pbk[:, 0:16], in1=bg[:], op=ALU.add), reads=[tpbk, t_bg], writes=[tG])
                            ig = Gt_[:, 8 * dr:8 * dr + 4]
                            fg = Gt_[:, 8 * dr + 4:8 * dr + 8]
                            op("act", lambda e: e.activation(out=sm_[:, 0:4], in_=fg, func=AF.Exp, scale=-1.0), reads=[tG], writes=[tS])
                            op("act", lambda e: e.activation(out=sm_[:, 4:8], in_=sm_[:, 0:4], func=AF.Ln, bias=one_c, scale=1.0),
                               reads=[tS, t_cv], writes=[tS])
                            op("pe", lambda e: e.matmul(pbk[:, 16:20], lhsT=MI[dr], rhs=sm_[:, 4:8], start=True, stop=True),
                               reads=[t_cst, tS], writes=[tpbk], signal=False)
                            op("pe", lambda e: e.matmul(pbk[:, 20:24], lhsT=ones, rhs=sm_[:, 4:8], start=True, stop=True),
                               reads=[t_cst, tS], writes=[tpbk])
                            op("act", lambda e: e.activation(out=scall[:, i, 0:8], in_=pbk[:, 16:24], func=AF.Exp, scale=-1.0), reads=[tpbk], writes=[t_scall])
                            op("dve", lambda e: e.tensor_tensor(out=sm_[:, 8:12], in0=ig, in1=pbk[:, 16:20], op=ALU.add), reads=[tG, tpbk], writes=[tS])
                            op("dve", lambda e: e.tensor_tensor(out=sm_[:, 12:16], in0=sm_[:, 8:12], in1=pbk[:, 20:24], op=ALU.subtract),
                               reads=[tS, tpbk], writes=[tS])
                            op("act", lambda e: e.activation(out=scall[:, i, 8:16], in_=sm_[:, 8:16], func=AF.Exp, bias=lna_c, scale=1.0),
                               reads=[tS, t_cv], writes=[t_scall])

                    def prep(dr, ti, full, i, gb, off):
                        b = i % 2
                        c0 = tile_col(ti)
                        tu = t_uT[ti]
                        uts = lambda kc: uT[:, kc, c0:c0 + 128]
                        qc_ = lambda h: qcS[gb][:, h, off:off + 128]
                        kc__ = lambda h: kcS[gb][:, h, off:off + 128]
                        s_ = scall[:, i, :]
                        pb, tpb = mbank()
                        pbb = pb[:, 0:256].bitcast(BF16)
                        for h in range(4):
                            op("pe", lambda e: e.transpose(out=pbb[:, h * 128:(h + 1) * 128], in_=kc__(h), identity=identb[:]),
                               reads=[t_kcS[gb], t_identb], writes=[tpb], signal=(h == 3))
                        for h in range(4):
                            op("dve", lambda e: e.tensor_scalar(out=kk[b][:, h * 128:(h + 1) * 128], in0=pbb[:, h * 128:(h + 1) * 128],
                                                                scalar1=s_[:, 12 + h:13 + h], scalar2=None, op0=ALU.mult),
                               reads=[tpb, t_scall], writes=[t_kk[b]])
                        yield
                        for half in range(2):
                            pb, tpb = mbank()
                            for kc in range(8):
                                op("pe", lambda e: e.matmul(pb[:, :], lhsT=uts(kc), rhs=wmix[:, kc, 1024 + half * 512:1536 + half * 512],
                                                            start=(kc == 0), stop=(kc == 7)),
                                   reads=t_wmix + [tu], writes=[tpb], signal=(kc == 7))
                            op("act", lambda e: e.copy(out=v[b][:, half * 512:(half + 1) * 512], in_=pb[:, :]), reads=[tpb], writes=[t_v[b]])
                        yield
                        if full and dr == 1:
                            for half in range(2):
                                pb, tpb = mbank()
                                for kc in range(8):
                                    op("pe", lambda e: e.matmul(pb[:, :], lhsT=uts(kc), rhs=wmix[:, kc, 2048 + half * 512:2560 + half * 512],
                                                                start=(kc == 0), stop=(kc == 7)),
                                       reads=t_wmix + [tu], writes=[tpb], signal=(kc == 7))
                                op("act", lambda e: e.activation(out=Gg[b][:, half * 512:(half + 1) * 512], in_=pb[:, :], func=AF.Tanh, scale=0.5),
                                   reads=[tpb], writes=[t_Gg[b]])
                            op("dve", lambda e: e.scalar_tensor_tensor(out=Gg[b][:], in0=Gg[b][:], scalar=1.0, in1=mnorm[:], op0=ALU.add, op1=ALU.mult),
                               reads=[t_Gg[b], t_mnorm], writes=[t_Gg[b]])

                    def scan(dr, ti, full, i, lt, gb, off):
                        b = i % 2
                        s_ = scall[:, i, :]
                        qc_ = lambda h: qcS[gb][:, h, off:off + 128]
                        kc__ = lambda h: kcS[gb][:, h, off:off + 128]
                        if full:
                            for h in range(4):
                                op("pe", lambda e: e.matmul(ps[0][:, h * 128:(h + 1) * 128], lhsT=kc__(h), rhs=qc_(h), start=True, stop=True),
                                   reads=[t_kcS[gb], t_qcS[gb]], writes=[t_ps[0]], signal=(h == 3))
                        yield
                        if full:
                            for h in range(4):
                                op("dve", lambda e: e.scalar_tensor_tensor(out=wT[:, h, :], in0=ps[0][:, h * 128:(h + 1) * 128], scalar=s_[:, 8 + h:9 + h],
                                                                           in1=MI[dr], op0=ALU.mult, op1=ALU.mult),
                                   reads=[t_ps[0], t_scall, t_cst], writes=[t_wT])
                        yield
                        if full:
                            for h in range(4):
                                pbk = ps[1 + h // 2]; tpbk = t_ps[1 + h // 2]
                                oc = (h % 2) * 256
                                op("pe", lambda e: e.matmul(pbk[:, oc:oc + 256], lhsT=wT[:, h, :], rhs=v[b][:, h * 256:(h + 1) * 256], start=True, stop=False),
                                   reads=[t_wT, t_v[b]], writes=[tpbk], signal=False)
                                op("pe", lambda e: e.matmul(pbk[:, oc:oc + 256], lhsT=qc_(h), rhs=Cb[:, h, :], start=False, stop=True),
                                   reads=[t_qcS[gb], t_Cb], writes=[tpbk], signal=(h % 2 == 1))
                            for h in range(4):
                                op("pe", lambda e: e.matmul(sms[:, 32 + h:33 + h], lhsT=wT[:, h, :], rhs=onesb[:, 0:1], start=True, stop=False),
                                   reads=[t_wT, t_onesb], writes=[t_smden], signal=False)
                                op("pe", lambda e: e.matmul(sms[:, 32 + h:33 + h], lhsT=qc_(h), rhs=nb[:, h:h + 1], start=False, stop=True),
                                   reads=[t_qcS[gb], t_nb], writes=[t_smden], signal=(h == 3))
                        def kvmm(hh):
                            for h in hh:
                                oc = (h % 2) * 256
                                op("pe", lambda e: e.matmul(ps[3][:, oc:oc + 256], lhsT=kk[b][:, h * 128:(h + 1) * 128], rhs=v[b][:, h * 256:(h + 1) * 256],
                                                            start=True, stop=True),
                                   reads=[t_kk[b], t_v[b]], writes=[t_ps[3]], signal=(h % 2 == 1))

                        def cupd(hh):
                            for h in hh:
                                oc = (h % 2) * 256
                                op("dve", lambda e: e.scalar_tensor_tensor(out=C[:, h, :], in0=C[:, h, :], scalar=s_[:, 4 + h:5 + h], in1=ps[3][:, oc:oc + 256],
                                                                           op0=ALU.mult, op1=ALU.add),
                                   reads=[t_ps[3], t_scall], writes=[t_C])
                        kvmm((0, 1))
                        for h in range(4):
                            op("pe", lambda e: e.matmul(sms[:, 40 + h:41 + h], lhsT=kk[b][:, h * 128:(h + 1) * 128], rhs=onesb[:, 0:1], start=True, stop=True),
                               reads=[t_kk[b], t_onesb], writes=[t_smkvn], signal=(h == 3))
                        yield
                        cupd((0, 1))
                        yield
                        kvmm((2, 3))
                        yield
                        cupd((2, 3))
                        op("dve", lambda e: e.tensor_tensor(out=nst[:], in0=nst[:], in1=s_[:, 4:8], op=ALU.mult), reads=[t_scall], writes=[t_n])
                        op("dve", lambda e: e.tensor_tensor(out=nst[:], in0=nst[:], in1=sms[:, 40:44], op=ALU.add), reads=[t_smkvn], writes=[t_n])
                        op("pool", lambda e: e.tensor_copy(out=Cb[:].rearrange("p h l -> p (h l)"), in_=C[:].rearrange("p h l -> p (h l)")),
                           reads=[t_C], writes=[t_Cb])
                        op("pool", lambda e: e.tensor_copy(out=nb[:], in_=nst[:]), reads=[t_n], writes=[t_nb])
                        yield
                        if not full:
                            return
                        op("dve", lambda e: e.tensor_tensor(out=rr[:, 0:4], in0=s_[:, 0:4], in1=sms[:, 32:36], op=ALU.mult),
                           reads=[t_scall, t_smden], writes=[t_rr])
                        op("dve", lambda e: e.tensor_scalar(out=rr[:, 4:8], in0=rr[:, 0:4], scalar1=-1.0, scalar2=None, op0=ALU.mult),
                           reads=[t_rr], writes=[t_rr])
                        op("dve", lambda e: e.tensor_tensor(out=rr[:, 4:8], in0=rr[:, 4:8], in1=rr[:, 0:4], op=ALU.max),
                           reads=[t_rr], writes=[t_rr])
                        op("dve", lambda e: e.tensor_scalar(out=rr[:, 4:8], in0=rr[:, 4:8], scalar1=1.0, scalar2=None, op0=ALU.max),
                           reads=[t_rr], writes=[t_rr])
                        op("dve", lambda e: e.reciprocal(out=rr[:, 8:12], in_=rr[:, 4:8]), reads=[t_rr], writes=[t_rr])
                        op("dve", lambda e: e.tensor_tensor(out=rr[:, 12:16], in0=rr[:, 8:12], in1=s_[:, 0:4], op=ALU.mult),
                           reads=[t_rr, t_scall], writes=[t_rr])
                        if dr == 0:
                            ob = hfl[b]
                            for h in range(4):
                                pbk = ps[1 + h // 2]; tpbk = t_ps[1 + h // 2]
                                oc = (h % 2) * 256
                                op("act", lambda e: e.activation(out=ob[:, h * 256:(h + 1) * 256], in_=pbk[:, oc:oc + 256], func=AF.Copy, scale=rr[:, 12 + h:13 + h]),
                                   reads=[tpbk, t_rr], writes=[t_hfl[b]])
                            k.dma("pool", hf_scr[lt * 128:(lt + 1) * 128, :], ob[:], reads=[t_hfl[b]], writes=[t_hf_scr[lt]])
                        else:
                            for h in range(4):
                                pbk = ps[1 + h // 2]; tpbk = t_ps[1 + h // 2]
                                oc = (h % 2) * 256
                                op("dve", lambda e: e.scalar_tensor_tensor(out=hs[:, h * 256:(h + 1) * 256], in0=pbk[:, oc:oc + 256], scalar=rr[:, 12 + h:13 + h],
                                                                           in1=hfl[b][:, h * 256:(h + 1) * 256], op0=ALU.mult, op1=ALU.add),
                                   reads=[tpbk, t_rr, t_hfl[b]], writes=[t_hs])
                            for h in range(4):
                                op("act", lambda e: e.activation(out=junk[:], in_=hs[:, h * 256:(h + 1) * 256], func=AF.Square, accum_out=ssq[:, h:h + 1]),
                                   reads=[t_hs], writes=[t_junk, t_ssq])
                            op("act", lambda e: e.activation(out=ssq[:, 4:8], in_=ssq[:, 0:4], func=AF.Ln, bias=eps_c, scale=1.0 / 256),
                               reads=[t_ssq, t_cv], writes=[t_ssq])
                            op("act", lambda e: e.activation(out=ssq[:, 4:8], in_=ssq[:, 4:8], func=AF.Exp, scale=-0.5), reads=[t_ssq], writes=[t_ssq])
                            for h in range(4):
                                op("dve", lambda e: e.scalar_tensor_tensor(out=yb[b][:, h * 256:(h + 1) * 256], in0=hs[:, h * 256:(h + 1) * 256],
                                                                           scalar=ssq[:, 4 + h:5 + h], in1=Gg[b][:, h * 256:(h + 1) * 256],
                                                                           op0=ALU.mult, op1=ALU.mult),
                                   reads=[t_hs, t_ssq, t_Gg[b]], writes=[t_yb[b]])
                            for cl in range(2):
                                k.dma("pool", ymv[2 * lt + cl], yb[b][cl * 64:(cl + 1) * 64, :], reads=[t_yb[b]], writes=[t_ym_scr])

                    for dr in range(2):
                        op("dve", lambda e: e.memset(C[:].rearrange("p h l -> p (h l)"), 0.0), writes=[t_C])
                        op("dve", lambda e: e.memset(Cb[:].rearrange("p h l -> p (h l)"), 0.0), writes=[t_Cb])
                        op("dve", lambda e: e.memset(nst[:], 0.0), writes=[t_n])
                        op("dve", lambda e: e.memset(nb[:], 0.0), writes=[t_nb])
                        if dr == 0:
                            seq = [(0, False, None), (1, False, None)] + [(2 + lt, True, lt) for lt in range(NT_LAT)]
                        else:
                            seq = [(1, False, None), (0, False, None)] + [(2 + lt, True, lt) for lt in range(NT_LAT - 1, -1, -1)]
                        groups = [seq[0:2]] + [seq[2 + 4 * g_:6 + 4 * g_] for g_ in range(NT_LAT // 4)]
                        ginfo = []
                        tinfo = []
                        for gi_, grp in enumerate(groups):
                            tiles = [t_[0] for t_ in grp]
                            c0g = min(tile_col(t_) for t_ in tiles)
                            ginfo.append((gi_ % 2, c0g, 128 * len(tiles), tiles))
                            for t_ in tiles:
                                tinfo.append((gi_, gi_ % 2, tile_col(t_) - c0g))
                        PQ = [convq(*gi__) for gi__ in ginfo]
                        gates_prologue(dr, seq)
                        exhaust(PQ[0])
                        exhaust(prep(dr, seq[0][0], seq[0][1], 0, tinfo[0][1], tinfo[0][2]))
                        for i, (ti, full, lt) in enumerate(seq):
                            if dr == 1 and full:
                                k.dma("sp", hfl[i % 2][:], hf_scr[lt * 128:(lt + 1) * 128, :], reads=[t_hf_scr[lt]], writes=[t_hfl[i % 2]])
                            gcur = tinfo[i][0]
                            nxt = PQ[gcur + 1] if gcur + 1 < len(PQ) else iter(())
                            nch = 8 // len(groups[gcur])
                            P = prep(dr, seq[i + 1][0], seq[i + 1][1], i + 1, tinfo[i + 1][1], tinfo[i + 1][2]) if i + 1 < len(seq) else iter(())
                            S_ = scan(dr, ti, full, i, lt, tinfo[i][1], tinfo[i][2])
                            step(S_); step(S_)
                            for _ in range(nch // 2):
                                step(nxt)
                            step(S_); step(S_)
                            for _ in range(nch - nch // 2):
                                step(nxt)
                            step(S_); step(S_)
                            exhaust(P)
                            exhaust(S_)

            k.barrier()
            load_w(wmix, t_wmix, win_d, 0, 3104)
            phase_U(False)
            k.barrier()
            if stage >= 1:
                gla_phase()
                k.barrier()
            if stage >= 2:
                load_w(wmix, t_wmix, win_d, 3104, 6192)
                phase_U(True)
                k.barrier()
                mlstm_phase()
                k.barrier()


        tb = [0]

        def tbank():
            i = tb[0] % 8
            tb[0] += 1
            return ps[i], t_ps[i]

        def load_w2(dst, t_dst, src, c0, c1, nk):
            for kc in range(nk):
                k.dma("pool", dst[:, kc, 0:c1 - c0], src[kc * 128:(kc + 1) * 128, c0:c1], writes=[t_dst])

        def tail_a(prefetch):
            with ExitStack() as st:
                wgt = alloc(st, "wgt", [128, 8, 2048], BF16)
                wbg = alloc(st, "wbg", [128, 8, D], BF16)
                wbm = alloc(st, "wbm", [128, 8, D], BF16)
                wo = alloc(st, "wo", [128, 8, D], BF16)
                xs = [alloc(st, "txs%d" % i, [128, D], F32) for i in range(2)]; t_xs = [Tok(), Tok()]
                xn = alloc(st, "txn", [128, D], F32); t_xn = Tok()
                small = alloc(st, "tsmall", [128, 2], F32); t_small = Tok()
                uTt = alloc(st, "uTt", [128, 8, 128], BF16); t_uTt = Tok()
                ygt = [alloc(st, "ygt%d" % i, [128, D], BF16) for i in range(2)]; t_ygt = [Tok(), Tok()]
                ymt = [alloc(st, "ymt%d" % i, [128, D], BF16) for i in range(2)]; t_ymt = [Tok(), Tok()]
                yTg = alloc(st, "yTg", [128, 8, 128], BF16); t_yTg = Tok()
                yTm = alloc(st, "yTm", [128, 8, 128], BF16); t_yTm = Tok()
                gsig = alloc(st, "gsig", [128, 2048], F32); t_gsig = Tok()
                ysg = alloc(st, "ysg", [128, D], F32); t_ysg = Tok()
                tmpm = alloc(st, "tmpm", [128, D], F32); t_tmpm = Tok()
                junk, t_junk = tmpm, t_tmpm
                ysb = alloc(st, "ysb", [128, D], BF16); t_ysb = Tok()
                ysT = alloc(st, "ysT", [128, 8, 128], BF16); t_ysT = Tok()
                x1t = [alloc(st, "x1t%d" % i, [128, D], F32) for i in range(2)]; t_x1t = [Tok(), Tok()]
                stagings = [(gsig[:, 0:1024], Tok()), (gsig[:, 1024:2048], Tok()), (ysg[:, :], Tok()), (tmpm[:, :], Tok())]
                pieces = []
                t_wgt, t_wbg, t_wbm, t_wo = [], [], [], []
                for kc in range(8):
                    for half in range(2):
                        tk = Tok(); t_wgt.append(tk)
                        pieces.append((wgt[:, kc, half * 1024:(half + 1) * 1024], win_d[kc * 128:(kc + 1) * 128, 6192 + half * 1024:6192 + (half + 1) * 1024], tk))
                for wt_, wd_, tl_ in ((wbg, wbg_d, t_wbg), (wbm, wbm_d, t_wbm), (wo, wo_d, t_wo)):
                    for kc in range(8):
                        tk = Tok(); tl_.append(tk)
                        pieces.append((wt_[:, kc, :], wd_[kc * 128:(kc + 1) * 128, :], tk))
                cengs = ["dve", "act", "pool"]
                for j, (dst, src, tk) in enumerate(pieces):
                    stg, tstg = stagings[j % 4]
                    k.dma("sp", stg, src, writes=[tstg])
                    ce = cengs[j % 3]
                    if ce == "act":
                        op("act", lambda e: e.copy(out=dst, in_=stg), reads=[tstg], writes=[tk])
                    else:
                        op(ce, lambda e: e.tensor_copy(out=dst, in_=stg), reads=[tstg], writes=[tk])
                for real, stgs in ((t_gsig, (stagings[0][1], stagings[1][1])), (t_ysg, (stagings[2][1],)), (t_tmpm, (stagings[3][1],))):
                    mr = {}
                    for ts in stgs:
                        if ts.w is not None:
                            mr[ts.w[0]] = max(mr.get(ts.w[0], 0), ts.w[1])
                        for s_k, v_k in ts.r.items():
                            mr[s_k] = max(mr.get(s_k, 0), v_k)
                    real.w = None
                    real.r = mr

                def tr8(src, t_src, dst, t_dst):
                    pb, tpb = tbank()
                    pbb = pb[:, :].bitcast(BF16)
                    for kc in range(8):
                        op("pe", lambda e: e.transpose(out=pbb[:, kc * 128:(kc + 1) * 128], in_=src[:, kc * 128:(kc + 1) * 128], identity=identb[:]),
                           reads=[t_src, t_identb], writes=[tpb], signal=(kc == 7))
                    op("act", lambda e: e.copy(out=dst[:].rearrange("p a b -> p (a b)"), in_=pbb[:, :]), reads=[tpb], writes=[t_dst])

                for i in range(NT_OWN):
                    b = i % 2
                    k.dma("sp", xs[b][:], x_d[i * 128:(i + 1) * 128, :], writes=[t_xs[b]])
                    k.dma("sp", ygt[b][:], ygla_scr[i * 128:(i + 1) * 128, :], reads=[t_ygla_scr[i]], writes=[t_ygt[b]])
                    k.dma("sp", ymt[b][:], ym_scr[i * 128:(i + 1) * 128, :], reads=[t_ym_scr], writes=[t_ymt[b]])
                    norm_transpose(xs[b][:], t_xs[b], xn, t_xn, small, t_small, junk, t_junk,
                                   lambda kc: scU[:, kc, 0:1], lambda kc: modpp[:, 0, kc, 0:1], [t_scU, t_modpp],
                                   lambda kc: uTt[:, kc, :], t_uTt)
                    for q in range(4):
                        pb, tpb = tbank()
                        for kc in range(8):
                            op("pe", lambda e: e.matmul(pb[:, :], lhsT=uTt[:, kc, :], rhs=wgt[:, kc, q * 512:(q + 1) * 512], start=(kc == 0), stop=(kc == 7)),
                               reads=[t_uTt] + t_wgt, writes=[tpb], signal=(kc == 7))
                        op("act", lambda e: e.activation(out=gsig[:, q * 512:(q + 1) * 512], in_=pb[:, :], func=AF.Sigmoid), reads=[tpb], writes=[t_gsig])
                    tr8(ygt[b], t_ygt[b], yTg, t_yTg)
                    tr8(ymt[b], t_ymt[b], yTm, t_yTm)
                    for half in range(2):
                        pb, tpb = tbank()
                        for kc in range(8):
                            op("pe", lambda e: e.matmul(pb[:, :], lhsT=yTg[:, kc, :], rhs=wbg[:, kc, half * 512:(half + 1) * 512], start=(kc == 0), stop=(kc == 7)),
                               reads=[t_yTg] + t_wbg, writes=[tpb], signal=(kc == 7))
                        op("dve", lambda e: e.tensor_tensor(out=ysg[:, half * 512:(half + 1) * 512], in0=pb[:, :], in1=gsig[:, half * 512:(half + 1) * 512], op=ALU.mult),
                           reads=[tpb, t_gsig], writes=[t_ysg])
                    for half in range(2):
                        pb, tpb = tbank()
                        for kc in range(8):
                            op("pe", lambda e: e.matmul(pb[:, :], lhsT=yTm[:, kc, :], rhs=wbm[:, kc, half * 512:(half + 1) * 512], start=(kc == 0), stop=(kc == 7)),
                               reads=[t_yTm] + t_wbm, writes=[tpb], signal=(kc == 7))
                        op("dve", lambda e: e.tensor_tensor(out=tmpm[:, half * 512:(half + 1) * 512], in0=pb[:, :], in1=gsig[:, 1024 + half * 512:1536 + half * 512], op=ALU.mult),
                           reads=[tpb, t_gsig], writes=[t_tmpm])
                    op("dve", lambda e: e.tensor_tensor(out=ysb[:], in0=ysg[:], in1=tmpm[:], op=ALU.add), reads=[t_ysg, t_tmpm], writes=[t_ysb])
                    tr8(ysb, t_ysb, ysT, t_ysT)
                    for half in range(2):
                        pb, tpb = tbank()
                        for kc in range(8):
                            op("pe", lambda e: e.matmul(pb[:, :], lhsT=ysT[:, kc, :], rhs=wo[:, kc, half * 512:(half + 1) * 512], start=(kc == 0), stop=(kc == 7)),
                               reads=[t_ysT] + t_wo, writes=[tpb], signal=(kc == 7))
                        op("dve", lambda e: e.tensor_tensor(out=x1t[b][:, half * 512:(half + 1) * 512], in0=pb[:, :], in1=G1[:, half * 512:(half + 1) * 512], op=ALU.mult),
                           reads=[tpb, t_G1], writes=[t_x1t[b]])
                    op("dve", lambda e: e.tensor_tensor(out=x1t[b][:], in0=x1t[b][:], in1=xs[b][:], op=ALU.add), reads=[t_xs[b]], writes=[t_x1t[b]])
                    k.dma("pool", x1_scr[i * 128:(i + 1) * 128, :], x1t[b][:], reads=[t_x1t[b]], writes=[t_x1_scr[i]])
                    for _ in range(3):
                        if prefetch:
                            prefetch.pop(0)()

        def tail_b(wfo, t_wfo, wfi0, t_wfi0, prefetch):
            with ExitStack() as st:
                while prefetch:
                    prefetch.pop(0)()
                wfiR = alloc(st, "wfiR", [128, 8, 2 * (DFF - 384)], BF16)

                def wcol(part, fc):
                    if fc < 3:
                        return wfi0, part * 384 + fc * 128
                    return wfiR, part * (DFF - 384) + (fc - 3) * 128
                fgbc = alloc(st, "fgbc", [128, D], F32); t_fgbc = Tok()
                x1t = [alloc(st, "fx1t%d" % i, [128, D], F32) for i in range(4)]; t_x1t = [Tok() for _ in range(4)]
                xn = alloc(st, "fxn", [128, D], F32); t_xn = Tok()
                small = alloc(st, "fsmall", [128, 4], F32); t_small = Tok()
                u2T = alloc(st, "u2T", [128, 8, 512], BF16); t_u2T = Tok()
                sa = [alloc(st, "sa%d" % i, [128, 512], F32) for i in range(1)]; t_sa = [Tok()]
                hT = alloc(st, "hT", [128, 22, 512], BF16); t_hT = Tok()
                x2 = alloc(st, "x2", [128, D], F32); t_x2 = Tok()
                k.dma("sp", fgbc[:], vbc_d[0:1, 2048:3072].to_broadcast([128, 1024]), writes=[t_fgbc])
                t_wfib = [[t_wfi0], [], [], []]
                fcb = [0, 3, 9, 15, 22]
                hTf = hT[:].rearrange("p a b -> p (a b)").bitcast(F32)
                stg_t = [Tok() for _ in range(5)]
                cengs = ["dve", "act", "pool"]
                j = 0
                for blk in range(1, 4):
                    f0, f1 = fcb[blk] * 128, fcb[blk + 1] * 128
                    for part in range(2):
                        for kc in range(8):
                            c_ = part * (DFF - 384) + f0 - 384
                            n_ = f1 - f0
                            stg = hTf[:, (j % 5) * 1024:(j % 5) * 1024 + n_]
                            tstg = stg_t[j % 5]
                            tk = Tok()
                            t_wfib[blk].append(tk)
                            k.dma("sp", stg, wfi_d[kc * 128:(kc + 1) * 128, part * DFF + f0:part * DFF + f1], writes=[tstg])
                            dst = wfiR[:, kc, c_:c_ + n_]
                            ce = cengs[j % 3]
                            if ce == "act":
                                op("act", lambda e: e.copy(out=dst, in_=stg), reads=[tstg], writes=[tk])
                            else:
                                op(ce, lambda e: e.tensor_copy(out=dst, in_=stg), reads=[tstg], writes=[tk])
                            j += 1
                mr = {}
                for ts in stg_t:
                    if ts.w is not None:
                        mr[ts.w[0]] = max(mr.get(ts.w[0], 0), ts.w[1])
                    for s_k, v_k in ts.r.items():
                        mr[s_k] = max(mr.get(s_k, 0), v_k)
                t_hT.w = None
                t_hT.r = mr
                for s_i in range(NT_OWN // 4):
                    for j in range(4):
                        ti = s_i * 4 + j
                        k.dma("sp", x1t[j][:], x1_scr[ti * 128:(ti + 1) * 128, :], reads=[t_x1_scr[ti]], writes=[t_x1t[j]])
                        norm_transpose(x1t[j][:], t_x1t[j], xn, t_xn, small, t_small, x2, t_x2,
                                       lambda kc: sc2[:, kc:kc + 1], lambda kc: modpp[:, 2, kc, 0:1], [t_sc2, t_modpp],
                                       lambda kc: u2T[:, kc, j * 128:(j + 1) * 128], t_u2T)
                    for fc in range(22):
                        pa, tpa = tbank()
                        for kc in range(8):
                            wt_, wc_ = wcol(0, fc)
                            op("pe", lambda e: e.matmul(pa[:, :], lhsT=wt_[:, kc, wc_:wc_ + 128], rhs=u2T[:, kc, :], start=(kc == 0), stop=(kc == 7)),
                               reads=t_wfib[0 if fc < 3 else (1 if fc < 9 else (2 if fc < 15 else 3))] + [t_u2T], writes=[tpa], signal=(kc == 7))
                        pbk, tpbk = tbank()
                        for kc in range(8):
                            wt_, wc_ = wcol(1, fc)
                            op("pe", lambda e: e.matmul(pbk[:, :], lhsT=wt_[:, kc, wc_:wc_ + 128], rhs=u2T[:, kc, :], start=(kc == 0), stop=(kc == 7)),
                               reads=t_wfib[0 if fc < 3 else (1 if fc < 9 else (2 if fc < 15 else 3))] + [t_u2T], writes=[tpbk], signal=(kc == 7))
                        sb_ = 0
                        op("act", lambda e: e.activation(out=sa[sb_][:], in_=pa[:, :], func=AF.Silu), reads=[tpa], writes=[t_sa[sb_]])
                        op("dve", lambda e: e.tensor_tensor(out=hT[:, fc, :], in0=sa[sb_][:], in1=pbk[:, :], op=ALU.mult),
                           reads=[t_sa[sb_], tpbk], writes=[t_hT])
                    junkv = u2T[:].rearrange("p a b -> p (a b)").bitcast(F32)[:, 0:1024]
                    for j in range(4):
                        ti = s_i * 4 + j
                        xb, txb = (x2, t_x2) if j % 2 == 0 else (xn, t_xn)
                        for half in range(2):
                            pb, tpb = tbank()
                            for fc in range(22):
                                op("pe", lambda e: e.matmul(pb[:, :], lhsT=hT[:, fc, j * 128:(j + 1) * 128], rhs=wfo[:, fc, half * 512:(half + 1) * 512],
                                                            start=(fc == 0), stop=(fc == 21)),
                                   reads=[t_hT, t_wfo], writes=[tpb], signal=(fc == 21))
                            op("dve", lambda e: e.tensor_tensor(out=xb[:, half * 512:(half + 1) * 512], in0=pb[:, :], in1=G2[:, half * 512:(half + 1) * 512], op=ALU.mult),
                               reads=[tpb, t_G2], writes=[txb])
                        op("dve", lambda e: e.tensor_tensor(out=xb[:], in0=xb[:], in1=x1t[j][:], op=ALU.add), reads=[t_x1t[j]], writes=[txb])
                        op("act", lambda e: e.activation(out=junkv, in_=xb[:], func=AF.Square, accum_out=small[:, 2:3]),
                           reads=[txb], writes=[t_u2T, t_small])
                        op("act", lambda e: e.activation(out=small[:, 3:4], in_=small[:, 2:3], func=AF.Sqrt, bias=eps_c, scale=1.0 / D),
                           reads=[t_small, t_cv], writes=[t_small])
                        op("dve", lambda e: e.reciprocal(out=small[:, 3:4], in_=small[:, 3:4]), reads=[t_small], writes=[t_small])
                        op("dve", lambda e: e.scalar_tensor_tensor(out=xb[:], in0=xb[:], scalar=small[:, 3:4], in1=fgbc[:], op0=ALU.mult, op1=ALU.mult),
                           reads=[t_small, t_fgbc], writes=[txb])
                        k.dma("pool", out_d[ti * 128:(ti + 1) * 128, :], xb[:], reads=[txb], writes=[t_out])

        if stage >= 3:
            with ExitStack() as stt:
                wfo = alloc(stt, "wfo", [128, 22, D], BF16); t_wfo = Tok()
                wfi0 = alloc(stt, "wfi0", [128, 8, 2 * 384], BF16); t_wfi0 = Tok()
                prefetch = []
                for part in range(2):
                    for kc in range(8):
                        prefetch.append(lambda part=part, kc=kc: k.dma("pool", wfi0[:, kc, part * 384:(part + 1) * 384],
                                                                      wfi_d[kc * 128:(kc + 1) * 128, part * DFF:part * DFF + 384], writes=[t_wfi0]))
                for fc in range(22):
                    prefetch.append(lambda fc=fc: k.dma("pool", wfo[:, fc, :], wfo_d[fc * 128:(fc + 1) * 128, :], writes=[t_wfo]))
                k.barrier()
                tail_a(prefetch)
                if stage >= 4:
                    k.barrier()
                    tail_b(wfo, t_wfo, wfi0, t_wfi0, prefetch)

        fin = list(t_ygla_scr) + [t_ym_scr, t_out] + t_x1_scr + dbg_list
        k.finish(fin, "sp")
        k.finish(fin, "pool")
        k.check_deadlock()
        print("build: ops", k.nops, "waits", k.nwaits, "cnt", {e: k.cnt[e] for e in k.eng})
    return nc


def _consts():
    c = np.zeros((128, 768), np.float32)
    m = np.arange(128)[:, None]
    l = np.arange(128)[None, :]
    c[:, 0:128] = np.eye(128)
    c[:, 128:256] = (m <= l)
    c[:, 256:384] = (m >= l)
    c[:, 384:512] = (m > l)
    c[:, 512:640] = (m < l)
    c[:, 640:768] = 1.0
    return c


def _variant(inp, flip):
    w_in = inp["w_in"][0]
    w_up = inp["gla_w_up"][0]
    b_dec = inp["gla_b_dec"][0]
    conv_w = inp["mlstm_conv_w"][0]
    b_gate = inp["mlstm_b_gate"][0]
    if flip:
        idx = np.arange(NIN)
        idx[3072:3088] = np.arange(3088, 3104)
        idx[3088:3104] = np.arange(3072, 3088)
        g0 = 6176
        idx[g0:g0 + 8] = np.arange(g0 + 8, g0 + 16)
        idx[g0 + 8:g0 + 16] = np.arange(g0, g0 + 8)
        w_in = w_in[:, idx]
        w_up = w_up[::-1]
        b_dec = b_dec[::-1]
        conv_w = conv_w[::-1]
        b_gate = b_gate[[2, 3, 0, 1]]
    b_ada = inp["b_ada"][0]
    pp = lambda v, n: np.asarray(v).reshape(n, 128).T
    vpp = np.concatenate([pp(b_ada, 48), pp(inp["norm1_g"][0], 8), pp(inp["norm2_g"][0], 8),
                          np.asarray(conv_w).reshape(3, 8, 128).transpose(2, 0, 1).reshape(128, 24),
                          pp(inp["mlstm_conv_b"][0], 8)], axis=1)
    vbc = np.concatenate([b_ada[2048:3072], b_ada[5120:6144], inp["final_g"], inp["gla_norm_g"][0],
                          inp["mlstm_norm_g"][0], np.asarray(b_gate).reshape(16)])[None, :]
    f = lambda a: np.ascontiguousarray(a, dtype=np.float32)
    return dict(w_in=f(w_in), w_up=f(w_up), bdec=f(np.asarray(b_dec).reshape(1, 1024)), vpp=f(vpp), vbc=f(vbc),
                w_ada=f(inp["w_ada"][0]), consts=_consts(), w_br_gla=f(inp["w_br_gla"][0]), w_br_m=f(inp["w_br_mlstm"][0]),
                w_out=f(inp["w_out"][0]), w_ffn_in=f(inp["w_ffn_in"][0]), w_ffn_out=f(inp["w_ffn_out"][0]))


def make_in_maps(inp, cores=None):
    inp = {k_: np.asarray(v) for k_, v in inp.items()}
    var = [_variant(inp, False), _variant(inp, True)]
    maps = []
    for b in range(4):
        for s in range(2):
            m = dict(var[s])
            xb = inp["x"][b]
            cb = inp["ctx"][b]
            if s:
                xb = xb[::-1]
                cb = cb[::-1]
            m["x"] = np.ascontiguousarray(xb, dtype=np.float32)
            m["ctx"] = np.ascontiguousarray(cb, dtype=np.float32)
            cc = np.stack([inp["c"][b], inp["c_ctx"]], -1).reshape(8, 128, 2).transpose(1, 0, 2)
            m["cct"] = np.ascontiguousarray(cc, dtype=np.float32)
            maps.append(m)
    return maps


def kernel(**inputs):
    nc = build(9)
    maps = make_in_maps(inputs)
    res = run_bass_kernel_spmd(nc, maps, core_ids=list(range(8)))
    out = np.zeros((4, T, D), np.float32)
    for b in range(4):
        out[b, 0:2048] = np.asarray(res.results[2 * b]["out"])
        out[b, 2048:] = np.asarray(res.results[2 * b + 1]["out"])[::-1]
    return out
```
